# Optimizing a Trainium2 kernel written in Bass

```python
import math
import jax, jax.numpy as jnp
from jax import lax
import numpy as np

D_MODEL = 2048
BATCH = 4
SEQ = 2048
DEPTH = 2

CTX_LEN = 256
GRID_W = 64
HEAD_DIM = 64
GROUP_WIDTH = D_MODEL // 4
A_HEADS = GROUP_WIDTH // HEAD_DIM
A_DK = HEAD_DIM
A_DV = HEAD_DIM
B_HEADS = GROUP_WIDTH // HEAD_DIM
B_DK = HEAD_DIM
B_DV = HEAD_DIM
C_DQK = HEAD_DIM
C_DV = 2 * HEAD_DIM
C_HEADS = GROUP_WIDTH // C_DV
D_WIDTH = GROUP_WIDTH
D_BLOCKS = 8
D_CONV = 4
D_CONV_LEFT = 1
RG_C = 8.0
MIX_WIDTH = A_HEADS * A_DV + B_HEADS * B_DV + C_HEADS * C_DV + D_WIDTH
IN_SIZES = (A_HEADS * A_DK, A_HEADS * A_DK, A_HEADS * A_DK, A_HEADS * A_DV, A_HEADS * A_DV,
            B_HEADS * B_DK, B_HEADS * B_DK, B_HEADS * B_DV, B_HEADS * B_DV,
            C_HEADS * 2 * C_DQK, C_HEADS * 2 * C_DQK, C_HEADS * C_DV,
            D_WIDTH, D_WIDTH)
IN_WIDTH = sum(IN_SIZES)
D_FF = ((8 * D_MODEL // 3 + 127) // 128) * 128
FFN_RESIDUAL = 0.5
N_MOD = 9
CHUNK = 16
Q_BLOCK = 128
ROPE_BASE = 10000.0
EPS = 1e-6

kernel_name = "hybrid_parallel_group_dit_trunk"


def rms_norm(t, g):
    tf = t.astype(jnp.float32)
    y = tf * lax.rsqrt(jnp.mean(tf * tf, axis=-1, keepdims=True) + EPS)
    return (y * g.astype(jnp.float32)).astype(t.dtype)


def modulate(t, shift, scale):
    return t * (1.0 + scale) + shift


def maybe_flip(t, axis, rev):
    return jnp.flip(t, axis=axis) if rev else t


def to_heads(t, h):
    b, n, _ = t.shape
    return t.reshape(b, n, h, -1).transpose(0, 2, 1, 3)


def from_heads(t):
    b, h, n, d = t.shape
    return t.transpose(0, 2, 1, 3).reshape(b, n, h * d)


def split_cols(z):
    return jnp.split(z, np.cumsum(IN_SIZES)[:-1].tolist(), axis=-1)


def axial_rope_tables(rows, head_dim):
    quarter = head_dim // 4
    inv_freq = ROPE_BASE ** (-jnp.arange(quarter, dtype=jnp.float32) / quarter)
    row = jnp.repeat(jnp.arange(rows, dtype=jnp.float32), GRID_W)
    col = jnp.tile(jnp.arange(GRID_W, dtype=jnp.float32), rows)
    ang = jnp.concatenate([row[:, None] * inv_freq, col[:, None] * inv_freq], axis=-1)
    return jnp.cos(ang), jnp.sin(ang)


def apply_rope(t, cos, sin):
    tf = t.astype(jnp.float32)
    t1, t2 = jnp.split(tf, 2, axis=-1)
    out = jnp.concatenate([t1 * cos - t2 * sin, t2 * cos + t1 * sin], axis=-1)
    return out.astype(t.dtype)


def lower_bounds_from_logits(lb_logits):
    cum = jnp.cumsum(jax.nn.softmax(lb_logits.astype(jnp.float32), axis=0), axis=0)
    return cum - cum[:1]


def gla_chunkwise(q, k, v, log_f, s0):
    out_dtype = v.dtype
    q, k, v, log_f = (t.astype(jnp.float32) for t in (q, k, v, log_f))
    b_, h, n_tok, dk = q.shape
    dv = v.shape[-1]
    n = n_tok // CHUNK
    rs = lambda t: t.reshape(b_, h, n, CHUNK, t.shape[-1])
    q, k, v, log_f = rs(q), rs(k), rs(v), rs(log_f)
    b = jnp.cumsum(log_f, axis=3)
    b_last = b[:, :, :, -1:]
    mask = jnp.tril(jnp.ones((CHUNK, CHUNK), bool))
    diff = b[:, :, :, :, None, :] - b[:, :, :, None, :, :]
    decay = jnp.exp(jnp.where(mask[:, :, None], diff, -jnp.inf))
    scores = jnp.einsum('bhnid,bhnjd,bhnijd->bhnij', q, k, decay)
    o_intra = jnp.einsum('bhnij,bhnjv->bhniv', scores, v)
    u = jnp.einsum('bhnjd,bhnjv->bhndv', k * jnp.exp(b_last - b), v)
    g = jnp.exp(b_last[:, :, :, 0])

    def step(s, xs):
        g_n, u_n = xs
        return g_n[..., None] * s + u_n, s

    s_final, s_prev = lax.scan(step, s0.astype(jnp.float32),
                               (jnp.moveaxis(g, 2, 0), jnp.moveaxis(u, 2, 0)))
    s_prev = jnp.moveaxis(s_prev, 0, 2)
    o_inter = jnp.einsum('bhnid,bhndv->bhniv', q * jnp.exp(b), s_prev)
    o = (o_intra + o_inter).reshape(b_, h, n_tok, dv)
    return o.astype(out_dtype), s_final


def bidirectional_gla(q_c, v_c, kf_c, q_x, v_x, kf_x):
    b_, h, _, dk = q_c.shape
    dv = v_c.shape[-1]
    o_c, o_x = 0.0, 0.0
    for d in range(2):
        rev = d == 1
        (k_c, lf_c), (k_x, lf_x) = kf_c[d], kf_x[d]
        s0 = jnp.zeros((b_, h, dk, dv), jnp.float32)
        oc, s_ctx = gla_chunkwise(maybe_flip(q_c, 2, rev), maybe_flip(k_c, 2, rev),
                                  maybe_flip(v_c, 2, rev), maybe_flip(lf_c, 2, rev), s0)
        ox, _ = gla_chunkwise(maybe_flip(q_x, 2, rev), maybe_flip(k_x, 2, rev),
                              maybe_flip(v_x, 2, rev), maybe_flip(lf_x, 2, rev), s_ctx)
        o_c = o_c + maybe_flip(oc, 2, rev)
        o_x = o_x + maybe_flip(ox, 2, rev)
    return o_c, o_x


def hgrn2_group(zc, zx, lower_bound, gain):
    def forget_key(zf, lb):
        lb_h = lb.astype(jnp.float32).reshape(A_HEADS, 1, A_DK)
        f = lb_h + (1.0 - lb_h) * jax.nn.sigmoid(to_heads(zf, A_HEADS).astype(jnp.float32))
        return (1.0 - f, jnp.log(f))

    def prep(z):
        zq, zff, zfb, zv, zg = z
        q = jax.nn.silu(to_heads(zq, A_HEADS))
        v = to_heads(zv, A_HEADS)
        kf = [forget_key(zff, lower_bound[0]), forget_key(zfb, lower_bound[1])]
        return q, v, kf, zg

    q_c, v_c, kf_c, g_c = prep(zc)
    q_x, v_x, kf_x, g_x = prep(zx)
    o_c, o_x = bidirectional_gla(q_c, v_c, kf_c, q_x, v_x, kf_x)
    out = lambda o, g: from_heads(rms_norm(o, gain)) * jax.nn.silu(g)
    return out(o_c, g_c), out(o_x, g_x)


def retention_group(zc, zx, gain, rope):
    log_decay = jnp.log1p(-jnp.exp2(-5.0 - jnp.arange(B_HEADS, dtype=jnp.float32)))

    def prep(z, use_rope):
        zq, zk, zv, zg = z
        q = to_heads(zq, B_HEADS)
        k = to_heads(zk, B_HEADS) * (B_DK ** -0.5)
        if use_rope:
            q, k = apply_rope(q, *rope), apply_rope(k, *rope)
        lf = jnp.broadcast_to(log_decay[:, None, None], q.shape)
        return q, to_heads(zv, B_HEADS), [(k, lf), (k, lf)], zg

    q_c, v_c, kf_c, g_c = prep(zc, False)
    q_x, v_x, kf_x, g_x = prep(zx, True)
    o_c, o_x = bidirectional_gla(q_c, v_c, kf_c, q_x, v_x, kf_x)
    out = lambda o, g: from_heads(rms_norm(o, gain)) * jax.nn.silu(g)
    return out(o_c, g_c), out(o_x, g_x)


def diff_attend(q1, q2, k1, k2, v, lam):
    scale = C_DQK ** -0.5
    s1 = jnp.einsum('bhqd,bhkd->bhqk', q1, k1).astype(jnp.float32) * scale
    s2 = jnp.einsum('bhqd,bhkd->bhqk', q2, k2).astype(jnp.float32) * scale
    a = jax.nn.softmax(s1, axis=-1) - lam * jax.nn.softmax(s2, axis=-1)
    return jnp.einsum('bhqk,bhkv->bhqv', a.astype(v.dtype), v)


def blocked_diff_attention(q1, q2, k1, k2, v, lam):
    b_, h, n_tok, d = q1.shape
    n = n_tok // Q_BLOCK
    qb = lambda t: jnp.moveaxis(t.reshape(b_, h, n, Q_BLOCK, d), 2, 0)
    out = lax.map(lambda xs: diff_attend(xs[0], xs[1], k1, k2, v, lam), (qb(q1), qb(q2)))
    return jnp.moveaxis(out, 0, 2).reshape(b_, h, n_tok, -1)


def diff_attn_group(zc, zx, lam_vecs, gain, rope, layer_idx, ctx_out):
    lam_init = 0.8 - 0.6 * math.exp(-0.3 * layer_idx)
    lv = lam_vecs.astype(jnp.float32)
    lam = jnp.exp(jnp.sum(lv[0] * lv[1])) - jnp.exp(jnp.sum(lv[2] * lv[3])) + lam_init

    def prep(z):
        zq, zk, zv = z
        q1, q2 = jnp.split(to_heads(zq, C_HEADS), 2, axis=-1)
        k1, k2 = jnp.split(to_heads(zk, C_HEADS), 2, axis=-1)
        return q1, q2, k1, k2, to_heads(zv, C_HEADS)

    q1c, q2c, k1c, k2c, vc = prep(zc)
    q1x, q2x, k1x, k2x, vx = prep(zx)
    q1x, q2x, k1x, k2x = (apply_rope(t, *rope) for t in (q1x, q2x, k1x, k2x))
    k1 = jnp.concatenate([k1c, k1x], axis=2)
    k2 = jnp.concatenate([k2c, k2x], axis=2)
    v = jnp.concatenate([vc, vx], axis=2)
    out = lambda o: from_heads(rms_norm(o, gain) * (1.0 - lam_init))
    o_x = out(blocked_diff_attention(q1x, q2x, k1, k2, v, lam))
    o_c = out(diff_attend(q1c, q2c, k1c, k2c, vc, lam)) if ctx_out else None
    return o_c, o_x


def depthwise_conv(t, w, b):
    out = lax.conv_general_dilated(t, w[:, None, :].astype(t.dtype), window_strides=(1,),
                                   padding=((D_CONV_LEFT, D_CONV - 1 - D_CONV_LEFT),),
                                   dimension_numbers=('NWC', 'WIO', 'NWC'),
                                   feature_group_count=t.shape[-1])
    return out + b


def block_diag_linear(t, w, b):
    b_, n, ch = t.shape
    y = jnp.einsum('btgi,gio->btgo', t.reshape(b_, n, w.shape[0], -1), w.astype(t.dtype))
    return y.reshape(b_, n, ch) + b


def rglru_coeffs(t, w_r, b_r, w_i, b_i, lam):
    t = t.astype(jnp.float32)
    r = jax.nn.sigmoid(block_diag_linear(t, w_r, b_r))
    i = jax.nn.sigmoid(block_diag_linear(t, w_i, b_i))
    log_a = -RG_C * r * jax.nn.softplus(-lam.astype(jnp.float32))
    return jnp.exp(log_a), jnp.sqrt(-jnp.expm1(2.0 * log_a)) * (i * t)


def linear_scan(a, u, h0):
    u = u.at[:, 0].add(a[:, 0] * h0)
    combine = lambda l, r: (r[0] * l[0], r[0] * l[1] + r[1])
    _, h = lax.associative_scan(combine, (a, u), axis=1)
    return h


def rglru_group(zc, zx, p):
    (xc, gc), (xx, gx) = zc, zx
    xc = depthwise_conv(xc, p['d_conv_w'], p['d_conv_b'])
    xx = depthwise_conv(xx, p['d_conv_w'], p['d_conv_b'])
    h_c_sum, h_x_sum = 0.0, 0.0
    for d in range(2):
        rev = d == 1
        coeffs = lambda t: rglru_coeffs(maybe_flip(t, 1, rev), p['d_w_r'][d], p['d_b_r'][d],
                                        p['d_w_i'][d], p['d_b_i'][d], p['d_lambda'][d])
        a_c, u_c = coeffs(xc)
        h_c = linear_scan(a_c, u_c, jnp.zeros((xc.shape[0], D_WIDTH), jnp.float32))
        a_x, u_x = coeffs(xx)
        h_x = linear_scan(a_x, u_x, h_c[:, -1])
        h_c_sum = h_c_sum + maybe_flip(h_c, 1, rev)
        h_x_sum = h_x_sum + maybe_flip(h_x, 1, rev)
    return (jax.nn.gelu(gc) * h_c_sum.astype(gc.dtype), jax.nn.gelu(gx) * h_x_sum.astype(gx.dtype))


def token_mixer(uc, ux, p, rope, layer_idx, ctx_out):
    zc = split_cols(uc @ p['w_in'])
    zx = split_cols(ux @ p['w_in'])
    a_c, a_x = hgrn2_group(zc[0:5], zx[0:5], p['lower_bound'], p['a_norm'])
    b_c, b_x = retention_group(zc[5:9], zx[5:9], p['b_norm'], rope)
    c_c, c_x = diff_attn_group(zc[9:12], zx[9:12], p['c_lambda'], p['c_norm'], rope, layer_idx, ctx_out)
    d_c, d_x = rglru_group(zc[12:14], zx[12:14], p)
    y_x = jnp.concatenate([a_x, b_x, c_x, d_x], axis=-1) @ p['w_out']
    y_c = jnp.concatenate([a_c, b_c, c_c, d_c], axis=-1) @ p['w_out'] if ctx_out else None
    return y_c, y_x


def ffn_sublayer(h, mod, p, ffn_idx, norm_idx):
    shift, scale, gate = mod
    u = modulate(rms_norm(h, p['norm_pre'][norm_idx]), shift, scale)
    g, up = jnp.split(u @ p['ffn_w_in'][ffn_idx], 2, axis=-1)
    y = (jax.nn.silu(g) * up) @ p['ffn_w_out'][ffn_idx]
    return h + FFN_RESIDUAL * gate * rms_norm(y, p['norm_post'][norm_idx])


def hybrid_layer(hc, hx, c_silu, cc_silu, p, rope, layer_idx, ctx_out):
    mod_x = jnp.split((c_silu @ p['w_ada'] + p['b_ada'])[:, None, :], N_MOD, axis=-1)
    mod_c = jnp.split((cc_silu @ p['w_ada'] + p['b_ada'])[None, None, :], N_MOD, axis=-1)
    hc = ffn_sublayer(hc, mod_c[0:3], p, 0, 0)
    hx = ffn_sublayer(hx, mod_x[0:3], p, 0, 0)
    uc = modulate(rms_norm(hc, p['norm_pre'][1]), mod_c[3], mod_c[4])
    ux = modulate(rms_norm(hx, p['norm_pre'][1]), mod_x[3], mod_x[4])
    y_c, y_x = token_mixer(uc, ux, p, rope, layer_idx, ctx_out)
    hx = hx + mod_x[5] * rms_norm(y_x, p['norm_post'][1])
    hx = ffn_sublayer(hx, mod_x[6:9], p, 1, 2)
    if ctx_out:
        hc = hc + mod_c[5] * rms_norm(y_c, p['norm_post'][1])
        hc = ffn_sublayer(hc, mod_c[6:9], p, 1, 2)
    else:
        hc = None
    return hc, hx


def setup_inputs(seed: int = 0) -> dict:
    key = jax.random.key(seed)
    ks = jax.random.split(key, 24)
    f32 = jnp.float32
    nrm = lambda k, shape, scale: jax.random.normal(k, shape, f32) * scale
    gain = lambda k, shape: 1.0 + 0.05 * jax.random.normal(k, shape, f32)
    bs = D_WIDTH // D_BLOCKS
    a0 = jax.random.uniform(ks[23], (DEPTH, 2, D_WIDTH), f32, 0.9, 0.999) ** (1.0 / RG_C)
    return {
        "x": nrm(ks[0], (BATCH, SEQ, D_MODEL), 1.0),
        "c": nrm(ks[1], (BATCH, D_MODEL), 1.0),
        "ctx": nrm(ks[2], (BATCH, CTX_LEN, D_MODEL), 1.0),
        "c_ctx": nrm(ks[3], (D_MODEL,), 1.0),
        "w_ada": nrm(ks[4], (DEPTH, D_MODEL, N_MOD * D_MODEL), 0.5 * D_MODEL ** -0.5),
        "b_ada": nrm(ks[5], (DEPTH, N_MOD * D_MODEL), 0.02),
        "norm_pre": gain(ks[6], (DEPTH, 3, D_MODEL)),
        "norm_post": gain(ks[7], (DEPTH, 3, D_MODEL)),
        "ffn_w_in": nrm(ks[8], (DEPTH, 2, D_MODEL, 2 * D_FF), D_MODEL ** -0.5),
        "ffn_w_out": nrm(ks[9], (DEPTH, 2, D_FF, D_MODEL), D_FF ** -0.5),
        "w_in": nrm(ks[10], (DEPTH, D_MODEL, IN_WIDTH), D_MODEL ** -0.5),
        "w_out": nrm(ks[11], (DEPTH, MIX_WIDTH, D_MODEL), MIX_WIDTH ** -0.5),
        "lb_logits": nrm(ks[12], (DEPTH, 2, A_HEADS * A_DK), 1.0),
        "a_norm": gain(ks[13], (DEPTH, A_DV)),
        "b_norm": gain(ks[14], (DEPTH, B_DV)),
        "c_lambda": nrm(ks[15], (DEPTH, 4, C_DQK), 0.1),
        "c_norm": gain(ks[16], (DEPTH, C_DV)),
        "d_conv_w": nrm(ks[17], (DEPTH, D_CONV, D_WIDTH), D_CONV ** -0.5),
        "d_conv_b": nrm(ks[18], (DEPTH, D_WIDTH), 0.02),
        "d_w_r": nrm(ks[19], (DEPTH, 2, D_BLOCKS, bs, bs), bs ** -0.5),
        "d_b_r": nrm(ks[20], (DEPTH, 2, D_WIDTH), 0.02),
        "d_w_i": nrm(ks[21], (DEPTH, 2, D_BLOCKS, bs, bs), bs ** -0.5),
        "d_b_i": nrm(ks[22], (DEPTH, 2, D_WIDTH), 0.02),
        "d_lambda": jnp.log(a0) - jnp.log1p(-a0),
    }


def reference(x, c, ctx, c_ctx, w_ada, b_ada, norm_pre, norm_post, ffn_w_in, ffn_w_out,
              w_in, w_out, lb_logits, a_norm, b_norm, c_lambda, c_norm, d_conv_w, d_conv_b,
              d_w_r, d_b_r, d_w_i, d_b_i, d_lambda):
    rows = x.shape[1] // GRID_W
    rope = axial_rope_tables(rows, HEAD_DIM)
    lower_bounds = lower_bounds_from_logits(lb_logits)
    c_silu = jax.nn.silu(c)
    cc_silu = jax.nn.silu(c_ctx)
    hc, hx = ctx, x
    for l in range(DEPTH):
        p = dict(w_ada=w_ada[l], b_ada=b_ada[l], norm_pre=norm_pre[l], norm_post=norm_post[l],
                 ffn_w_in=ffn_w_in[l], ffn_w_out=ffn_w_out[l], w_in=w_in[l], w_out=w_out[l],
                 lower_bound=lower_bounds[l], a_norm=a_norm[l], b_norm=b_norm[l],
                 c_lambda=c_lambda[l], c_norm=c_norm[l], d_conv_w=d_conv_w[l], d_conv_b=d_conv_b[l],
                 d_w_r=d_w_r[l], d_b_r=d_b_r[l], d_w_i=d_w_i[l], d_b_i=d_b_i[l], d_lambda=d_lambda[l])
        hc, hx = hybrid_layer(hc, hx, c_silu, cc_silu, p, rope, l, l < DEPTH - 1)
    return hx
```

```python
import math
import numpy as np
import ml_dtypes
import concourse.bass as bass
import concourse.mybir as mybir
from concourse.bass_utils import run_bass_kernel_spmd

F32 = mybir.dt.float32
BF16 = mybir.dt.bfloat16
AF = mybir.ActivationFunctionType
ALU = mybir.AluOpType

N_DMA_SEMS = 6
D = 2048
KC = 16
DFF = 5504
NJ = 43
T = 2304
TT = 384
NTT = 6
CTX = 256
SEQ = 2048
EPS = 1e-6
CH = 32
NCH = T // CH
INW = 7168
NCORE = 8
SZ_ADA = 256 * 9 * D
SZ_FIN = 256 * 2 * DFF
SZ_FOUT = DFF * 256
SZ_WIN = 256 * INW
SZ_WOUT = 256 * D
OFF_FIN = 0
OFF_WIN = OFF_FIN + 4 * SZ_FIN
NSH0 = OFF_WIN + 2 * SZ_WIN
OFF_ADA = 0
OFF_FOUT = OFF_ADA + 2 * SZ_ADA
OFF_WOUT = OFF_FOUT + 4 * SZ_FOUT
NSH1 = OFF_WOUT + 2 * SZ_WOUT
NSHG = (NSH0, NSH1)
NSH = NSH0 + NSH1
assert NSH % 2048 == 0


class Prog:
    def __init__(self, nc, arena_bytes):
        self.nc = nc
        self.ops = []
        self.lastw = {}
        self.readers = {}
        self.arena = nc.alloc_sbuf_tensor("arena", [128, arena_bytes // 4], F32)
        self.arena_bytes = arena_bytes
        self.off = 0
        self.hw = 0
        self.last_on = {}
        self.dma_hist = {}

    def alloc(self, shape, dtype=F32):
        esz = 4 if dtype == F32 else 2
        n = 1
        for s in shape[1:]:
            n *= s
        nbytes = (n * esz + 31) // 32 * 32
        assert self.off + nbytes <= self.arena_bytes, ("SBUF arena overflow", self.off, nbytes)
        a = self.arena[:, self.off // 4:(self.off + nbytes) // 4]
        if dtype != F32:
            a = a.bitcast(dtype)
        a = a[0:shape[0], 0:n]
        if len(shape) == 3:
            a = a.rearrange("p (a b) -> p a b", a=shape[1])
        elif len(shape) == 4:
            a = a.rearrange("p (a b c) -> p a b c", a=shape[1], b=shape[2])
        self.off += nbytes
        self.hw = max(self.hw, self.off)
        return a

    def mark(self):
        return self.off

    def release(self, m):
        self.barrier()
        self.off = m

    def _expand(self, keys):
        g = getattr(self, "groups", None)
        if not g:
            return keys
        out = []
        for k in keys:
            out.extend(g.get(k, (k,)))
        return out

    def op(self, eng, fn, reads=(), writes=(), dma=False, force=False):
        reads = self._expand(reads)
        writes = self._expand(writes)
        i = len(self.ops)
        deps = set()
        for r in reads:
            if r in self.lastw:
                deps.add(self.lastw[r])
        for w in writes:
            if w in self.lastw:
                deps.add(self.lastw[w])
            for rd in self.readers.get(w, ()):
                deps.add(rd)
        for r in reads:
            self.readers.setdefault(r, []).append(i)
        for w in writes:
            self.lastw[w] = i
            self.readers[w] = []
        self.ops.append(dict(eng=eng, fn=fn, deps=deps, dma=dma, force=(force or getattr(self, "force_all", False))))
        if dma:
            self.dma_hist.setdefault(eng, []).append(i)
        else:
            self.last_on[eng] = i
        return i

    def barrier(self):
        last = [v for v in self.last_on.values()]
        dm = []
        for q, h in self.dma_hist.items():
            dm += h[-N_DMA_SEMS:]
        engs = set(self.last_on.keys()) | set(self.dma_hist.keys()) | {"tensor", "vector", "scalar", "gpsimd", "sync"}
        for e in sorted(engs):
            i = len(self.ops)
            self.ops.append(dict(eng=e, fn=lambda en: en.nop(), deps=set(last) | set(dm), dma=False))
            self.last_on[e] = i
        self.lastw = {}
        self.readers = {}

    def mm(self, out, lhsT, rhs, start=True, stop=True, reads=(), writes=()):
        return self.op("tensor", lambda e: e.matmul(out, lhsT, rhs, start=start, stop=stop), reads, writes)

    def dma(self, q, out, in_, reads=(), writes=(), **kw):
        return self.op(q, lambda e: e.dma_start(out=out, in_=in_, **kw), reads, writes, dma=True)

    def V(self, fn, reads=(), writes=(), force=False):
        return self.op("vector", fn, reads, writes, force=force)

    def S(self, fn, reads=(), writes=()):
        return self.op("scalar", fn, reads, writes)

    def G(self, fn, reads=(), writes=()):
        return self.op("gpsimd", fn, reads, writes)

    def emit(self):
        nc = self.nc
        ops = self.ops
        need_sig = [False] * len(ops)
        for i, o in enumerate(ops):
            for d in o["deps"]:
                pd = ops[d]
                if pd["dma"]:
                    continue
                if pd["eng"] != o["eng"] or o["dma"] or o.get("force"):
                    need_sig[d] = True
        used = sorted({o["eng"] for o in ops})
        esem = {e: nc.alloc_semaphore("es_" + e) for e in used}
        dq = [e for e in used if any(o["dma"] and o["eng"] == e for o in ops)]
        dsems = {e: [nc.alloc_semaphore("ds_%s%d" % (e, k)) for k in range(N_DMA_SEMS)] for e in dq}
        cnt = {e: 0 for e in used}
        dcnt = {e: 0 for e in used}
        for i, o in enumerate(ops):
            if o["dma"]:
                k = dcnt[o["eng"]]
                dcnt[o["eng"]] += 1
                o["dsem"] = dsems[o["eng"]][k % N_DMA_SEMS]
                o["dval"] = 16 * (k // N_DMA_SEMS + 1)
            elif need_sig[i]:
                cnt[o["eng"]] += 1
                o["sig"] = cnt[o["eng"]]
        streams = {e: [] for e in used}
        for i, o in enumerate(ops):
            streams[o["eng"]].append(i)
        self.n_waits = 0

        def run_engine(ename, eng):
            waited = {}

            def wait(sem, val):
                key = id(sem)
                if waited.get(key, 0) >= val:
                    return
                waited[key] = val
                eng.wait_ge(sem, val)
                self.n_waits += 1

            for i in streams.get(ename, ()):
                o = ops[i]
                for d in sorted(o["deps"]):
                    pd = ops[d]
                    if pd["dma"]:
                        wait(pd["dsem"], pd["dval"])
                    elif "sig" in pd:
                        if pd["eng"] == ename and not o["dma"] and not o.get("force"):
                            continue
                        wait(esem[pd["eng"]], pd["sig"])
                if o["dma"]:
                    if o["dval"] > 16:
                        wait(o["dsem"], o["dval"] - 16)
                    o["fn"](eng).then_inc(o["dsem"], 16)
                else:
                    ins = o["fn"](eng)
                    if "sig" in o:
                        ins.then_inc(esem[ename], 1)
            if ename in dsems:
                done = {}
                for i in streams.get(ename, ()):
                    o = ops[i]
                    if o["dma"]:
                        done[id(o["dsem"])] = (o["dsem"], o["dval"])
                for sem, val in done.values():
                    wait(sem, val)

        with nc.Block() as block:
            if "sync" in streams:
                @block.sync
                def _(e):
                    run_engine("sync", e)
            if "tensor" in streams:
                @block.tensor
                def _(e):
                    run_engine("tensor", e)
            if "vector" in streams:
                @block.vector
                def _(e):
                    run_engine("vector", e)
            if "scalar" in streams:
                @block.scalar
                def _(e):
                    run_engine("scalar", e)
            if "gpsimd" in streams:
                @block.gpsimd
                def _(e):
                    run_engine("gpsimd", e)


def tile_segs(tt):
    if tt == 0:
        return [(0, 256, 1), (256, 384, 0)]
    return [(0, 384, 0)]


class Core:
    def __init__(self, nc, gathered):
        self.nc = nc
        self.gathered = gathered
        self.P = Prog(nc, 204 * 1024)
        P = self.P
        self.ps = [nc.alloc_psum_tensor("ps%d" % i, [128, 512], F32) for i in range(7)]
        self.psT = nc.alloc_psum_tensor("psT", [128, 1024], BF16)
        self.ps_rr = 0
        self.ext = {}
        P.force_all = True
        self.ones_bf = P.alloc([128, 128], BF16)
        self.ones1 = P.alloc([128, 128], BF16)
        self.ones64 = P.alloc([128, 128], BF16)
        self.ones128 = P.alloc([128, 128], BF16)
        P.G(lambda e: e.memset(self.ones_bf, 1.0 / D), [], ["c_ones"])
        P.G(lambda e: e.memset(self.ones1, 1.0), [], ["c_ones"])
        P.G(lambda e: e.memset(self.ones64, 1.0 / 64), [], ["c_ones"])
        P.G(lambda e: e.memset(self.ones128, 1.0 / 128), [], ["c_ones"])
        self.eps_t = P.alloc([128, 1])
        P.G(lambda e: e.memset(self.eps_t, EPS), [], ["eps_t"])
        self.one_t = P.alloc([128, 1])
        P.G(lambda e: e.memset(self.one_t, 1.0), [], ["eps_t"])
        self.ident = P.alloc([128, 128])
        self.ident_bf = P.alloc([128, 128], BF16)
        P.G(lambda e: e.memset(self.ident, 0.0), [], ["ident"])
        P.G(lambda e: e.affine_select(self.ident, self.ident, pattern=[[-1, 128]], compare_op=ALU.not_equal,
                                      fill=1.0, base=0, channel_multiplier=1), ["ident"], ["ident"])
        P.V(lambda e: e.tensor_copy(self.ident_bf, self.ident), ["ident"], ["ident_bf"])
        self.mask = [P.alloc([CH, CH]), P.alloc([CH, CH])]
        for d in range(2):
            m = self.mask[d]
            P.G(lambda e, m=m: e.memset(m, 1.0), [], [("mask", d)])
            pat, cm = ([[1, CH]], -1) if d == 0 else ([[-1, CH]], 1)
            P.G(lambda e, m=m, pat=pat, cm=cm: e.affine_select(m, m, pattern=pat, compare_op=ALU.is_ge, fill=0.0,
                                                              base=0, channel_multiplier=cm),
                [("mask", d)], [("mask", d)])
        P.force_all = False

    def din(self, name, shape, dtype=F32):
        if name not in self.ext:
            self.ext[name] = self.nc.dram_tensor(name, list(shape), dtype, kind="ExternalInput").ap()
        return self.ext[name]

    def dout(self, name, shape, dtype=F32):
        return self.nc.dram_tensor(name, list(shape), dtype, kind="ExternalOutput").ap()

    def dint(self, name, shape, dtype=F32, **kw):
        return self.nc.dram_tensor(name, list(shape), dtype, **kw).ap()

    def psum(self, lo=0, hi=6):
        i = lo + self.ps_rr % (hi - lo)
        self.ps_rr += 1
        return self.ps[i], ("ps", i)

    def gather_weights(self):
        P = self.P
        nc = self.nc
        self.G = []
        ccsem = nc.alloc_semaphore("ccsem")
        rg = [list(range(NCORE))]
        mk = P.mark()
        for gi, NS in enumerate(NSHG):
            R = NS // 2048
            wsh = self.din("wshard%d" % gi, [R, 2048])
            wbf = self.dint("wbf%d" % gi, [R, 2048], BF16)
            G = self.dint("wgath%d" % gi, [NCORE * R, 2048], BF16, addr_space="Shared")
            n = NS // 128
            cs = n // 32
            src = wsh.rearrange("a c -> (a c)").rearrange("(p n) -> p n", p=128)
            dst = wbf.rearrange("a c -> (a c)").rearrange("(p n) -> p n", p=128)
            bb = [P.alloc([128, cs], BF16) for _ in range(3)]
            for i in range(32):
                b, bk = bb[i % 3], ("castb", gi, i % 3)
                P.dma("gpsimd", b, src[:, i * cs:(i + 1) * cs], writes=[bk])
                P.dma("sync", dst[:, i * cs:(i + 1) * cs], b, reads=[bk], writes=[("wbf", gi)])

            def ccfn(e, wbf=wbf, G=G, gi=gi):
                e.collective_compute("AllGather", ALU.bypass, replica_groups=rg, ins=[wbf.opt()],
                                     outs=[G.opt()]).then_inc(ccsem)
                e.wait_ge(ccsem, gi + 1)
                return e.nop()
            P.op("gpsimd", ccfn, reads=[("wbf", gi)], writes=["wgath"])
            self.G.append(G.rearrange("(r a) c -> r (a c)", r=NCORE))
        P.release(mk)

    def _wmeta(self, name, idx):
        one = getattr(self, "test_one", False)
        if name == "w_ada":
            return 1, OFF_ADA + idx * SZ_ADA, 9 * D, [1 if one else 2, D, 9 * D]
        if name == "ffn_w_in":
            return 0, OFF_FIN + idx * SZ_FIN, 2 * DFF, [1 if one else 4, D, 2 * DFF]
        if name == "w_in":
            return 0, OFF_WIN + idx * SZ_WIN, INW, [2, D, INW]
        if name == "w_out":
            return 1, OFF_WOUT + idx * SZ_WOUT, D, [2, D, D]
        raise KeyError(name)

    def load_w(self, dst, name, idx, c0, ncols, key):
        P = self.P
        grp, off, cols, shp = self._wmeta(name, idx)
        if self.gathered:
            if not hasattr(P, "groups"):
                P.groups = {}
            P.groups[key] = [(key, "r", r) for r in range(NCORE)]
            for r in range(NCORE):
                src = self.G[grp][r:r + 1, off:off + 256 * cols].rearrange("o (kk p c) -> p (o kk) c", p=128, c=cols)[
                    :, :, c0:c0 + ncols]
                P.dma("gpsimd", dst[:, 2 * r:2 * r + 2, :], src, reads=["wgath"], writes=[(key, "r", r)])
        else:
            w = self.din(name, shp)
            src = w[idx].rearrange("(k p) c -> p k c", p=128)[:, :, c0:c0 + ncols]
            P.dma("gpsimd", dst, src, writes=[key])

    def load_w_fout(self, dst, idx, j0, j1, m, key):
        P = self.P
        if self.gathered:
            off = OFF_FOUT + idx * SZ_FOUT
            r = m // 2
            src = self.G[1][r:r + 1, off:off + SZ_FOUT].rearrange("o (j p c) -> p (o j) c", p=128, c=256)[
                :, j0:j1, (m % 2) * 128:(m % 2) * 128 + 128]
            P.dma("gpsimd", dst, src, reads=["wgath"], writes=[key])
        else:
            w = self.din("ffn_w_out", [1 if getattr(self, "test_one", False) else 4, DFF, D])
            src = w[idx].rearrange("(j p) c -> p j c", p=128)[:, j0:j1, m * 128:(m + 1) * 128]
            P.dma("gpsimd", dst, src, writes=[key])

    def load_small(self):
        P = self.P
        self.gpre = P.alloc([128, 2 * 3 * KC])
        self.gpost = P.alloc([128, 2 * 3 * KC])
        self.bada = P.alloc([128, 2 * 144])
        self.cs_f = P.alloc([128, 2 * KC])
        P.dma("sync", self.gpre, self.din("gpre", [128, 2 * 3 * KC]), writes=["gpre"])
        P.dma("sync", self.gpost, self.din("gpost", [128, 2 * 3 * KC]), writes=["gpost"])
        P.dma("sync", self.bada, self.din("bada", [128, 2 * 144]), writes=["bada"])
        P.dma("sync", self.cs_f, self.din("cvec", [128, 2 * KC]), writes=["cs_f"])
        self.csT = P.alloc([128, KC, 2], BF16)
        P.S(lambda e: e.activation(self.csT.rearrange("p k v -> p v k"),
                                   self.cs_f.rearrange("p (v k) -> p v k", v=2), AF.Silu),
            ["cs_f"], ["csT"])
        self.mod = [None, None]
        self.Apre = {}
        self.Cg = {}

    def emit_mods(self, l):
        P = self.P
        mod = P.alloc([128, 144, 2])
        self.mod[l] = mod
        for s in range(3):
            self.Apre[(l, s)] = P.alloc([128, KC, 2])
            self.Cg[(l, s)] = P.alloc([128, KC, 2])
        mk = P.mark()
        wb = [P.alloc([128, KC, 256], BF16) for _ in range(3)]
        pst, psk = self.ps[6], ("ps", 6)
        for g in range(72):
            b, bk = wb[g % 3], ("wada", g % 3)
            self.load_w(b, "w_ada", l, g * 256, 256, bk)
            for jj in range(2):
                cc = g * 2 + jj
                for k in range(KC):
                    P.mm(pst[:, cc * 2:cc * 2 + 2], b[:, k, jj * 128:(jj + 1) * 128], self.csT[:, k, :],
                         start=(k == 0), stop=(k == KC - 1), reads=[bk, "csT"], writes=[psk])
        P.force_all = True
        P.V(lambda e: e.tensor_tensor(mod, pst[:, 0:288].rearrange("p (c v) -> p c v", v=2),
                                      self.bada[:, l * 144:(l + 1) * 144].unsqueeze(2).to_broadcast([128, 144, 2]),
                                      ALU.add),
            [psk, "bada"], [("mod", l)])
        for s in range(3):
            A = self.Apre[(l, s)]
            C = self.Cg[(l, s)]
            sc = mod[:, (3 * s + 1) * KC:(3 * s + 2) * KC, :]
            gt = mod[:, (3 * s + 2) * KC:(3 * s + 3) * KC, :]
            o0 = (l * 3 + s) * KC
            gp = self.gpre[:, o0:o0 + KC].unsqueeze(2).to_broadcast([128, KC, 2])
            gq = self.gpost[:, o0:o0 + KC].unsqueeze(2).to_broadcast([128, KC, 2])
            P.V(lambda e, A=A, sc=sc, gp=gp: e.scalar_tensor_tensor(A, sc, 1.0, gp, ALU.add, ALU.mult),
                [("mod", l), "gpre"], [("A", l, s)])
            rs = 1.0 if s == 1 else 0.5
            P.V(lambda e, C=C, gt=gt, gq=gq, rs=rs: e.scalar_tensor_tensor(C, gt, rs, gq, ALU.mult, ALU.mult),
                [("mod", l), "gpost"], [("C", l, s)])
        P.force_all = False
        P.release(mk)

    def shift(self, l, s, k, v):
        return self.mod[l][:, 3 * s * KC + k, v:v + 1]

    def alloc_tile_bufs(self):
        P = self.P
        self.h = P.alloc([128, KC, TT])
        self.u = P.alloc([128, KC, TT], BF16)
        self.hT = P.alloc([128, NJ, TT], BF16)
        self.y = P.alloc([128, KC, TT])
        self.wb = [P.alloc([128, 4096], BF16) for _ in range(4)]
        self.wbi = 0
        self.sqb = [P.alloc([128, TT], BF16) for _ in range(2)]
        self.tmpf = [P.alloc([128, TT]) for _ in range(2)]
        self.rstd = P.alloc([128, TT])
        self.sg = [P.alloc([128, TT]) for _ in range(2)]

    def emit_norm_mod(self, l, s, src, src_key, dst, dst_key, segs):
        P = self.P
        pss, pssk = self.ps[6], ("ps", 6)
        rstd = self.rstd
        for k in range(KC):
            b, bk = self.sqb[k % 2], ("sqb", k % 2)
            P.S(lambda e, b=b, k=k: e.activation(b, src[:, k, :], AF.Square), [src_key], [bk])
            P.mm(pss[:, 0:TT], self.ones_bf, b, start=(k == 0), stop=(k == KC - 1),
                 reads=[bk, "c_ones"], writes=[pssk])
        P.S(lambda e: e.activation(rstd, pss[:, 0:TT], AF.Sqrt, bias=self.eps_t), [pssk, "eps_t"], ["rstd"])
        P.V(lambda e: e.reciprocal(rstd, rstd), ["rstd"], ["rstd"])
        A = self.Apre[(l, s)]
        for k in range(KC):
            t, tk = self.tmpf[k % 2], ("tmpf", k % 2)
            for (a, bnd, v) in segs:
                P.V(lambda e, t=t, k=k, a=a, bnd=bnd, v=v: e.scalar_tensor_tensor(
                    t[:, a:bnd], src[:, k, a:bnd], A[:, k, v:v + 1], rstd[:, a:bnd], ALU.mult, ALU.mult),
                    [src_key, "rstd", ("A", l, s)], [tk])
            for (a, bnd, v) in segs:
                P.S(lambda e, t=t, k=k, a=a, bnd=bnd, v=v: e.activation(
                    dst[:, k, a:bnd], t[:, a:bnd], AF.Identity, bias=self.shift(l, s, k, v)),
                    [tk, ("mod", l)], [dst_key])

    def emit_post_resid(self, l, s, segs):
        P = self.P
        pss, pssk = self.ps[6], ("ps", 6)
        rstd, h, y = self.rstd, self.h, self.y
        P.S(lambda e: e.activation(rstd, pss[:, 0:TT], AF.Sqrt, bias=self.eps_t), [pssk, "eps_t"], ["rstd"])
        P.V(lambda e: e.reciprocal(rstd, rstd), ["rstd"], ["rstd"])
        C = self.Cg[(l, s)]
        for k in range(KC):
            t, tk = self.tmpf[k % 2], ("tmpf", k % 2)
            for (a, bnd, v) in segs:
                P.V(lambda e, t=t, k=k, a=a, bnd=bnd, v=v: e.scalar_tensor_tensor(
                    t[:, a:bnd], y[:, k, a:bnd], C[:, k, v:v + 1], rstd[:, a:bnd], ALU.mult, ALU.mult),
                    [("y", k), "rstd", ("C", l, s)], [tk])
            P.V(lambda e, t=t, k=k: e.tensor_tensor(h[:, k, :], h[:, k, :], t, ALU.add), [tk, "h"], ["h"])

    def evac_y(self, m, py, pyk):
        P = self.P
        pss, pssk = self.ps[6], ("ps", 6)
        P.V(lambda e: e.tensor_copy(self.y[:, m, :], py[:, 0:TT]), [pyk], [("y", m)])
        import os
        if os.environ.get("KDBG2", "") == "ev1":
            return
        b2, b2k = self.sqb[m % 2], ("sqb", m % 2)
        P.S(lambda e: e.activation(b2, self.y[:, m, :], AF.Square), [("y", m)], [b2k])
        if os.environ.get("KDBG2", "") == "ev2":
            return
        P.mm(pss[:, 0:TT], self.ones_bf, b2, start=(m == 0), stop=(m == KC - 1),
             reads=[b2k, "c_ones"], writes=[pssk])

    def emit_ffn_tile(self, l, f, s, segs):
        P = self.P
        u, hT = self.u, self.hT
        wb = self.wb
        self.emit_norm_mod(l, s, self.h, "h", u, "u", segs)
        idx = l * 2 + f
        if not hasattr(self, "wc"):
            self.wc, self.wc_done = {}, set()
        if idx not in self.wc:
            self.wc[idx] = (self.dint("wci%d" % idx, [22, 2, 128, 4096], BF16),
                            self.dint("wco%d" % idx, [KC, 2, 128, 2816], BF16))
        wci, wco = self.wc[idx]
        cached = idx in self.wc_done
        import os
        dbg = int(os.environ.get("KDBG", "9"))
        if dbg < 2:
            return
        for g in range((NJ + 1) // 2):
            nj = 2 if 2 * g + 1 < NJ else 1
            bg, bu = wb[self.wbi % 4], wb[(self.wbi + 1) % 4]
            kg, ku = ("wb", self.wbi % 4), ("wb", (self.wbi + 1) % 4)
            self.wbi += 2
            bgv = bg[:, 0:KC * nj * 128].rearrange("p (k c) -> p k c", k=KC)
            buv = bu[:, 0:KC * nj * 128].rearrange("p (k c) -> p k c", k=KC)
            nel = KC * nj * 128
            if not cached:
                self.load_w(bgv, "ffn_w_in", idx, g * 256, nj * 128, kg)
                self.load_w(buv, "ffn_w_in", idx, DFF + g * 256, nj * 128, ku)
                P.dma("sync", wci[g, 0, :, 0:nel], bg[:, 0:nel], reads=[kg], writes=[("wc", idx, "i", g, 0)])
                P.dma("sync", wci[g, 1, :, 0:nel], bu[:, 0:nel], reads=[ku], writes=[("wc", idx, "i", g, 1)])
            else:
                P.dma("gpsimd", bg[:, 0:nel], wci[g, 0, :, 0:nel], reads=[("wc", idx, "i", g, 0)], writes=[kg])
                P.dma("gpsimd", bu[:, 0:nel], wci[g, 1, :, 0:nel], reads=[("wc", idx, "i", g, 1)], writes=[ku])
            for jj in range(nj):
                j = 2 * g + jj
                pg, pgk = self.psum()
                pu, puk = self.psum()
                for k in range(KC):
                    P.mm(pg[:, 0:TT], bgv[:, k, jj * 128:(jj + 1) * 128], u[:, k, :],
                         start=(k == 0), stop=(k == KC - 1), reads=[kg, "u"], writes=[pgk])
                for k in range(KC):
                    P.mm(pu[:, 0:TT], buv[:, k, jj * 128:(jj + 1) * 128], u[:, k, :],
                         start=(k == 0), stop=(k == KC - 1), reads=[ku, "u"], writes=[puk])
                sgb, sk = self.sg[j % 2], ("sg", j % 2)
                P.S(lambda e, sgb=sgb, pg=pg: e.activation(sgb, pg[:, 0:TT], AF.Silu), [pgk], [sk])
                P.V(lambda e, sgb=sgb, pu=pu, j=j: e.tensor_tensor(hT[:, j, :], sgb, pu[:, 0:TT], ALU.mult),
                    [sk, puk], [("hT", j)])
        if dbg < 3:
            return
        for m in range(KC):
            py, pyk = self.psum()
            for half in range(2):
                j0, j1 = (0, 22) if half == 0 else (22, NJ)
                b, bk = wb[self.wbi % 4], ("wb", self.wbi % 4)
                self.wbi += 1
                bv = b[:, 0:(j1 - j0) * 128].rearrange("p (j c) -> p j c", c=128)
                nel = (j1 - j0) * 128
                if not cached:
                    self.load_w_fout(bv, idx, j0, j1, m, bk)
                    P.dma("sync", wco[m, half, :, 0:nel], b[:, 0:nel], reads=[bk], writes=[("wc", idx, "o", m, half)])
                else:
                    P.dma("gpsimd", b[:, 0:nel], wco[m, half, :, 0:nel], reads=[("wc", idx, "o", m, half)], writes=[bk])
                if os.environ.get("KDBG2", "") == "dma":
                    continue
                for j in range(j0, j1):
                    P.mm(py[:, 0:TT], bv[:, j - j0, :], hT[:, j, :], start=(j == 0), stop=(j == NJ - 1),
                         reads=[bk, ("hT", j)], writes=[pyk])
            if os.environ.get("KDBG2", "") in ("dma", "mm"):
                continue
            self.evac_y(m, py, pyk)
        self.wc_done.add(idx)
        self.emit_post_resid(l, s, segs)

    def emit_wout_tile(self, l, tt, segs):
        P = self.P
        mixt = self.hT[:, 0:KC, :]
        t0 = tt * TT
        for k in range(KC):
            P.dma("sync", mixt[:, k, :], self.Mx[k * 128:(k + 1) * 128, t0:t0 + TT], reads=["Mx"], writes=[("hT", k)])
        for mp in range(KC // 2):
            b, bk = self.wb[self.wbi % 4], ("wb", self.wbi % 4)
            self.wbi += 1
            bv = b[:, 0:KC * 256].rearrange("p (k c) -> p k c", k=KC)
            self.load_w(bv, "w_out", l, mp * 256, 256, bk)
            for jj in range(2):
                m = 2 * mp + jj
                py, pyk = self.psum()
                for k in range(KC):
                    P.mm(py[:, 0:TT], bv[:, k, jj * 128:(jj + 1) * 128], mixt[:, k, :],
                         start=(k == 0), stop=(k == KC - 1), reads=[bk, ("hT", k)], writes=[pyk])
                self.evac_y(m, py, pyk)
        self.emit_post_resid(l, 1, segs)

    def load_mixer_params(self):
        P = self.P
        P.force_all = True
        ld = lambda name, shape: (P.alloc(shape), self.din(name, shape))
        def L(name, shape):
            t, src = ld(name, shape)
            P.dma("sync", t, src, writes=[name])
            return t
        self.lbl = L("lbl", [64, 2 * 2 * 8])
        self.anorm = L("anorm", [64, 2])
        self.bnorm = L("bnorm", [64, 2])
        self.cnorm = L("cnorm", [128, 2])
        self.clam = L("clam", [64, 2 * 4])
        self.convw = L("convw", [128, 2 * 4 * 4])
        self.convb = L("convb", [128, 2 * 4])
        self.dbr = L("dbr", [128, 2 * 2 * 4])
        self.dbi = L("dbi", [128, 2 * 2 * 4])
        self.dlam = L("dlam", [128, 2 * 2 * 4])
        self.cos_d = self.din("rope_cos", [64, T])
        self.sin_d = self.din("rope_sin", [64, T])
        self.lb1 = P.alloc([64, 16])
        self.omlb1 = P.alloc([64, 16])
        P.V(lambda e: e.tensor_tensor(self.lb1, self.lbl[:, 16:32], self.lbl[:, 0:16], ALU.subtract), ["lbl"], ["lb1"])
        P.S(lambda e: e.activation(self.lb1, self.lb1, AF.Sigmoid), ["lb1"], ["lb1"])
        P.V(lambda e: e.tensor_scalar(self.omlb1, self.lb1, -1.0, 1.0, ALU.mult, ALU.add), ["lb1"], ["omlb1"])
        self.zero_t = P.alloc([128, 1])
        P.G(lambda e: e.memset(self.zero_t, 0.0), [], ["eps_t"])
        self.cm = P.alloc([64, T])
        P.G(lambda e: e.memset(self.cm, 1.0), [], ["cm"])
        P.G(lambda e: e.memset(self.cm.rearrange("p (c i) -> p c i", i=CH)[:, :, 0:1], 0.0), ["cm"], ["cm"])
        self.neglam = P.alloc([128, 2])
        self.cgain = P.alloc([128, 2])
        prod = P.alloc([64, 2, 2])
        cl = self.clam.rearrange("p (l f) -> p l f", l=2)
        P.V(lambda e: e.tensor_tensor(prod[:, :, 0], cl[:, :, 0], cl[:, :, 1], ALU.mult), ["clam"], ["prod"])
        P.V(lambda e: e.tensor_tensor(prod[:, :, 1], cl[:, :, 2], cl[:, :, 3], ALU.mult), ["clam", "prod"], ["prod"])
        prodb = P.alloc([64, 4], BF16)
        P.V(lambda e: e.tensor_copy(prodb, prod.rearrange("p l f -> p (l f)")), ["prod"], ["prodb"])
        pl, plk = self.ps[6], ("ps", 6)
        P.mm(pl[:, 0:4], self.ones1[0:64, :], prodb, reads=["c_ones", "prodb"], writes=[plk])
        ex = P.alloc([128, 4])
        P.S(lambda e: e.activation(ex, pl[:, 0:4], AF.Exp), [plk], ["ex"])
        for l in range(2):
            li = 0.8 - 0.6 * math.exp(-0.3 * l)
            P.V(lambda e, l=l: e.tensor_tensor(self.neglam[:, l:l + 1], ex[:, 2 * l + 1:2 * l + 2], ex[:, 2 * l:2 * l + 1],
                                               ALU.subtract), ["ex"], ["neglam"])
            P.V(lambda e, l=l, li=li: e.tensor_scalar(self.neglam[:, l:l + 1], self.neglam[:, l:l + 1], -li, None, ALU.add),
                ["neglam"], ["neglam"])
            P.V(lambda e, l=l, li=li: e.tensor_scalar(self.cgain[:, l:l + 1], self.cnorm[:, l:l + 1], 1.0 - li, None, ALU.mult),
                ["cnorm"], ["cgain"])
        self.c1 = P.alloc([128, 16])
        P.S(lambda e: e.activation(self.c1, self.dlam, AF.Exp, scale=-1.0), ["dlam"], ["c1"])
        P.S(lambda e: e.activation(self.c1, self.c1, AF.Ln, bias=self.one_t), ["c1", "eps_t"], ["c1"])
        P.V(lambda e: e.tensor_scalar(self.c1, self.c1, -8.0, None, ALU.mult), ["c1"], ["c1"])
        P.force_all = False

    def alloc_scratch(self):
        self.H = self.dint("Hres", [128, KC, T])
        self.U = self.dint("Umix", [128, KC, T], BF16)
        self.Z = self.dint("Zfm", [INW, T])
        self.Zt = self.dint("Ztm", [T, 1536])
        self.Mx = self.dint("Mx", [2048, T], BF16)

    def emit_proj(self, l):
        P = self.P
        mk = P.mark()
        uf = P.alloc([128, KC, T], BF16)
        P.dma("sync", uf, self.U, reads=["U"], writes=["uf"])
        wb = [P.alloc([128, KC, 256], BF16) for _ in range(3)]
        ev = [P.alloc([128, 512]) for _ in range(4)]
        tbl = [(0, 512), (512, 1024), (1024, 1536), (1536, 2048), (2048, 2304)]
        n = 0
        for g in range(28):
            b, bk = wb[g % 3], ("pw", g % 3)
            self.load_w(b, "w_in", l, g * 256, 256, bk)
            for jj in range(2):
                r0 = g * 256 + jj * 128
                for (a, e_) in tbl:
                    nn = e_ - a
                    ps, psk = self.psum()
                    for k in range(KC):
                        P.mm(ps[:, 0:nn], b[:, k, jj * 128:(jj + 1) * 128], uf[:, k, a:e_],
                             start=(k == 0), stop=(k == KC - 1), reads=[bk, "uf"], writes=[psk])
                    evb, evk = ev[n % 4], ("ev", n % 4)
                    if n % 2 == 0:
                        P.V(lambda e, evb=evb, ps=ps, nn=nn: e.tensor_copy(evb[:, 0:nn], ps[:, 0:nn]), [psk], [evk])
                    else:
                        P.S(lambda e, evb=evb, ps=ps, nn=nn: e.copy(evb[:, 0:nn], ps[:, 0:nn]), [psk], [evk])
                    P.dma("sync", self.Z[r0:r0 + 128, a:e_], evb[:, 0:nn], reads=[evk], writes=[("Z", r0 // 128)])
                    n += 1
        for bi, blk in enumerate([3, 7, 11]):
            for half in range(2):
                b, bk = wb[n % 3], ("pw", n % 3)
                self.load_w(b, "w_in", l, blk * 512 + half * 256, 256, bk)
                for tb in range(T // 128):
                    ps, psk = self.psum()
                    for k in range(KC):
                        P.mm(ps[:, 0:256], uf[:, k, tb * 128:(tb + 1) * 128], b[:, k, :],
                             start=(k == 0), stop=(k == KC - 1), reads=[bk, "uf"], writes=[psk])
                    evb, evk = ev[n % 4], ("ev", n % 4)
                    if n % 2 == 0:
                        P.V(lambda e, evb=evb, ps=ps: e.tensor_copy(evb[:, 0:256], ps[:, 0:256]), [psk], [evk])
                    else:
                        P.S(lambda e, evb=evb, ps=ps: e.copy(evb[:, 0:256], ps[:, 0:256]), [psk], [evk])
                    c0 = bi * 512 + half * 256
                    P.dma("sync", self.Zt[tb * 128:(tb + 1) * 128, c0:c0 + 256], evb[:, 0:256], reads=[evk],
                          writes=[("Zt", bi, half, tb)])
                    n += 1
        P.release(mk)

    def emit_gla(self, l, kind, heads):
        P = self.P
        mk = P.mark()
        Fa = lambda: P.alloc([64, T])
        Ba = lambda: P.alloc([64, T], BF16)
        q, kk, lf, bf, b, tmp = [Fa() for _ in range(6)]
        E = tmp
        kh = [Ba(), Ba()]
        Z, Zt = self.Z, self.Zt
        three = lambda a: a.rearrange("p (c i) -> p c i", i=CH)
        LS = []
        for hi, h in enumerate(heads):
            ls = dict(qt=[Ba(), Ba()], kt=[Ba(), Ba()],
                      khT=[P.alloc([CH, NCH, 64], BF16) for _ in range(2)],
                      vtm=P.alloc([CH, NCH, 64], BF16), oacc=Fa(),
                      gdec=[P.alloc([64, NCH]) for _ in range(2)],
                      S32=[P.alloc([64, 64]) for _ in range(2)],
                      Sbf=[P.alloc([64, 64], BF16) for _ in range(2)],
                      sTb=[[P.alloc([CH, CH], BF16) for _ in range(2)] for _ in range(2)])
            LS.append(ls)
        for hi, h in enumerate(heads):
            ls = LS[hi]
            qt, kt, khT, vtm, oacc, gdec = ls["qt"], ls["kt"], ls["khT"], ls["vtm"], ls["oacc"], ls["gdec"]
            rows = lambda blk, h=h: Z[blk * 512 + 64 * h:blk * 512 + 64 * h + 64, :]

            def rows_sw(dst, blk, key, h=h):
                r0 = blk * 512 + 64 * h
                P.dma("sync", dst[0:32, :], Z[r0 + 32:r0 + 64, :], writes=[key])
                P.dma("sync", dst[32:64, :], Z[r0:r0 + 32, :], writes=[key])

            if kind == "A":
                P.dma("sync", q, rows(0), writes=["q"])
                P.dma("gpsimd", vtm, Zt[:, 64 * h:64 * h + 64].rearrange("(c p) d -> p c d", p=CH),
                      writes=[("vtm", hi)])
                P.S(lambda e: e.activation(q, q, AF.Silu), ["q"], ["q"])
            else:
                cos, sin = bf, b
                P.dma("sync", cos, self.cos_d, writes=["bf"])
                P.dma("sync", sin, self.sin_d, writes=["b"])
                P.dma("gpsimd", vtm, Zt[:, 512 + 64 * h:512 + 64 * h + 64].rearrange("(c p) d -> p c d", p=CH),
                      writes=[("vtm", hi)])
                for (dst, blk, sc) in ((q, 5, 1.0), (kk, 6, 0.125)):
                    dk = "q" if dst is q else "kk"
                    P.dma("sync", dst, rows(blk), writes=[dk])
                    rows_sw(tmp, blk, "tmp")
                    P.V(lambda e, dst=dst: e.tensor_tensor(dst, dst, cos, ALU.mult), [dk, "bf"], [dk])
                    P.V(lambda e: e.tensor_tensor(tmp, tmp, sin, ALU.mult), ["tmp", "b"], ["tmp"])
                    P.V(lambda e, dst=dst: e.tensor_tensor(dst, dst, tmp, ALU.add), [dk, "tmp"], [dk])
                    if sc != 1.0:
                        P.V(lambda e, dst=dst, sc=sc: e.tensor_scalar(dst, dst, sc, None, ALU.mult), [dk], [dk])
                lg = math.log1p(-2.0 ** (-5.0 - h))
            P.G(lambda e, oacc=oacc: e.memset(oacc, 0.0), [], [("oacc", hi)])
            for d in range(2):
                if kind == "A":
                    P.dma("sync", tmp, rows(1 + d), writes=["tmp"])
                    P.S(lambda e: e.activation(tmp, tmp, AF.Sigmoid), ["tmp"], ["tmp"])
                    if l == 0:
                        lbs, oms = self.zero_t[0:64, :], self.one_t[0:64, :]
                    else:
                        lbs, oms = self.lb1[:, d * 8 + h:d * 8 + h + 1], self.omlb1[:, d * 8 + h:d * 8 + h + 1]
                    P.V(lambda e, lbs=lbs, oms=oms: e.tensor_scalar(tmp, tmp, oms, lbs, ALU.mult, ALU.add),
                        ["tmp", "lb1", "omlb1", "eps_t"], ["tmp"])
                    P.V(lambda e: e.tensor_scalar(kk, tmp, -1.0, 1.0, ALU.mult, ALU.add), ["tmp"], ["kk"])
                    P.S(lambda e: e.activation(lf, tmp, AF.Ln), ["tmp"], ["lf"])
                elif d == 0:
                    P.G(lambda e, lg=lg: e.memset(lf, lg), [], ["lf"])
                P.V(lambda e: e.tensor_tensor_scan(bf, self.cm, lf, 0.0, ALU.mult, ALU.add), ["cm", "lf"], ["bf"])
                btot = three(bf)[:, :, CH - 1:CH]
                btb = btot.to_broadcast([64, NCH, CH])
                if d == 0:
                    P.V(lambda e: e.tensor_copy(b, bf), ["bf"], ["b"])
                else:
                    P.V(lambda e, btb=btb: e.tensor_tensor(three(b), btb, three(bf), ALU.subtract), ["bf"], ["b"])
                    P.V(lambda e: e.tensor_tensor(b, b, lf, ALU.add), ["b", "lf"], ["b"])
                P.S(lambda e: e.activation(E, b, AF.Exp), ["b"], ["tmp"])
                P.V(lambda e, d=d, qt=qt: e.tensor_tensor(qt[d], q, E, ALU.mult), ["q", "tmp"], [("qt", hi, d)])
                P.S(lambda e: e.activation(E, b, AF.Exp, scale=-1.0), ["b"], ["tmp"])
                P.V(lambda e, d=d, kt=kt: e.tensor_tensor(kt[d], kk, E, ALU.mult), ["kk", "tmp"], [("kt", hi, d)])
                P.V(lambda e, btb=btb: e.tensor_tensor(three(E), btb, three(b), ALU.subtract), ["bf", "b"], ["tmp"])
                P.S(lambda e: e.activation(E, E, AF.Exp), ["tmp"], ["tmp"])
                P.V(lambda e, d=d: e.tensor_tensor(kh[d], kk, E, ALU.mult), ["kk", "tmp"], [("kh", d)])
                P.S(lambda e, d=d, gdec=gdec, btot=btot: e.activation(gdec[d], btot.rearrange("p c o -> p (c o)"), AF.Exp),
                    ["bf"], [("gdec", hi, d)])
                for c0 in range(0, NCH, 8):
                    ptb, ptk = self.psT, "psT"
                    for c in range(c0, c0 + 8):
                        P.op("tensor", lambda e, d=d, c=c, c0=c0: e.transpose(
                            ptb[0:CH, (c - c0) * 64:(c - c0) * 64 + 64], kh[d][:, c * CH:(c + 1) * CH],
                            self.ident_bf[0:64, 0:64]), [("kh", d), "ident_bf"], [ptk])
                    P.V(lambda e, d=d, c0=c0, khT=khT: e.tensor_copy(
                        khT[d][:, c0:c0 + 8, :], ptb[0:CH, 0:512].rearrange("p (c k) -> p c k", k=64)),
                        [ptk], [("khT", hi, d)])
        order = [list(range(NCH)), list(range(7, -1, -1)) + list(range(NCH - 1, 7, -1))]
        for i in range(NCH):
            for hi in range(len(heads)):
                ls = LS[hi]
                qt, kt, khT, vtm, oacc, gdec = ls["qt"], ls["kt"], ls["khT"], ls["vtm"], ls["oacc"], ls["gdec"]
                S32, Sbf, sTb = ls["S32"], ls["Sbf"], ls["sTb"]
                for d in range(2):
                    c = order[d][i]
                    ts = slice(c * CH, (c + 1) * CH)
                    first = (i == 0)
                    pss_, psk = self.psum()
                    P.mm(pss_[0:CH, 0:CH], kt[d][:, ts], qt[d][:, ts], reads=[("kt", hi, d), ("qt", hi, d)], writes=[psk])
                    sT, sTk = sTb[d][i % 2], ("sT", hi, d, i % 2)
                    P.V(lambda e, sT=sT, pss_=pss_, d=d: e.tensor_tensor(sT, pss_[0:CH, 0:CH], self.mask[d], ALU.mult),
                        [psk, ("mask", d)], [sTk])
                    po, pok = self.psum()
                    if not first:
                        P.mm(po[0:64, 0:CH], Sbf[d], qt[d][:, ts], start=True, stop=False,
                             reads=[("Sbf", hi, d), ("qt", hi, d)], writes=[pok])
                    P.mm(po[0:64, 0:CH], vtm[:, c, :], sT, start=first, stop=True, reads=[("vtm", hi), sTk], writes=[pok])
                    P.V(lambda e, po=po, ts=ts, oacc=oacc: e.tensor_tensor(oacc[:, ts], oacc[:, ts], po[0:64, 0:CH], ALU.add),
                        [pok, ("oacc", hi)], [("oacc", hi)])
                    pu, puk = self.psum()
                    P.mm(pu[0:64, 0:64], khT[d][:, c, :], vtm[:, c, :], reads=[("khT", hi, d), ("vtm", hi)], writes=[puk])
                    if first:
                        P.V(lambda e, d=d, pu=pu, S32=S32: e.tensor_copy(S32[d], pu[0:64, 0:64]), [puk], [("S32", hi, d)])
                    else:
                        P.V(lambda e, d=d, pu=pu, c=c, S32=S32, gdec=gdec: e.scalar_tensor_tensor(
                            S32[d], S32[d], gdec[d][:, c:c + 1], pu[0:64, 0:64], ALU.mult, ALU.add),
                            [puk, ("S32", hi, d), ("gdec", hi, d)], [("S32", hi, d)])
                    P.S(lambda e, d=d, S32=S32, Sbf=Sbf: e.copy(Sbf[d], S32[d]), [("S32", hi, d)], [("Sbf", hi, d)])
        sqb = [P.alloc([64, 512], BF16) for _ in range(2)]
        ob = [P.alloc([64, 512], BF16) for _ in range(2)]
        rs = [P.alloc([64, 512]) for _ in range(2)]
        zg = q
        for hi, h in enumerate(heads):
            oacc = LS[hi]["oacc"]
            if kind == "A":
                mrow, gain, gblk = 64 * h, self.anorm[:, l:l + 1], 4
            else:
                mrow, gain, gblk = 512 + 64 * h, self.bnorm[:, l:l + 1], 8
            P.dma("sync", zg, Z[gblk * 512 + 64 * h:gblk * 512 + 64 * h + 64, :], writes=["q"])
            P.S(lambda e: e.activation(zg, zg, AF.Silu), ["q"], ["q"])
            for bi_, (a, e_) in enumerate([(0, 512), (512, 1024), (1024, 1536), (1536, 2048), (2048, 2304)]):
                nn = e_ - a
                sq, sqk = sqb[bi_ % 2], ("gsq", bi_ % 2)
                P.S(lambda e, sq=sq, a=a, e_=e_, nn=nn, oacc=oacc: e.activation(sq[:, 0:nn], oacc[:, a:e_], AF.Square),
                    [("oacc", hi)], [sqk])
                pn, pnk = self.psum()
                P.mm(pn[0:64, 0:nn], self.ones64[0:64, 0:64], sq[:, 0:nn], reads=[sqk, "c_ones"], writes=[pnk])
                r, rk = rs[bi_ % 2], ("grs", bi_ % 2)
                P.S(lambda e, r=r, pn=pn, nn=nn: e.activation(r[:, 0:nn], pn[0:64, 0:nn], AF.Sqrt, bias=self.eps_t[0:64, :]),
                    [pnk, "eps_t"], [rk])
                P.V(lambda e, r=r, nn=nn: e.reciprocal(r[:, 0:nn], r[:, 0:nn]), [rk], [rk])
                P.V(lambda e, r=r, a=a, e_=e_, nn=nn, oacc=oacc, gain=gain: e.scalar_tensor_tensor(
                    r[:, 0:nn], oacc[:, a:e_], gain, r[:, 0:nn], ALU.mult, ALU.mult),
                    [rk, ("oacc", hi), "anorm", "bnorm"], [rk])
                o_, ok_ = ob[bi_ % 2], ("gob", bi_ % 2)
                P.V(lambda e, o_=o_, r=r, a=a, e_=e_, nn=nn: e.tensor_tensor(o_[:, 0:nn], r[:, 0:nn], zg[:, a:e_], ALU.mult),
                    [rk, "q"], [ok_])
                P.dma("sync", self.Mx[mrow:mrow + 64, a:e_], o_[:, 0:nn], reads=[ok_], writes=[("Mx", mrow)])
        P.release(mk)

    def emit_diffattn(self, l, h, ctx_out):
        P = self.P
        mk = P.mark()
        Z, Zt = self.Z, self.Zt
        t1 = P.alloc([128, T])
        t2 = P.alloc([128, T])
        cos = P.alloc([128, T])
        sin = P.alloc([128, T])
        qb = P.alloc([128, T], BF16)
        kb_ = P.alloc([128, T], BF16)
        vtm = P.alloc([128, T // 128, 128], BF16)
        for hf in range(2):
            P.dma("sync", cos[64 * hf:64 * hf + 64, :], self.cos_d, writes=["cos"])
            P.dma("sync", sin[64 * hf:64 * hf + 64, :], self.sin_d, writes=["sin"])
        P.dma("gpsimd", vtm, Zt[:, 1024 + 128 * h:1024 + 128 * h + 128].rearrange("(kb p) d -> p kb d", p=128),
              writes=["vtm"])
        for (dst, blk) in ((qb, 9), (kb_, 10)):
            r0 = blk * 512 + 128 * h
            P.dma("sync", t1, Z[r0:r0 + 128, :], writes=["t1"])
            for hf in range(2):
                P.dma("sync", t2[64 * hf:64 * hf + 32, :], Z[r0 + 64 * hf + 32:r0 + 64 * hf + 64, :], writes=["t2"])
                P.dma("sync", t2[64 * hf + 32:64 * hf + 64, :], Z[r0 + 64 * hf:r0 + 64 * hf + 32, :], writes=["t2"])
            P.V(lambda e: e.tensor_tensor(t1, t1, cos, ALU.mult), ["t1", "cos"], ["t1"])
            P.V(lambda e: e.tensor_tensor(t2, t2, sin, ALU.mult), ["t2", "sin"], ["t2"])
            dk = "qb" if dst is qb else "kb"
            P.V(lambda e, dst=dst: e.tensor_tensor(dst, t1, t2, ALU.add), ["t1", "t2"], [dk])
        pex = [P.alloc([128, 512], BF16) for _ in range(3)]
        rr = [P.alloc([128, 512]) for _ in range(2)]
        oo = [P.alloc([128, 512]) for _ in range(2)]
        sqo = P.alloc([128, 512], BF16)
        outb = [P.alloc([128, 512], BF16) for _ in range(2)]
        neglam = self.neglam[:, l:l + 1]
        cgain = self.cgain[:, l:l + 1]
        qblocks = [(CTX + 512 * i, 512, T // 128) for i in range(4)]
        if ctx_out:
            qblocks.append((0, CTX, CTX // 128))
        npx = 0
        for qi, (q0, nq, nkb) in enumerate(qblocks):
            accO = [(self.ps[2], ("ps", 2)), (self.ps[3], ("ps", 3))]
            accS = [(self.ps[4], ("ps", 4)), (self.ps[5], ("ps", 5))]
            for kbi in range(nkb):
                for hf in range(2):
                    pS, pSk = self.psum(0, 2)
                    P.mm(pS[:, 0:nq], kb_[64 * hf:64 * hf + 64, kbi * 128:(kbi + 1) * 128],
                         qb[64 * hf:64 * hf + 64, q0:q0 + nq], reads=["kb", "qb"], writes=[pSk])
                    px, pxk = pex[npx % 3], ("pex", npx % 3)
                    npx += 1
                    P.S(lambda e, px=px, pS=pS, nq=nq: e.activation(px[:, 0:nq], pS[:, 0:nq], AF.Exp, scale=0.125),
                        [pSk], [pxk])
                    P.mm(accO[hf][0][:, 0:nq], vtm[:, kbi, :], px[:, 0:nq], start=(kbi == 0), stop=(kbi == nkb - 1),
                         reads=["vtm", pxk], writes=[accO[hf][1]])
                    P.mm(accS[hf][0][:, 0:nq], self.ones1, px[:, 0:nq], start=(kbi == 0), stop=(kbi == nkb - 1),
                         reads=["c_ones", pxk], writes=[accS[hf][1]])
            for hf in range(2):
                r, rk = rr[hf], ("rr", hf)
                o, ok_ = oo[hf], ("oo", hf)
                P.V(lambda e, r=r, hf=hf, nq=nq, accS=accS: e.reciprocal(r[:, 0:nq], accS[hf][0][:, 0:nq]),
                    [accS[hf][1]], [rk])
                P.V(lambda e, r=r, o=o, hf=hf, nq=nq, accO=accO: e.tensor_tensor(o[:, 0:nq], accO[hf][0][:, 0:nq],
                                                                                r[:, 0:nq], ALU.mult),
                    [accO[hf][1], rk], [ok_])
            o = oo[0]
            P.V(lambda e, nq=nq: e.scalar_tensor_tensor(oo[0][:, 0:nq], oo[1][:, 0:nq], neglam, oo[0][:, 0:nq],
                                                        ALU.mult, ALU.add),
                [("oo", 0), ("oo", 1), "neglam"], [("oo", 0)])
            P.S(lambda e, nq=nq: e.activation(sqo[:, 0:nq], oo[0][:, 0:nq], AF.Square), [("oo", 0)], ["sqo"])
            pn, pnk = self.ps[6], ("ps", 6)
            P.mm(pn[:, 0:nq], self.ones128, sqo[:, 0:nq], reads=["sqo", "c_ones"], writes=[pnk])
            r = rr[0]
            P.S(lambda e, nq=nq: e.activation(rr[0][:, 0:nq], pn[:, 0:nq], AF.Sqrt, bias=self.eps_t), [pnk, "eps_t"],
                [("rr", 0)])
            P.V(lambda e, nq=nq: e.reciprocal(rr[0][:, 0:nq], rr[0][:, 0:nq]), [("rr", 0)], [("rr", 0)])
            ob, obk = outb[qi % 2], ("outb", qi % 2)
            P.V(lambda e, nq=nq, ob=ob: e.scalar_tensor_tensor(ob[:, 0:nq], oo[0][:, 0:nq], cgain, rr[0][:, 0:nq],
                                                               ALU.mult, ALU.mult),
                [("oo", 0), ("rr", 0), "cgain"], [obk])
            P.dma("sync", self.Mx[1024 + 128 * h:1024 + 128 * h + 128, q0:q0 + nq], ob[:, 0:nq], reads=[obk],
                  writes=[("Mx", 1024 + 128 * h)])
        P.release(mk)

    def emit_rglru(self, l, cc):
        P = self.P
        mk = P.mark()
        Z = self.Z
        Fa = lambda: P.alloc([128, T])
        x, gate, xc, ra, ia, aa, uu, hs, hsum = [Fa() for _ in range(9)]
        xcb = P.alloc([128, T], BF16)
        outb = P.alloc([128, T], BF16)
        P.dma("sync", x, Z[12 * 512 + 128 * cc:12 * 512 + 128 * cc + 128, :], writes=["x"])
        P.dma("sync", gate, Z[13 * 512 + 128 * cc:13 * 512 + 128 * cc + 128, :], writes=["gate"])
        cw = lambda j: self.convw[:, (l * 4 + cc) * 4 + j:(l * 4 + cc) * 4 + j + 1]
        cb = self.convb[:, l * 4 + cc:l * 4 + cc + 1]
        P.S(lambda e: e.activation(xc, x, AF.Identity, bias=cb, scale=cw(1)), ["x", "convw", "convb"], ["xc"])
        for (a, e_) in ((0, CTX), (CTX, T)):
            P.V(lambda e, a=a, e_=e_: e.scalar_tensor_tensor(xc[:, a + 1:e_], x[:, a:e_ - 1], cw(0), xc[:, a + 1:e_],
                                                            ALU.mult, ALU.add), ["x", "xc", "convw"], ["xc"])
            P.V(lambda e, a=a, e_=e_: e.scalar_tensor_tensor(xc[:, a:e_ - 1], x[:, a + 1:e_], cw(2), xc[:, a:e_ - 1],
                                                            ALU.mult, ALU.add), ["x", "xc", "convw"], ["xc"])
            P.V(lambda e, a=a, e_=e_: e.scalar_tensor_tensor(xc[:, a:e_ - 2], x[:, a + 2:e_], cw(3), xc[:, a:e_ - 2],
                                                            ALU.mult, ALU.add), ["x", "xc", "convw"], ["xc"])
        P.V(lambda e: e.tensor_copy(xcb, xc), ["xc"], ["xcb"])
        wbd = [[P.alloc([128, 128], BF16) for _ in range(2)] for _ in range(2)]
        dwr = self.din("d_w_r", [2, 2, 8, 64, 64])
        dwi = self.din("d_w_i", [2, 2, 8, 64, 64])
        for d in range(2):
            for ri, src in enumerate((dwr, dwi)):
                w = wbd[d][ri]
                wk = ("wbd", d, ri)
                P.G(lambda e, w=w: e.memset(w, 0.0), [], [wk])
                for g2 in range(2):
                    P.dma("gpsimd", w[64 * g2:64 * g2 + 64, 64 * g2:64 * g2 + 64], src[l, d, 2 * cc + g2], writes=[wk])
        tbl = [(0, 512), (512, 1024), (1024, 1536), (1536, 2048), (2048, 2304)]
        for d in range(2):
            pidx = (l * 2 + d) * 4 + cc
            br = self.dbr[:, pidx:pidx + 1]
            bi = self.dbi[:, pidx:pidx + 1]
            c1 = self.c1[:, pidx:pidx + 1]
            for (a, e_) in tbl:
                nn = e_ - a
                pr, prk = self.psum()
                P.mm(pr[:, 0:nn], wbd[d][0], xcb[:, a:e_], reads=[("wbd", d, 0), "xcb"], writes=[prk])
                P.S(lambda e, pr=pr, a=a, e_=e_, nn=nn, br=br: e.activation(ra[:, a:e_], pr[:, 0:nn], AF.Sigmoid, bias=br),
                    [prk, "dbr"], ["ra"])
                pi_, pik = self.psum()
                P.mm(pi_[:, 0:nn], wbd[d][1], xcb[:, a:e_], reads=[("wbd", d, 1), "xcb"], writes=[pik])
                P.S(lambda e, pi_=pi_, a=a, e_=e_, nn=nn, bi=bi: e.activation(ia[:, a:e_], pi_[:, 0:nn], AF.Sigmoid, bias=bi),
                    [pik, "dbi"], ["ia"])
            P.S(lambda e, c1=c1: e.activation(aa, ra, AF.Exp, scale=c1), ["ra", "c1"], ["aa"])
            P.V(lambda e: e.tensor_tensor(ra, aa, aa, ALU.mult), ["aa", "ra"], ["ra"])
            P.S(lambda e: e.activation(ra, ra, AF.Sqrt, bias=self.one_t, scale=-1.0), ["ra", "eps_t"], ["ra"])
            P.V(lambda e: e.tensor_tensor(uu, ia, xc, ALU.mult), ["ia", "xc"], ["uu"])
            P.V(lambda e: e.tensor_tensor(uu, uu, ra, ALU.mult), ["uu", "ra"], ["uu"])
            if d == 0:
                P.V(lambda e: e.tensor_tensor_scan(hsum, aa, uu, 0.0, ALU.mult, ALU.add), ["aa", "uu"], ["hsum"])
            else:
                rv = lambda t_, a, e_: t_[:, a:e_][:, ::-1]
                P.V(lambda e: e.tensor_tensor_scan(rv(hs, 0, CTX), rv(aa, 0, CTX), rv(uu, 0, CTX), 0.0, ALU.mult, ALU.add),
                    ["aa", "uu"], ["hs"])
                P.V(lambda e: e.tensor_tensor_scan(rv(hs, CTX, T), rv(aa, CTX, T), rv(uu, CTX, T), hs[:, 0:1],
                                                   ALU.mult, ALU.add), ["aa", "uu", "hs"], ["hs"], force=True)
                P.V(lambda e: e.tensor_tensor(hsum, hsum, hs, ALU.add), ["hsum", "hs"], ["hsum"])
        P.S(lambda e: e.activation(x, gate, AF.Square), ["gate", "x"], ["x"])
        P.V(lambda e: e.tensor_scalar(x, x, 0.044715, 1.0, ALU.mult, ALU.add), ["x"], ["x"])
        P.V(lambda e: e.tensor_tensor(x, x, gate, ALU.mult), ["x", "gate"], ["x"])
        P.S(lambda e: e.activation(x, x, AF.Sigmoid, scale=1.5957691216057308), ["x"], ["x"])
        P.V(lambda e: e.tensor_tensor(x, x, gate, ALU.mult), ["x", "gate"], ["x"])
        P.V(lambda e: e.tensor_tensor(outb, x, hsum, ALU.mult), ["x", "hsum"], ["outb"])
        P.dma("sync", self.Mx[1536 + 128 * cc:1536 + 128 * cc + 128, :], outb, reads=["outb"], writes=[("Mx", 1536 + 128 * cc)])
        P.release(mk)

    def emit_pass(self, which, xT=None, out=None):
        P = self.P
        mk = P.mark()
        self.alloc_tile_bufs()
        h = self.h
        for tt in range(NTT):
            segs = tile_segs(tt)
            t0 = tt * TT
            if which == 0:
                P.dma("sync", h, xT[:, :, t0:t0 + TT], writes=["h"])
                self.emit_ffn_tile(0, 0, 0, segs)
                nl = 0
            elif which == 1:
                P.dma("sync", h, self.H[:, :, t0:t0 + TT], reads=[("H", tt)], writes=["h"])
                self.emit_wout_tile(0, tt, segs)
                self.emit_ffn_tile(0, 1, 2, segs)
                self.emit_ffn_tile(1, 0, 0, segs)
                nl = 1
            else:
                P.dma("sync", h, self.H[:, :, t0:t0 + TT], reads=[("H", tt)], writes=["h"])
                self.emit_wout_tile(1, tt, segs)
                self.emit_ffn_tile(1, 1, 2, segs)
                if tt == 0:
                    P.dma("sync", out[:, :, 0:128], h[:, :, 256:384], reads=["h"])
                else:
                    o0 = 128 + (tt - 1) * TT
                    P.dma("sync", out[:, :, o0:o0 + TT], h, reads=["h"])
                continue
            self.emit_norm_mod(nl, 1, h, "h", self.u, "u", segs)
            P.dma("sync", self.U[:, :, t0:t0 + TT], self.u, reads=["u"], writes=[("U", tt)])
            P.dma("sync", self.H[:, :, t0:t0 + TT], h, reads=["h"], writes=[("H", tt)])
        P.release(mk)

    def emit_mixer(self, l, parts="ABCD"):
        self.emit_proj(l)
        if "A" in parts:
            for h in range(0, 8, 2):
                self.emit_gla(l, "A", [h, h + 1])
        if "B" in parts:
            for h in range(0, 8, 2):
                self.emit_gla(l, "B", [h, h + 1])
        if "C" in parts:
            for h in range(4):
                self.emit_diffattn(l, h, l == 0)
        if "D" in parts:
            for cc in range(4):
                self.emit_rglru(l, cc)


def build_full(gathered=True):
    nc = bass.Bass("TRN2", target_bir_lowering=False)
    C = Core(nc, gathered)
    if gathered:
        C.gather_weights()
    C.load_small()
    C.load_mixer_params()
    C.alloc_scratch()
    xT = C.din("xT", [128, KC, T])
    out = C.dout("out", [128, KC, SEQ])
    C.emit_mods(0)
    C.emit_mods(1)
    C.emit_pass(0, xT=xT)
    C.emit_mixer(0)
    C.emit_pass(1)
    C.emit_mixer(1)
    C.emit_pass(2, out=out)
    C.P.emit()
    return nc, C


def build_pass0_test():
    nc = bass.Bass("TRN2", target_bir_lowering=False)
    C = Core(nc, False)
    C.test_one = True
    C.load_small()
    C.H = C.dout("H_out", [128, KC, T])
    C.U = C.dout("U_out", [128, KC, T], BF16)
    xT = C.din("xT", [128, KC, T])
    C.emit_mods(0)
    modo = C.dout("mod_out", [128, 144, 2])
    C.P.dma("sync", modo, C.mod[0], reads=[("mod", 0)])
    C.emit_pass(0, xT=xT)
    C.P.emit()
    return nc, C


def build_mixer_test(l, parts):
    nc = bass.Bass("TRN2", target_bir_lowering=False)
    C = Core(nc, False)
    C.load_mixer_params()
    C.alloc_scratch()
    uin = C.din("U_in", [128, KC, T], BF16)
    mout = C.dout("Mx_out", [2048, T], BF16)
    C.P.dma("sync", C.U, uin, writes=["U"])
    C.P.barrier()
    C.emit_mixer(l, parts)
    C.P.dma("sync", mout, C.Mx, reads=[("Mx", r) for r in range(0, 2048, 64)])
    C.P.emit()
    return nc, C


def fm(a):
    a = np.asarray(a, np.float32)
    lead = a.shape[:-1]
    r = a.reshape(lead + (a.shape[-1] // 128, 128))
    return np.ascontiguousarray(np.moveaxis(r, -1, 0))


def tokens_T(x, ctx, b):
    t = np.concatenate([ctx[b], x[b]], 0)
    return np.ascontiguousarray(t.T.reshape(KC, 128, T).transpose(1, 0, 2))


def rope_tables():
    quarter = 16
    inv = 10000.0 ** (-np.arange(quarter, dtype=np.float32) / quarter)
    rows = SEQ // 64
    row = np.repeat(np.arange(rows, dtype=np.float32), 64)
    col = np.tile(np.arange(64, dtype=np.float32), rows)
    ang = np.concatenate([row[:, None] * inv, col[:, None] * inv], -1).astype(np.float32)
    cos = np.cos(ang).T
    sin = np.sin(ang).T
    c2 = np.ones((64, T), np.float32)
    s2 = np.zeros((64, T), np.float32)
    c2[0:32, CTX:] = cos
    c2[32:64, CTX:] = cos
    s2[0:32, CTX:] = -sin
    s2[32:64, CTX:] = sin
    return c2, s2


def small_inputs(inp, b):
    f32 = lambda a: np.asarray(a, np.float32)
    d = {}
    d["gpre"] = fm(inp["norm_pre"]).reshape(128, -1)
    d["gpost"] = fm(inp["norm_post"]).reshape(128, -1)
    d["bada"] = np.ascontiguousarray(f32(inp["b_ada"]).reshape(2, 144, 128).transpose(2, 0, 1)).reshape(128, -1)
    d["cvec"] = fm(np.stack([f32(inp["c"])[b], f32(inp["c_ctx"])], 0)).reshape(128, -1)
    return d


def mixer_inputs(inp):
    f32 = lambda a: np.asarray(a, np.float32)
    d = {}
    d["lbl"] = np.ascontiguousarray(f32(inp["lb_logits"]).reshape(2, 2, 8, 64).transpose(3, 0, 1, 2)).reshape(64, -1)
    d["anorm"] = np.ascontiguousarray(f32(inp["a_norm"]).T)
    d["bnorm"] = np.ascontiguousarray(f32(inp["b_norm"]).T)
    d["cnorm"] = np.ascontiguousarray(f32(inp["c_norm"]).T)
    d["clam"] = np.ascontiguousarray(f32(inp["c_lambda"]).transpose(2, 0, 1)).reshape(64, -1)
    d["convw"] = np.ascontiguousarray(f32(inp["d_conv_w"]).reshape(2, 4, 4, 128).transpose(3, 0, 2, 1)).reshape(128, -1)
    d["convb"] = np.ascontiguousarray(f32(inp["d_conv_b"]).reshape(2, 4, 128).transpose(2, 0, 1)).reshape(128, -1)
    for nm, src in (("dbr", "d_b_r"), ("dbi", "d_b_i"), ("dlam", "d_lambda")):
        d[nm] = np.ascontiguousarray(f32(inp[src]).reshape(2, 2, 4, 128).transpose(3, 0, 1, 2)).reshape(128, -1)
    c2, s2 = rope_tables()
    d["rope_cos"] = c2
    d["rope_sin"] = s2
    d["d_w_r"] = f32(inp["d_w_r"])
    d["d_w_i"] = f32(inp["d_w_i"])
    return d


def weight_shards(inp, r):
    f32 = lambda a: np.asarray(a, np.float32)
    rs = slice(256 * r, 256 * r + 256)
    p0, p1 = [], []
    for l in range(2):
        for f in range(2):
            p0.append(f32(inp["ffn_w_in"])[l, f, rs, :].reshape(-1))
    for l in range(2):
        p0.append(f32(inp["w_in"])[l, rs, :].reshape(-1))
    for l in range(2):
        p1.append(f32(inp["w_ada"])[l, rs, :].reshape(-1))
    for l in range(2):
        for f in range(2):
            p1.append(np.ascontiguousarray(f32(inp["ffn_w_out"])[l, f, :, rs]).reshape(-1))
    for l in range(2):
        p1.append(f32(inp["w_out"])[l, rs, :].reshape(-1))
    w0 = np.concatenate(p0)
    w1 = np.concatenate(p1)
    assert w0.size == NSH0 and w1.size == NSH1
    return w0.reshape(NSH0 // 2048, 2048), w1.reshape(NSH1 // 2048, 2048)


_CACHE = {}


USE_GATHER = False
N_USED = 4


def kernel(**inp):
    if "nc" not in _CACHE:
        _CACHE["nc"] = build_full(USE_GATHER)[0]
    nc = _CACHE["nc"]
    f32 = lambda a: np.ascontiguousarray(np.asarray(a, np.float32))
    mi = mixer_inputs(inp)
    shared = {}
    if not USE_GATHER:
        shared = dict(w_ada=f32(inp["w_ada"]), ffn_w_in=f32(inp["ffn_w_in"]).reshape(4, D, 2 * DFF),
                      ffn_w_out=f32(inp["ffn_w_out"]).reshape(4, DFF, D), w_in=f32(inp["w_in"]), w_out=f32(inp["w_out"]))
    in_maps = []
    for c in range(N_USED):
        b = c % 4
        m = dict(mi)
        m.update(shared)
        m.update(small_inputs(inp, b))
        m["xT"] = tokens_T(np.asarray(inp["x"], np.float32), np.asarray(inp["ctx"], np.float32), b)
        if USE_GATHER:
            m["wshard0"], m["wshard1"] = weight_shards(inp, c)
        in_maps.append(m)
    res = run_bass_kernel_spmd(nc, in_maps, core_ids=list(range(N_USED)))
    outs = []
    for b in range(4):
        o = np.asarray(res.results[b]["out"])
        outs.append(o.transpose(2, 1, 0).reshape(SEQ, D))
    return np.stack(outs, 0).astype(np.float32)
```

```python
import math
import numpy as np
import ml_dtypes
import concourse.bass as bass
import concourse.mybir as mybir
from concourse.bass_utils import run_bass_kernel_spmd

F32 = mybir.dt.float32
BF16 = mybir.dt.bfloat16
AF = mybir.ActivationFunctionType
ALU = mybir.AluOpType

N_DMA_SEMS = 6
D = 2048
KC = 16
DFF = 5504
NJ = 43
T = 2304
TT = 384
NTT = 6
CTX = 256
SEQ = 2048
EPS = 1e-6
CH = 32
NCH = T // CH
INW = 7168
NCORE = 8
SZ_ADA = 256 * 9 * D
SZ_FIN = 256 * 2 * DFF
SZ_FOUT = DFF * 256
SZ_WIN = 256 * INW
SZ_WOUT = 256 * D
OFF_FIN = 0
OFF_WIN = OFF_FIN + 4 * SZ_FIN
NSH0 = OFF_WIN + 2 * SZ_WIN
OFF_ADA = 0
OFF_FOUT = OFF_ADA + 2 * SZ_ADA
OFF_WOUT = OFF_FOUT + 4 * SZ_FOUT
NSH1 = OFF_WOUT + 2 * SZ_WOUT
NSHG = (NSH0, NSH1)
NSH = NSH0 + NSH1
assert NSH % 2048 == 0


class Prog:
    def __init__(self, nc, arena_bytes):
        self.nc = nc
        self.ops = []
        self.lastw = {}
        self.readers = {}
        self.arena = nc.alloc_sbuf_tensor("arena", [128, arena_bytes // 4], F32)
        self.arena_bytes = arena_bytes
        self.off = 0
        self.hw = 0
        self.last_on = {}
        self.dma_hist = {}

    def alloc(self, shape, dtype=F32):
        esz = 4 if dtype == F32 else 2
        n = 1
        for s in shape[1:]:
            n *= s
        nbytes = (n * esz + 31) // 32 * 32
        assert self.off + nbytes <= self.arena_bytes, ("SBUF arena overflow", self.off, nbytes)
        a = self.arena[:, self.off // 4:(self.off + nbytes) // 4]
        if dtype != F32:
            a = a.bitcast(dtype)
        a = a[0:shape[0], 0:n]
        if len(shape) == 3:
            a = a.rearrange("p (a b) -> p a b", a=shape[1])
        elif len(shape) == 4:
            a = a.rearrange("p (a b c) -> p a b c", a=shape[1], b=shape[2])
        self.off += nbytes
        self.hw = max(self.hw, self.off)
        return a

    def mark(self):
        return self.off

    def release(self, m):
        self.barrier()
        self.off = m

    def _expand(self, keys):
        g = getattr(self, "groups", None)
        if not g:
            return keys
        out = []
        for k in keys:
            out.extend(g.get(k, (k,)))
        return out

    def op(self, eng, fn, reads=(), writes=(), dma=False, force=False):
        reads = self._expand(reads)
        writes = self._expand(writes)
        i = len(self.ops)
        deps = set()
        for r in reads:
            if r in self.lastw:
                deps.add(self.lastw[r])
        for w in writes:
            if w in self.lastw:
                deps.add(self.lastw[w])
            for rd in self.readers.get(w, ()):
                deps.add(rd)
        for r in reads:
            self.readers.setdefault(r, []).append(i)
        for w in writes:
            self.lastw[w] = i
            self.readers[w] = []
        self.ops.append(dict(eng=eng, fn=fn, deps=deps, dma=dma, force=(force or getattr(self, "force_all", False))))
        if dma:
            self.dma_hist.setdefault(eng, []).append(i)
        else:
            self.last_on[eng] = i
        return i

    def barrier(self):
        last = [v for v in self.last_on.values()]
        dm = []
        for q, h in self.dma_hist.items():
            dm += h[-N_DMA_SEMS:]
        engs = set(self.last_on.keys()) | set(self.dma_hist.keys()) | {"tensor", "vector", "scalar", "gpsimd", "sync"}
        for e in sorted(engs):
            i = len(self.ops)
            self.ops.append(dict(eng=e, fn=lambda en: en.nop(), deps=set(last) | set(dm), dma=False))
            self.last_on[e] = i
        self.lastw = {}
        self.readers = {}

    def mm(self, out, lhsT, rhs, start=True, stop=True, reads=(), writes=()):
        return self.op("tensor", lambda e: e.matmul(out, lhsT, rhs, start=start, stop=stop), reads, writes)

    def dma(self, q, out, in_, reads=(), writes=(), **kw):
        return self.op(q, lambda e: e.dma_start(out=out, in_=in_, **kw), reads, writes, dma=True)

    def V(self, fn, reads=(), writes=(), force=False):
        return self.op("vector", fn, reads, writes, force=force)

    def S(self, fn, reads=(), writes=()):
        return self.op("scalar", fn, reads, writes)

    def G(self, fn, reads=(), writes=()):
        return self.op("gpsimd", fn, reads, writes)

    def emit(self):
        nc = self.nc
        ops = self.ops
        need_sig = [False] * len(ops)
        for i, o in enumerate(ops):
            for d in o["deps"]:
                pd = ops[d]
                if pd["dma"]:
                    continue
                if pd["eng"] != o["eng"] or o["dma"] or o.get("force"):
                    need_sig[d] = True
        used = sorted({o["eng"] for o in ops})
        esem = {e: nc.alloc_semaphore("es_" + e) for e in used}
        dq = [e for e in used if any(o["dma"] and o["eng"] == e for o in ops)]
        dsems = {e: [nc.alloc_semaphore("ds_%s%d" % (e, k)) for k in range(N_DMA_SEMS)] for e in dq}
        cnt = {e: 0 for e in used}
        dcnt = {e: 0 for e in used}
        for i, o in enumerate(ops):
            if o["dma"]:
                k = dcnt[o["eng"]]
                dcnt[o["eng"]] += 1
                o["dsem"] = dsems[o["eng"]][k % N_DMA_SEMS]
                o["dval"] = 16 * (k // N_DMA_SEMS + 1)
            elif need_sig[i]:
                cnt[o["eng"]] += 1
                o["sig"] = cnt[o["eng"]]
        streams = {e: [] for e in used}
        for i, o in enumerate(ops):
            streams[o["eng"]].append(i)
        self.n_waits = 0

        def run_engine(ename, eng):
            waited = {}

            def wait(sem, val):
                key = id(sem)
                if waited.get(key, 0) >= val:
                    return
                waited[key] = val
                eng.wait_ge(sem, val)
                self.n_waits += 1

            for i in streams.get(ename, ()):
                o = ops[i]
                for d in sorted(o["deps"]):
                    pd = ops[d]
                    if pd["dma"]:
                        wait(pd["dsem"], pd["dval"])
                    elif "sig" in pd:
                        if pd["eng"] == ename and not o["dma"] and not o.get("force"):
                            continue
                        wait(esem[pd["eng"]], pd["sig"])
                if o["dma"]:
                    if o["dval"] > 16:
                        wait(o["dsem"], o["dval"] - 16)
                    o["fn"](eng).then_inc(o["dsem"], 16)
                else:
                    ins = o["fn"](eng)
                    if "sig" in o:
                        ins.then_inc(esem[ename], 1)
            if ename in dsems:
                done = {}
                for i in streams.get(ename, ()):
                    o = ops[i]
                    if o["dma"]:
                        done[id(o["dsem"])] = (o["dsem"], o["dval"])
                for sem, val in done.values():
                    wait(sem, val)

        with nc.Block() as block:
            if "sync" in streams:
                @block.sync
                def _(e):
                    run_engine("sync", e)
            if "tensor" in streams:
                @block.tensor
                def _(e):
                    run_engine("tensor", e)
            if "vector" in streams:
                @block.vector
                def _(e):
                    run_engine("vector", e)
            if "scalar" in streams:
                @block.scalar
                def _(e):
                    run_engine("scalar", e)
            if "gpsimd" in streams:
                @block.gpsimd
                def _(e):
                    run_engine("gpsimd", e)


def tile_segs(tt):
    if tt == 0:
        return [(0, 256, 1), (256, 384, 0)]
    return [(0, 384, 0)]


class Core:
    def __init__(self, nc, gathered):
        self.nc = nc
        self.gathered = gathered
        self.P = Prog(nc, 204 * 1024)
        P = self.P
        self.ps = [nc.alloc_psum_tensor("ps%d" % i, [128, 512], F32) for i in range(7)]
        self.psT = nc.alloc_psum_tensor("psT", [128, 1024], BF16)
        self.ps_rr = 0
        self.ext = {}
        P.force_all = True
        self.ones_bf = P.alloc([128, 128], BF16)
        self.ones1 = P.alloc([128, 128], BF16)
        self.ones64 = P.alloc([128, 128], BF16)
        self.ones128 = P.alloc([128, 128], BF16)
        P.G(lambda e: e.memset(self.ones_bf, 1.0 / D), [], ["c_ones"])
        P.G(lambda e: e.memset(self.ones1, 1.0), [], ["c_ones"])
        P.G(lambda e: e.memset(self.ones64, 1.0 / 64), [], ["c_ones"])
        P.G(lambda e: e.memset(self.ones128, 1.0 / 128), [], ["c_ones"])
        self.eps_t = P.alloc([128, 1])
        P.G(lambda e: e.memset(self.eps_t, EPS), [], ["eps_t"])
        self.one_t = P.alloc([128, 1])
        P.G(lambda e: e.memset(self.one_t, 1.0), [], ["eps_t"])
        self.ident = P.alloc([128, 128])
        self.ident_bf = P.alloc([128, 128], BF16)
        P.G(lambda e: e.memset(self.ident, 0.0), [], ["ident"])
        P.G(lambda e: e.affine_select(self.ident, self.ident, pattern=[[-1, 128]], compare_op=ALU.not_equal,
                                      fill=1.0, base=0, channel_multiplier=1), ["ident"], ["ident"])
        P.V(lambda e: e.tensor_copy(self.ident_bf, self.ident), ["ident"], ["ident_bf"])
        self.mask = [P.alloc([CH, CH]), P.alloc([CH, CH])]
        for d in range(2):
            m = self.mask[d]
            P.G(lambda e, m=m: e.memset(m, 1.0), [], [("mask", d)])
            pat, cm = ([[1, CH]], -1) if d == 0 else ([[-1, CH]], 1)
            P.G(lambda e, m=m, pat=pat, cm=cm: e.affine_select(m, m, pattern=pat, compare_op=ALU.is_ge, fill=0.0,
                                                              base=0, channel_multiplier=cm),
                [("mask", d)], [("mask", d)])
        P.force_all = False

    def din(self, name, shape, dtype=F32):
        if name not in self.ext:
            self.ext[name] = self.nc.dram_tensor(name, list(shape), dtype, kind="ExternalInput").ap()
        return self.ext[name]

    def dout(self, name, shape, dtype=F32):
        return self.nc.dram_tensor(name, list(shape), dtype, kind="ExternalOutput").ap()

    def dint(self, name, shape, dtype=F32, **kw):
        return self.nc.dram_tensor(name, list(shape), dtype, **kw).ap()

    def psum(self, lo=0, hi=6):
        i = lo + self.ps_rr % (hi - lo)
        self.ps_rr += 1
        return self.ps[i], ("ps", i)

    def gather_weights(self):
        P = self.P
        nc = self.nc
        self.G = []
        ccsem = nc.alloc_semaphore("ccsem")
        rg = [list(range(NCORE))]
        mk = P.mark()
        for gi, NS in enumerate(NSHG):
            R = NS // 2048
            wsh = self.din("wshard%d" % gi, [R, 2048])
            wbf = self.dint("wbf%d" % gi, [R, 2048], BF16)
            G = self.dint("wgath%d" % gi, [NCORE * R, 2048], BF16, addr_space="Shared")
            n = NS // 128
            cs = n // 32
            src = wsh.rearrange("a c -> (a c)").rearrange("(p n) -> p n", p=128)
            dst = wbf.rearrange("a c -> (a c)").rearrange("(p n) -> p n", p=128)
            bb = [P.alloc([128, cs], BF16) for _ in range(3)]
            for i in range(32):
                b, bk = bb[i % 3], ("castb", gi, i % 3)
                P.dma("gpsimd", b, src[:, i * cs:(i + 1) * cs], writes=[bk])
                P.dma("sync", dst[:, i * cs:(i + 1) * cs], b, reads=[bk], writes=[("wbf", gi)])

            def ccfn(e, wbf=wbf, G=G, gi=gi):
                e.collective_compute("AllGather", ALU.bypass, replica_groups=rg, ins=[wbf.opt()],
                                     outs=[G.opt()]).then_inc(ccsem)
                e.wait_ge(ccsem, gi + 1)
                return e.nop()
            P.op("gpsimd", ccfn, reads=[("wbf", gi)], writes=["wgath"])
            self.G.append(G.rearrange("(r a) c -> r (a c)", r=NCORE))
        P.release(mk)

    def _wmeta(self, name, idx):
        one = getattr(self, "test_one", False)
        if name == "w_ada":
            return 1, OFF_ADA + idx * SZ_ADA, 9 * D, [1 if one else 2, D, 9 * D]
        if name == "ffn_w_in":
            return 0, OFF_FIN + idx * SZ_FIN, 2 * DFF, [1 if one else 4, D, 2 * DFF]
        if name == "w_in":
            return 0, OFF_WIN + idx * SZ_WIN, INW, [2, D, INW]
        if name == "w_out":
            return 1, OFF_WOUT + idx * SZ_WOUT, D, [2, D, D]
        raise KeyError(name)

    def load_w(self, dst, name, idx, c0, ncols, key):
        P = self.P
        grp, off, cols, shp = self._wmeta(name, idx)
        if self.gathered:
            if not hasattr(P, "groups"):
                P.groups = {}
            P.groups[key] = [(key, "r", r) for r in range(NCORE)]
            for r in range(NCORE):
                src = self.G[grp][r:r + 1, off:off + 256 * cols].rearrange("o (kk p c) -> p (o kk) c", p=128, c=cols)[
                    :, :, c0:c0 + ncols]
                P.dma("gpsimd", dst[:, 2 * r:2 * r + 2, :], src, reads=["wgath"], writes=[(key, "r", r)])
        else:
            w = self.din(name, shp)
            src = w[idx].rearrange("(k p) c -> p k c", p=128)[:, :, c0:c0 + ncols]
            P.dma("gpsimd", dst, src, writes=[key])

    def load_w_fout(self, dst, idx, j0, j1, m, key):
        P = self.P
        if self.gathered:
            off = OFF_FOUT + idx * SZ_FOUT
            r = m // 2
            src = self.G[1][r:r + 1, off:off + SZ_FOUT].rearrange("o (j p c) -> p (o j) c", p=128, c=256)[
                :, j0:j1, (m % 2) * 128:(m % 2) * 128 + 128]
            P.dma("gpsimd", dst, src, reads=["wgath"], writes=[key])
        else:
            w = self.din("ffn_w_out", [1 if getattr(self, "test_one", False) else 4, DFF, D])
            src = w[idx].rearrange("(j p) c -> p j c", p=128)[:, j0:j1, m * 128:(m + 1) * 128]
            P.dma("gpsimd", dst, src, writes=[key])

    def load_small(self):
        P = self.P
        self.gpre = P.alloc([128, 2 * 3 * KC])
        self.gpost = P.alloc([128, 2 * 3 * KC])
        self.bada = P.alloc([128, 2 * 144])
        self.cs_f = P.alloc([128, 2 * KC])
        P.dma("sync", self.gpre, self.din("gpre", [128, 2 * 3 * KC]), writes=["gpre"])
        P.dma("sync", self.gpost, self.din("gpost", [128, 2 * 3 * KC]), writes=["gpost"])
        P.dma("sync", self.bada, self.din("bada", [128, 2 * 144]), writes=["bada"])
        P.dma("sync", self.cs_f, self.din("cvec", [128, 2 * KC]), writes=["cs_f"])
        self.csT = P.alloc([128, KC, 2], BF16)
        P.S(lambda e: e.activation(self.csT.rearrange("p k v -> p v k"),
                                   self.cs_f.rearrange("p (v k) -> p v k", v=2), AF.Silu),
            ["cs_f"], ["csT"])
        self.mod = [None, None]
        self.Apre = {}
        self.Cg = {}

    def emit_mods(self, l):
        P = self.P
        mod = P.alloc([128, 144, 2])
        self.mod[l] = mod
        for s in range(3):
            self.Apre[(l, s)] = P.alloc([128, KC, 2])
            self.Cg[(l, s)] = P.alloc([128, KC, 2])
        mk = P.mark()
        wb = [P.alloc([128, KC, 256], BF16) for _ in range(3)]
        pst, psk = self.ps[6], ("ps", 6)
        for g in range(72):
            b, bk = wb[g % 3], ("wada", g % 3)
            self.load_w(b, "w_ada", l, g * 256, 256, bk)
            for jj in range(2):
                cc = g * 2 + jj
                for k in range(KC):
                    P.mm(pst[:, cc * 2:cc * 2 + 2], b[:, k, jj * 128:(jj + 1) * 128], self.csT[:, k, :],
                         start=(k == 0), stop=(k == KC - 1), reads=[bk, "csT"], writes=[psk])
        P.force_all = True
        P.V(lambda e: e.tensor_tensor(mod, pst[:, 0:288].rearrange("p (c v) -> p c v", v=2),
                                      self.bada[:, l * 144:(l + 1) * 144].unsqueeze(2).to_broadcast([128, 144, 2]),
                                      ALU.add),
            [psk, "bada"], [("mod", l)])
        for s in range(3):
            A = self.Apre[(l, s)]
            C = self.Cg[(l, s)]
            sc = mod[:, (3 * s + 1) * KC:(3 * s + 2) * KC, :]
            gt = mod[:, (3 * s + 2) * KC:(3 * s + 3) * KC, :]
            o0 = (l * 3 + s) * KC
            gp = self.gpre[:, o0:o0 + KC].unsqueeze(2).to_broadcast([128, KC, 2])
            gq = self.gpost[:, o0:o0 + KC].unsqueeze(2).to_broadcast([128, KC, 2])
            P.V(lambda e, A=A, sc=sc, gp=gp: e.scalar_tensor_tensor(A, sc, 1.0, gp, ALU.add, ALU.mult),
                [("mod", l), "gpre"], [("A", l, s)])
            rs = 1.0 if s == 1 else 0.5
            P.V(lambda e, C=C, gt=gt, gq=gq, rs=rs: e.scalar_tensor_tensor(C, gt, rs, gq, ALU.mult, ALU.mult),
                [("mod", l), "gpost"], [("C", l, s)])
        P.force_all = False
        P.release(mk)

    def shift(self, l, s, k, v):
        return self.mod[l][:, 3 * s * KC + k, v:v + 1]

    def alloc_tile_bufs(self):
        P = self.P
        self.h = P.alloc([128, KC, TT])
        self.u = P.alloc([128, KC, TT], BF16)
        self.hT = P.alloc([128, NJ, TT], BF16)
        self.y = P.alloc([128, KC, TT])
        self.wb = [P.alloc([128, 4096], BF16) for _ in range(4)]
        self.wbi = 0
        self.sqb = [P.alloc([128, TT], BF16) for _ in range(2)]
        self.tmpf = [P.alloc([128, TT]) for _ in range(2)]
        self.rstd = P.alloc([128, TT])
        self.sg = [P.alloc([128, TT]) for _ in range(2)]

    def emit_norm_mod(self, l, s, src, src_key, dst, dst_key, segs):
        P = self.P
        pss, pssk = self.ps[6], ("ps", 6)
        rstd = self.rstd
        for k in range(KC):
            b, bk = self.sqb[k % 2], ("sqb", k % 2)
            P.S(lambda e, b=b, k=k: e.activation(b, src[:, k, :], AF.Square), [src_key], [bk])
            P.mm(pss[:, 0:TT], self.ones_bf, b, start=(k == 0), stop=(k == KC - 1),
                 reads=[bk, "c_ones"], writes=[pssk])
        P.S(lambda e: e.activation(rstd, pss[:, 0:TT], AF.Sqrt, bias=self.eps_t), [pssk, "eps_t"], ["rstd"])
        P.V(lambda e: e.reciprocal(rstd, rstd), ["rstd"], ["rstd"])
        A = self.Apre[(l, s)]
        for k in range(KC):
            t, tk = self.tmpf[k % 2], ("tmpf", k % 2)
            for (a, bnd, v) in segs:
                P.V(lambda e, t=t, k=k, a=a, bnd=bnd, v=v: e.scalar_tensor_tensor(
                    t[:, a:bnd], src[:, k, a:bnd], A[:, k, v:v + 1], rstd[:, a:bnd], ALU.mult, ALU.mult),
                    [src_key, "rstd", ("A", l, s)], [tk])
            for (a, bnd, v) in segs:
                P.S(lambda e, t=t, k=k, a=a, bnd=bnd, v=v: e.activation(
                    dst[:, k, a:bnd], t[:, a:bnd], AF.Identity, bias=self.shift(l, s, k, v)),
                    [tk, ("mod", l)], [dst_key])

    def emit_post_resid(self, l, s, segs):
        P = self.P
        pss, pssk = self.ps[6], ("ps", 6)
        rstd, h, y = self.rstd, self.h, self.y
        P.S(lambda e: e.activation(rstd, pss[:, 0:TT], AF.Sqrt, bias=self.eps_t), [pssk, "eps_t"], ["rstd"])
        P.V(lambda e: e.reciprocal(rstd, rstd), ["rstd"], ["rstd"])
        C = self.Cg[(l, s)]
        for k in range(KC):
            t, tk = self.tmpf[k % 2], ("tmpf", k % 2)
            for (a, bnd, v) in segs:
                P.V(lambda e, t=t, k=k, a=a, bnd=bnd, v=v: e.scalar_tensor_tensor(
                    t[:, a:bnd], y[:, k, a:bnd], C[:, k, v:v + 1], rstd[:, a:bnd], ALU.mult, ALU.mult),
                    [("y", k), "rstd", ("C", l, s)], [tk])
            P.V(lambda e, t=t, k=k: e.tensor_tensor(h[:, k, :], h[:, k, :], t, ALU.add), [tk, "h"], ["h"])

    def evac_y(self, m, py, pyk):
        P = self.P
        pss, pssk = self.ps[6], ("ps", 6)
        P.V(lambda e: e.tensor_copy(self.y[:, m, :], py[:, 0:TT]), [pyk], [("y", m)])
        import os
        if os.environ.get("KDBG2", "") == "ev1":
            return
        b2, b2k = self.sqb[m % 2], ("sqb", m % 2)
        P.S(lambda e: e.activation(b2, self.y[:, m, :], AF.Square), [("y", m)], [b2k])
        if os.environ.get("KDBG2", "") == "ev2":
            return
        P.mm(pss[:, 0:TT], self.ones_bf, b2, start=(m == 0), stop=(m == KC - 1),
             reads=[b2k, "c_ones"], writes=[pssk])

    def emit_ffn_tile(self, l, f, s, segs):
        P = self.P
        u, hT = self.u, self.hT
        wb = self.wb
        self.emit_norm_mod(l, s, self.h, "h", u, "u", segs)
        idx = l * 2 + f
        if not hasattr(self, "wc"):
            self.wc, self.wc_done = {}, set()
        if idx not in self.wc:
            self.wc[idx] = (self.dint("wci%d" % idx, [22, 2, 128, 4096], BF16),
                            self.dint("wco%d" % idx, [KC, 2, 128, 2816], BF16))
        wci, wco = self.wc[idx]
        cached = idx in self.wc_done
        import os
        dbg = int(os.environ.get("KDBG", "9"))
        if dbg < 2:
            return
        for g in range((NJ + 1) // 2):
            nj = 2 if 2 * g + 1 < NJ else 1
            bg, bu = wb[self.wbi % 4], wb[(self.wbi + 1) % 4]
            kg, ku = ("wb", self.wbi % 4), ("wb", (self.wbi + 1) % 4)
            self.wbi += 2
            bgv = bg[:, 0:KC * nj * 128].rearrange("p (k c) -> p k c", k=KC)
            buv = bu[:, 0:KC * nj * 128].rearrange("p (k c) -> p k c", k=KC)
            nel = KC * nj * 128
            if not cached:
                self.load_w(bgv, "ffn_w_in", idx, g * 256, nj * 128, kg)
                self.load_w(buv, "ffn_w_in", idx, DFF + g * 256, nj * 128, ku)
                P.dma("sync", wci[g, 0, :, 0:nel], bg[:, 0:nel], reads=[kg], writes=[("wc", idx, "i", g, 0)])
                P.dma("sync", wci[g, 1, :, 0:nel], bu[:, 0:nel], reads=[ku], writes=[("wc", idx, "i", g, 1)])
            else:
                P.dma("gpsimd", bg[:, 0:nel], wci[g, 0, :, 0:nel], reads=[("wc", idx, "i", g, 0)], writes=[kg])
                P.dma("gpsimd", bu[:, 0:nel], wci[g, 1, :, 0:nel], reads=[("wc", idx, "i", g, 1)], writes=[ku])
            for jj in range(nj):
                j = 2 * g + jj
                pg, pgk = self.psum()
                pu, puk = self.psum()
                for k in range(KC):
                    P.mm(pg[:, 0:TT], bgv[:, k, jj * 128:(jj + 1) * 128], u[:, k, :],
                         start=(k == 0), stop=(k == KC - 1), reads=[kg, "u"], writes=[pgk])
                for k in range(KC):
                    P.mm(pu[:, 0:TT], buv[:, k, jj * 128:(jj + 1) * 128], u[:, k, :],
                         start=(k == 0), stop=(k == KC - 1), reads=[ku, "u"], writes=[puk])
                sgb, sk = self.sg[j % 2], ("sg", j % 2)
                P.S(lambda e, sgb=sgb, pg=pg: e.activation(sgb, pg[:, 0:TT], AF.Silu), [pgk], [sk])
                P.V(lambda e, sgb=sgb, pu=pu, j=j: e.tensor_tensor(hT[:, j, :], sgb, pu[:, 0:TT], ALU.mult),
                    [sk, puk], [("hT", j)])
        if dbg < 3:
            return
        for m in range(KC):
            py, pyk = self.psum()
            for half in range(2):
                j0, j1 = (0, 22) if half == 0 else (22, NJ)
                b, bk = wb[self.wbi % 4], ("wb", self.wbi % 4)
                self.wbi += 1
                bv = b[:, 0:(j1 - j0) * 128].rearrange("p (j c) -> p j c", c=128)
                nel = (j1 - j0) * 128
                if not cached:
                    self.load_w_fout(bv, idx, j0, j1, m, bk)
                    P.dma("sync", wco[m, half, :, 0:nel], b[:, 0:nel], reads=[bk], writes=[("wc", idx, "o", m, half)])
                else:
                    P.dma("gpsimd", b[:, 0:nel], wco[m, half, :, 0:nel], reads=[("wc", idx, "o", m, half)], writes=[bk])
                if os.environ.get("KDBG2", "") == "dma":
                    continue
                for j in range(j0, j1):
                    P.mm(py[:, 0:TT], bv[:, j - j0, :], hT[:, j, :], start=(j == 0), stop=(j == NJ - 1),
                         reads=[bk, ("hT", j)], writes=[pyk])
            if os.environ.get("KDBG2", "") in ("dma", "mm"):
                continue
            self.evac_y(m, py, pyk)
        self.wc_done.add(idx)
        self.emit_post_resid(l, s, segs)

    def emit_wout_tile(self, l, tt, segs):
        P = self.P
        mixt = self.hT[:, 0:KC, :]
        t0 = tt * TT
        for k in range(KC):
            P.dma("sync", mixt[:, k, :], self.Mx[k * 128:(k + 1) * 128, t0:t0 + TT], reads=["Mx"], writes=[("hT", k)])
        for mp in range(KC // 2):
            b, bk = self.wb[self.wbi % 4], ("wb", self.wbi % 4)
            self.wbi += 1
            bv = b[:, 0:KC * 256].rearrange("p (k c) -> p k c", k=KC)
            self.load_w(bv, "w_out", l, mp * 256, 256, bk)
            for jj in range(2):
                m = 2 * mp + jj
                py, pyk = self.psum()
                for k in range(KC):
                    P.mm(py[:, 0:TT], bv[:, k, jj * 128:(jj + 1) * 128], mixt[:, k, :],
                         start=(k == 0), stop=(k == KC - 1), reads=[bk, ("hT", k)], writes=[pyk])
                self.evac_y(m, py, pyk)
        self.emit_post_resid(l, 1, segs)

    def load_mixer_params(self):
        P = self.P
        P.force_all = True
        ld = lambda name, shape: (P.alloc(shape), self.din(name, shape))
        def L(name, shape):
            t, src = ld(name, shape)
            P.dma("sync", t, src, writes=[name])
            return t
        self.lbl = L("lbl", [64, 2 * 2 * 8])
        self.anorm = L("anorm", [64, 2])
        self.bnorm = L("bnorm", [64, 2])
        self.cnorm = L("cnorm", [128, 2])
        self.clam = L("clam", [64, 2 * 4])
        self.convw = L("convw", [128, 2 * 4 * 4])
        self.convb = L("convb", [128, 2 * 4])
        self.dbr = L("dbr", [128, 2 * 2 * 4])
        self.dbi = L("dbi", [128, 2 * 2 * 4])
        self.dlam = L("dlam", [128, 2 * 2 * 4])
        self.cos_d = self.din("rope_cos", [64, T])
        self.sin_d = self.din("rope_sin", [64, T])
        self.lb1 = P.alloc([64, 16])
        self.omlb1 = P.alloc([64, 16])
        P.V(lambda e: e.tensor_tensor(self.lb1, self.lbl[:, 16:32], self.lbl[:, 0:16], ALU.subtract), ["lbl"], ["lb1"])
        P.S(lambda e: e.activation(self.lb1, self.lb1, AF.Sigmoid), ["lb1"], ["lb1"])
        P.V(lambda e: e.tensor_scalar(self.omlb1, self.lb1, -1.0, 1.0, ALU.mult, ALU.add), ["lb1"], ["omlb1"])
        self.zero_t = P.alloc([128, 1])
        P.G(lambda e: e.memset(self.zero_t, 0.0), [], ["eps_t"])
        self.cm = P.alloc([64, T])
        P.G(lambda e: e.memset(self.cm, 1.0), [], ["cm"])
        P.G(lambda e: e.memset(self.cm.rearrange("p (c i) -> p c i", i=CH)[:, :, 0:1], 0.0), ["cm"], ["cm"])
        self.neglam = P.alloc([128, 2])
        self.cgain = P.alloc([128, 2])
        prod = P.alloc([64, 2, 2])
        cl = self.clam.rearrange("p (l f) -> p l f", l=2)
        P.V(lambda e: e.tensor_tensor(prod[:, :, 0], cl[:, :, 0], cl[:, :, 1], ALU.mult), ["clam"], ["prod"])
        P.V(lambda e: e.tensor_tensor(prod[:, :, 1], cl[:, :, 2], cl[:, :, 3], ALU.mult), ["clam", "prod"], ["prod"])
        prodb = P.alloc([64, 4], BF16)
        P.V(lambda e: e.tensor_copy(prodb, prod.rearrange("p l f -> p (l f)")), ["prod"], ["prodb"])
        pl, plk = self.ps[6], ("ps", 6)
        P.mm(pl[:, 0:4], self.ones1[0:64, :], prodb, reads=["c_ones", "prodb"], writes=[plk])
        ex = P.alloc([128, 4])
        P.S(lambda e: e.activation(ex, pl[:, 0:4], AF.Exp), [plk], ["ex"])
        for l in range(2):
            li = 0.8 - 0.6 * math.exp(-0.3 * l)
            P.V(lambda e, l=l: e.tensor_tensor(self.neglam[:, l:l + 1], ex[:, 2 * l + 1:2 * l + 2], ex[:, 2 * l:2 * l + 1],
                                               ALU.subtract), ["ex"], ["neglam"])
            P.V(lambda e, l=l, li=li: e.tensor_scalar(self.neglam[:, l:l + 1], self.neglam[:, l:l + 1], -li, None, ALU.add),
                ["neglam"], ["neglam"])
            P.V(lambda e, l=l, li=li: e.tensor_scalar(self.cgain[:, l:l + 1], self.cnorm[:, l:l + 1], 1.0 - li, None, ALU.mult),
                ["cnorm"], ["cgain"])
        self.c1 = P.alloc([128, 16])
        P.S(lambda e: e.activation(self.c1, self.dlam, AF.Exp, scale=-1.0), ["dlam"], ["c1"])
        P.S(lambda e: e.activation(self.c1, self.c1, AF.Ln, bias=self.one_t), ["c1", "eps_t"], ["c1"])
        P.V(lambda e: e.tensor_scalar(self.c1, self.c1, -8.0, None, ALU.mult), ["c1"], ["c1"])
        P.force_all = False

    def alloc_scratch(self):
        self.H = self.dint("Hres", [128, KC, T])
        self.U = self.dint("Umix", [128, KC, T], BF16)
        self.Z = self.dint("Zfm", [INW, T])
        self.Zt = self.dint("Ztm", [T, 1536])
        self.Mx = self.dint("Mx", [2048, T], BF16)

    def emit_proj(self, l):
        P = self.P
        mk = P.mark()
        uf = P.alloc([128, KC, T], BF16)
        P.dma("sync", uf, self.U, reads=["U"], writes=["uf"])
        wb = [P.alloc([128, KC, 256], BF16) for _ in range(3)]
        ev = [P.alloc([128, 512]) for _ in range(4)]
        tbl = [(0, 512), (512, 1024), (1024, 1536), (1536, 2048), (2048, 2304)]
        n = 0
        for g in range(28):
            b, bk = wb[g % 3], ("pw", g % 3)
            self.load_w(b, "w_in", l, g * 256, 256, bk)
            for jj in range(2):
                r0 = g * 256 + jj * 128
                for (a, e_) in tbl:
                    nn = e_ - a
                    ps, psk = self.psum()
                    for k in range(KC):
                        P.mm(ps[:, 0:nn], b[:, k, jj * 128:(jj + 1) * 128], uf[:, k, a:e_],
                             start=(k == 0), stop=(k == KC - 1), reads=[bk, "uf"], writes=[psk])
                    evb, evk = ev[n % 4], ("ev", n % 4)
                    if n % 2 == 0:
                        P.V(lambda e, evb=evb, ps=ps, nn=nn: e.tensor_copy(evb[:, 0:nn], ps[:, 0:nn]), [psk], [evk])
                    else:
                        P.S(lambda e, evb=evb, ps=ps, nn=nn: e.copy(evb[:, 0:nn], ps[:, 0:nn]), [psk], [evk])
                    P.dma("sync", self.Z[r0:r0 + 128, a:e_], evb[:, 0:nn], reads=[evk], writes=[("Z", r0 // 128)])
                    n += 1
        for bi, blk in enumerate([3, 7, 11]):
            for half in range(2):
                b, bk = wb[n % 3], ("pw", n % 3)
                self.load_w(b, "w_in", l, blk * 512 + half * 256, 256, bk)
                for tb in range(T // 128):
                    ps, psk = self.psum()
                    for k in range(KC):
                        P.mm(ps[:, 0:256], uf[:, k, tb * 128:(tb + 1) * 128], b[:, k, :],
                             start=(k == 0), stop=(k == KC - 1), reads=[bk, "uf"], writes=[psk])
                    evb, evk = ev[n % 4], ("ev", n % 4)
                    if n % 2 == 0:
                        P.V(lambda e, evb=evb, ps=ps: e.tensor_copy(evb[:, 0:256], ps[:, 0:256]), [psk], [evk])
                    else:
                        P.S(lambda e, evb=evb, ps=ps: e.copy(evb[:, 0:256], ps[:, 0:256]), [psk], [evk])
                    c0 = bi * 512 + half * 256
                    P.dma("sync", self.Zt[tb * 128:(tb + 1) * 128, c0:c0 + 256], evb[:, 0:256], reads=[evk],
                          writes=[("Zt", bi, half, tb)])
                    n += 1
        P.release(mk)

    def emit_gla(self, l, kind, heads):
        P = self.P
        mk = P.mark()
        Fa = lambda: P.alloc([64, T])
        Ba = lambda: P.alloc([64, T], BF16)
        q, kk, lf, bf, b, tmp = [Fa() for _ in range(6)]
        E = tmp
        kh = [Ba(), Ba()]
        Z, Zt = self.Z, self.Zt
        three = lambda a: a.rearrange("p (c i) -> p c i", i=CH)
        LS = []
        for hi, h in enumerate(heads):
            ls = dict(qt=[Ba(), Ba()], kt=[Ba(), Ba()],
                      khT=[P.alloc([CH, NCH, 64], BF16) for _ in range(2)],
                      vtm=P.alloc([CH, NCH, 64], BF16), oacc=Fa(),
                      gdec=[P.alloc([64, NCH]) for _ in range(2)],
                      S32=[P.alloc([64, 64]) for _ in range(2)],
                      Sbf=[P.alloc([64, 64], BF16) for _ in range(2)],
                      sTb=[[P.alloc([CH, CH], BF16) for _ in range(2)] for _ in range(2)])
            LS.append(ls)
        for hi, h in enumerate(heads):
            ls = LS[hi]
            qt, kt, khT, vtm, oacc, gdec = ls["qt"], ls["kt"], ls["khT"], ls["vtm"], ls["oacc"], ls["gdec"]
            rows = lambda blk, h=h: Z[blk * 512 + 64 * h:blk * 512 + 64 * h + 64, :]

            def rows_sw(dst, blk, key, h=h):
                r0 = blk * 512 + 64 * h
                P.dma("sync", dst[0:32, :], Z[r0 + 32:r0 + 64, :], writes=[key])
                P.dma("sync", dst[32:64, :], Z[r0:r0 + 32, :], writes=[key])

            if kind == "A":
                P.dma("sync", q, rows(0), writes=["q"])
                P.dma("gpsimd", vtm, Zt[:, 64 * h:64 * h + 64].rearrange("(c p) d -> p c d", p=CH),
                      writes=[("vtm", hi)])
                P.S(lambda e: e.activation(q, q, AF.Silu), ["q"], ["q"])
            else:
                cos, sin = bf, b
                P.dma("sync", cos, self.cos_d, writes=["bf"])
                P.dma("sync", sin, self.sin_d, writes=["b"])
                P.dma("gpsimd", vtm, Zt[:, 512 + 64 * h:512 + 64 * h + 64].rearrange("(c p) d -> p c d", p=CH),
                      writes=[("vtm", hi)])
                for (dst, blk, sc) in ((q, 5, 1.0), (kk, 6, 0.125)):
                    dk = "q" if dst is q else "kk"
                    P.dma("sync", dst, rows(blk), writes=[dk])
                    rows_sw(tmp, blk, "tmp")
                    P.V(lambda e, dst=dst: e.tensor_tensor(dst, dst, cos, ALU.mult), [dk, "bf"], [dk])
                    P.V(lambda e: e.tensor_tensor(tmp, tmp, sin, ALU.mult), ["tmp", "b"], ["tmp"])
                    P.V(lambda e, dst=dst: e.tensor_tensor(dst, dst, tmp, ALU.add), [dk, "tmp"], [dk])
                    if sc != 1.0:
                        P.V(lambda e, dst=dst, sc=sc: e.tensor_scalar(dst, dst, sc, None, ALU.mult), [dk], [dk])
                lg = math.log1p(-2.0 ** (-5.0 - h))
            P.G(lambda e, oacc=oacc: e.memset(oacc, 0.0), [], [("oacc", hi)])
            for d in range(2):
                if kind == "A":
                    P.dma("sync", tmp, rows(1 + d), writes=["tmp"])
                    P.S(lambda e: e.activation(tmp, tmp, AF.Sigmoid), ["tmp"], ["tmp"])
                    if l == 0:
                        lbs, oms = self.zero_t[0:64, :], self.one_t[0:64, :]
                    else:
                        lbs, oms = self.lb1[:, d * 8 + h:d * 8 + h + 1], self.omlb1[:, d * 8 + h:d * 8 + h + 1]
                    P.V(lambda e, lbs=lbs, oms=oms: e.tensor_scalar(tmp, tmp, oms, lbs, ALU.mult, ALU.add),
                        ["tmp", "lb1", "omlb1", "eps_t"], ["tmp"])
                    P.V(lambda e: e.tensor_scalar(kk, tmp, -1.0, 1.0, ALU.mult, ALU.add), ["tmp"], ["kk"])
                    P.S(lambda e: e.activation(lf, tmp, AF.Ln), ["tmp"], ["lf"])
                elif d == 0:
                    P.G(lambda e, lg=lg: e.memset(lf, lg), [], ["lf"])
                P.V(lambda e: e.tensor_tensor_scan(bf, self.cm, lf, 0.0, ALU.mult, ALU.add), ["cm", "lf"], ["bf"])
                btot = three(bf)[:, :, CH - 1:CH]
                btb = btot.to_broadcast([64, NCH, CH])
                if d == 0:
                    P.V(lambda e: e.tensor_copy(b, bf), ["bf"], ["b"])
                else:
                    P.V(lambda e, btb=btb: e.tensor_tensor(three(b), btb, three(bf), ALU.subtract), ["bf"], ["b"])
                    P.V(lambda e: e.tensor_tensor(b, b, lf, ALU.add), ["b", "lf"], ["b"])
                P.S(lambda e: e.activation(E, b, AF.Exp), ["b"], ["tmp"])
                P.V(lambda e, d=d, qt=qt: e.tensor_tensor(qt[d], q, E, ALU.mult), ["q", "tmp"], [("qt", hi, d)])
                P.S(lambda e: e.activation(E, b, AF.Exp, scale=-1.0), ["b"], ["tmp"])
                P.V(lambda e, d=d, kt=kt: e.tensor_tensor(kt[d], kk, E, ALU.mult), ["kk", "tmp"], [("kt", hi, d)])
                P.V(lambda e, btb=btb: e.tensor_tensor(three(E), btb, three(b), ALU.subtract), ["bf", "b"], ["tmp"])
                P.S(lambda e: e.activation(E, E, AF.Exp), ["tmp"], ["tmp"])
                P.V(lambda e, d=d: e.tensor_tensor(kh[d], kk, E, ALU.mult), ["kk", "tmp"], [("kh", d)])
                P.S(lambda e, d=d, gdec=gdec, btot=btot: e.activation(gdec[d], btot.rearrange("p c o -> p (c o)"), AF.Exp),
                    ["bf"], [("gdec", hi, d)])
                for c0 in range(0, NCH, 8):
                    ptb, ptk = self.psT, "psT"
                    for c in range(c0, c0 + 8):
                        P.op("tensor", lambda e, d=d, c=c, c0=c0: e.transpose(
                            ptb[0:CH, (c - c0) * 64:(c - c0) * 64 + 64], kh[d][:, c * CH:(c + 1) * CH],
                            self.ident_bf[0:64, 0:64]), [("kh", d), "ident_bf"], [ptk])
                    P.V(lambda e, d=d, c0=c0, khT=khT: e.tensor_copy(
                        khT[d][:, c0:c0 + 8, :], ptb[0:CH, 0:512].rearrange("p (c k) -> p c k", k=64)),
                        [ptk], [("khT", hi, d)])
        order = [list(range(NCH)), list(range(7, -1, -1)) + list(range(NCH - 1, 7, -1))]
        nstep = 0
        for i in range(NCH):
            for hi in range(len(heads)):
                ls = LS[hi]
                qt, kt, khT, vtm, oacc, gdec = ls["qt"], ls["kt"], ls["khT"], ls["vtm"], ls["oacc"], ls["gdec"]
                S32, Sbf, sTb = ls["S32"], ls["Sbf"], ls["sTb"]
                for d in range(2):
                    c = order[d][i]
                    ts = slice(c * CH, (c + 1) * CH)
                    first = (i == 0)
                    bank, slot = nstep % 6, (nstep // 6) % 4
                    nstep += 1
                    base = slot * 128
                    pbank = self.ps[bank]
                    pss_, psk = pbank[:, base:base + 32], ("psr", bank, slot, 0)
                    po, pok = pbank[:, base + 32:base + 64], ("psr", bank, slot, 1)
                    pu, puk = pbank[:, base + 64:base + 128], ("psr", bank, slot, 2)
                    P.mm(pss_[0:CH, 0:CH], kt[d][:, ts], qt[d][:, ts], reads=[("kt", hi, d), ("qt", hi, d)], writes=[psk])
                    sT, sTk = sTb[d][i % 2], ("sT", hi, d, i % 2)
                    P.V(lambda e, sT=sT, pss_=pss_, d=d: e.tensor_tensor(sT, pss_[0:CH, 0:CH], self.mask[d], ALU.mult),
                        [psk, ("mask", d)], [sTk])
                    if not first:
                        P.mm(po[0:64, 0:CH], Sbf[d], qt[d][:, ts], start=True, stop=False,
                             reads=[("Sbf", hi, d), ("qt", hi, d)], writes=[pok])
                    P.mm(po[0:64, 0:CH], vtm[:, c, :], sT, start=first, stop=True, reads=[("vtm", hi), sTk], writes=[pok])
                    P.V(lambda e, po=po, ts=ts, oacc=oacc: e.tensor_tensor(oacc[:, ts], oacc[:, ts], po[0:64, 0:CH], ALU.add),
                        [pok, ("oacc", hi)], [("oacc", hi)])
                    P.mm(pu[0:64, 0:64], khT[d][:, c, :], vtm[:, c, :], reads=[("khT", hi, d), ("vtm", hi)], writes=[puk])
                    if first:
                        P.V(lambda e, d=d, pu=pu, S32=S32: e.tensor_copy(S32[d], pu[0:64, 0:64]), [puk], [("S32", hi, d)])
                    else:
                        P.V(lambda e, d=d, pu=pu, c=c, S32=S32, gdec=gdec: e.scalar_tensor_tensor(
                            S32[d], S32[d], gdec[d][:, c:c + 1], pu[0:64, 0:64], ALU.mult, ALU.add),
                            [puk, ("S32", hi, d), ("gdec", hi, d)], [("S32", hi, d)])
                    P.S(lambda e, d=d, S32=S32, Sbf=Sbf: e.copy(Sbf[d], S32[d]), [("S32", hi, d)], [("Sbf", hi, d)])
        sqb = [P.alloc([64, 512], BF16) for _ in range(2)]
        ob = [P.alloc([64, 512], BF16) for _ in range(2)]
        rs = [P.alloc([64, 512]) for _ in range(2)]
        zg = q
        for hi, h in enumerate(heads):
            oacc = LS[hi]["oacc"]
            if kind == "A":
                mrow, gain, gblk = 64 * h, self.anorm[:, l:l + 1], 4
            else:
                mrow, gain, gblk = 512 + 64 * h, self.bnorm[:, l:l + 1], 8
            P.dma("sync", zg, Z[gblk * 512 + 64 * h:gblk * 512 + 64 * h + 64, :], writes=["q"])
            P.S(lambda e: e.activation(zg, zg, AF.Silu), ["q"], ["q"])
            for bi_, (a, e_) in enumerate([(0, 512), (512, 1024), (1024, 1536), (1536, 2048), (2048, 2304)]):
                nn = e_ - a
                sq, sqk = sqb[bi_ % 2], ("gsq", bi_ % 2)
                P.S(lambda e, sq=sq, a=a, e_=e_, nn=nn, oacc=oacc: e.activation(sq[:, 0:nn], oacc[:, a:e_], AF.Square),
                    [("oacc", hi)], [sqk])
                pn, pnk = self.ps[6], ("ps", 6)
                P.mm(pn[0:64, 0:nn], self.ones64[0:64, 0:64], sq[:, 0:nn], reads=[sqk, "c_ones"], writes=[pnk])
                r, rk = rs[bi_ % 2], ("grs", bi_ % 2)
                P.S(lambda e, r=r, pn=pn, nn=nn: e.activation(r[:, 0:nn], pn[0:64, 0:nn], AF.Sqrt, bias=self.eps_t[0:64, :]),
                    [pnk, "eps_t"], [rk])
                P.V(lambda e, r=r, nn=nn: e.reciprocal(r[:, 0:nn], r[:, 0:nn]), [rk], [rk])
                P.V(lambda e, r=r, a=a, e_=e_, nn=nn, oacc=oacc, gain=gain: e.scalar_tensor_tensor(
                    r[:, 0:nn], oacc[:, a:e_], gain, r[:, 0:nn], ALU.mult, ALU.mult),
                    [rk, ("oacc", hi), "anorm", "bnorm"], [rk])
                o_, ok_ = ob[bi_ % 2], ("gob", bi_ % 2)
                P.V(lambda e, o_=o_, r=r, a=a, e_=e_, nn=nn: e.tensor_tensor(o_[:, 0:nn], r[:, 0:nn], zg[:, a:e_], ALU.mult),
                    [rk, "q"], [ok_])
                P.dma("sync", self.Mx[mrow:mrow + 64, a:e_], o_[:, 0:nn], reads=[ok_], writes=[("Mx", mrow)])
        P.release(mk)

    def emit_diffattn(self, l, h, ctx_out):
        P = self.P
        mk = P.mark()
        Z, Zt = self.Z, self.Zt
        t1 = P.alloc([128, T])
        t2 = P.alloc([128, T])
        cos = P.alloc([128, T])
        sin = P.alloc([128, T])
        qb = P.alloc([128, T], BF16)
        kb_ = P.alloc([128, T], BF16)
        vtm = P.alloc([128, T // 128, 128], BF16)
        for hf in range(2):
            P.dma("sync", cos[64 * hf:64 * hf + 64, :], self.cos_d, writes=["cos"])
            P.dma("sync", sin[64 * hf:64 * hf + 64, :], self.sin_d, writes=["sin"])
        P.dma("gpsimd", vtm, Zt[:, 1024 + 128 * h:1024 + 128 * h + 128].rearrange("(kb p) d -> p kb d", p=128),
              writes=["vtm"])
        for (dst, blk) in ((qb, 9), (kb_, 10)):
            r0 = blk * 512 + 128 * h
            P.dma("sync", t1, Z[r0:r0 + 128, :], writes=["t1"])
            for hf in range(2):
                P.dma("sync", t2[64 * hf:64 * hf + 32, :], Z[r0 + 64 * hf + 32:r0 + 64 * hf + 64, :], writes=["t2"])
                P.dma("sync", t2[64 * hf + 32:64 * hf + 64, :], Z[r0 + 64 * hf:r0 + 64 * hf + 32, :], writes=["t2"])
            P.V(lambda e: e.tensor_tensor(t1, t1, cos, ALU.mult), ["t1", "cos"], ["t1"])
            P.V(lambda e: e.tensor_tensor(t2, t2, sin, ALU.mult), ["t2", "sin"], ["t2"])
            dk = "qb" if dst is qb else "kb"
            P.V(lambda e, dst=dst: e.tensor_tensor(dst, t1, t2, ALU.add), ["t1", "t2"], [dk])
        pex = [P.alloc([128, 512], BF16) for _ in range(3)]
        rr = [P.alloc([128, 512]) for _ in range(2)]
        oo = [P.alloc([128, 512]) for _ in range(2)]
        sqo = P.alloc([128, 512], BF16)
        outb = [P.alloc([128, 512], BF16) for _ in range(2)]
        neglam = self.neglam[:, l:l + 1]
        cgain = self.cgain[:, l:l + 1]
        qblocks = [(CTX + 512 * i, 512, T // 128) for i in range(4)]
        if ctx_out:
            qblocks.append((0, CTX, CTX // 128))
        npx = 0
        for qi, (q0, nq, nkb) in enumerate(qblocks):
            accO = [(self.ps[2], ("ps", 2)), (self.ps[3], ("ps", 3))]
            accS = [(self.ps[4], ("ps", 4)), (self.ps[5], ("ps", 5))]
            for kbi in range(nkb):
                for hf in range(2):
                    pS, pSk = self.psum(0, 2)
                    P.mm(pS[:, 0:nq], kb_[64 * hf:64 * hf + 64, kbi * 128:(kbi + 1) * 128],
                         qb[64 * hf:64 * hf + 64, q0:q0 + nq], reads=["kb", "qb"], writes=[pSk])
                    px, pxk = pex[npx % 3], ("pex", npx % 3)
                    npx += 1
                    P.S(lambda e, px=px, pS=pS, nq=nq: e.activation(px[:, 0:nq], pS[:, 0:nq], AF.Exp, scale=0.125),
                        [pSk], [pxk])
                    P.mm(accO[hf][0][:, 0:nq], vtm[:, kbi, :], px[:, 0:nq], start=(kbi == 0), stop=(kbi == nkb - 1),
                         reads=["vtm", pxk], writes=[accO[hf][1]])
                    P.mm(accS[hf][0][:, 0:nq], self.ones1, px[:, 0:nq], start=(kbi == 0), stop=(kbi == nkb - 1),
                         reads=["c_ones", pxk], writes=[accS[hf][1]])
            for hf in range(2):
                r, rk = rr[hf], ("rr", hf)
                o, ok_ = oo[hf], ("oo", hf)
                P.V(lambda e, r=r, hf=hf, nq=nq, accS=accS: e.reciprocal(r[:, 0:nq], accS[hf][0][:, 0:nq]),
                    [accS[hf][1]], [rk])
                P.V(lambda e, r=r, o=o, hf=hf, nq=nq, accO=accO: e.tensor_tensor(o[:, 0:nq], accO[hf][0][:, 0:nq],
                                                                                r[:, 0:nq], ALU.mult),
                    [accO[hf][1], rk], [ok_])
            o = oo[0]
            P.V(lambda e, nq=nq: e.scalar_tensor_tensor(oo[0][:, 0:nq], oo[1][:, 0:nq], neglam, oo[0][:, 0:nq],
                                                        ALU.mult, ALU.add),
                [("oo", 0), ("oo", 1), "neglam"], [("oo", 0)])
            P.S(lambda e, nq=nq: e.activation(sqo[:, 0:nq], oo[0][:, 0:nq], AF.Square), [("oo", 0)], ["sqo"])
            pn, pnk = self.ps[6], ("ps", 6)
            P.mm(pn[:, 0:nq], self.ones128, sqo[:, 0:nq], reads=["sqo", "c_ones"], writes=[pnk])
            r = rr[0]
            P.S(lambda e, nq=nq: e.activation(rr[0][:, 0:nq], pn[:, 0:nq], AF.Sqrt, bias=self.eps_t), [pnk, "eps_t"],
                [("rr", 0)])
            P.V(lambda e, nq=nq: e.reciprocal(rr[0][:, 0:nq], rr[0][:, 0:nq]), [("rr", 0)], [("rr", 0)])
            ob, obk = outb[qi % 2], ("outb", qi % 2)
            P.V(lambda e, nq=nq, ob=ob: e.scalar_tensor_tensor(ob[:, 0:nq], oo[0][:, 0:nq], cgain, rr[0][:, 0:nq],
                                                               ALU.mult, ALU.mult),
                [("oo", 0), ("rr", 0), "cgain"], [obk])
            P.dma("sync", self.Mx[1024 + 128 * h:1024 + 128 * h + 128, q0:q0 + nq], ob[:, 0:nq], reads=[obk],
                  writes=[("Mx", 1024 + 128 * h)])
        P.release(mk)

    def emit_rglru(self, l, cc):
        P = self.P
        mk = P.mark()
        Z = self.Z
        Fa = lambda: P.alloc([128, T])
        x, gate, xc, ra, ia, aa, uu, hs, hsum = [Fa() for _ in range(9)]
        xcb = P.alloc([128, T], BF16)
        outb = P.alloc([128, T], BF16)
        P.dma("sync", x, Z[12 * 512 + 128 * cc:12 * 512 + 128 * cc + 128, :], writes=["x"])
        P.dma("sync", gate, Z[13 * 512 + 128 * cc:13 * 512 + 128 * cc + 128, :], writes=["gate"])
        cw = lambda j: self.convw[:, (l * 4 + cc) * 4 + j:(l * 4 + cc) * 4 + j + 1]
        cb = self.convb[:, l * 4 + cc:l * 4 + cc + 1]
        P.S(lambda e: e.activation(xc, x, AF.Identity, bias=cb, scale=cw(1)), ["x", "convw", "convb"], ["xc"])
        for (a, e_) in ((0, CTX), (CTX, T)):
            P.V(lambda e, a=a, e_=e_: e.scalar_tensor_tensor(xc[:, a + 1:e_], x[:, a:e_ - 1], cw(0), xc[:, a + 1:e_],
                                                            ALU.mult, ALU.add), ["x", "xc", "convw"], ["xc"])
            P.V(lambda e, a=a, e_=e_: e.scalar_tensor_tensor(xc[:, a:e_ - 1], x[:, a + 1:e_], cw(2), xc[:, a:e_ - 1],
                                                            ALU.mult, ALU.add), ["x", "xc", "convw"], ["xc"])
            P.V(lambda e, a=a, e_=e_: e.scalar_tensor_tensor(xc[:, a:e_ - 2], x[:, a + 2:e_], cw(3), xc[:, a:e_ - 2],
                                                            ALU.mult, ALU.add), ["x", "xc", "convw"], ["xc"])
        P.V(lambda e: e.tensor_copy(xcb, xc), ["xc"], ["xcb"])
        wbd = [[P.alloc([128, 128], BF16) for _ in range(2)] for _ in range(2)]
        dwr = self.din("d_w_r", [2, 2, 8, 64, 64])
        dwi = self.din("d_w_i", [2, 2, 8, 64, 64])
        for d in range(2):
            for ri, src in enumerate((dwr, dwi)):
                w = wbd[d][ri]
                wk = ("wbd", d, ri)
                P.G(lambda e, w=w: e.memset(w, 0.0), [], [wk])
                for g2 in range(2):
                    P.dma("gpsimd", w[64 * g2:64 * g2 + 64, 64 * g2:64 * g2 + 64], src[l, d, 2 * cc + g2], writes=[wk])
        tbl = [(0, 512), (512, 1024), (1024, 1536), (1536, 2048), (2048, 2304)]
        for d in range(2):
            pidx = (l * 2 + d) * 4 + cc
            br = self.dbr[:, pidx:pidx + 1]
            bi = self.dbi[:, pidx:pidx + 1]
            c1 = self.c1[:, pidx:pidx + 1]
            for (a, e_) in tbl:
                nn = e_ - a
                pr, prk = self.psum()
                P.mm(pr[:, 0:nn], wbd[d][0], xcb[:, a:e_], reads=[("wbd", d, 0), "xcb"], writes=[prk])
                P.S(lambda e, pr=pr, a=a, e_=e_, nn=nn, br=br: e.activation(ra[:, a:e_], pr[:, 0:nn], AF.Sigmoid, bias=br),
                    [prk, "dbr"], ["ra"])
                pi_, pik = self.psum()
                P.mm(pi_[:, 0:nn], wbd[d][1], xcb[:, a:e_], reads=[("wbd", d, 1), "xcb"], writes=[pik])
                P.S(lambda e, pi_=pi_, a=a, e_=e_, nn=nn, bi=bi: e.activation(ia[:, a:e_], pi_[:, 0:nn], AF.Sigmoid, bias=bi),
                    [pik, "dbi"], ["ia"])
            P.S(lambda e, c1=c1: e.activation(aa, ra, AF.Exp, scale=c1), ["ra", "c1"], ["aa"])
            P.V(lambda e: e.tensor_tensor(ra, aa, aa, ALU.mult), ["aa", "ra"], ["ra"])
            P.S(lambda e: e.activation(ra, ra, AF.Sqrt, bias=self.one_t, scale=-1.0), ["ra", "eps_t"], ["ra"])
            P.V(lambda e: e.tensor_tensor(uu, ia, xc, ALU.mult), ["ia", "xc"], ["uu"])
            P.V(lambda e: e.tensor_tensor(uu, uu, ra, ALU.mult), ["uu", "ra"], ["uu"])
            if d == 0:
                P.V(lambda e: e.tensor_tensor_scan(hsum, aa, uu, 0.0, ALU.mult, ALU.add), ["aa", "uu"], ["hsum"])
            else:
                rv = lambda t_, a, e_: t_[:, a:e_][:, ::-1]
                P.V(lambda e: e.tensor_tensor_scan(rv(hs, 0, CTX), rv(aa, 0, CTX), rv(uu, 0, CTX), 0.0, ALU.mult, ALU.add),
                    ["aa", "uu"], ["hs"])
                P.V(lambda e: e.tensor_tensor_scan(rv(hs, CTX, T), rv(aa, CTX, T), rv(uu, CTX, T), hs[:, 0:1],
                                                   ALU.mult, ALU.add), ["aa", "uu", "hs"], ["hs"], force=True)
                P.V(lambda e: e.tensor_tensor(hsum, hsum, hs, ALU.add), ["hsum", "hs"], ["hsum"])
        P.S(lambda e: e.activation(x, gate, AF.Square), ["gate", "x"], ["x"])
        P.V(lambda e: e.tensor_scalar(x, x, 0.044715, 1.0, ALU.mult, ALU.add), ["x"], ["x"])
        P.V(lambda e: e.tensor_tensor(x, x, gate, ALU.mult), ["x", "gate"], ["x"])
        P.S(lambda e: e.activation(x, x, AF.Sigmoid, scale=1.5957691216057308), ["x"], ["x"])
        P.V(lambda e: e.tensor_tensor(x, x, gate, ALU.mult), ["x", "gate"], ["x"])
        P.V(lambda e: e.tensor_tensor(outb, x, hsum, ALU.mult), ["x", "hsum"], ["outb"])
        P.dma("sync", self.Mx[1536 + 128 * cc:1536 + 128 * cc + 128, :], outb, reads=["outb"], writes=[("Mx", 1536 + 128 * cc)])
        P.release(mk)

    def emit_pass(self, which, xT=None, out=None):
        P = self.P
        mk = P.mark()
        self.alloc_tile_bufs()
        h = self.h
        for tt in range(NTT):
            segs = tile_segs(tt)
            t0 = tt * TT
            if which == 0:
                P.dma("sync", h, xT[:, :, t0:t0 + TT], writes=["h"])
                self.emit_ffn_tile(0, 0, 0, segs)
                nl = 0
            elif which == 1:
                P.dma("sync", h, self.H[:, :, t0:t0 + TT], reads=[("H", tt)], writes=["h"])
                self.emit_wout_tile(0, tt, segs)
                self.emit_ffn_tile(0, 1, 2, segs)
                self.emit_ffn_tile(1, 0, 0, segs)
                nl = 1
            else:
                P.dma("sync", h, self.H[:, :, t0:t0 + TT], reads=[("H", tt)], writes=["h"])
                self.emit_wout_tile(1, tt, segs)
                self.emit_ffn_tile(1, 1, 2, segs)
                if tt == 0:
                    P.dma("sync", out[:, :, 0:128], h[:, :, 256:384], reads=["h"])
                else:
                    o0 = 128 + (tt - 1) * TT
                    P.dma("sync", out[:, :, o0:o0 + TT], h, reads=["h"])
                continue
            self.emit_norm_mod(nl, 1, h, "h", self.u, "u", segs)
            P.dma("sync", self.U[:, :, t0:t0 + TT], self.u, reads=["u"], writes=[("U", tt)])
            P.dma("sync", self.H[:, :, t0:t0 + TT], h, reads=["h"], writes=[("H", tt)])
        P.release(mk)

    def emit_mixer(self, l, parts="ABCD"):
        self.emit_proj(l)
        if "A" in parts:
            for h in range(0, 8, 2):
                self.emit_gla(l, "A", [h, h + 1])
        if "B" in parts:
            for h in range(0, 8, 2):
                self.emit_gla(l, "B", [h, h + 1])
        if "C" in parts:
            for h in range(4):
                self.emit_diffattn(l, h, l == 0)
        if "D" in parts:
            for cc in range(4):
                self.emit_rglru(l, cc)


def build_full(gathered=True):
    nc = bass.Bass("TRN2", target_bir_lowering=False)
    C = Core(nc, gathered)
    if gathered:
        C.gather_weights()
    C.load_small()
    C.load_mixer_params()
    C.alloc_scratch()
    xT = C.din("xT", [128, KC, T])
    out = C.dout("out", [128, KC, SEQ])
    C.emit_mods(0)
    C.emit_mods(1)
    C.emit_pass(0, xT=xT)
    C.emit_mixer(0)
    C.emit_pass(1)
    C.emit_mixer(1)
    C.emit_pass(2, out=out)
    C.P.emit()
    return nc, C


def build_pass0_test():
    nc = bass.Bass("TRN2", target_bir_lowering=False)
    C = Core(nc, False)
    C.test_one = True
    C.load_small()
    C.H = C.dout("H_out", [128, KC, T])
    C.U = C.dout("U_out", [128, KC, T], BF16)
    xT = C.din("xT", [128, KC, T])
    C.emit_mods(0)
    modo = C.dout("mod_out", [128, 144, 2])
    C.P.dma("sync", modo, C.mod[0], reads=[("mod", 0)])
    C.emit_pass(0, xT=xT)
    C.P.emit()
    return nc, C


def build_mixer_test(l, parts):
    nc = bass.Bass("TRN2", target_bir_lowering=False)
    C = Core(nc, False)
    C.load_mixer_params()
    C.alloc_scratch()
    uin = C.din("U_in", [128, KC, T], BF16)
    mout = C.dout("Mx_out", [2048, T], BF16)
    C.P.dma("sync", C.U, uin, writes=["U"])
    C.P.barrier()
    C.emit_mixer(l, parts)
    C.P.dma("sync", mout, C.Mx, reads=[("Mx", r) for r in range(0, 2048, 64)])
    C.P.emit()
    return nc, C


def fm(a):
    a = np.asarray(a, np.float32)
    lead = a.shape[:-1]
    r = a.reshape(lead + (a.shape[-1] // 128, 128))
    return np.ascontiguousarray(np.moveaxis(r, -1, 0))


def tokens_T(x, ctx, b):
    t = np.concatenate([ctx[b], x[b]], 0)
    return np.ascontiguousarray(t.T.reshape(KC, 128, T).transpose(1, 0, 2))


def rope_tables():
    quarter = 16
    inv = 10000.0 ** (-np.arange(quarter, dtype=np.float32) / quarter)
    rows = SEQ // 64
    row = np.repeat(np.arange(rows, dtype=np.float32), 64)
    col = np.tile(np.arange(64, dtype=np.float32), rows)
    ang = np.concatenate([row[:, None] * inv, col[:, None] * inv], -1).astype(np.float32)
    cos = np.cos(ang).T
    sin = np.sin(ang).T
    c2 = np.ones((64, T), np.float32)
    s2 = np.zeros((64, T), np.float32)
    c2[0:32, CTX:] = cos
    c2[32:64, CTX:] = cos
    s2[0:32, CTX:] = -sin
    s2[32:64, CTX:] = sin
    return c2, s2


def small_inputs(inp, b):
    f32 = lambda a: np.asarray(a, np.float32)
    d = {}
    d["gpre"] = fm(inp["norm_pre"]).reshape(128, -1)
    d["gpost"] = fm(inp["norm_post"]).reshape(128, -1)
    d["bada"] = np.ascontiguousarray(f32(inp["b_ada"]).reshape(2, 144, 128).transpose(2, 0, 1)).reshape(128, -1)
    d["cvec"] = fm(np.stack([f32(inp["c"])[b], f32(inp["c_ctx"])], 0)).reshape(128, -1)
    return d


def mixer_inputs(inp):
    f32 = lambda a: np.asarray(a, np.float32)
    d = {}
    d["lbl"] = np.ascontiguousarray(f32(inp["lb_logits"]).reshape(2, 2, 8, 64).transpose(3, 0, 1, 2)).reshape(64, -1)
    d["anorm"] = np.ascontiguousarray(f32(inp["a_norm"]).T)
    d["bnorm"] = np.ascontiguousarray(f32(inp["b_norm"]).T)
    d["cnorm"] = np.ascontiguousarray(f32(inp["c_norm"]).T)
    d["clam"] = np.ascontiguousarray(f32(inp["c_lambda"]).transpose(2, 0, 1)).reshape(64, -1)
    d["convw"] = np.ascontiguousarray(f32(inp["d_conv_w"]).reshape(2, 4, 4, 128).transpose(3, 0, 2, 1)).reshape(128, -1)
    d["convb"] = np.ascontiguousarray(f32(inp["d_conv_b"]).reshape(2, 4, 128).transpose(2, 0, 1)).reshape(128, -1)
    for nm, src in (("dbr", "d_b_r"), ("dbi", "d_b_i"), ("dlam", "d_lambda")):
        d[nm] = np.ascontiguousarray(f32(inp[src]).reshape(2, 2, 4, 128).transpose(3, 0, 1, 2)).reshape(128, -1)
    c2, s2 = rope_tables()
    d["rope_cos"] = c2
    d["rope_sin"] = s2
    d["d_w_r"] = f32(inp["d_w_r"])
    d["d_w_i"] = f32(inp["d_w_i"])
    return d


def weight_shards(inp, r):
    f32 = lambda a: np.asarray(a, np.float32)
    rs = slice(256 * r, 256 * r + 256)
    p0, p1 = [], []
    for l in range(2):
        for f in range(2):
            p0.append(f32(inp["ffn_w_in"])[l, f, rs, :].reshape(-1))
    for l in range(2):
        p0.append(f32(inp["w_in"])[l, rs, :].reshape(-1))
    for l in range(2):
        p1.append(f32(inp["w_ada"])[l, rs, :].reshape(-1))
    for l in range(2):
        for f in range(2):
            p1.append(np.ascontiguousarray(f32(inp["ffn_w_out"])[l, f, :, rs]).reshape(-1))
    for l in range(2):
        p1.append(f32(inp["w_out"])[l, rs, :].reshape(-1))
    w0 = np.concatenate(p0)
    w1 = np.concatenate(p1)
    assert w0.size == NSH0 and w1.size == NSH1
    return w0.reshape(NSH0 // 2048, 2048), w1.reshape(NSH1 // 2048, 2048)


_CACHE = {}


USE_GATHER = False
N_USED = 4


def kernel(**inp):
    if "nc" not in _CACHE:
        _CACHE["nc"] = build_full(USE_GATHER)[0]
    nc = _CACHE["nc"]
    f32 = lambda a: np.ascontiguousarray(np.asarray(a, np.float32))
    mi = mixer_inputs(inp)
    shared = {}
    if not USE_GATHER:
        shared = dict(w_ada=f32(inp["w_ada"]), ffn_w_in=f32(inp["ffn_w_in"]).reshape(4, D, 2 * DFF),
                      ffn_w_out=f32(inp["ffn_w_out"]).reshape(4, DFF, D), w_in=f32(inp["w_in"]), w_out=f32(inp["w_out"]))
    in_maps = []
    for c in range(N_USED):
        b = c % 4
        m = dict(mi)
        m.update(shared)
        m.update(small_inputs(inp, b))
        m["xT"] = tokens_T(np.asarray(inp["x"], np.float32), np.asarray(inp["ctx"], np.float32), b)
        if USE_GATHER:
            m["wshard0"], m["wshard1"] = weight_shards(inp, c)
        in_maps.append(m)
    res = run_bass_kernel_spmd(nc, in_maps, core_ids=list(range(N_USED)))
    outs = []
    for b in range(4):
        o = np.asarray(res.results[b]["out"])
        outs.append(o.transpose(2, 1, 0).reshape(SEQ, D))
    return np.stack(outs, 0).astype(np.float32)
```

```python
import math
import numpy as np
import ml_dtypes
import concourse.bass as bass
import concourse.mybir as mybir
from concourse.bass_utils import run_bass_kernel_spmd

F32 = mybir.dt.float32
BF16 = mybir.dt.bfloat16
AF = mybir.ActivationFunctionType
ALU = mybir.AluOpType

N_DMA_SEMS = 6
D = 2048
KC = 16
DFF = 5504
NJ = 43
T = 2304
TT = 384
NTT = 6
CTX = 256
SEQ = 2048
EPS = 1e-6
CH = 32
NCH = T // CH
INW = 7168
NCORE = 8
SZ_ADA = 256 * 9 * D
SZ_FIN = 256 * 2 * DFF
SZ_FOUT = DFF * 256
SZ_WIN = 256 * INW
SZ_WOUT = 256 * D
OFF_FIN = 0
OFF_WIN = OFF_FIN + 4 * SZ_FIN
NSH0 = OFF_WIN + 2 * SZ_WIN
OFF_ADA = 0
OFF_FOUT = OFF_ADA + 2 * SZ_ADA
OFF_WOUT = OFF_FOUT + 4 * SZ_FOUT
NSH1 = OFF_WOUT + 2 * SZ_WOUT
NSHG = (NSH0, NSH1)
NSH = NSH0 + NSH1
assert NSH % 2048 == 0


class Prog:
    def __init__(self, nc, arena_bytes):
        self.nc = nc
        self.ops = []
        self.lastw = {}
        self.readers = {}
        self.arena = nc.alloc_sbuf_tensor("arena", [128, arena_bytes // 4], F32)
        self.arena_bytes = arena_bytes
        self.off = 0
        self.hw = 0
        self.last_on = {}
        self.dma_hist = {}

    def alloc(self, shape, dtype=F32):
        esz = 4 if dtype == F32 else 2
        n = 1
        for s in shape[1:]:
            n *= s
        nbytes = (n * esz + 31) // 32 * 32
        assert self.off + nbytes <= self.arena_bytes, ("SBUF arena overflow", self.off, nbytes)
        a = self.arena[:, self.off // 4:(self.off + nbytes) // 4]
        if dtype != F32:
            a = a.bitcast(dtype)
        a = a[0:shape[0], 0:n]
        if len(shape) == 3:
            a = a.rearrange("p (a b) -> p a b", a=shape[1])
        elif len(shape) == 4:
            a = a.rearrange("p (a b c) -> p a b c", a=shape[1], b=shape[2])
        self.off += nbytes
        self.hw = max(self.hw, self.off)
        return a

    def mark(self):
        return self.off

    def release(self, m):
        self.barrier()
        self.off = m

    def _expand(self, keys):
        g = getattr(self, "groups", None)
        if not g:
            return keys
        out = []
        for k in keys:
            out.extend(g.get(k, (k,)))
        return out

    def op(self, eng, fn, reads=(), writes=(), dma=False, force=False):
        reads = self._expand(reads)
        writes = self._expand(writes)
        i = len(self.ops)
        deps = set()
        for r in reads:
            if r in self.lastw:
                deps.add(self.lastw[r])
        for w in writes:
            if w in self.lastw:
                deps.add(self.lastw[w])
            for rd in self.readers.get(w, ()):
                deps.add(rd)
        for r in reads:
            self.readers.setdefault(r, []).append(i)
        for w in writes:
            self.lastw[w] = i
            self.readers[w] = []
        self.ops.append(dict(eng=eng, fn=fn, deps=deps, dma=dma, force=(force or getattr(self, "force_all", False))))
        if dma:
            self.dma_hist.setdefault(eng, []).append(i)
        else:
            self.last_on[eng] = i
        return i

    def barrier(self):
        last = [v for v in self.last_on.values()]
        dm = []
        for q, h in self.dma_hist.items():
            dm += h[-N_DMA_SEMS:]
        engs = set(self.last_on.keys()) | set(self.dma_hist.keys()) | {"tensor", "vector", "scalar", "gpsimd", "sync"}
        for e in sorted(engs):
            i = len(self.ops)
            self.ops.append(dict(eng=e, fn=lambda en: en.nop(), deps=set(last) | set(dm), dma=False))
            self.last_on[e] = i
        self.lastw = {}
        self.readers = {}

    def mm(self, out, lhsT, rhs, start=True, stop=True, reads=(), writes=()):
        return self.op("tensor", lambda e: e.matmul(out, lhsT, rhs, start=start, stop=stop), reads, writes)

    def dma(self, q, out, in_, reads=(), writes=(), **kw):
        return self.op(q, lambda e: e.dma_start(out=out, in_=in_, **kw), reads, writes, dma=True)

    def V(self, fn, reads=(), writes=(), force=False):
        return self.op("vector", fn, reads, writes, force=force)

    def S(self, fn, reads=(), writes=()):
        return self.op("scalar", fn, reads, writes)

    def G(self, fn, reads=(), writes=()):
        return self.op("gpsimd", fn, reads, writes)

    def emit(self):
        nc = self.nc
        ops = self.ops
        need_sig = [False] * len(ops)
        for i, o in enumerate(ops):
            for d in o["deps"]:
                pd = ops[d]
                if pd["dma"]:
                    continue
                if pd["eng"] != o["eng"] or o["dma"] or o.get("force"):
                    need_sig[d] = True
        used = sorted({o["eng"] for o in ops})
        esem = {e: nc.alloc_semaphore("es_" + e) for e in used}
        dq = [e for e in used if any(o["dma"] and o["eng"] == e for o in ops)]
        dsems = {e: [nc.alloc_semaphore("ds_%s%d" % (e, k)) for k in range(N_DMA_SEMS)] for e in dq}
        cnt = {e: 0 for e in used}
        dcnt = {e: 0 for e in used}
        for i, o in enumerate(ops):
            if o["dma"]:
                k = dcnt[o["eng"]]
                dcnt[o["eng"]] += 1
                o["dsem"] = dsems[o["eng"]][k % N_DMA_SEMS]
                o["dval"] = 16 * (k // N_DMA_SEMS + 1)
            elif need_sig[i]:
                cnt[o["eng"]] += 1
                o["sig"] = cnt[o["eng"]]
        streams = {e: [] for e in used}
        for i, o in enumerate(ops):
            streams[o["eng"]].append(i)
        self.n_waits = 0

        def run_engine(ename, eng):
            waited = {}

            def wait(sem, val):
                key = id(sem)
                if waited.get(key, 0) >= val:
                    return
                waited[key] = val
                eng.wait_ge(sem, val)
                self.n_waits += 1

            for i in streams.get(ename, ()):
                o = ops[i]
                for d in sorted(o["deps"]):
                    pd = ops[d]
                    if pd["dma"]:
                        wait(pd["dsem"], pd["dval"])
                    elif "sig" in pd:
                        if pd["eng"] == ename and not o["dma"] and not o.get("force"):
                            continue
                        wait(esem[pd["eng"]], pd["sig"])
                if o["dma"]:
                    if o["dval"] > 16:
                        wait(o["dsem"], o["dval"] - 16)
                    o["fn"](eng).then_inc(o["dsem"], 16)
                else:
                    ins = o["fn"](eng)
                    if "sig" in o:
                        ins.then_inc(esem[ename], 1)
            if ename in dsems:
                done = {}
                for i in streams.get(ename, ()):
                    o = ops[i]
                    if o["dma"]:
                        done[id(o["dsem"])] = (o["dsem"], o["dval"])
                for sem, val in done.values():
                    wait(sem, val)

        with nc.Block() as block:
            if "sync" in streams:
                @block.sync
                def _(e):
                    run_engine("sync", e)
            if "tensor" in streams:
                @block.tensor
                def _(e):
                    run_engine("tensor", e)
            if "vector" in streams:
                @block.vector
                def _(e):
                    run_engine("vector", e)
            if "scalar" in streams:
                @block.scalar
                def _(e):
                    run_engine("scalar", e)
            if "gpsimd" in streams:
                @block.gpsimd
                def _(e):
                    run_engine("gpsimd", e)


def tile_segs(tt):
    if tt == 0:
        return [(0, 256, 1), (256, 384, 0)]
    return [(0, 384, 0)]


class Core:
    def __init__(self, nc, gathered):
        self.nc = nc
        self.gathered = gathered
        self.P = Prog(nc, 204 * 1024)
        P = self.P
        self.ps = [nc.alloc_psum_tensor("ps%d" % i, [128, 512], F32) for i in range(7)]
        self.psT = nc.alloc_psum_tensor("psT", [128, 1024], BF16)
        self.ps_rr = 0
        self.ext = {}
        P.force_all = True
        self.ones_bf = P.alloc([128, 128], BF16)
        self.ones1 = P.alloc([128, 128], BF16)
        self.ones64 = P.alloc([128, 128], BF16)
        self.ones128 = P.alloc([128, 128], BF16)
        P.G(lambda e: e.memset(self.ones_bf, 1.0 / D), [], ["c_ones"])
        P.G(lambda e: e.memset(self.ones1, 1.0), [], ["c_ones"])
        P.G(lambda e: e.memset(self.ones64, 1.0 / 64), [], ["c_ones"])
        P.G(lambda e: e.memset(self.ones128, 1.0 / 128), [], ["c_ones"])
        self.eps_t = P.alloc([128, 1])
        P.G(lambda e: e.memset(self.eps_t, EPS), [], ["eps_t"])
        self.one_t = P.alloc([128, 1])
        P.G(lambda e: e.memset(self.one_t, 1.0), [], ["eps_t"])
        self.ident = P.alloc([128, 128])
        self.ident_bf = P.alloc([128, 128], BF16)
        P.G(lambda e: e.memset(self.ident, 0.0), [], ["ident"])
        P.G(lambda e: e.affine_select(self.ident, self.ident, pattern=[[-1, 128]], compare_op=ALU.not_equal,
                                      fill=1.0, base=0, channel_multiplier=1), ["ident"], ["ident"])
        P.V(lambda e: e.tensor_copy(self.ident_bf, self.ident), ["ident"], ["ident_bf"])
        self.mask = [P.alloc([CH, CH]), P.alloc([CH, CH])]
        for d in range(2):
            m = self.mask[d]
            P.G(lambda e, m=m: e.memset(m, 1.0), [], [("mask", d)])
            pat, cm = ([[1, CH]], -1) if d == 0 else ([[-1, CH]], 1)
            P.G(lambda e, m=m, pat=pat, cm=cm: e.affine_select(m, m, pattern=pat, compare_op=ALU.is_ge, fill=0.0,
                                                              base=0, channel_multiplier=cm),
                [("mask", d)], [("mask", d)])
        P.force_all = False

    def din(self, name, shape, dtype=F32):
        if name not in self.ext:
            self.ext[name] = self.nc.dram_tensor(name, list(shape), dtype, kind="ExternalInput").ap()
        return self.ext[name]

    def dout(self, name, shape, dtype=F32):
        return self.nc.dram_tensor(name, list(shape), dtype, kind="ExternalOutput").ap()

    def dint(self, name, shape, dtype=F32, **kw):
        return self.nc.dram_tensor(name, list(shape), dtype, **kw).ap()

    def psum(self, lo=0, hi=6):
        i = lo + self.ps_rr % (hi - lo)
        self.ps_rr += 1
        return self.ps[i], ("ps", i)

    def gather_weights(self):
        P = self.P
        nc = self.nc
        self.G = []
        ccsem = nc.alloc_semaphore("ccsem")
        rg = [list(range(NCORE))]
        mk = P.mark()
        for gi, NS in enumerate(NSHG):
            R = NS // 2048
            wsh = self.din("wshard%d" % gi, [R, 2048])
            wbf = self.dint("wbf%d" % gi, [R, 2048], BF16)
            G = self.dint("wgath%d" % gi, [NCORE * R, 2048], BF16, addr_space="Shared")
            n = NS // 128
            cs = n // 32
            src = wsh.rearrange("a c -> (a c)").rearrange("(p n) -> p n", p=128)
            dst = wbf.rearrange("a c -> (a c)").rearrange("(p n) -> p n", p=128)
            bb = [P.alloc([128, cs], BF16) for _ in range(3)]
            for i in range(32):
                b, bk = bb[i % 3], ("castb", gi, i % 3)
                P.dma("gpsimd", b, src[:, i * cs:(i + 1) * cs], writes=[bk])
                P.dma("sync", dst[:, i * cs:(i + 1) * cs], b, reads=[bk], writes=[("wbf", gi)])

            def ccfn(e, wbf=wbf, G=G, gi=gi):
                e.collective_compute("AllGather", ALU.bypass, replica_groups=rg, ins=[wbf.opt()],
                                     outs=[G.opt()]).then_inc(ccsem)
                e.wait_ge(ccsem, gi + 1)
                return e.nop()
            P.op("gpsimd", ccfn, reads=[("wbf", gi)], writes=["wgath"])
            self.G.append(G.rearrange("(r a) c -> r (a c)", r=NCORE))
        P.release(mk)

    def _wmeta(self, name, idx):
        one = getattr(self, "test_one", False)
        if name == "w_ada":
            return 1, OFF_ADA + idx * SZ_ADA, 9 * D, [1 if one else 2, D, 9 * D]
        if name == "ffn_w_in":
            return 0, OFF_FIN + idx * SZ_FIN, 2 * DFF, [1 if one else 4, D, 2 * DFF]
        if name == "w_in":
            return 0, OFF_WIN + idx * SZ_WIN, INW, [2, D, INW]
        if name == "w_out":
            return 1, OFF_WOUT + idx * SZ_WOUT, D, [2, D, D]
        raise KeyError(name)

    def load_w(self, dst, name, idx, c0, ncols, key):
        P = self.P
        grp, off, cols, shp = self._wmeta(name, idx)
        if self.gathered:
            if not hasattr(P, "groups"):
                P.groups = {}
            P.groups[key] = [(key, "r", r) for r in range(NCORE)]
            for r in range(NCORE):
                src = self.G[grp][r:r + 1, off:off + 256 * cols].rearrange("o (kk p c) -> p (o kk) c", p=128, c=cols)[
                    :, :, c0:c0 + ncols]
                P.dma("gpsimd", dst[:, 2 * r:2 * r + 2, :], src, reads=["wgath"], writes=[(key, "r", r)])
        else:
            w = self.din(name, shp)
            src = w[idx].rearrange("(k p) c -> p k c", p=128)[:, :, c0:c0 + ncols]
            P.dma("gpsimd", dst, src, writes=[key])

    def load_w_fout(self, dst, idx, j0, j1, m, key):
        P = self.P
        if self.gathered:
            off = OFF_FOUT + idx * SZ_FOUT
            r = m // 2
            src = self.G[1][r:r + 1, off:off + SZ_FOUT].rearrange("o (j p c) -> p (o j) c", p=128, c=256)[
                :, j0:j1, (m % 2) * 128:(m % 2) * 128 + 128]
            P.dma("gpsimd", dst, src, reads=["wgath"], writes=[key])
        else:
            w = self.din("ffn_w_out", [1 if getattr(self, "test_one", False) else 4, DFF, D])
            src = w[idx].rearrange("(j p) c -> p j c", p=128)[:, j0:j1, m * 128:(m + 1) * 128]
            P.dma("gpsimd", dst, src, writes=[key])

    def load_small(self):
        P = self.P
        self.gpre = P.alloc([128, 2 * 3 * KC])
        self.gpost = P.alloc([128, 2 * 3 * KC])
        self.bada = P.alloc([128, 2 * 144])
        self.cs_f = P.alloc([128, 2 * KC])
        P.dma("sync", self.gpre, self.din("gpre", [128, 2 * 3 * KC]), writes=["gpre"])
        P.dma("sync", self.gpost, self.din("gpost", [128, 2 * 3 * KC]), writes=["gpost"])
        P.dma("sync", self.bada, self.din("bada", [128, 2 * 144]), writes=["bada"])
        P.dma("sync", self.cs_f, self.din("cvec", [128, 2 * KC]), writes=["cs_f"])
        self.csT = P.alloc([128, KC, 2], BF16)
        P.S(lambda e: e.activation(self.csT.rearrange("p k v -> p v k"),
                                   self.cs_f.rearrange("p (v k) -> p v k", v=2), AF.Silu),
            ["cs_f"], ["csT"])
        self.mod = [None, None]
        self.Apre = {}
        self.Cg = {}

    def emit_mods(self, l):
        P = self.P
        mod = P.alloc([128, 144, 2])
        self.mod[l] = mod
        for s in range(3):
            self.Apre[(l, s)] = P.alloc([128, KC, 2])
            self.Cg[(l, s)] = P.alloc([128, KC, 2])
        mk = P.mark()
        wb = [P.alloc([128, KC, 256], BF16) for _ in range(3)]
        pst, psk = self.ps[6], ("ps", 6)
        for g in range(72):
            b, bk = wb[g % 3], ("wada", g % 3)
            self.load_w(b, "w_ada", l, g * 256, 256, bk)
            for jj in range(2):
                cc = g * 2 + jj
                for k in range(KC):
                    P.mm(pst[:, cc * 2:cc * 2 + 2], b[:, k, jj * 128:(jj + 1) * 128], self.csT[:, k, :],
                         start=(k == 0), stop=(k == KC - 1), reads=[bk, "csT"], writes=[psk])
        P.force_all = True
        P.V(lambda e: e.tensor_tensor(mod, pst[:, 0:288].rearrange("p (c v) -> p c v", v=2),
                                      self.bada[:, l * 144:(l + 1) * 144].unsqueeze(2).to_broadcast([128, 144, 2]),
                                      ALU.add),
            [psk, "bada"], [("mod", l)])
        for s in range(3):
            A = self.Apre[(l, s)]
            C = self.Cg[(l, s)]
            sc = mod[:, (3 * s + 1) * KC:(3 * s + 2) * KC, :]
            gt = mod[:, (3 * s + 2) * KC:(3 * s + 3) * KC, :]
            o0 = (l * 3 + s) * KC
            gp = self.gpre[:, o0:o0 + KC].unsqueeze(2).to_broadcast([128, KC, 2])
            gq = self.gpost[:, o0:o0 + KC].unsqueeze(2).to_broadcast([128, KC, 2])
            P.V(lambda e, A=A, sc=sc, gp=gp: e.scalar_tensor_tensor(A, sc, 1.0, gp, ALU.add, ALU.mult),
                [("mod", l), "gpre"], [("A", l, s)])
            rs = 1.0 if s == 1 else 0.5
            P.V(lambda e, C=C, gt=gt, gq=gq, rs=rs: e.scalar_tensor_tensor(C, gt, rs, gq, ALU.mult, ALU.mult),
                [("mod", l), "gpost"], [("C", l, s)])
        P.force_all = False
        P.release(mk)

    def shift(self, l, s, k, v):
        return self.mod[l][:, 3 * s * KC + k, v:v + 1]

    def alloc_tile_bufs(self):
        P = self.P
        self.h = P.alloc([128, KC, TT])
        self.u = P.alloc([128, KC, TT], BF16)
        self.hT = P.alloc([128, NJ, TT], BF16)
        self.y = P.alloc([128, KC, TT])
        self.wb = [P.alloc([128, 4096], BF16) for _ in range(4)]
        self.wbi = 0
        self.sqb = [P.alloc([128, TT], BF16) for _ in range(2)]
        self.tmpf = [P.alloc([128, TT]) for _ in range(2)]
        self.rstd = P.alloc([128, TT])
        self.sg = [P.alloc([128, TT]) for _ in range(2)]

    def emit_norm_mod(self, l, s, src, src_key, dst, dst_key, segs):
        P = self.P
        pss, pssk = self.ps[6], ("ps", 6)
        rstd = self.rstd
        for k in range(KC):
            b, bk = self.sqb[k % 2], ("sqb", k % 2)
            P.S(lambda e, b=b, k=k: e.activation(b, src[:, k, :], AF.Square), [src_key], [bk])
            P.mm(pss[:, 0:TT], self.ones_bf, b, start=(k == 0), stop=(k == KC - 1),
                 reads=[bk, "c_ones"], writes=[pssk])
        P.S(lambda e: e.activation(rstd, pss[:, 0:TT], AF.Sqrt, bias=self.eps_t), [pssk, "eps_t"], ["rstd"])
        P.V(lambda e: e.reciprocal(rstd, rstd), ["rstd"], ["rstd"])
        A = self.Apre[(l, s)]
        for k in range(KC):
            t, tk = self.tmpf[k % 2], ("tmpf", k % 2)
            for (a, bnd, v) in segs:
                P.V(lambda e, t=t, k=k, a=a, bnd=bnd, v=v: e.scalar_tensor_tensor(
                    t[:, a:bnd], src[:, k, a:bnd], A[:, k, v:v + 1], rstd[:, a:bnd], ALU.mult, ALU.mult),
                    [src_key, "rstd", ("A", l, s)], [tk])
            for (a, bnd, v) in segs:
                P.S(lambda e, t=t, k=k, a=a, bnd=bnd, v=v: e.activation(
                    dst[:, k, a:bnd], t[:, a:bnd], AF.Identity, bias=self.shift(l, s, k, v)),
                    [tk, ("mod", l)], [dst_key])

    def emit_post_resid(self, l, s, segs):
        P = self.P
        pss, pssk = self.ps[6], ("ps", 6)
        rstd, h, y = self.rstd, self.h, self.y
        P.S(lambda e: e.activation(rstd, pss[:, 0:TT], AF.Sqrt, bias=self.eps_t), [pssk, "eps_t"], ["rstd"])
        P.V(lambda e: e.reciprocal(rstd, rstd), ["rstd"], ["rstd"])
        C = self.Cg[(l, s)]
        for k in range(KC):
            t, tk = self.tmpf[k % 2], ("tmpf", k % 2)
            for (a, bnd, v) in segs:
                P.V(lambda e, t=t, k=k, a=a, bnd=bnd, v=v: e.scalar_tensor_tensor(
                    t[:, a:bnd], y[:, k, a:bnd], C[:, k, v:v + 1], rstd[:, a:bnd], ALU.mult, ALU.mult),
                    [("y", k), "rstd", ("C", l, s)], [tk])
            P.V(lambda e, t=t, k=k: e.tensor_tensor(h[:, k, :], h[:, k, :], t, ALU.add), [tk, "h"], ["h"])

    def evac_y(self, m, py, pyk):
        P = self.P
        pss, pssk = self.ps[6], ("ps", 6)
        P.V(lambda e: e.tensor_copy(self.y[:, m, :], py[:, 0:TT]), [pyk], [("y", m)])
        import os
        if os.environ.get("KDBG2", "") == "ev1":
            return
        b2, b2k = self.sqb[m % 2], ("sqb", m % 2)
        P.S(lambda e: e.activation(b2, self.y[:, m, :], AF.Square), [("y", m)], [b2k])
        if os.environ.get("KDBG2", "") == "ev2":
            return
        P.mm(pss[:, 0:TT], self.ones_bf, b2, start=(m == 0), stop=(m == KC - 1),
             reads=[b2k, "c_ones"], writes=[pssk])

    def emit_ffn_tile(self, l, f, s, segs):
        P = self.P
        u, hT = self.u, self.hT
        wb = self.wb
        self.emit_norm_mod(l, s, self.h, "h", u, "u", segs)
        idx = l * 2 + f
        if not hasattr(self, "wc"):
            self.wc, self.wc_done = {}, set()
        if idx not in self.wc:
            self.wc[idx] = (self.dint("wci%d" % idx, [22, 2, 128, 4096], BF16),
                            self.dint("wco%d" % idx, [KC, 2, 128, 2816], BF16))
        wci, wco = self.wc[idx]
        cached = idx in self.wc_done
        import os
        dbg = int(os.environ.get("KDBG", "9"))
        if dbg < 2:
            return
        for g in range((NJ + 1) // 2):
            nj = 2 if 2 * g + 1 < NJ else 1
            bg, bu = wb[self.wbi % 4], wb[(self.wbi + 1) % 4]
            kg, ku = ("wb", self.wbi % 4), ("wb", (self.wbi + 1) % 4)
            self.wbi += 2
            bgv = bg[:, 0:KC * nj * 128].rearrange("p (k c) -> p k c", k=KC)
            buv = bu[:, 0:KC * nj * 128].rearrange("p (k c) -> p k c", k=KC)
            nel = KC * nj * 128
            if not cached:
                self.load_w(bgv, "ffn_w_in", idx, g * 256, nj * 128, kg)
                self.load_w(buv, "ffn_w_in", idx, DFF + g * 256, nj * 128, ku)
                P.dma("sync", wci[g, 0, :, 0:nel], bg[:, 0:nel], reads=[kg], writes=[("wc", idx, "i", g, 0)])
                P.dma("sync", wci[g, 1, :, 0:nel], bu[:, 0:nel], reads=[ku], writes=[("wc", idx, "i", g, 1)])
            else:
                P.dma("gpsimd", bg[:, 0:nel], wci[g, 0, :, 0:nel], reads=[("wc", idx, "i", g, 0)], writes=[kg])
                P.dma("gpsimd", bu[:, 0:nel], wci[g, 1, :, 0:nel], reads=[("wc", idx, "i", g, 1)], writes=[ku])
            for jj in range(nj):
                j = 2 * g + jj
                pg, pgk = self.psum()
                pu, puk = self.psum()
                for k in range(KC):
                    P.mm(pg[:, 0:TT], bgv[:, k, jj * 128:(jj + 1) * 128], u[:, k, :],
                         start=(k == 0), stop=(k == KC - 1), reads=[kg, "u"], writes=[pgk])
                for k in range(KC):
                    P.mm(pu[:, 0:TT], buv[:, k, jj * 128:(jj + 1) * 128], u[:, k, :],
                         start=(k == 0), stop=(k == KC - 1), reads=[ku, "u"], writes=[puk])
                sgb, sk = self.sg[j % 2], ("sg", j % 2)
                P.S(lambda e, sgb=sgb, pg=pg: e.activation(sgb, pg[:, 0:TT], AF.Silu), [pgk], [sk])
                P.V(lambda e, sgb=sgb, pu=pu, j=j: e.tensor_tensor(hT[:, j, :], sgb, pu[:, 0:TT], ALU.mult),
                    [sk, puk], [("hT", j)])
        if dbg < 3:
            return
        for m in range(KC):
            py, pyk = self.psum()
            for half in range(2):
                j0, j1 = (0, 22) if half == 0 else (22, NJ)
                b, bk = wb[self.wbi % 4], ("wb", self.wbi % 4)
                self.wbi += 1
                bv = b[:, 0:(j1 - j0) * 128].rearrange("p (j c) -> p j c", c=128)
                nel = (j1 - j0) * 128
                if not cached:
                    self.load_w_fout(bv, idx, j0, j1, m, bk)
                    P.dma("sync", wco[m, half, :, 0:nel], b[:, 0:nel], reads=[bk], writes=[("wc", idx, "o", m, half)])
                else:
                    P.dma("gpsimd", b[:, 0:nel], wco[m, half, :, 0:nel], reads=[("wc", idx, "o", m, half)], writes=[bk])
                if os.environ.get("KDBG2", "") == "dma":
                    continue
                for j in range(j0, j1):
                    P.mm(py[:, 0:TT], bv[:, j - j0, :], hT[:, j, :], start=(j == 0), stop=(j == NJ - 1),
                         reads=[bk, ("hT", j)], writes=[pyk])
            if os.environ.get("KDBG2", "") in ("dma", "mm"):
                continue
            self.evac_y(m, py, pyk)
        self.wc_done.add(idx)
        self.emit_post_resid(l, s, segs)

    def emit_wout_tile(self, l, tt, segs):
        P = self.P
        mixt = self.hT[:, 0:KC, :]
        t0 = tt * TT
        for k in range(KC):
            P.dma("sync", mixt[:, k, :], self.Mx[k * 128:(k + 1) * 128, t0:t0 + TT], reads=["Mx"], writes=[("hT", k)])
        for mp in range(KC // 2):
            b, bk = self.wb[self.wbi % 4], ("wb", self.wbi % 4)
            self.wbi += 1
            bv = b[:, 0:KC * 256].rearrange("p (k c) -> p k c", k=KC)
            self.load_w(bv, "w_out", l, mp * 256, 256, bk)
            for jj in range(2):
                m = 2 * mp + jj
                py, pyk = self.psum()
                for k in range(KC):
                    P.mm(py[:, 0:TT], bv[:, k, jj * 128:(jj + 1) * 128], mixt[:, k, :],
                         start=(k == 0), stop=(k == KC - 1), reads=[bk, ("hT", k)], writes=[pyk])
                self.evac_y(m, py, pyk)
        self.emit_post_resid(l, 1, segs)

    def load_mixer_params(self):
        P = self.P
        P.force_all = True
        ld = lambda name, shape: (P.alloc(shape), self.din(name, shape))
        def L(name, shape):
            t, src = ld(name, shape)
            P.dma("sync", t, src, writes=[name])
            return t
        self.lbl = L("lbl", [64, 2 * 2 * 8])
        self.anorm = L("anorm", [64, 2])
        self.bnorm = L("bnorm", [64, 2])
        self.cnorm = L("cnorm", [128, 2])
        self.clam = L("clam", [64, 2 * 4])
        self.convw = L("convw", [128, 2 * 4 * 4])
        self.convb = L("convb", [128, 2 * 4])
        self.dbr = L("dbr", [128, 2 * 2 * 4])
        self.dbi = L("dbi", [128, 2 * 2 * 4])
        self.dlam = L("dlam", [128, 2 * 2 * 4])
        self.cos_d = self.din("rope_cos", [64, T])
        self.sin_d = self.din("rope_sin", [64, T])
        self.lb1 = P.alloc([64, 16])
        self.omlb1 = P.alloc([64, 16])
        P.V(lambda e: e.tensor_tensor(self.lb1, self.lbl[:, 16:32], self.lbl[:, 0:16], ALU.subtract), ["lbl"], ["lb1"])
        P.S(lambda e: e.activation(self.lb1, self.lb1, AF.Sigmoid), ["lb1"], ["lb1"])
        P.V(lambda e: e.tensor_scalar(self.omlb1, self.lb1, -1.0, 1.0, ALU.mult, ALU.add), ["lb1"], ["omlb1"])
        self.zero_t = P.alloc([128, 1])
        P.G(lambda e: e.memset(self.zero_t, 0.0), [], ["eps_t"])
        self.cm = P.alloc([64, T])
        P.G(lambda e: e.memset(self.cm, 1.0), [], ["cm"])
        P.G(lambda e: e.memset(self.cm.rearrange("p (c i) -> p c i", i=CH)[:, :, 0:1], 0.0), ["cm"], ["cm"])
        self.neglam = P.alloc([128, 2])
        self.cgain = P.alloc([128, 2])
        prod = P.alloc([64, 2, 2])
        cl = self.clam.rearrange("p (l f) -> p l f", l=2)
        P.V(lambda e: e.tensor_tensor(prod[:, :, 0], cl[:, :, 0], cl[:, :, 1], ALU.mult), ["clam"], ["prod"])
        P.V(lambda e: e.tensor_tensor(prod[:, :, 1], cl[:, :, 2], cl[:, :, 3], ALU.mult), ["clam", "prod"], ["prod"])
        prodb = P.alloc([64, 4], BF16)
        P.V(lambda e: e.tensor_copy(prodb, prod.rearrange("p l f -> p (l f)")), ["prod"], ["prodb"])
        pl, plk = self.ps[6], ("ps", 6)
        P.mm(pl[:, 0:4], self.ones1[0:64, :], prodb, reads=["c_ones", "prodb"], writes=[plk])
        ex = P.alloc([128, 4])
        P.S(lambda e: e.activation(ex, pl[:, 0:4], AF.Exp), [plk], ["ex"])
        for l in range(2):
            li = 0.8 - 0.6 * math.exp(-0.3 * l)
            P.V(lambda e, l=l: e.tensor_tensor(self.neglam[:, l:l + 1], ex[:, 2 * l + 1:2 * l + 2], ex[:, 2 * l:2 * l + 1],
                                               ALU.subtract), ["ex"], ["neglam"])
            P.V(lambda e, l=l, li=li: e.tensor_scalar(self.neglam[:, l:l + 1], self.neglam[:, l:l + 1], -li, None, ALU.add),
                ["neglam"], ["neglam"])
            P.V(lambda e, l=l, li=li: e.tensor_scalar(self.cgain[:, l:l + 1], self.cnorm[:, l:l + 1], 1.0 - li, None, ALU.mult),
                ["cnorm"], ["cgain"])
        self.c1 = P.alloc([128, 16])
        P.S(lambda e: e.activation(self.c1, self.dlam, AF.Exp, scale=-1.0), ["dlam"], ["c1"])
        P.S(lambda e: e.activation(self.c1, self.c1, AF.Ln, bias=self.one_t), ["c1", "eps_t"], ["c1"])
        P.V(lambda e: e.tensor_scalar(self.c1, self.c1, -8.0, None, ALU.mult), ["c1"], ["c1"])
        P.force_all = False

    def alloc_scratch(self):
        self.H = self.dint("Hres", [128, KC, T])
        self.U = self.dint("Umix", [128, KC, T], BF16)
        self.Z = self.dint("Zfm", [INW, T])
        self.Zt = self.dint("Ztm", [T, 1536])
        self.Mx = self.dint("Mx", [2048, T], BF16)

    def emit_proj(self, l):
        P = self.P
        mk = P.mark()
        uf = P.alloc([128, KC, T], BF16)
        P.dma("sync", uf, self.U, reads=["U"], writes=["uf"])
        wb = [P.alloc([128, KC, 256], BF16) for _ in range(3)]
        ev = [P.alloc([128, 512]) for _ in range(4)]
        tbl = [(0, 512), (512, 1024), (1024, 1536), (1536, 2048), (2048, 2304)]
        n = 0
        for g in range(28):
            b, bk = wb[g % 3], ("pw", g % 3)
            self.load_w(b, "w_in", l, g * 256, 256, bk)
            for jj in range(2):
                r0 = g * 256 + jj * 128
                for (a, e_) in tbl:
                    nn = e_ - a
                    ps, psk = self.psum()
                    for k in range(KC):
                        P.mm(ps[:, 0:nn], b[:, k, jj * 128:(jj + 1) * 128], uf[:, k, a:e_],
                             start=(k == 0), stop=(k == KC - 1), reads=[bk, "uf"], writes=[psk])
                    evb, evk = ev[n % 4], ("ev", n % 4)
                    if n % 2 == 0:
                        P.V(lambda e, evb=evb, ps=ps, nn=nn: e.tensor_copy(evb[:, 0:nn], ps[:, 0:nn]), [psk], [evk])
                    else:
                        P.S(lambda e, evb=evb, ps=ps, nn=nn: e.copy(evb[:, 0:nn], ps[:, 0:nn]), [psk], [evk])
                    P.dma("sync", self.Z[r0:r0 + 128, a:e_], evb[:, 0:nn], reads=[evk], writes=[("Z", r0 // 128)])
                    n += 1
        for bi, blk in enumerate([3, 7, 11]):
            for half in range(2):
                b, bk = wb[n % 3], ("pw", n % 3)
                self.load_w(b, "w_in", l, blk * 512 + half * 256, 256, bk)
                for tb in range(T // 128):
                    ps, psk = self.psum()
                    for k in range(KC):
                        P.mm(ps[:, 0:256], uf[:, k, tb * 128:(tb + 1) * 128], b[:, k, :],
                             start=(k == 0), stop=(k == KC - 1), reads=[bk, "uf"], writes=[psk])
                    evb, evk = ev[n % 4], ("ev", n % 4)
                    if n % 2 == 0:
                        P.V(lambda e, evb=evb, ps=ps: e.tensor_copy(evb[:, 0:256], ps[:, 0:256]), [psk], [evk])
                    else:
                        P.S(lambda e, evb=evb, ps=ps: e.copy(evb[:, 0:256], ps[:, 0:256]), [psk], [evk])
                    c0 = bi * 512 + half * 256
                    P.dma("sync", self.Zt[tb * 128:(tb + 1) * 128, c0:c0 + 256], evb[:, 0:256], reads=[evk],
                          writes=[("Zt", bi, half, tb)])
                    n += 1
        P.release(mk)

    def emit_gla(self, l, kind, heads):
        P = self.P
        mk = P.mark()
        CH = 64 if kind == "B" else 32
        NCH = T // CH
        NCTX = CTX // CH
        Fa = lambda: P.alloc([64, T])
        Ba = lambda: P.alloc([64, T], BF16)
        q, kk, lf, bf, b, tmp = [Fa() for _ in range(6)]
        E = tmp
        kh = [Ba(), Ba()]
        Z, Zt = self.Z, self.Zt
        three = lambda a: a.rearrange("p (c i) -> p c i", i=CH)
        if CH == 32:
            cm, masks = self.cm, self.mask
        else:
            cm = Fa()
            P.op("gpsimd", lambda e: e.memset(cm, 1.0), [], ["cmL"], force=True)
            P.op("gpsimd", lambda e: e.memset(cm.rearrange("p (c i) -> p c i", i=CH)[:, :, 0:1], 0.0), ["cmL"], ["cmL"],
                 force=True)
            masks = [P.alloc([CH, CH]), P.alloc([CH, CH])]
            for d in range(2):
                m = masks[d]
                pat, cmul = ([[1, CH]], -1) if d == 0 else ([[-1, CH]], 1)
                P.op("gpsimd", lambda e, m=m: e.memset(m, 1.0), [], [("maskL", d)], force=True)
                P.op("gpsimd", lambda e, m=m, pat=pat, cmul=cmul: e.affine_select(
                    m, m, pattern=pat, compare_op=ALU.is_ge, fill=0.0, base=0, channel_multiplier=cmul),
                    [("maskL", d)], [("maskL", d)], force=True)
        LS = []
        for hi, h in enumerate(heads):
            ls = dict(qt=[Ba(), Ba()], kt=[Ba(), Ba()],
                      khT=[P.alloc([CH, NCH, 64], BF16) for _ in range(2)],
                      vtm=P.alloc([CH, NCH, 64], BF16), oacc=Fa(),
                      gdec=[P.alloc([64, NCH]) for _ in range(2)],
                      S32=[P.alloc([64, 64]) for _ in range(2)],
                      Sbf=[P.alloc([64, 64], BF16) for _ in range(2)],
                      sTb=[[P.alloc([CH, CH], BF16) for _ in range(2)] for _ in range(2)])
            LS.append(ls)
        for hi, h in enumerate(heads):
            ls = LS[hi]
            qt, kt, khT, vtm, oacc, gdec = ls["qt"], ls["kt"], ls["khT"], ls["vtm"], ls["oacc"], ls["gdec"]
            rows = lambda blk, h=h: Z[blk * 512 + 64 * h:blk * 512 + 64 * h + 64, :]

            def rows_sw(dst, blk, key, h=h):
                r0 = blk * 512 + 64 * h
                P.dma("sync", dst[0:32, :], Z[r0 + 32:r0 + 64, :], writes=[key])
                P.dma("sync", dst[32:64, :], Z[r0:r0 + 32, :], writes=[key])

            if kind == "A":
                P.dma("sync", q, rows(0), writes=["q"])
                P.dma("gpsimd", vtm, Zt[:, 64 * h:64 * h + 64].rearrange("(c p) d -> p c d", p=CH),
                      writes=[("vtm", hi)])
                P.S(lambda e: e.activation(q, q, AF.Silu), ["q"], ["q"])
            else:
                cos, sin = bf, b
                P.dma("sync", cos, self.cos_d, writes=["bf"])
                P.dma("sync", sin, self.sin_d, writes=["b"])
                P.dma("gpsimd", vtm, Zt[:, 512 + 64 * h:512 + 64 * h + 64].rearrange("(c p) d -> p c d", p=CH),
                      writes=[("vtm", hi)])
                for (dst, blk, sc) in ((q, 5, 1.0), (kk, 6, 0.125)):
                    dk = "q" if dst is q else "kk"
                    P.dma("sync", dst, rows(blk), writes=[dk])
                    rows_sw(tmp, blk, "tmp")
                    P.V(lambda e, dst=dst: e.tensor_tensor(dst, dst, cos, ALU.mult), [dk, "bf"], [dk])
                    P.V(lambda e: e.tensor_tensor(tmp, tmp, sin, ALU.mult), ["tmp", "b"], ["tmp"])
                    P.V(lambda e, dst=dst: e.tensor_tensor(dst, dst, tmp, ALU.add), [dk, "tmp"], [dk])
                    if sc != 1.0:
                        P.V(lambda e, dst=dst, sc=sc: e.tensor_scalar(dst, dst, sc, None, ALU.mult), [dk], [dk])
                lg = math.log1p(-2.0 ** (-5.0 - h))
            P.G(lambda e, oacc=oacc: e.memset(oacc, 0.0), [], [("oacc", hi)])
            for d in range(2):
                if kind == "A":
                    P.dma("sync", tmp, rows(1 + d), writes=["tmp"])
                    P.S(lambda e: e.activation(tmp, tmp, AF.Sigmoid), ["tmp"], ["tmp"])
                    if l == 0:
                        lbs, oms = self.zero_t[0:64, :], self.one_t[0:64, :]
                    else:
                        lbs, oms = self.lb1[:, d * 8 + h:d * 8 + h + 1], self.omlb1[:, d * 8 + h:d * 8 + h + 1]
                    P.V(lambda e, lbs=lbs, oms=oms: e.tensor_scalar(tmp, tmp, oms, lbs, ALU.mult, ALU.add),
                        ["tmp", "lb1", "omlb1", "eps_t"], ["tmp"])
                    P.V(lambda e: e.tensor_scalar(kk, tmp, -1.0, 1.0, ALU.mult, ALU.add), ["tmp"], ["kk"])
                    P.S(lambda e: e.activation(lf, tmp, AF.Ln), ["tmp"], ["lf"])
                elif d == 0:
                    P.G(lambda e, lg=lg: e.memset(lf, lg), [], ["lf"])
                P.V(lambda e: e.tensor_tensor_scan(bf, cm, lf, 0.0, ALU.mult, ALU.add), ["cm", "cmL", "lf"], ["bf"])
                btot = three(bf)[:, :, CH - 1:CH]
                btb = btot.to_broadcast([64, NCH, CH])
                if d == 0:
                    P.V(lambda e: e.tensor_copy(b, bf), ["bf"], ["b"])
                else:
                    P.V(lambda e, btb=btb: e.tensor_tensor(three(b), btb, three(bf), ALU.subtract), ["bf"], ["b"])
                    P.V(lambda e: e.tensor_tensor(b, b, lf, ALU.add), ["b", "lf"], ["b"])
                P.S(lambda e: e.activation(E, b, AF.Exp), ["b"], ["tmp"])
                P.V(lambda e, d=d, qt=qt: e.tensor_tensor(qt[d], q, E, ALU.mult), ["q", "tmp"], [("qt", hi, d)])
                P.S(lambda e: e.activation(E, b, AF.Exp, scale=-1.0), ["b"], ["tmp"])
                P.V(lambda e, d=d, kt=kt: e.tensor_tensor(kt[d], kk, E, ALU.mult), ["kk", "tmp"], [("kt", hi, d)])
                P.V(lambda e, btb=btb: e.tensor_tensor(three(E), btb, three(b), ALU.subtract), ["bf", "b"], ["tmp"])
                P.S(lambda e: e.activation(E, E, AF.Exp), ["tmp"], ["tmp"])
                P.V(lambda e, d=d: e.tensor_tensor(kh[d], kk, E, ALU.mult), ["kk", "tmp"], [("kh", d)])
                P.S(lambda e, d=d, gdec=gdec, btot=btot: e.activation(gdec[d], btot.rearrange("p c o -> p (c o)"), AF.Exp),
                    ["bf"], [("gdec", hi, d)])
                for c0 in range(0, NCH, 8):
                    ptb, ptk = self.psT, "psT"
                    nb = min(8, NCH - c0)
                    for c in range(c0, c0 + nb):
                        P.op("tensor", lambda e, d=d, c=c, c0=c0: e.transpose(
                            ptb[0:CH, (c - c0) * 64:(c - c0) * 64 + 64], kh[d][:, c * CH:(c + 1) * CH],
                            self.ident_bf[0:64, 0:64]), [("kh", d), "ident_bf"], [ptk])
                    P.V(lambda e, d=d, c0=c0, khT=khT, nb=nb: e.tensor_copy(
                        khT[d][:, c0:c0 + nb, :], ptb[0:CH, 0:nb * 64].rearrange("p (c k) -> p c k", k=64)),
                        [ptk], [("khT", hi, d)])
        order = [list(range(NCH)), list(range(NCTX - 1, -1, -1)) + list(range(NCH - 1, NCTX - 1, -1))]
        slot_w = 4 * CH
        nslots = 512 // slot_w
        nstep = 0
        for i in range(NCH):
            for hi in range(len(heads)):
                ls = LS[hi]
                qt, kt, khT, vtm, oacc, gdec = ls["qt"], ls["kt"], ls["khT"], ls["vtm"], ls["oacc"], ls["gdec"]
                S32, Sbf, sTb = ls["S32"], ls["Sbf"], ls["sTb"]
                for d in range(2):
                    c = order[d][i]
                    ts = slice(c * CH, (c + 1) * CH)
                    first = (i == 0)
                    bank, slot = nstep % 6, (nstep // 6) % nslots
                    nstep += 1
                    base = slot * slot_w
                    pbank = self.ps[bank]
                    pss_, psk = pbank[:, base:base + CH], ("psr", bank, slot, 0)
                    po, pok = pbank[:, base + CH:base + 2 * CH], ("psr", bank, slot, 1)
                    pu, puk = pbank[:, base + 2 * CH:base + 2 * CH + 64], ("psr", bank, slot, 2)
                    P.mm(pss_[0:CH, 0:CH], kt[d][:, ts], qt[d][:, ts], reads=[("kt", hi, d), ("qt", hi, d)], writes=[psk])
                    sT, sTk = sTb[d][i % 2], ("sT", hi, d, i % 2)
                    P.V(lambda e, sT=sT, pss_=pss_, d=d: e.tensor_tensor(sT, pss_[0:CH, 0:CH], masks[d], ALU.mult),
                        [psk, ("mask", d), ("maskL", d)], [sTk])
                    if not first:
                        P.mm(po[0:64, 0:CH], Sbf[d], qt[d][:, ts], start=True, stop=False,
                             reads=[("Sbf", hi, d), ("qt", hi, d)], writes=[pok])
                    P.mm(po[0:64, 0:CH], vtm[:, c, :], sT, start=first, stop=True, reads=[("vtm", hi), sTk], writes=[pok])
                    P.V(lambda e, po=po, ts=ts, oacc=oacc: e.tensor_tensor(oacc[:, ts], oacc[:, ts], po[0:64, 0:CH], ALU.add),
                        [pok, ("oacc", hi)], [("oacc", hi)])
                    P.mm(pu[0:64, 0:64], khT[d][:, c, :], vtm[:, c, :], reads=[("khT", hi, d), ("vtm", hi)], writes=[puk])
                    if first:
                        P.V(lambda e, d=d, pu=pu, S32=S32: e.tensor_copy(S32[d], pu[0:64, 0:64]), [puk], [("S32", hi, d)])
                    else:
                        P.V(lambda e, d=d, pu=pu, c=c, S32=S32, gdec=gdec: e.scalar_tensor_tensor(
                            S32[d], S32[d], gdec[d][:, c:c + 1], pu[0:64, 0:64], ALU.mult, ALU.add),
                            [puk, ("S32", hi, d), ("gdec", hi, d)], [("S32", hi, d)])
                    P.S(lambda e, d=d, S32=S32, Sbf=Sbf: e.copy(Sbf[d], S32[d]), [("S32", hi, d)], [("Sbf", hi, d)])
        sqb = [P.alloc([64, 512], BF16) for _ in range(2)]
        ob = [P.alloc([64, 512], BF16) for _ in range(2)]
        rs = [P.alloc([64, 512]) for _ in range(2)]
        zg = q
        for hi, h in enumerate(heads):
            oacc = LS[hi]["oacc"]
            if kind == "A":
                mrow, gain, gblk = 64 * h, self.anorm[:, l:l + 1], 4
            else:
                mrow, gain, gblk = 512 + 64 * h, self.bnorm[:, l:l + 1], 8
            P.dma("sync", zg, Z[gblk * 512 + 64 * h:gblk * 512 + 64 * h + 64, :], writes=["q"])
            P.S(lambda e: e.activation(zg, zg, AF.Silu), ["q"], ["q"])
            for bi_, (a, e_) in enumerate([(0, 512), (512, 1024), (1024, 1536), (1536, 2048), (2048, 2304)]):
                nn = e_ - a
                sq, sqk = sqb[bi_ % 2], ("gsq", bi_ % 2)
                P.S(lambda e, sq=sq, a=a, e_=e_, nn=nn, oacc=oacc: e.activation(sq[:, 0:nn], oacc[:, a:e_], AF.Square),
                    [("oacc", hi)], [sqk])
                pn, pnk = self.ps[6], ("ps", 6)
                P.mm(pn[0:64, 0:nn], self.ones64[0:64, 0:64], sq[:, 0:nn], reads=[sqk, "c_ones"], writes=[pnk])
                r, rk = rs[bi_ % 2], ("grs", bi_ % 2)
                P.S(lambda e, r=r, pn=pn, nn=nn: e.activation(r[:, 0:nn], pn[0:64, 0:nn], AF.Sqrt, bias=self.eps_t[0:64, :]),
                    [pnk, "eps_t"], [rk])
                P.V(lambda e, r=r, nn=nn: e.reciprocal(r[:, 0:nn], r[:, 0:nn]), [rk], [rk])
                P.V(lambda e, r=r, a=a, e_=e_, nn=nn, oacc=oacc, gain=gain: e.scalar_tensor_tensor(
                    r[:, 0:nn], oacc[:, a:e_], gain, r[:, 0:nn], ALU.mult, ALU.mult),
                    [rk, ("oacc", hi), "anorm", "bnorm"], [rk])
                o_, ok_ = ob[bi_ % 2], ("gob", bi_ % 2)
                P.V(lambda e, o_=o_, r=r, a=a, e_=e_, nn=nn: e.tensor_tensor(o_[:, 0:nn], r[:, 0:nn], zg[:, a:e_], ALU.mult),
                    [rk, "q"], [ok_])
                P.dma("sync", self.Mx[mrow:mrow + 64, a:e_], o_[:, 0:nn], reads=[ok_], writes=[("Mx", mrow)])
        P.release(mk)

    def emit_diffattn(self, l, h, ctx_out):
        P = self.P
        mk = P.mark()
        Z, Zt = self.Z, self.Zt
        t1 = P.alloc([128, T])
        t2 = P.alloc([128, T])
        cos = P.alloc([128, T])
        sin = P.alloc([128, T])
        qb = P.alloc([128, T], BF16)
        kb_ = P.alloc([128, T], BF16)
        vtm = P.alloc([128, T // 128, 128], BF16)
        for hf in range(2):
            P.dma("sync", cos[64 * hf:64 * hf + 64, :], self.cos_d, writes=["cos"])
            P.dma("sync", sin[64 * hf:64 * hf + 64, :], self.sin_d, writes=["sin"])
        P.dma("gpsimd", vtm, Zt[:, 1024 + 128 * h:1024 + 128 * h + 128].rearrange("(kb p) d -> p kb d", p=128),
              writes=["vtm"])
        for (dst, blk) in ((qb, 9), (kb_, 10)):
            r0 = blk * 512 + 128 * h
            P.dma("sync", t1, Z[r0:r0 + 128, :], writes=["t1"])
            for hf in range(2):
                P.dma("sync", t2[64 * hf:64 * hf + 32, :], Z[r0 + 64 * hf + 32:r0 + 64 * hf + 64, :], writes=["t2"])
                P.dma("sync", t2[64 * hf + 32:64 * hf + 64, :], Z[r0 + 64 * hf:r0 + 64 * hf + 32, :], writes=["t2"])
            P.V(lambda e: e.tensor_tensor(t1, t1, cos, ALU.mult), ["t1", "cos"], ["t1"])
            P.V(lambda e: e.tensor_tensor(t2, t2, sin, ALU.mult), ["t2", "sin"], ["t2"])
            dk = "qb" if dst is qb else "kb"
            P.V(lambda e, dst=dst: e.tensor_tensor(dst, t1, t2, ALU.add), ["t1", "t2"], [dk])
        pex = [P.alloc([128, 512], BF16) for _ in range(3)]
        rr = [P.alloc([128, 512]) for _ in range(2)]
        oo = [P.alloc([128, 512]) for _ in range(2)]
        sqo = P.alloc([128, 512], BF16)
        outb = [P.alloc([128, 512], BF16) for _ in range(2)]
        neglam = self.neglam[:, l:l + 1]
        cgain = self.cgain[:, l:l + 1]
        qblocks = [(CTX + 512 * i, 512, T // 128) for i in range(4)]
        if ctx_out:
            qblocks.append((0, CTX, CTX // 128))
        npx = 0
        for qi, (q0, nq, nkb) in enumerate(qblocks):
            accO = [(self.ps[2], ("ps", 2)), (self.ps[3], ("ps", 3))]
            accS = [(self.ps[4], ("ps", 4)), (self.ps[5], ("ps", 5))]
            for kbi in range(nkb):
                for hf in range(2):
                    pS, pSk = self.psum(0, 2)
                    P.mm(pS[:, 0:nq], kb_[64 * hf:64 * hf + 64, kbi * 128:(kbi + 1) * 128],
                         qb[64 * hf:64 * hf + 64, q0:q0 + nq], reads=["kb", "qb"], writes=[pSk])
                    px, pxk = pex[npx % 3], ("pex", npx % 3)
                    npx += 1
                    P.S(lambda e, px=px, pS=pS, nq=nq: e.activation(px[:, 0:nq], pS[:, 0:nq], AF.Exp, scale=0.125),
                        [pSk], [pxk])
                    P.mm(accO[hf][0][:, 0:nq], vtm[:, kbi, :], px[:, 0:nq], start=(kbi == 0), stop=(kbi == nkb - 1),
                         reads=["vtm", pxk], writes=[accO[hf][1]])
                    P.mm(accS[hf][0][:, 0:nq], self.ones1, px[:, 0:nq], start=(kbi == 0), stop=(kbi == nkb - 1),
                         reads=["c_ones", pxk], writes=[accS[hf][1]])
            for hf in range(2):
                r, rk = rr[hf], ("rr", hf)
                o, ok_ = oo[hf], ("oo", hf)
                P.V(lambda e, r=r, hf=hf, nq=nq, accS=accS: e.reciprocal(r[:, 0:nq], accS[hf][0][:, 0:nq]),
                    [accS[hf][1]], [rk])
                P.V(lambda e, r=r, o=o, hf=hf, nq=nq, accO=accO: e.tensor_tensor(o[:, 0:nq], accO[hf][0][:, 0:nq],
                                                                                r[:, 0:nq], ALU.mult),
                    [accO[hf][1], rk], [ok_])
            o = oo[0]
            P.V(lambda e, nq=nq: e.scalar_tensor_tensor(oo[0][:, 0:nq], oo[1][:, 0:nq], neglam, oo[0][:, 0:nq],
                                                        ALU.mult, ALU.add),
                [("oo", 0), ("oo", 1), "neglam"], [("oo", 0)])
            P.S(lambda e, nq=nq: e.activation(sqo[:, 0:nq], oo[0][:, 0:nq], AF.Square), [("oo", 0)], ["sqo"])
            pn, pnk = self.ps[6], ("ps", 6)
            P.mm(pn[:, 0:nq], self.ones128, sqo[:, 0:nq], reads=["sqo", "c_ones"], writes=[pnk])
            r = rr[0]
            P.S(lambda e, nq=nq: e.activation(rr[0][:, 0:nq], pn[:, 0:nq], AF.Sqrt, bias=self.eps_t), [pnk, "eps_t"],
                [("rr", 0)])
            P.V(lambda e, nq=nq: e.reciprocal(rr[0][:, 0:nq], rr[0][:, 0:nq]), [("rr", 0)], [("rr", 0)])
            ob, obk = outb[qi % 2], ("outb", qi % 2)
            P.V(lambda e, nq=nq, ob=ob: e.scalar_tensor_tensor(ob[:, 0:nq], oo[0][:, 0:nq], cgain, rr[0][:, 0:nq],
                                                               ALU.mult, ALU.mult),
                [("oo", 0), ("rr", 0), "cgain"], [obk])
            P.dma("sync", self.Mx[1024 + 128 * h:1024 + 128 * h + 128, q0:q0 + nq], ob[:, 0:nq], reads=[obk],
                  writes=[("Mx", 1024 + 128 * h)])
        P.release(mk)

    def emit_rglru(self, l, cc):
        P = self.P
        mk = P.mark()
        Z = self.Z
        Fa = lambda: P.alloc([128, T])
        x, gate, xc, ra, ia, aa, uu, hs, hsum = [Fa() for _ in range(9)]
        xcb = P.alloc([128, T], BF16)
        outb = P.alloc([128, T], BF16)
        P.dma("sync", x, Z[12 * 512 + 128 * cc:12 * 512 + 128 * cc + 128, :], writes=["x"])
        P.dma("sync", gate, Z[13 * 512 + 128 * cc:13 * 512 + 128 * cc + 128, :], writes=["gate"])
        cw = lambda j: self.convw[:, (l * 4 + cc) * 4 + j:(l * 4 + cc) * 4 + j + 1]
        cb = self.convb[:, l * 4 + cc:l * 4 + cc + 1]
        P.S(lambda e: e.activation(xc, x, AF.Identity, bias=cb, scale=cw(1)), ["x", "convw", "convb"], ["xc"])
        for (a, e_) in ((0, CTX), (CTX, T)):
            P.V(lambda e, a=a, e_=e_: e.scalar_tensor_tensor(xc[:, a + 1:e_], x[:, a:e_ - 1], cw(0), xc[:, a + 1:e_],
                                                            ALU.mult, ALU.add), ["x", "xc", "convw"], ["xc"])
            P.V(lambda e, a=a, e_=e_: e.scalar_tensor_tensor(xc[:, a:e_ - 1], x[:, a + 1:e_], cw(2), xc[:, a:e_ - 1],
                                                            ALU.mult, ALU.add), ["x", "xc", "convw"], ["xc"])
            P.V(lambda e, a=a, e_=e_: e.scalar_tensor_tensor(xc[:, a:e_ - 2], x[:, a + 2:e_], cw(3), xc[:, a:e_ - 2],
                                                            ALU.mult, ALU.add), ["x", "xc", "convw"], ["xc"])
        P.V(lambda e: e.tensor_copy(xcb, xc), ["xc"], ["xcb"])
        wbd = [[P.alloc([128, 128], BF16) for _ in range(2)] for _ in range(2)]
        dwr = self.din("d_w_r", [2, 2, 8, 64, 64])
        dwi = self.din("d_w_i", [2, 2, 8, 64, 64])
        for d in range(2):
            for ri, src in enumerate((dwr, dwi)):
                w = wbd[d][ri]
                wk = ("wbd", d, ri)
                P.G(lambda e, w=w: e.memset(w, 0.0), [], [wk])
                for g2 in range(2):
                    P.dma("gpsimd", w[64 * g2:64 * g2 + 64, 64 * g2:64 * g2 + 64], src[l, d, 2 * cc + g2], writes=[wk])
        tbl = [(0, 512), (512, 1024), (1024, 1536), (1536, 2048), (2048, 2304)]
        for d in range(2):
            pidx = (l * 2 + d) * 4 + cc
            br = self.dbr[:, pidx:pidx + 1]
            bi = self.dbi[:, pidx:pidx + 1]
            c1 = self.c1[:, pidx:pidx + 1]
            for (a, e_) in tbl:
                nn = e_ - a
                pr, prk = self.psum()
                P.mm(pr[:, 0:nn], wbd[d][0], xcb[:, a:e_], reads=[("wbd", d, 0), "xcb"], writes=[prk])
                P.S(lambda e, pr=pr, a=a, e_=e_, nn=nn, br=br: e.activation(ra[:, a:e_], pr[:, 0:nn], AF.Sigmoid, bias=br),
                    [prk, "dbr"], ["ra"])
                pi_, pik = self.psum()
                P.mm(pi_[:, 0:nn], wbd[d][1], xcb[:, a:e_], reads=[("wbd", d, 1), "xcb"], writes=[pik])
                P.S(lambda e, pi_=pi_, a=a, e_=e_, nn=nn, bi=bi: e.activation(ia[:, a:e_], pi_[:, 0:nn], AF.Sigmoid, bias=bi),
                    [pik, "dbi"], ["ia"])
            P.S(lambda e, c1=c1: e.activation(aa, ra, AF.Exp, scale=c1), ["ra", "c1"], ["aa"])
            P.V(lambda e: e.tensor_tensor(ra, aa, aa, ALU.mult), ["aa", "ra"], ["ra"])
            P.S(lambda e: e.activation(ra, ra, AF.Sqrt, bias=self.one_t, scale=-1.0), ["ra", "eps_t"], ["ra"])
            P.V(lambda e: e.tensor_tensor(uu, ia, xc, ALU.mult), ["ia", "xc"], ["uu"])
            P.V(lambda e: e.tensor_tensor(uu, uu, ra, ALU.mult), ["uu", "ra"], ["uu"])
            if d == 0:
                P.V(lambda e: e.tensor_tensor_scan(hsum, aa, uu, 0.0, ALU.mult, ALU.add), ["aa", "uu"], ["hsum"])
            else:
                rv = lambda t_, a, e_: t_[:, a:e_][:, ::-1]
                P.V(lambda e: e.tensor_tensor_scan(rv(hs, 0, CTX), rv(aa, 0, CTX), rv(uu, 0, CTX), 0.0, ALU.mult, ALU.add),
                    ["aa", "uu"], ["hs"])
                P.V(lambda e: e.tensor_tensor_scan(rv(hs, CTX, T), rv(aa, CTX, T), rv(uu, CTX, T), hs[:, 0:1],
                                                   ALU.mult, ALU.add), ["aa", "uu", "hs"], ["hs"], force=True)
                P.V(lambda e: e.tensor_tensor(hsum, hsum, hs, ALU.add), ["hsum", "hs"], ["hsum"])
        P.S(lambda e: e.activation(x, gate, AF.Square), ["gate", "x"], ["x"])
        P.V(lambda e: e.tensor_scalar(x, x, 0.044715, 1.0, ALU.mult, ALU.add), ["x"], ["x"])
        P.V(lambda e: e.tensor_tensor(x, x, gate, ALU.mult), ["x", "gate"], ["x"])
        P.S(lambda e: e.activation(x, x, AF.Sigmoid, scale=1.5957691216057308), ["x"], ["x"])
        P.V(lambda e: e.tensor_tensor(x, x, gate, ALU.mult), ["x", "gate"], ["x"])
        P.V(lambda e: e.tensor_tensor(outb, x, hsum, ALU.mult), ["x", "hsum"], ["outb"])
        P.dma("sync", self.Mx[1536 + 128 * cc:1536 + 128 * cc + 128, :], outb, reads=["outb"], writes=[("Mx", 1536 + 128 * cc)])
        P.release(mk)

    def emit_pass(self, which, xT=None, out=None):
        P = self.P
        mk = P.mark()
        self.alloc_tile_bufs()
        h = self.h
        for tt in range(NTT):
            segs = tile_segs(tt)
            t0 = tt * TT
            if which == 0:
                P.dma("sync", h, xT[:, :, t0:t0 + TT], writes=["h"])
                self.emit_ffn_tile(0, 0, 0, segs)
                nl = 0
            elif which == 1:
                P.dma("sync", h, self.H[:, :, t0:t0 + TT], reads=[("H", tt)], writes=["h"])
                self.emit_wout_tile(0, tt, segs)
                self.emit_ffn_tile(0, 1, 2, segs)
                self.emit_ffn_tile(1, 0, 0, segs)
                nl = 1
            else:
                P.dma("sync", h, self.H[:, :, t0:t0 + TT], reads=[("H", tt)], writes=["h"])
                self.emit_wout_tile(1, tt, segs)
                self.emit_ffn_tile(1, 1, 2, segs)
                if tt == 0:
                    P.dma("sync", out[:, :, 0:128], h[:, :, 256:384], reads=["h"])
                else:
                    o0 = 128 + (tt - 1) * TT
                    P.dma("sync", out[:, :, o0:o0 + TT], h, reads=["h"])
                continue
            self.emit_norm_mod(nl, 1, h, "h", self.u, "u", segs)
            P.dma("sync", self.U[:, :, t0:t0 + TT], self.u, reads=["u"], writes=[("U", tt)])
            P.dma("sync", self.H[:, :, t0:t0 + TT], h, reads=["h"], writes=[("H", tt)])
        P.release(mk)

    def emit_mixer(self, l, parts="ABCD"):
        self.emit_proj(l)
        if "A" in parts:
            for h in range(0, 8, 2):
                self.emit_gla(l, "A", [h, h + 1])
        if "B" in parts:
            for h in range(0, 8, 2):
                self.emit_gla(l, "B", [h, h + 1])
        if "C" in parts:
            for h in range(4):
                self.emit_diffattn(l, h, l == 0)
        if "D" in parts:
            for cc in range(4):
                self.emit_rglru(l, cc)


def build_full(gathered=True):
    nc = bass.Bass("TRN2", target_bir_lowering=False)
    C = Core(nc, gathered)
    if gathered:
        C.gather_weights()
    C.load_small()
    C.load_mixer_params()
    C.alloc_scratch()
    xT = C.din("xT", [128, KC, T])
    out = C.dout("out", [128, KC, SEQ])
    C.emit_mods(0)
    C.emit_mods(1)
    C.emit_pass(0, xT=xT)
    C.emit_mixer(0)
    C.emit_pass(1)
    C.emit_mixer(1)
    C.emit_pass(2, out=out)
    C.P.emit()
    return nc, C


def build_pass0_test():
    nc = bass.Bass("TRN2", target_bir_lowering=False)
    C = Core(nc, False)
    C.test_one = True
    C.load_small()
    C.H = C.dout("H_out", [128, KC, T])
    C.U = C.dout("U_out", [128, KC, T], BF16)
    xT = C.din("xT", [128, KC, T])
    C.emit_mods(0)
    modo = C.dout("mod_out", [128, 144, 2])
    C.P.dma("sync", modo, C.mod[0], reads=[("mod", 0)])
    C.emit_pass(0, xT=xT)
    C.P.emit()
    return nc, C


def build_mixer_test(l, parts):
    nc = bass.Bass("TRN2", target_bir_lowering=False)
    C = Core(nc, False)
    C.load_mixer_params()
    C.alloc_scratch()
    uin = C.din("U_in", [128, KC, T], BF16)
    mout = C.dout("Mx_out", [2048, T], BF16)
    C.P.dma("sync", C.U, uin, writes=["U"])
    C.P.barrier()
    C.emit_mixer(l, parts)
    C.P.dma("sync", mout, C.Mx, reads=[("Mx", r) for r in range(0, 2048, 64)])
    C.P.emit()
    return nc, C


def fm(a):
    a = np.asarray(a, np.float32)
    lead = a.shape[:-1]
    r = a.reshape(lead + (a.shape[-1] // 128, 128))
    return np.ascontiguousarray(np.moveaxis(r, -1, 0))


def tokens_T(x, ctx, b):
    t = np.concatenate([ctx[b], x[b]], 0)
    return np.ascontiguousarray(t.T.reshape(KC, 128, T).transpose(1, 0, 2))


def rope_tables():
    quarter = 16
    inv = 10000.0 ** (-np.arange(quarter, dtype=np.float32) / quarter)
    rows = SEQ // 64
    row = np.repeat(np.arange(rows, dtype=np.float32), 64)
    col = np.tile(np.arange(64, dtype=np.float32), rows)
    ang = np.concatenate([row[:, None] * inv, col[:, None] * inv], -1).astype(np.float32)
    cos = np.cos(ang).T
    sin = np.sin(ang).T
    c2 = np.ones((64, T), np.float32)
    s2 = np.zeros((64, T), np.float32)
    c2[0:32, CTX:] = cos
    c2[32:64, CTX:] = cos
    s2[0:32, CTX:] = -sin
    s2[32:64, CTX:] = sin
    return c2, s2


def small_inputs(inp, b):
    f32 = lambda a: np.asarray(a, np.float32)
    d = {}
    d["gpre"] = fm(inp["norm_pre"]).reshape(128, -1)
    d["gpost"] = fm(inp["norm_post"]).reshape(128, -1)
    d["bada"] = np.ascontiguousarray(f32(inp["b_ada"]).reshape(2, 144, 128).transpose(2, 0, 1)).reshape(128, -1)
    d["cvec"] = fm(np.stack([f32(inp["c"])[b], f32(inp["c_ctx"])], 0)).reshape(128, -1)
    return d


def mixer_inputs(inp):
    f32 = lambda a: np.asarray(a, np.float32)
    d = {}
    d["lbl"] = np.ascontiguousarray(f32(inp["lb_logits"]).reshape(2, 2, 8, 64).transpose(3, 0, 1, 2)).reshape(64, -1)
    d["anorm"] = np.ascontiguousarray(f32(inp["a_norm"]).T)
    d["bnorm"] = np.ascontiguousarray(f32(inp["b_norm"]).T)
    d["cnorm"] = np.ascontiguousarray(f32(inp["c_norm"]).T)
    d["clam"] = np.ascontiguousarray(f32(inp["c_lambda"]).transpose(2, 0, 1)).reshape(64, -1)
    d["convw"] = np.ascontiguousarray(f32(inp["d_conv_w"]).reshape(2, 4, 4, 128).transpose(3, 0, 2, 1)).reshape(128, -1)
    d["convb"] = np.ascontiguousarray(f32(inp["d_conv_b"]).reshape(2, 4, 128).transpose(2, 0, 1)).reshape(128, -1)
    for nm, src in (("dbr", "d_b_r"), ("dbi", "d_b_i"), ("dlam", "d_lambda")):
        d[nm] = np.ascontiguousarray(f32(inp[src]).reshape(2, 2, 4, 128).transpose(3, 0, 1, 2)).reshape(128, -1)
    c2, s2 = rope_tables()
    d["rope_cos"] = c2
    d["rope_sin"] = s2
    d["d_w_r"] = f32(inp["d_w_r"])
    d["d_w_i"] = f32(inp["d_w_i"])
    return d


def weight_shards(inp, r):
    f32 = lambda a: np.asarray(a, np.float32)
    rs = slice(256 * r, 256 * r + 256)
    p0, p1 = [], []
    for l in range(2):
        for f in range(2):
            p0.append(f32(inp["ffn_w_in"])[l, f, rs, :].reshape(-1))
    for l in range(2):
        p0.append(f32(inp["w_in"])[l, rs, :].reshape(-1))
    for l in range(2):
        p1.append(f32(inp["w_ada"])[l, rs, :].reshape(-1))
    for l in range(2):
        for f in range(2):
            p1.append(np.ascontiguousarray(f32(inp["ffn_w_out"])[l, f, :, rs]).reshape(-1))
    for l in range(2):
        p1.append(f32(inp["w_out"])[l, rs, :].reshape(-1))
    w0 = np.concatenate(p0)
    w1 = np.concatenate(p1)
    assert w0.size == NSH0 and w1.size == NSH1
    return w0.reshape(NSH0 // 2048, 2048), w1.reshape(NSH1 // 2048, 2048)


_CACHE = {}


USE_GATHER = False
N_USED = 4


def kernel(**inp):
    if "nc" not in _CACHE:
        _CACHE["nc"] = build_full(USE_GATHER)[0]
    nc = _CACHE["nc"]
    f32 = lambda a: np.ascontiguousarray(np.asarray(a, np.float32))
    mi = mixer_inputs(inp)
    shared = {}
    if not USE_GATHER:
        shared = dict(w_ada=f32(inp["w_ada"]), ffn_w_in=f32(inp["ffn_w_in"]).reshape(4, D, 2 * DFF),
                      ffn_w_out=f32(inp["ffn_w_out"]).reshape(4, DFF, D), w_in=f32(inp["w_in"]), w_out=f32(inp["w_out"]))
    in_maps = []
    for c in range(N_USED):
        b = c % 4
        m = dict(mi)
        m.update(shared)
        m.update(small_inputs(inp, b))
        m["xT"] = tokens_T(np.asarray(inp["x"], np.float32), np.asarray(inp["ctx"], np.float32), b)
        if USE_GATHER:
            m["wshard0"], m["wshard1"] = weight_shards(inp, c)
        in_maps.append(m)
    res = run_bass_kernel_spmd(nc, in_maps, core_ids=list(range(N_USED)))
    outs = []
    for b in range(4):
        o = np.asarray(res.results[b]["out"])
        outs.append(o.transpose(2, 1, 0).reshape(SEQ, D))
    return np.stack(outs, 0).astype(np.float32)
```

```python
import math
import numpy as np
import ml_dtypes
import concourse.bass as bass
import concourse.mybir as mybir
from concourse.bass_utils import run_bass_kernel_spmd

F32 = mybir.dt.float32
BF16 = mybir.dt.bfloat16
AF = mybir.ActivationFunctionType
ALU = mybir.AluOpType

N_DMA_SEMS = 6
D = 2048
KC = 16
DFF = 5504
NJ = 43
T = 2304
TT = 384
NTT = 6
CTX = 256
SEQ = 2048
EPS = 1e-6
CH = 32
NCH = T // CH
INW = 7168
NCORE = 8
SZ_ADA = 256 * 9 * D
SZ_FIN = 256 * 2 * DFF
SZ_FOUT = DFF * 256
SZ_WIN = 256 * INW
SZ_WOUT = 256 * D
OFF_FIN = 0
OFF_WIN = OFF_FIN + 4 * SZ_FIN
NSH0 = OFF_WIN + 2 * SZ_WIN
OFF_ADA = 0
OFF_FOUT = OFF_ADA + 2 * SZ_ADA
OFF_WOUT = OFF_FOUT + 4 * SZ_FOUT
NSH1 = OFF_WOUT + 2 * SZ_WOUT
NSHG = (NSH0, NSH1)
NSH = NSH0 + NSH1
assert NSH % 2048 == 0


class Prog:
    def __init__(self, nc, arena_bytes):
        self.nc = nc
        self.ops = []
        self.lastw = {}
        self.readers = {}
        self.arena = nc.alloc_sbuf_tensor("arena", [128, arena_bytes // 4], F32)
        self.arena_bytes = arena_bytes
        self.off = 0
        self.hw = 0
        self.last_on = {}
        self.dma_hist = {}

    def alloc(self, shape, dtype=F32):
        esz = 4 if dtype == F32 else 2
        n = 1
        for s in shape[1:]:
            n *= s
        nbytes = (n * esz + 31) // 32 * 32
        assert self.off + nbytes <= self.arena_bytes, ("SBUF arena overflow", self.off, nbytes)
        a = self.arena[:, self.off // 4:(self.off + nbytes) // 4]
        if dtype != F32:
            a = a.bitcast(dtype)
        a = a[0:shape[0], 0:n]
        if len(shape) == 3:
            a = a.rearrange("p (a b) -> p a b", a=shape[1])
        elif len(shape) == 4:
            a = a.rearrange("p (a b c) -> p a b c", a=shape[1], b=shape[2])
        self.off += nbytes
        self.hw = max(self.hw, self.off)
        return a

    def mark(self):
        return self.off

    def release(self, m):
        self.barrier()
        self.off = m

    def _expand(self, keys):
        g = getattr(self, "groups", None)
        if not g:
            return keys
        out = []
        for k in keys:
            out.extend(g.get(k, (k,)))
        return out

    def op(self, eng, fn, reads=(), writes=(), dma=False, force=False):
        reads = self._expand(reads)
        writes = self._expand(writes)
        i = len(self.ops)
        deps = set()
        for r in reads:
            if r in self.lastw:
                deps.add(self.lastw[r])
        for w in writes:
            if w in self.lastw:
                deps.add(self.lastw[w])
            for rd in self.readers.get(w, ()):
                deps.add(rd)
        for r in reads:
            self.readers.setdefault(r, []).append(i)
        for w in writes:
            self.lastw[w] = i
            self.readers[w] = []
        self.ops.append(dict(eng=eng, fn=fn, deps=deps, dma=dma, force=(force or getattr(self, "force_all", False))))
        if dma:
            self.dma_hist.setdefault(eng, []).append(i)
        else:
            self.last_on[eng] = i
        return i

    def barrier(self):
        last = [v for v in self.last_on.values()]
        dm = []
        for q, h in self.dma_hist.items():
            dm += h[-N_DMA_SEMS:]
        engs = set(self.last_on.keys()) | set(self.dma_hist.keys()) | {"tensor", "vector", "scalar", "gpsimd", "sync"}
        for e in sorted(engs):
            i = len(self.ops)
            self.ops.append(dict(eng=e, fn=lambda en: en.nop(), deps=set(last) | set(dm), dma=False))
            self.last_on[e] = i
        self.lastw = {}
        self.readers = {}

    def mm(self, out, lhsT, rhs, start=True, stop=True, reads=(), writes=()):
        return self.op("tensor", lambda e: e.matmul(out, lhsT, rhs, start=start, stop=stop), reads, writes)

    def dma(self, q, out, in_, reads=(), writes=(), **kw):
        return self.op(q, lambda e: e.dma_start(out=out, in_=in_, **kw), reads, writes, dma=True)

    def V(self, fn, reads=(), writes=(), force=False):
        return self.op("vector", fn, reads, writes, force=force)

    def S(self, fn, reads=(), writes=()):
        return self.op("scalar", fn, reads, writes)

    def G(self, fn, reads=(), writes=()):
        return self.op("gpsimd", fn, reads, writes)

    def emit(self):
        nc = self.nc
        ops = self.ops
        need_sig = [False] * len(ops)
        for i, o in enumerate(ops):
            for d in o["deps"]:
                pd = ops[d]
                if pd["dma"]:
                    continue
                if pd["eng"] != o["eng"] or o["dma"] or o.get("force"):
                    need_sig[d] = True
        used = sorted({o["eng"] for o in ops})
        esem = {e: nc.alloc_semaphore("es_" + e) for e in used}
        dq = [e for e in used if any(o["dma"] and o["eng"] == e for o in ops)]
        dsems = {e: [nc.alloc_semaphore("ds_%s%d" % (e, k)) for k in range(N_DMA_SEMS)] for e in dq}
        cnt = {e: 0 for e in used}
        dcnt = {e: 0 for e in used}
        for i, o in enumerate(ops):
            if o["dma"]:
                k = dcnt[o["eng"]]
                dcnt[o["eng"]] += 1
                o["dsem"] = dsems[o["eng"]][k % N_DMA_SEMS]
                o["dval"] = 16 * (k // N_DMA_SEMS + 1)
            elif need_sig[i]:
                cnt[o["eng"]] += 1
                o["sig"] = cnt[o["eng"]]
        streams = {e: [] for e in used}
        for i, o in enumerate(ops):
            streams[o["eng"]].append(i)
        self.n_waits = 0

        def run_engine(ename, eng):
            waited = {}

            def wait(sem, val):
                key = id(sem)
                if waited.get(key, 0) >= val:
                    return
                waited[key] = val
                eng.wait_ge(sem, val)
                self.n_waits += 1

            for i in streams.get(ename, ()):
                o = ops[i]
                for d in sorted(o["deps"]):
                    pd = ops[d]
                    if pd["dma"]:
                        wait(pd["dsem"], pd["dval"])
                    elif "sig" in pd:
                        if pd["eng"] == ename and not o["dma"] and not o.get("force"):
                            continue
                        wait(esem[pd["eng"]], pd["sig"])
                if o["dma"]:
                    if o["dval"] > 16:
                        wait(o["dsem"], o["dval"] - 16)
                    o["fn"](eng).then_inc(o["dsem"], 16)
                else:
                    ins = o["fn"](eng)
                    if "sig" in o:
                        ins.then_inc(esem[ename], 1)
            if ename in dsems:
                done = {}
                for i in streams.get(ename, ()):
                    o = ops[i]
                    if o["dma"]:
                        done[id(o["dsem"])] = (o["dsem"], o["dval"])
                for sem, val in done.values():
                    wait(sem, val)

        with nc.Block() as block:
            if "sync" in streams:
                @block.sync
                def _(e):
                    run_engine("sync", e)
            if "tensor" in streams:
                @block.tensor
                def _(e):
                    run_engine("tensor", e)
            if "vector" in streams:
                @block.vector
                def _(e):
                    run_engine("vector", e)
            if "scalar" in streams:
                @block.scalar
                def _(e):
                    run_engine("scalar", e)
            if "gpsimd" in streams:
                @block.gpsimd
                def _(e):
                    run_engine("gpsimd", e)


def tile_segs(tt):
    if tt == 0:
        return [(0, 256, 1), (256, 384, 0)]
    return [(0, 384, 0)]


class Core:
    def __init__(self, nc, gathered):
        self.nc = nc
        self.gathered = gathered
        self.P = Prog(nc, 204 * 1024)
        P = self.P
        self.ps = [nc.alloc_psum_tensor("ps%d" % i, [128, 512], F32) for i in range(7)]
        self.psT = nc.alloc_psum_tensor("psT", [128, 1024], BF16)
        self.ps_rr = 0
        self.ext = {}
        P.force_all = True
        self.ones_bf = P.alloc([128, 128], BF16)
        self.ones1 = P.alloc([128, 128], BF16)
        self.ones64 = P.alloc([128, 128], BF16)
        self.ones128 = P.alloc([128, 128], BF16)
        P.G(lambda e: e.memset(self.ones_bf, 1.0 / D), [], ["c_ones"])
        P.G(lambda e: e.memset(self.ones1, 1.0), [], ["c_ones"])
        P.G(lambda e: e.memset(self.ones64, 1.0 / 64), [], ["c_ones"])
        P.G(lambda e: e.memset(self.ones128, 1.0 / 128), [], ["c_ones"])
        self.eps_t = P.alloc([128, 1])
        P.G(lambda e: e.memset(self.eps_t, EPS), [], ["eps_t"])
        self.one_t = P.alloc([128, 1])
        P.G(lambda e: e.memset(self.one_t, 1.0), [], ["eps_t"])
        self.ident = P.alloc([128, 128])
        self.ident_bf = P.alloc([128, 128], BF16)
        P.G(lambda e: e.memset(self.ident, 0.0), [], ["ident"])
        P.G(lambda e: e.affine_select(self.ident, self.ident, pattern=[[-1, 128]], compare_op=ALU.not_equal,
                                      fill=1.0, base=0, channel_multiplier=1), ["ident"], ["ident"])
        P.V(lambda e: e.tensor_copy(self.ident_bf, self.ident), ["ident"], ["ident_bf"])
        self.mask = [P.alloc([CH, CH]), P.alloc([CH, CH])]
        for d in range(2):
            m = self.mask[d]
            P.G(lambda e, m=m: e.memset(m, 1.0), [], [("mask", d)])
            pat, cm = ([[1, CH]], -1) if d == 0 else ([[-1, CH]], 1)
            P.G(lambda e, m=m, pat=pat, cm=cm: e.affine_select(m, m, pattern=pat, compare_op=ALU.is_ge, fill=0.0,
                                                              base=0, channel_multiplier=cm),
                [("mask", d)], [("mask", d)])
        P.force_all = False

    def din(self, name, shape, dtype=F32):
        if name not in self.ext:
            self.ext[name] = self.nc.dram_tensor(name, list(shape), dtype, kind="ExternalInput").ap()
        return self.ext[name]

    def dout(self, name, shape, dtype=F32):
        return self.nc.dram_tensor(name, list(shape), dtype, kind="ExternalOutput").ap()

    def dint(self, name, shape, dtype=F32, **kw):
        return self.nc.dram_tensor(name, list(shape), dtype, **kw).ap()

    def psum(self, lo=0, hi=6):
        i = lo + self.ps_rr % (hi - lo)
        self.ps_rr += 1
        return self.ps[i], ("ps", i)

    def gather_weights(self):
        P = self.P
        nc = self.nc
        self.G = []
        ccsem = nc.alloc_semaphore("ccsem")
        rg = [list(range(NCORE))]
        mk = P.mark()
        for gi, NS in enumerate(NSHG):
            R = NS // 2048
            wsh = self.din("wshard%d" % gi, [R, 2048])
            wbf = self.dint("wbf%d" % gi, [R, 2048], BF16)
            G = self.dint("wgath%d" % gi, [NCORE * R, 2048], BF16, addr_space="Shared")
            n = NS // 128
            cs = n // 32
            src = wsh.rearrange("a c -> (a c)").rearrange("(p n) -> p n", p=128)
            dst = wbf.rearrange("a c -> (a c)").rearrange("(p n) -> p n", p=128)
            bb = [P.alloc([128, cs], BF16) for _ in range(3)]
            for i in range(32):
                b, bk = bb[i % 3], ("castb", gi, i % 3)
                P.dma("gpsimd", b, src[:, i * cs:(i + 1) * cs], writes=[bk])
                P.dma("sync", dst[:, i * cs:(i + 1) * cs], b, reads=[bk], writes=[("wbf", gi)])

            def ccfn(e, wbf=wbf, G=G, gi=gi):
                e.collective_compute("AllGather", ALU.bypass, replica_groups=rg, ins=[wbf.opt()],
                                     outs=[G.opt()]).then_inc(ccsem)
                e.wait_ge(ccsem, gi + 1)
                return e.nop()
            P.op("gpsimd", ccfn, reads=[("wbf", gi)], writes=["wgath"])
            self.G.append(G.rearrange("(r a) c -> r (a c)", r=NCORE))
        P.release(mk)

    def _wmeta(self, name, idx):
        one = getattr(self, "test_one", False)
        if name == "w_ada":
            return 1, OFF_ADA + idx * SZ_ADA, 9 * D, [1 if one else 2, D, 9 * D]
        if name == "ffn_w_in":
            return 0, OFF_FIN + idx * SZ_FIN, 2 * DFF, [1 if one else 4, D, 2 * DFF]
        if name == "w_in":
            return 0, OFF_WIN + idx * SZ_WIN, INW, [2, D, INW]
        if name == "w_out":
            return 1, OFF_WOUT + idx * SZ_WOUT, D, [2, D, D]
        raise KeyError(name)

    def load_w(self, dst, name, idx, c0, ncols, key):
        P = self.P
        grp, off, cols, shp = self._wmeta(name, idx)
        if self.gathered:
            if not hasattr(P, "groups"):
                P.groups = {}
            P.groups[key] = [(key, "r", r) for r in range(NCORE)]
            for r in range(NCORE):
                src = self.G[grp][r:r + 1, off:off + 256 * cols].rearrange("o (kk p c) -> p (o kk) c", p=128, c=cols)[
                    :, :, c0:c0 + ncols]
                P.dma("gpsimd", dst[:, 2 * r:2 * r + 2, :], src, reads=["wgath"], writes=[(key, "r", r)])
        else:
            w = self.din(name, shp)
            src = w[idx].rearrange("(k p) c -> p k c", p=128)[:, :, c0:c0 + ncols]
            P.dma("gpsimd", dst, src, writes=[key])

    def load_w_fout(self, dst, idx, j0, j1, m, key):
        P = self.P
        if self.gathered:
            off = OFF_FOUT + idx * SZ_FOUT
            r = m // 2
            src = self.G[1][r:r + 1, off:off + SZ_FOUT].rearrange("o (j p c) -> p (o j) c", p=128, c=256)[
                :, j0:j1, (m % 2) * 128:(m % 2) * 128 + 128]
            P.dma("gpsimd", dst, src, reads=["wgath"], writes=[key])
        else:
            w = self.din("ffn_w_out", [1 if getattr(self, "test_one", False) else 4, DFF, D])
            src = w[idx].rearrange("(j p) c -> p j c", p=128)[:, j0:j1, m * 128:(m + 1) * 128]
            P.dma("gpsimd", dst, src, writes=[key])

    def load_small(self):
        P = self.P
        self.gpre = P.alloc([128, 2 * 3 * KC])
        self.gpost = P.alloc([128, 2 * 3 * KC])
        self.bada = P.alloc([128, 2 * 144])
        self.cs_f = P.alloc([128, 2 * KC])
        P.dma("sync", self.gpre, self.din("gpre", [128, 2 * 3 * KC]), writes=["gpre"])
        P.dma("sync", self.gpost, self.din("gpost", [128, 2 * 3 * KC]), writes=["gpost"])
        P.dma("sync", self.bada, self.din("bada", [128, 2 * 144]), writes=["bada"])
        P.dma("sync", self.cs_f, self.din("cvec", [128, 2 * KC]), writes=["cs_f"])
        self.csT = P.alloc([128, KC, 2], BF16)
        P.S(lambda e: e.activation(self.csT.rearrange("p k v -> p v k"),
                                   self.cs_f.rearrange("p (v k) -> p v k", v=2), AF.Silu),
            ["cs_f"], ["csT"])
        self.mod = [None, None]
        self.Apre = {}
        self.Cg = {}

    def emit_mods(self, l):
        P = self.P
        mod = P.alloc([128, 144, 2])
        self.mod[l] = mod
        for s in range(3):
            self.Apre[(l, s)] = P.alloc([128, KC, 2])
            self.Cg[(l, s)] = P.alloc([128, KC, 2])
        mk = P.mark()
        wb = [P.alloc([128, KC, 256], BF16) for _ in range(3)]
        pst, psk = self.ps[6], ("ps", 6)
        for g in range(72):
            b, bk = wb[g % 3], ("wada", g % 3)
            self.load_w(b, "w_ada", l, g * 256, 256, bk)
            for jj in range(2):
                cc = g * 2 + jj
                for k in range(KC):
                    P.mm(pst[:, cc * 2:cc * 2 + 2], b[:, k, jj * 128:(jj + 1) * 128], self.csT[:, k, :],
                         start=(k == 0), stop=(k == KC - 1), reads=[bk, "csT"], writes=[psk])
        P.force_all = True
        P.V(lambda e: e.tensor_tensor(mod, pst[:, 0:288].rearrange("p (c v) -> p c v", v=2),
                                      self.bada[:, l * 144:(l + 1) * 144].unsqueeze(2).to_broadcast([128, 144, 2]),
                                      ALU.add),
            [psk, "bada"], [("mod", l)])
        for s in range(3):
            A = self.Apre[(l, s)]
            C = self.Cg[(l, s)]
            sc = mod[:, (3 * s + 1) * KC:(3 * s + 2) * KC, :]
            gt = mod[:, (3 * s + 2) * KC:(3 * s + 3) * KC, :]
            o0 = (l * 3 + s) * KC
            gp = self.gpre[:, o0:o0 + KC].unsqueeze(2).to_broadcast([128, KC, 2])
            gq = self.gpost[:, o0:o0 + KC].unsqueeze(2).to_broadcast([128, KC, 2])
            P.V(lambda e, A=A, sc=sc, gp=gp: e.scalar_tensor_tensor(A, sc, 1.0, gp, ALU.add, ALU.mult),
                [("mod", l), "gpre"], [("A", l, s)])
            rs = 1.0 if s == 1 else 0.5
            P.V(lambda e, C=C, gt=gt, gq=gq, rs=rs: e.scalar_tensor_tensor(C, gt, rs, gq, ALU.mult, ALU.mult),
                [("mod", l), "gpost"], [("C", l, s)])
        P.force_all = False
        P.release(mk)

    def shift(self, l, s, k, v):
        return self.mod[l][:, 3 * s * KC + k, v:v + 1]

    def alloc_tile_bufs(self):
        P = self.P
        self.h = P.alloc([128, KC, TT])
        self.u = P.alloc([128, KC, TT], BF16)
        self.hT = P.alloc([128, NJ, TT], BF16)
        self.y = P.alloc([128, KC, TT])
        self.wb = [P.alloc([128, 4096], BF16) for _ in range(4)]
        self.wbi = 0
        self.sqb = [P.alloc([128, TT], BF16) for _ in range(2)]
        self.tmpf = [P.alloc([128, TT]) for _ in range(2)]
        self.rstd = P.alloc([128, TT])
        self.sg = [P.alloc([128, TT]) for _ in range(2)]

    def emit_norm_mod(self, l, s, src, src_key, dst, dst_key, segs):
        P = self.P
        pss, pssk = self.ps[6], ("ps", 6)
        rstd = self.rstd
        for k in range(KC):
            b, bk = self.sqb[k % 2], ("sqb", k % 2)
            P.S(lambda e, b=b, k=k: e.activation(b, src[:, k, :], AF.Square), [src_key], [bk])
            P.mm(pss[:, 0:TT], self.ones_bf, b, start=(k == 0), stop=(k == KC - 1),
                 reads=[bk, "c_ones"], writes=[pssk])
        P.S(lambda e: e.activation(rstd, pss[:, 0:TT], AF.Sqrt, bias=self.eps_t), [pssk, "eps_t"], ["rstd"])
        P.V(lambda e: e.reciprocal(rstd, rstd), ["rstd"], ["rstd"])
        A = self.Apre[(l, s)]
        for k in range(KC):
            t, tk = self.tmpf[k % 2], ("tmpf", k % 2)
            for (a, bnd, v) in segs:
                P.V(lambda e, t=t, k=k, a=a, bnd=bnd, v=v: e.scalar_tensor_tensor(
                    t[:, a:bnd], src[:, k, a:bnd], A[:, k, v:v + 1], rstd[:, a:bnd], ALU.mult, ALU.mult),
                    [src_key, "rstd", ("A", l, s)], [tk])
            for (a, bnd, v) in segs:
                P.S(lambda e, t=t, k=k, a=a, bnd=bnd, v=v: e.activation(
                    dst[:, k, a:bnd], t[:, a:bnd], AF.Identity, bias=self.shift(l, s, k, v)),
                    [tk, ("mod", l)], [dst_key])

    def emit_post_resid(self, l, s, segs):
        P = self.P
        pss, pssk = self.ps[6], ("ps", 6)
        rstd, h, y = self.rstd, self.h, self.y
        P.S(lambda e: e.activation(rstd, pss[:, 0:TT], AF.Sqrt, bias=self.eps_t), [pssk, "eps_t"], ["rstd"])
        P.V(lambda e: e.reciprocal(rstd, rstd), ["rstd"], ["rstd"])
        C = self.Cg[(l, s)]
        for k in range(KC):
            t, tk = self.tmpf[k % 2], ("tmpf", k % 2)
            for (a, bnd, v) in segs:
                P.V(lambda e, t=t, k=k, a=a, bnd=bnd, v=v: e.scalar_tensor_tensor(
                    t[:, a:bnd], y[:, k, a:bnd], C[:, k, v:v + 1], rstd[:, a:bnd], ALU.mult, ALU.mult),
                    [("y", k), "rstd", ("C", l, s)], [tk])
            P.V(lambda e, t=t, k=k: e.tensor_tensor(h[:, k, :], h[:, k, :], t, ALU.add), [tk, "h"], ["h"])

    def evac_y(self, m, py, pyk):
        P = self.P
        pss, pssk = self.ps[6], ("ps", 6)
        P.V(lambda e: e.tensor_copy(self.y[:, m, :], py[:, 0:TT]), [pyk], [("y", m)])
        import os
        if os.environ.get("KDBG2", "") == "ev1":
            return
        b2, b2k = self.sqb[m % 2], ("sqb", m % 2)
        P.S(lambda e: e.activation(b2, self.y[:, m, :], AF.Square), [("y", m)], [b2k])
        if os.environ.get("KDBG2", "") == "ev2":
            return
        P.mm(pss[:, 0:TT], self.ones_bf, b2, start=(m == 0), stop=(m == KC - 1),
             reads=[b2k, "c_ones"], writes=[pssk])

    def emit_ffn_tile(self, l, f, s, segs):
        P = self.P
        u, hT = self.u, self.hT
        wb = self.wb
        self.emit_norm_mod(l, s, self.h, "h", u, "u", segs)
        idx = l * 2 + f
        if not hasattr(self, "wc"):
            self.wc, self.wc_done = {}, set()
        if idx not in self.wc:
            self.wc[idx] = (self.dint("wci%d" % idx, [22, 2, 128, 4096], BF16),
                            self.dint("wco%d" % idx, [KC, 2, 128, 2816], BF16))
        wci, wco = self.wc[idx]
        cached = idx in self.wc_done
        import os
        dbg = int(os.environ.get("KDBG", "9"))
        if dbg < 2:
            return
        for g in range((NJ + 1) // 2):
            nj = 2 if 2 * g + 1 < NJ else 1
            bg, bu = wb[self.wbi % 4], wb[(self.wbi + 1) % 4]
            kg, ku = ("wb", self.wbi % 4), ("wb", (self.wbi + 1) % 4)
            self.wbi += 2
            bgv = bg[:, 0:KC * nj * 128].rearrange("p (k c) -> p k c", k=KC)
            buv = bu[:, 0:KC * nj * 128].rearrange("p (k c) -> p k c", k=KC)
            nel = KC * nj * 128
            if not cached:
                self.load_w(bgv, "ffn_w_in", idx, g * 256, nj * 128, kg)
                self.load_w(buv, "ffn_w_in", idx, DFF + g * 256, nj * 128, ku)
                P.dma("sync", wci[g, 0, :, 0:nel], bg[:, 0:nel], reads=[kg], writes=[("wc", idx, "i", g, 0)])
                P.dma("sync", wci[g, 1, :, 0:nel], bu[:, 0:nel], reads=[ku], writes=[("wc", idx, "i", g, 1)])
            else:
                P.dma("gpsimd", bg[:, 0:nel], wci[g, 0, :, 0:nel], reads=[("wc", idx, "i", g, 0)], writes=[kg])
                P.dma("gpsimd", bu[:, 0:nel], wci[g, 1, :, 0:nel], reads=[("wc", idx, "i", g, 1)], writes=[ku])
            for jj in range(nj):
                j = 2 * g + jj
                pg, pgk = self.psum()
                pu, puk = self.psum()
                for k in range(KC):
                    P.mm(pg[:, 0:TT], bgv[:, k, jj * 128:(jj + 1) * 128], u[:, k, :],
                         start=(k == 0), stop=(k == KC - 1), reads=[kg, "u"], writes=[pgk])
                for k in range(KC):
                    P.mm(pu[:, 0:TT], buv[:, k, jj * 128:(jj + 1) * 128], u[:, k, :],
                         start=(k == 0), stop=(k == KC - 1), reads=[ku, "u"], writes=[puk])
                sgb, sk = self.sg[j % 2], ("sg", j % 2)
                P.S(lambda e, sgb=sgb, pg=pg: e.activation(sgb, pg[:, 0:TT], AF.Silu), [pgk], [sk])
                P.V(lambda e, sgb=sgb, pu=pu, j=j: e.tensor_tensor(hT[:, j, :], sgb, pu[:, 0:TT], ALU.mult),
                    [sk, puk], [("hT", j)])
        if dbg < 3:
            return
        for m in range(KC):
            py, pyk = self.psum()
            for half in range(2):
                j0, j1 = (0, 22) if half == 0 else (22, NJ)
                b, bk = wb[self.wbi % 4], ("wb", self.wbi % 4)
                self.wbi += 1
                bv = b[:, 0:(j1 - j0) * 128].rearrange("p (j c) -> p j c", c=128)
                nel = (j1 - j0) * 128
                if not cached:
                    self.load_w_fout(bv, idx, j0, j1, m, bk)
                    P.dma("sync", wco[m, half, :, 0:nel], b[:, 0:nel], reads=[bk], writes=[("wc", idx, "o", m, half)])
                else:
                    P.dma("gpsimd", b[:, 0:nel], wco[m, half, :, 0:nel], reads=[("wc", idx, "o", m, half)], writes=[bk])
                if os.environ.get("KDBG2", "") == "dma":
                    continue
                for j in range(j0, j1):
                    P.mm(py[:, 0:TT], bv[:, j - j0, :], hT[:, j, :], start=(j == 0), stop=(j == NJ - 1),
                         reads=[bk, ("hT", j)], writes=[pyk])
            if os.environ.get("KDBG2", "") in ("dma", "mm"):
                continue
            self.evac_y(m, py, pyk)
        self.wc_done.add(idx)
        self.emit_post_resid(l, s, segs)

    def emit_wout_tile(self, l, tt, segs):
        P = self.P
        mixt = self.hT[:, 0:KC, :]
        t0 = tt * TT
        for k in range(KC):
            P.dma("sync", mixt[:, k, :], self.Mx[k * 128:(k + 1) * 128, t0:t0 + TT], reads=["Mx"], writes=[("hT", k)])
        for mp in range(KC // 2):
            b, bk = self.wb[self.wbi % 4], ("wb", self.wbi % 4)
            self.wbi += 1
            bv = b[:, 0:KC * 256].rearrange("p (k c) -> p k c", k=KC)
            self.load_w(bv, "w_out", l, mp * 256, 256, bk)
            for jj in range(2):
                m = 2 * mp + jj
                py, pyk = self.psum()
                for k in range(KC):
                    P.mm(py[:, 0:TT], bv[:, k, jj * 128:(jj + 1) * 128], mixt[:, k, :],
                         start=(k == 0), stop=(k == KC - 1), reads=[bk, ("hT", k)], writes=[pyk])
                self.evac_y(m, py, pyk)
        self.emit_post_resid(l, 1, segs)

    def load_mixer_params(self):
        P = self.P
        P.force_all = True
        ld = lambda name, shape: (P.alloc(shape), self.din(name, shape))
        def L(name, shape):
            t, src = ld(name, shape)
            P.dma("sync", t, src, writes=[name])
            return t
        self.lbl = L("lbl", [64, 2 * 2 * 8])
        self.anorm = L("anorm", [64, 2])
        self.bnorm = L("bnorm", [64, 2])
        self.cnorm = L("cnorm", [128, 2])
        self.clam = L("clam", [64, 2 * 4])
        self.convw = L("convw", [128, 2 * 4 * 4])
        self.convb = L("convb", [128, 2 * 4])
        self.dbr = L("dbr", [128, 2 * 2 * 4])
        self.dbi = L("dbi", [128, 2 * 2 * 4])
        self.dlam = L("dlam", [128, 2 * 2 * 4])
        self.cos_d = self.din("rope_cos", [64, T])
        self.sin_d = self.din("rope_sin", [64, T])
        self.lb1 = P.alloc([64, 16])
        self.omlb1 = P.alloc([64, 16])
        P.V(lambda e: e.tensor_tensor(self.lb1, self.lbl[:, 16:32], self.lbl[:, 0:16], ALU.subtract), ["lbl"], ["lb1"])
        P.S(lambda e: e.activation(self.lb1, self.lb1, AF.Sigmoid), ["lb1"], ["lb1"])
        P.V(lambda e: e.tensor_scalar(self.omlb1, self.lb1, -1.0, 1.0, ALU.mult, ALU.add), ["lb1"], ["omlb1"])
        self.zero_t = P.alloc([128, 1])
        P.G(lambda e: e.memset(self.zero_t, 0.0), [], ["eps_t"])
        self.cm = P.alloc([64, T])
        P.G(lambda e: e.memset(self.cm, 1.0), [], ["cm"])
        P.G(lambda e: e.memset(self.cm.rearrange("p (c i) -> p c i", i=CH)[:, :, 0:1], 0.0), ["cm"], ["cm"])
        self.neglam = P.alloc([128, 2])
        self.cgain = P.alloc([128, 2])
        prod = P.alloc([64, 2, 2])
        cl = self.clam.rearrange("p (l f) -> p l f", l=2)
        P.V(lambda e: e.tensor_tensor(prod[:, :, 0], cl[:, :, 0], cl[:, :, 1], ALU.mult), ["clam"], ["prod"])
        P.V(lambda e: e.tensor_tensor(prod[:, :, 1], cl[:, :, 2], cl[:, :, 3], ALU.mult), ["clam", "prod"], ["prod"])
        prodb = P.alloc([64, 4], BF16)
        P.V(lambda e: e.tensor_copy(prodb, prod.rearrange("p l f -> p (l f)")), ["prod"], ["prodb"])
        pl, plk = self.ps[6], ("ps", 6)
        P.mm(pl[:, 0:4], self.ones1[0:64, :], prodb, reads=["c_ones", "prodb"], writes=[plk])
        ex = P.alloc([128, 4])
        P.S(lambda e: e.activation(ex, pl[:, 0:4], AF.Exp), [plk], ["ex"])
        for l in range(2):
            li = 0.8 - 0.6 * math.exp(-0.3 * l)
            P.V(lambda e, l=l: e.tensor_tensor(self.neglam[:, l:l + 1], ex[:, 2 * l + 1:2 * l + 2], ex[:, 2 * l:2 * l + 1],
                                               ALU.subtract), ["ex"], ["neglam"])
            P.V(lambda e, l=l, li=li: e.tensor_scalar(self.neglam[:, l:l + 1], self.neglam[:, l:l + 1], -li, None, ALU.add),
                ["neglam"], ["neglam"])
            P.V(lambda e, l=l, li=li: e.tensor_scalar(self.cgain[:, l:l + 1], self.cnorm[:, l:l + 1], 1.0 - li, None, ALU.mult),
                ["cnorm"], ["cgain"])
        self.c1 = P.alloc([128, 16])
        P.S(lambda e: e.activation(self.c1, self.dlam, AF.Exp, scale=-1.0), ["dlam"], ["c1"])
        P.S(lambda e: e.activation(self.c1, self.c1, AF.Ln, bias=self.one_t), ["c1", "eps_t"], ["c1"])
        P.V(lambda e: e.tensor_scalar(self.c1, self.c1, -8.0, None, ALU.mult), ["c1"], ["c1"])
        P.force_all = False

    def alloc_scratch(self):
        self.H = self.dint("Hres", [128, KC, T])
        self.U = self.dint("Umix", [128, KC, T], BF16)
        self.Z = self.dint("Zfm", [INW, T])
        self.Zt = self.dint("Ztm", [T, 1536])
        self.Mx = self.dint("Mx", [2048, T], BF16)

    def emit_proj(self, l):
        P = self.P
        mk = P.mark()
        uf = P.alloc([128, KC, T], BF16)
        P.dma("sync", uf, self.U, reads=["U"], writes=["uf"])
        wb = [P.alloc([128, KC, 256], BF16) for _ in range(3)]
        ev = [P.alloc([128, 512]) for _ in range(4)]
        tbl = [(0, 512), (512, 1024), (1024, 1536), (1536, 2048), (2048, 2304)]
        n = 0
        for g in range(28):
            b, bk = wb[g % 3], ("pw", g % 3)
            self.load_w(b, "w_in", l, g * 256, 256, bk)
            for jj in range(2):
                r0 = g * 256 + jj * 128
                for (a, e_) in tbl:
                    nn = e_ - a
                    ps, psk = self.psum()
                    for k in range(KC):
                        P.mm(ps[:, 0:nn], b[:, k, jj * 128:(jj + 1) * 128], uf[:, k, a:e_],
                             start=(k == 0), stop=(k == KC - 1), reads=[bk, "uf"], writes=[psk])
                    evb, evk = ev[n % 4], ("ev", n % 4)
                    if n % 2 == 0:
                        P.V(lambda e, evb=evb, ps=ps, nn=nn: e.tensor_copy(evb[:, 0:nn], ps[:, 0:nn]), [psk], [evk])
                    else:
                        P.S(lambda e, evb=evb, ps=ps, nn=nn: e.copy(evb[:, 0:nn], ps[:, 0:nn]), [psk], [evk])
                    P.dma("sync", self.Z[r0:r0 + 128, a:e_], evb[:, 0:nn], reads=[evk], writes=[("Z", r0 // 128)])
                    n += 1
        for bi, blk in enumerate([3, 7, 11]):
            for half in range(2):
                b, bk = wb[n % 3], ("pw", n % 3)
                self.load_w(b, "w_in", l, blk * 512 + half * 256, 256, bk)
                for tb in range(T // 128):
                    ps, psk = self.psum()
                    for k in range(KC):
                        P.mm(ps[:, 0:256], uf[:, k, tb * 128:(tb + 1) * 128], b[:, k, :],
                             start=(k == 0), stop=(k == KC - 1), reads=[bk, "uf"], writes=[psk])
                    evb, evk = ev[n % 4], ("ev", n % 4)
                    if n % 2 == 0:
                        P.V(lambda e, evb=evb, ps=ps: e.tensor_copy(evb[:, 0:256], ps[:, 0:256]), [psk], [evk])
                    else:
                        P.S(lambda e, evb=evb, ps=ps: e.copy(evb[:, 0:256], ps[:, 0:256]), [psk], [evk])
                    c0 = bi * 512 + half * 256
                    P.dma("sync", self.Zt[tb * 128:(tb + 1) * 128, c0:c0 + 256], evb[:, 0:256], reads=[evk],
                          writes=[("Zt", bi, half, tb)])
                    n += 1
        P.release(mk)

    def emit_gla(self, l, kind, heads):
        P = self.P
        mk = P.mark()
        CH = 128 if kind == "B" else 32
        NCH = T // CH
        NCTX = CTX // CH
        Fa = lambda: P.alloc([64, T])
        Ba = lambda: P.alloc([64, T], BF16)
        q, kk, lf, bf, b, tmp = [Fa() for _ in range(6)]
        E = tmp
        kh = [Ba(), Ba()]
        Z, Zt = self.Z, self.Zt
        three = lambda a: a.rearrange("p (c i) -> p c i", i=CH)
        if CH == 32:
            cm, masks = self.cm, self.mask
        else:
            cm = Fa()
            P.op("gpsimd", lambda e: e.memset(cm, 1.0), [], ["cmL"], force=True)
            P.op("gpsimd", lambda e: e.memset(cm.rearrange("p (c i) -> p c i", i=CH)[:, :, 0:1], 0.0), ["cmL"], ["cmL"],
                 force=True)
            masks = [P.alloc([CH, CH]), P.alloc([CH, CH])]
            for d in range(2):
                m = masks[d]
                pat, cmul = ([[1, CH]], -1) if d == 0 else ([[-1, CH]], 1)
                P.op("gpsimd", lambda e, m=m: e.memset(m, 1.0), [], [("maskL", d)], force=True)
                P.op("gpsimd", lambda e, m=m, pat=pat, cmul=cmul: e.affine_select(
                    m, m, pattern=pat, compare_op=ALU.is_ge, fill=0.0, base=0, channel_multiplier=cmul),
                    [("maskL", d)], [("maskL", d)], force=True)
        LS = []
        for hi, h in enumerate(heads):
            ls = dict(qt=[Ba(), Ba()], kt=[Ba(), Ba()],
                      khT=[P.alloc([CH, NCH, 64], BF16) for _ in range(2)],
                      vtm=P.alloc([CH, NCH, 64], BF16), oacc=Fa(),
                      gdec=[P.alloc([64, NCH]) for _ in range(2)],
                      S32=[P.alloc([64, 64]) for _ in range(2)],
                      Sbf=[P.alloc([64, 64], BF16) for _ in range(2)],
                      sTb=[[P.alloc([CH, CH], BF16) for _ in range(2)] for _ in range(2)])
            LS.append(ls)
        for hi, h in enumerate(heads):
            ls = LS[hi]
            qt, kt, khT, vtm, oacc, gdec = ls["qt"], ls["kt"], ls["khT"], ls["vtm"], ls["oacc"], ls["gdec"]
            rows = lambda blk, h=h: Z[blk * 512 + 64 * h:blk * 512 + 64 * h + 64, :]

            def rows_sw(dst, blk, key, h=h):
                r0 = blk * 512 + 64 * h
                P.dma("sync", dst[0:32, :], Z[r0 + 32:r0 + 64, :], writes=[key])
                P.dma("sync", dst[32:64, :], Z[r0:r0 + 32, :], writes=[key])

            if kind == "A":
                P.dma("sync", q, rows(0), writes=["q"])
                P.dma("gpsimd", vtm, Zt[:, 64 * h:64 * h + 64].rearrange("(c p) d -> p c d", p=CH),
                      writes=[("vtm", hi)])
                P.S(lambda e: e.activation(q, q, AF.Silu), ["q"], ["q"])
            else:
                cos, sin = bf, b
                P.dma("sync", cos, self.cos_d, writes=["bf"])
                P.dma("sync", sin, self.sin_d, writes=["b"])
                P.dma("gpsimd", vtm, Zt[:, 512 + 64 * h:512 + 64 * h + 64].rearrange("(c p) d -> p c d", p=CH),
                      writes=[("vtm", hi)])
                for (dst, blk, sc) in ((q, 5, 1.0), (kk, 6, 0.125)):
                    dk = "q" if dst is q else "kk"
                    P.dma("sync", dst, rows(blk), writes=[dk])
                    rows_sw(tmp, blk, "tmp")
                    P.V(lambda e, dst=dst: e.tensor_tensor(dst, dst, cos, ALU.mult), [dk, "bf"], [dk])
                    P.V(lambda e: e.tensor_tensor(tmp, tmp, sin, ALU.mult), ["tmp", "b"], ["tmp"])
                    P.V(lambda e, dst=dst: e.tensor_tensor(dst, dst, tmp, ALU.add), [dk, "tmp"], [dk])
                    if sc != 1.0:
                        P.V(lambda e, dst=dst, sc=sc: e.tensor_scalar(dst, dst, sc, None, ALU.mult), [dk], [dk])
                lg = math.log1p(-2.0 ** (-5.0 - h))
            P.G(lambda e, oacc=oacc: e.memset(oacc, 0.0), [], [("oacc", hi)])
            for d in range(2):
                if kind == "A":
                    P.dma("sync", tmp, rows(1 + d), writes=["tmp"])
                    P.S(lambda e: e.activation(tmp, tmp, AF.Sigmoid), ["tmp"], ["tmp"])
                    if l == 0:
                        lbs, oms = self.zero_t[0:64, :], self.one_t[0:64, :]
                    else:
                        lbs, oms = self.lb1[:, d * 8 + h:d * 8 + h + 1], self.omlb1[:, d * 8 + h:d * 8 + h + 1]
                    P.V(lambda e, lbs=lbs, oms=oms: e.tensor_scalar(tmp, tmp, oms, lbs, ALU.mult, ALU.add),
                        ["tmp", "lb1", "omlb1", "eps_t"], ["tmp"])
                    P.V(lambda e: e.tensor_scalar(kk, tmp, -1.0, 1.0, ALU.mult, ALU.add), ["tmp"], ["kk"])
                    P.S(lambda e: e.activation(lf, tmp, AF.Ln), ["tmp"], ["lf"])
                elif d == 0:
                    P.G(lambda e, lg=lg: e.memset(lf, lg), [], ["lf"])
                P.V(lambda e: e.tensor_tensor_scan(bf, cm, lf, 0.0, ALU.mult, ALU.add), ["cm", "cmL", "lf"], ["bf"])
                btot = three(bf)[:, :, CH - 1:CH]
                btb = btot.to_broadcast([64, NCH, CH])
                if d == 0:
                    P.V(lambda e: e.tensor_copy(b, bf), ["bf"], ["b"])
                else:
                    P.V(lambda e, btb=btb: e.tensor_tensor(three(b), btb, three(bf), ALU.subtract), ["bf"], ["b"])
                    P.V(lambda e: e.tensor_tensor(b, b, lf, ALU.add), ["b", "lf"], ["b"])
                P.S(lambda e: e.activation(E, b, AF.Exp), ["b"], ["tmp"])
                P.V(lambda e, d=d, qt=qt: e.tensor_tensor(qt[d], q, E, ALU.mult), ["q", "tmp"], [("qt", hi, d)])
                P.S(lambda e: e.activation(E, b, AF.Exp, scale=-1.0), ["b"], ["tmp"])
                P.V(lambda e, d=d, kt=kt: e.tensor_tensor(kt[d], kk, E, ALU.mult), ["kk", "tmp"], [("kt", hi, d)])
                P.V(lambda e, btb=btb: e.tensor_tensor(three(E), btb, three(b), ALU.subtract), ["bf", "b"], ["tmp"])
                P.S(lambda e: e.activation(E, E, AF.Exp), ["tmp"], ["tmp"])
                P.V(lambda e, d=d: e.tensor_tensor(kh[d], kk, E, ALU.mult), ["kk", "tmp"], [("kh", d)])
                P.S(lambda e, d=d, gdec=gdec, btot=btot: e.activation(gdec[d], btot.rearrange("p c o -> p (c o)"), AF.Exp),
                    ["bf"], [("gdec", hi, d)])
                for c0 in range(0, NCH, 8):
                    ptb, ptk = self.psT, "psT"
                    nb = min(8, NCH - c0)
                    for c in range(c0, c0 + nb):
                        P.op("tensor", lambda e, d=d, c=c, c0=c0: e.transpose(
                            ptb[0:CH, (c - c0) * 64:(c - c0) * 64 + 64], kh[d][:, c * CH:(c + 1) * CH],
                            self.ident_bf[0:64, 0:64]), [("kh", d), "ident_bf"], [ptk])
                    P.V(lambda e, d=d, c0=c0, khT=khT, nb=nb: e.tensor_copy(
                        khT[d][:, c0:c0 + nb, :], ptb[0:CH, 0:nb * 64].rearrange("p (c k) -> p c k", k=64)),
                        [ptk], [("khT", hi, d)])
        order = [list(range(NCH)), list(range(NCTX - 1, -1, -1)) + list(range(NCH - 1, NCTX - 1, -1))]
        slot_w = 4 * CH
        nslots = 512 // slot_w
        nstep = 0
        for i in range(NCH):
            for hi in range(len(heads)):
                ls = LS[hi]
                qt, kt, khT, vtm, oacc, gdec = ls["qt"], ls["kt"], ls["khT"], ls["vtm"], ls["oacc"], ls["gdec"]
                S32, Sbf, sTb = ls["S32"], ls["Sbf"], ls["sTb"]
                for d in range(2):
                    c = order[d][i]
                    ts = slice(c * CH, (c + 1) * CH)
                    first = (i == 0)
                    bank, slot = nstep % 6, (nstep // 6) % nslots
                    nstep += 1
                    base = slot * slot_w
                    pbank = self.ps[bank]
                    pss_, psk = pbank[:, base:base + CH], ("psr", bank, slot, 0)
                    po, pok = pbank[:, base + CH:base + 2 * CH], ("psr", bank, slot, 1)
                    pu, puk = pbank[:, base + 2 * CH:base + 2 * CH + 64], ("psr", bank, slot, 2)
                    P.mm(pss_[0:CH, 0:CH], kt[d][:, ts], qt[d][:, ts], reads=[("kt", hi, d), ("qt", hi, d)], writes=[psk])
                    sT, sTk = sTb[d][i % 2], ("sT", hi, d, i % 2)
                    P.V(lambda e, sT=sT, pss_=pss_, d=d: e.tensor_tensor(sT, pss_[0:CH, 0:CH], masks[d], ALU.mult),
                        [psk, ("mask", d), ("maskL", d)], [sTk])
                    if not first:
                        P.mm(po[0:64, 0:CH], Sbf[d], qt[d][:, ts], start=True, stop=False,
                             reads=[("Sbf", hi, d), ("qt", hi, d)], writes=[pok])
                    P.mm(po[0:64, 0:CH], vtm[:, c, :], sT, start=first, stop=True, reads=[("vtm", hi), sTk], writes=[pok])
                    P.V(lambda e, po=po, ts=ts, oacc=oacc: e.tensor_tensor(oacc[:, ts], oacc[:, ts], po[0:64, 0:CH], ALU.add),
                        [pok, ("oacc", hi)], [("oacc", hi)])
                    P.mm(pu[0:64, 0:64], khT[d][:, c, :], vtm[:, c, :], reads=[("khT", hi, d), ("vtm", hi)], writes=[puk])
                    if first:
                        P.V(lambda e, d=d, pu=pu, S32=S32: e.tensor_copy(S32[d], pu[0:64, 0:64]), [puk], [("S32", hi, d)])
                    else:
                        P.V(lambda e, d=d, pu=pu, c=c, S32=S32, gdec=gdec: e.scalar_tensor_tensor(
                            S32[d], S32[d], gdec[d][:, c:c + 1], pu[0:64, 0:64], ALU.mult, ALU.add),
                            [puk, ("S32", hi, d), ("gdec", hi, d)], [("S32", hi, d)])
                    P.S(lambda e, d=d, S32=S32, Sbf=Sbf: e.copy(Sbf[d], S32[d]), [("S32", hi, d)], [("Sbf", hi, d)])
        sqb = [P.alloc([64, 512], BF16) for _ in range(2)]
        ob = [P.alloc([64, 512], BF16) for _ in range(2)]
        rs = [P.alloc([64, 512]) for _ in range(2)]
        zg = q
        for hi, h in enumerate(heads):
            oacc = LS[hi]["oacc"]
            if kind == "A":
                mrow, gain, gblk = 64 * h, self.anorm[:, l:l + 1], 4
            else:
                mrow, gain, gblk = 512 + 64 * h, self.bnorm[:, l:l + 1], 8
            P.dma("sync", zg, Z[gblk * 512 + 64 * h:gblk * 512 + 64 * h + 64, :], writes=["q"])
            P.S(lambda e: e.activation(zg, zg, AF.Silu), ["q"], ["q"])
            for bi_, (a, e_) in enumerate([(0, 512), (512, 1024), (1024, 1536), (1536, 2048), (2048, 2304)]):
                nn = e_ - a
                sq, sqk = sqb[bi_ % 2], ("gsq", bi_ % 2)
                P.S(lambda e, sq=sq, a=a, e_=e_, nn=nn, oacc=oacc: e.activation(sq[:, 0:nn], oacc[:, a:e_], AF.Square),
                    [("oacc", hi)], [sqk])
                pn, pnk = self.ps[6], ("ps", 6)
                P.mm(pn[0:64, 0:nn], self.ones64[0:64, 0:64], sq[:, 0:nn], reads=[sqk, "c_ones"], writes=[pnk])
                r, rk = rs[bi_ % 2], ("grs", bi_ % 2)
                P.S(lambda e, r=r, pn=pn, nn=nn: e.activation(r[:, 0:nn], pn[0:64, 0:nn], AF.Sqrt, bias=self.eps_t[0:64, :]),
                    [pnk, "eps_t"], [rk])
                P.V(lambda e, r=r, nn=nn: e.reciprocal(r[:, 0:nn], r[:, 0:nn]), [rk], [rk])
                P.V(lambda e, r=r, a=a, e_=e_, nn=nn, oacc=oacc, gain=gain: e.scalar_tensor_tensor(
                    r[:, 0:nn], oacc[:, a:e_], gain, r[:, 0:nn], ALU.mult, ALU.mult),
                    [rk, ("oacc", hi), "anorm", "bnorm"], [rk])
                o_, ok_ = ob[bi_ % 2], ("gob", bi_ % 2)
                P.V(lambda e, o_=o_, r=r, a=a, e_=e_, nn=nn: e.tensor_tensor(o_[:, 0:nn], r[:, 0:nn], zg[:, a:e_], ALU.mult),
                    [rk, "q"], [ok_])
                P.dma("sync", self.Mx[mrow:mrow + 64, a:e_], o_[:, 0:nn], reads=[ok_], writes=[("Mx", mrow)])
        P.release(mk)

    def emit_diffattn(self, l, h, ctx_out):
        P = self.P
        mk = P.mark()
        Z, Zt = self.Z, self.Zt
        t1 = P.alloc([128, T])
        t2 = P.alloc([128, T])
        cos = P.alloc([128, T])
        sin = P.alloc([128, T])
        qb = P.alloc([128, T], BF16)
        kb_ = P.alloc([128, T], BF16)
        vtm = P.alloc([128, T // 128, 128], BF16)
        for hf in range(2):
            P.dma("sync", cos[64 * hf:64 * hf + 64, :], self.cos_d, writes=["cos"])
            P.dma("sync", sin[64 * hf:64 * hf + 64, :], self.sin_d, writes=["sin"])
        P.dma("gpsimd", vtm, Zt[:, 1024 + 128 * h:1024 + 128 * h + 128].rearrange("(kb p) d -> p kb d", p=128),
              writes=["vtm"])
        for (dst, blk) in ((qb, 9), (kb_, 10)):
            r0 = blk * 512 + 128 * h
            P.dma("sync", t1, Z[r0:r0 + 128, :], writes=["t1"])
            for hf in range(2):
                P.dma("sync", t2[64 * hf:64 * hf + 32, :], Z[r0 + 64 * hf + 32:r0 + 64 * hf + 64, :], writes=["t2"])
                P.dma("sync", t2[64 * hf + 32:64 * hf + 64, :], Z[r0 + 64 * hf:r0 + 64 * hf + 32, :], writes=["t2"])
            P.V(lambda e: e.tensor_tensor(t1, t1, cos, ALU.mult), ["t1", "cos"], ["t1"])
            P.V(lambda e: e.tensor_tensor(t2, t2, sin, ALU.mult), ["t2", "sin"], ["t2"])
            dk = "qb" if dst is qb else "kb"
            P.V(lambda e, dst=dst: e.tensor_tensor(dst, t1, t2, ALU.add), ["t1", "t2"], [dk])
        pex = [P.alloc([128, 512], BF16) for _ in range(3)]
        rr = [P.alloc([128, 512]) for _ in range(2)]
        oo = [P.alloc([128, 512]) for _ in range(2)]
        sqo = P.alloc([128, 512], BF16)
        outb = [P.alloc([128, 512], BF16) for _ in range(2)]
        neglam = self.neglam[:, l:l + 1]
        cgain = self.cgain[:, l:l + 1]
        qblocks = [(CTX + 512 * i, 512, T // 128) for i in range(4)]
        if ctx_out:
            qblocks.append((0, CTX, CTX // 128))
        npx = 0
        for qi, (q0, nq, nkb) in enumerate(qblocks):
            accO = [(self.ps[2], ("ps", 2)), (self.ps[3], ("ps", 3))]
            accS = [(self.ps[4], ("ps", 4)), (self.ps[5], ("ps", 5))]
            for kbi in range(nkb):
                for hf in range(2):
                    pS, pSk = self.psum(0, 2)
                    P.mm(pS[:, 0:nq], kb_[64 * hf:64 * hf + 64, kbi * 128:(kbi + 1) * 128],
                         qb[64 * hf:64 * hf + 64, q0:q0 + nq], reads=["kb", "qb"], writes=[pSk])
                    px, pxk = pex[npx % 3], ("pex", npx % 3)
                    npx += 1
                    P.S(lambda e, px=px, pS=pS, nq=nq: e.activation(px[:, 0:nq], pS[:, 0:nq], AF.Exp, scale=0.125),
                        [pSk], [pxk])
                    P.mm(accO[hf][0][:, 0:nq], vtm[:, kbi, :], px[:, 0:nq], start=(kbi == 0), stop=(kbi == nkb - 1),
                         reads=["vtm", pxk], writes=[accO[hf][1]])
                    P.mm(accS[hf][0][:, 0:nq], self.ones1, px[:, 0:nq], start=(kbi == 0), stop=(kbi == nkb - 1),
                         reads=["c_ones", pxk], writes=[accS[hf][1]])
            for hf in range(2):
                r, rk = rr[hf], ("rr", hf)
                o, ok_ = oo[hf], ("oo", hf)
                P.V(lambda e, r=r, hf=hf, nq=nq, accS=accS: e.reciprocal(r[:, 0:nq], accS[hf][0][:, 0:nq]),
                    [accS[hf][1]], [rk])
                P.V(lambda e, r=r, o=o, hf=hf, nq=nq, accO=accO: e.tensor_tensor(o[:, 0:nq], accO[hf][0][:, 0:nq],
                                                                                r[:, 0:nq], ALU.mult),
                    [accO[hf][1], rk], [ok_])
            o = oo[0]
            P.V(lambda e, nq=nq: e.scalar_tensor_tensor(oo[0][:, 0:nq], oo[1][:, 0:nq], neglam, oo[0][:, 0:nq],
                                                        ALU.mult, ALU.add),
                [("oo", 0), ("oo", 1), "neglam"], [("oo", 0)])
            P.S(lambda e, nq=nq: e.activation(sqo[:, 0:nq], oo[0][:, 0:nq], AF.Square), [("oo", 0)], ["sqo"])
            pn, pnk = self.ps[6], ("ps", 6)
            P.mm(pn[:, 0:nq], self.ones128, sqo[:, 0:nq], reads=["sqo", "c_ones"], writes=[pnk])
            r = rr[0]
            P.S(lambda e, nq=nq: e.activation(rr[0][:, 0:nq], pn[:, 0:nq], AF.Sqrt, bias=self.eps_t), [pnk, "eps_t"],
                [("rr", 0)])
            P.V(lambda e, nq=nq: e.reciprocal(rr[0][:, 0:nq], rr[0][:, 0:nq]), [("rr", 0)], [("rr", 0)])
            ob, obk = outb[qi % 2], ("outb", qi % 2)
            P.V(lambda e, nq=nq, ob=ob: e.scalar_tensor_tensor(ob[:, 0:nq], oo[0][:, 0:nq], cgain, rr[0][:, 0:nq],
                                                               ALU.mult, ALU.mult),
                [("oo", 0), ("rr", 0), "cgain"], [obk])
            P.dma("sync", self.Mx[1024 + 128 * h:1024 + 128 * h + 128, q0:q0 + nq], ob[:, 0:nq], reads=[obk],
                  writes=[("Mx", 1024 + 128 * h)])
        P.release(mk)

    def emit_rglru(self, l, cc):
        P = self.P
        mk = P.mark()
        Z = self.Z
        Fa = lambda: P.alloc([128, T])
        x, gate, xc, ra, ia, aa, uu, hs, hsum = [Fa() for _ in range(9)]
        xcb = P.alloc([128, T], BF16)
        outb = P.alloc([128, T], BF16)
        P.dma("sync", x, Z[12 * 512 + 128 * cc:12 * 512 + 128 * cc + 128, :], writes=["x"])
        P.dma("sync", gate, Z[13 * 512 + 128 * cc:13 * 512 + 128 * cc + 128, :], writes=["gate"])
        cw = lambda j: self.convw[:, (l * 4 + cc) * 4 + j:(l * 4 + cc) * 4 + j + 1]
        cb = self.convb[:, l * 4 + cc:l * 4 + cc + 1]
        P.S(lambda e: e.activation(xc, x, AF.Identity, bias=cb, scale=cw(1)), ["x", "convw", "convb"], ["xc"])
        for (a, e_) in ((0, CTX), (CTX, T)):
            P.V(lambda e, a=a, e_=e_: e.scalar_tensor_tensor(xc[:, a + 1:e_], x[:, a:e_ - 1], cw(0), xc[:, a + 1:e_],
                                                            ALU.mult, ALU.add), ["x", "xc", "convw"], ["xc"])
            P.V(lambda e, a=a, e_=e_: e.scalar_tensor_tensor(xc[:, a:e_ - 1], x[:, a + 1:e_], cw(2), xc[:, a:e_ - 1],
                                                            ALU.mult, ALU.add), ["x", "xc", "convw"], ["xc"])
            P.V(lambda e, a=a, e_=e_: e.scalar_tensor_tensor(xc[:, a:e_ - 2], x[:, a + 2:e_], cw(3), xc[:, a:e_ - 2],
                                                            ALU.mult, ALU.add), ["x", "xc", "convw"], ["xc"])
        P.V(lambda e: e.tensor_copy(xcb, xc), ["xc"], ["xcb"])
        wbd = [[P.alloc([128, 128], BF16) for _ in range(2)] for _ in range(2)]
        dwr = self.din("d_w_r", [2, 2, 8, 64, 64])
        dwi = self.din("d_w_i", [2, 2, 8, 64, 64])
        for d in range(2):
            for ri, src in enumerate((dwr, dwi)):
                w = wbd[d][ri]
                wk = ("wbd", d, ri)
                P.G(lambda e, w=w: e.memset(w, 0.0), [], [wk])
                for g2 in range(2):
                    P.dma("gpsimd", w[64 * g2:64 * g2 + 64, 64 * g2:64 * g2 + 64], src[l, d, 2 * cc + g2], writes=[wk])
        tbl = [(0, 512), (512, 1024), (1024, 1536), (1536, 2048), (2048, 2304)]
        for d in range(2):
            pidx = (l * 2 + d) * 4 + cc
            br = self.dbr[:, pidx:pidx + 1]
            bi = self.dbi[:, pidx:pidx + 1]
            c1 = self.c1[:, pidx:pidx + 1]
            for (a, e_) in tbl:
                nn = e_ - a
                pr, prk = self.psum()
                P.mm(pr[:, 0:nn], wbd[d][0], xcb[:, a:e_], reads=[("wbd", d, 0), "xcb"], writes=[prk])
                P.S(lambda e, pr=pr, a=a, e_=e_, nn=nn, br=br: e.activation(ra[:, a:e_], pr[:, 0:nn], AF.Sigmoid, bias=br),
                    [prk, "dbr"], ["ra"])
                pi_, pik = self.psum()
                P.mm(pi_[:, 0:nn], wbd[d][1], xcb[:, a:e_], reads=[("wbd", d, 1), "xcb"], writes=[pik])
                P.S(lambda e, pi_=pi_, a=a, e_=e_, nn=nn, bi=bi: e.activation(ia[:, a:e_], pi_[:, 0:nn], AF.Sigmoid, bias=bi),
                    [pik, "dbi"], ["ia"])
            P.S(lambda e, c1=c1: e.activation(aa, ra, AF.Exp, scale=c1), ["ra", "c1"], ["aa"])
            P.V(lambda e: e.tensor_tensor(ra, aa, aa, ALU.mult), ["aa", "ra"], ["ra"])
            P.S(lambda e: e.activation(ra, ra, AF.Sqrt, bias=self.one_t, scale=-1.0), ["ra", "eps_t"], ["ra"])
            P.V(lambda e: e.tensor_tensor(uu, ia, xc, ALU.mult), ["ia", "xc"], ["uu"])
            P.V(lambda e: e.tensor_tensor(uu, uu, ra, ALU.mult), ["uu", "ra"], ["uu"])
            if d == 0:
                P.V(lambda e: e.tensor_tensor_scan(hsum, aa, uu, 0.0, ALU.mult, ALU.add), ["aa", "uu"], ["hsum"])
            else:
                rv = lambda t_, a, e_: t_[:, a:e_][:, ::-1]
                P.V(lambda e: e.tensor_tensor_scan(rv(hs, 0, CTX), rv(aa, 0, CTX), rv(uu, 0, CTX), 0.0, ALU.mult, ALU.add),
                    ["aa", "uu"], ["hs"])
                P.V(lambda e: e.tensor_tensor_scan(rv(hs, CTX, T), rv(aa, CTX, T), rv(uu, CTX, T), hs[:, 0:1],
                                                   ALU.mult, ALU.add), ["aa", "uu", "hs"], ["hs"], force=True)
                P.V(lambda e: e.tensor_tensor(hsum, hsum, hs, ALU.add), ["hsum", "hs"], ["hsum"])
        P.S(lambda e: e.activation(x, gate, AF.Square), ["gate", "x"], ["x"])
        P.V(lambda e: e.tensor_scalar(x, x, 0.044715, 1.0, ALU.mult, ALU.add), ["x"], ["x"])
        P.V(lambda e: e.tensor_tensor(x, x, gate, ALU.mult), ["x", "gate"], ["x"])
        P.S(lambda e: e.activation(x, x, AF.Sigmoid, scale=1.5957691216057308), ["x"], ["x"])
        P.V(lambda e: e.tensor_tensor(x, x, gate, ALU.mult), ["x", "gate"], ["x"])
        P.V(lambda e: e.tensor_tensor(outb, x, hsum, ALU.mult), ["x", "hsum"], ["outb"])
        P.dma("sync", self.Mx[1536 + 128 * cc:1536 + 128 * cc + 128, :], outb, reads=["outb"], writes=[("Mx", 1536 + 128 * cc)])
        P.release(mk)

    def emit_pass(self, which, xT=None, out=None):
        P = self.P
        mk = P.mark()
        self.alloc_tile_bufs()
        h = self.h
        for tt in range(NTT):
            segs = tile_segs(tt)
            t0 = tt * TT
            if which == 0:
                P.dma("sync", h, xT[:, :, t0:t0 + TT], writes=["h"])
                self.emit_ffn_tile(0, 0, 0, segs)
                nl = 0
            elif which == 1:
                P.dma("sync", h, self.H[:, :, t0:t0 + TT], reads=[("H", tt)], writes=["h"])
                self.emit_wout_tile(0, tt, segs)
                self.emit_ffn_tile(0, 1, 2, segs)
                self.emit_ffn_tile(1, 0, 0, segs)
                nl = 1
            else:
                P.dma("sync", h, self.H[:, :, t0:t0 + TT], reads=[("H", tt)], writes=["h"])
                self.emit_wout_tile(1, tt, segs)
                self.emit_ffn_tile(1, 1, 2, segs)
                if tt == 0:
                    P.dma("sync", out[:, :, 0:128], h[:, :, 256:384], reads=["h"])
                else:
                    o0 = 128 + (tt - 1) * TT
                    P.dma("sync", out[:, :, o0:o0 + TT], h, reads=["h"])
                continue
            self.emit_norm_mod(nl, 1, h, "h", self.u, "u", segs)
            P.dma("sync", self.U[:, :, t0:t0 + TT], self.u, reads=["u"], writes=[("U", tt)])
            P.dma("sync", self.H[:, :, t0:t0 + TT], h, reads=["h"], writes=[("H", tt)])
        P.release(mk)

    def emit_mixer(self, l, parts="ABCD"):
        self.emit_proj(l)
        if "A" in parts:
            for h in range(0, 8, 2):
                self.emit_gla(l, "A", [h, h + 1])
        if "B" in parts:
            for h in range(0, 8, 2):
                self.emit_gla(l, "B", [h, h + 1])
        if "C" in parts:
            for h in range(4):
                self.emit_diffattn(l, h, l == 0)
        if "D" in parts:
            for cc in range(4):
                self.emit_rglru(l, cc)


def build_full(gathered=True):
    nc = bass.Bass("TRN2", target_bir_lowering=False)
    C = Core(nc, gathered)
    if gathered:
        C.gather_weights()
    C.load_small()
    C.load_mixer_params()
    C.alloc_scratch()
    xT = C.din("xT", [128, KC, T])
    out = C.dout("out", [128, KC, SEQ])
    C.emit_mods(0)
    C.emit_mods(1)
    C.emit_pass(0, xT=xT)
    C.emit_mixer(0)
    C.emit_pass(1)
    C.emit_mixer(1)
    C.emit_pass(2, out=out)
    C.P.emit()
    return nc, C


def build_pass0_test():
    nc = bass.Bass("TRN2", target_bir_lowering=False)
    C = Core(nc, False)
    C.test_one = True
    C.load_small()
    C.H = C.dout("H_out", [128, KC, T])
    C.U = C.dout("U_out", [128, KC, T], BF16)
    xT = C.din("xT", [128, KC, T])
    C.emit_mods(0)
    modo = C.dout("mod_out", [128, 144, 2])
    C.P.dma("sync", modo, C.mod[0], reads=[("mod", 0)])
    C.emit_pass(0, xT=xT)
    C.P.emit()
    return nc, C


def build_mixer_test(l, parts):
    nc = bass.Bass("TRN2", target_bir_lowering=False)
    C = Core(nc, False)
    C.load_mixer_params()
    C.alloc_scratch()
    uin = C.din("U_in", [128, KC, T], BF16)
    mout = C.dout("Mx_out", [2048, T], BF16)
    C.P.dma("sync", C.U, uin, writes=["U"])
    C.P.barrier()
    C.emit_mixer(l, parts)
    C.P.dma("sync", mout, C.Mx, reads=[("Mx", r) for r in range(0, 2048, 64)])
    C.P.emit()
    return nc, C


def fm(a):
    a = np.asarray(a, np.float32)
    lead = a.shape[:-1]
    r = a.reshape(lead + (a.shape[-1] // 128, 128))
    return np.ascontiguousarray(np.moveaxis(r, -1, 0))


def tokens_T(x, ctx, b):
    t = np.concatenate([ctx[b], x[b]], 0)
    return np.ascontiguousarray(t.T.reshape(KC, 128, T).transpose(1, 0, 2))


def rope_tables():
    quarter = 16
    inv = 10000.0 ** (-np.arange(quarter, dtype=np.float32) / quarter)
    rows = SEQ // 64
    row = np.repeat(np.arange(rows, dtype=np.float32), 64)
    col = np.tile(np.arange(64, dtype=np.float32), rows)
    ang = np.concatenate([row[:, None] * inv, col[:, None] * inv], -1).astype(np.float32)
    cos = np.cos(ang).T
    sin = np.sin(ang).T
    c2 = np.ones((64, T), np.float32)
    s2 = np.zeros((64, T), np.float32)
    c2[0:32, CTX:] = cos
    c2[32:64, CTX:] = cos
    s2[0:32, CTX:] = -sin
    s2[32:64, CTX:] = sin
    return c2, s2


def small_inputs(inp, b):
    f32 = lambda a: np.asarray(a, np.float32)
    d = {}
    d["gpre"] = fm(inp["norm_pre"]).reshape(128, -1)
    d["gpost"] = fm(inp["norm_post"]).reshape(128, -1)
    d["bada"] = np.ascontiguousarray(f32(inp["b_ada"]).reshape(2, 144, 128).transpose(2, 0, 1)).reshape(128, -1)
    d["cvec"] = fm(np.stack([f32(inp["c"])[b], f32(inp["c_ctx"])], 0)).reshape(128, -1)
    return d


def mixer_inputs(inp):
    f32 = lambda a: np.asarray(a, np.float32)
    d = {}
    d["lbl"] = np.ascontiguousarray(f32(inp["lb_logits"]).reshape(2, 2, 8, 64).transpose(3, 0, 1, 2)).reshape(64, -1)
    d["anorm"] = np.ascontiguousarray(f32(inp["a_norm"]).T)
    d["bnorm"] = np.ascontiguousarray(f32(inp["b_norm"]).T)
    d["cnorm"] = np.ascontiguousarray(f32(inp["c_norm"]).T)
    d["clam"] = np.ascontiguousarray(f32(inp["c_lambda"]).transpose(2, 0, 1)).reshape(64, -1)
    d["convw"] = np.ascontiguousarray(f32(inp["d_conv_w"]).reshape(2, 4, 4, 128).transpose(3, 0, 2, 1)).reshape(128, -1)
    d["convb"] = np.ascontiguousarray(f32(inp["d_conv_b"]).reshape(2, 4, 128).transpose(2, 0, 1)).reshape(128, -1)
    for nm, src in (("dbr", "d_b_r"), ("dbi", "d_b_i"), ("dlam", "d_lambda")):
        d[nm] = np.ascontiguousarray(f32(inp[src]).reshape(2, 2, 4, 128).transpose(3, 0, 1, 2)).reshape(128, -1)
    c2, s2 = rope_tables()
    d["rope_cos"] = c2
    d["rope_sin"] = s2
    d["d_w_r"] = f32(inp["d_w_r"])
    d["d_w_i"] = f32(inp["d_w_i"])
    return d


def weight_shards(inp, r):
    f32 = lambda a: np.asarray(a, np.float32)
    rs = slice(256 * r, 256 * r + 256)
    p0, p1 = [], []
    for l in range(2):
        for f in range(2):
            p0.append(f32(inp["ffn_w_in"])[l, f, rs, :].reshape(-1))
    for l in range(2):
        p0.append(f32(inp["w_in"])[l, rs, :].reshape(-1))
    for l in range(2):
        p1.append(f32(inp["w_ada"])[l, rs, :].reshape(-1))
    for l in range(2):
        for f in range(2):
            p1.append(np.ascontiguousarray(f32(inp["ffn_w_out"])[l, f, :, rs]).reshape(-1))
    for l in range(2):
        p1.append(f32(inp["w_out"])[l, rs, :].reshape(-1))
    w0 = np.concatenate(p0)
    w1 = np.concatenate(p1)
    assert w0.size == NSH0 and w1.size == NSH1
    return w0.reshape(NSH0 // 2048, 2048), w1.reshape(NSH1 // 2048, 2048)


_CACHE = {}


USE_GATHER = False
N_USED = 4


def kernel(**inp):
    if "nc" not in _CACHE:
        _CACHE["nc"] = build_full(USE_GATHER)[0]
    nc = _CACHE["nc"]
    f32 = lambda a: np.ascontiguousarray(np.asarray(a, np.float32))
    mi = mixer_inputs(inp)
    shared = {}
    if not USE_GATHER:
        shared = dict(w_ada=f32(inp["w_ada"]), ffn_w_in=f32(inp["ffn_w_in"]).reshape(4, D, 2 * DFF),
                      ffn_w_out=f32(inp["ffn_w_out"]).reshape(4, DFF, D), w_in=f32(inp["w_in"]), w_out=f32(inp["w_out"]))
    in_maps = []
    for c in range(N_USED):
        b = c % 4
        m = dict(mi)
        m.update(shared)
        m.update(small_inputs(inp, b))
        m["xT"] = tokens_T(np.asarray(inp["x"], np.float32), np.asarray(inp["ctx"], np.float32), b)
        if USE_GATHER:
            m["wshard0"], m["wshard1"] = weight_shards(inp, c)
        in_maps.append(m)
    res = run_bass_kernel_spmd(nc, in_maps, core_ids=list(range(N_USED)))
    outs = []
    for b in range(4):
        o = np.asarray(res.results[b]["out"])
        outs.append(o.transpose(2, 1, 0).reshape(SEQ, D))
    return np.stack(outs, 0).astype(np.float32)
```

```python
import math
import numpy as np
import ml_dtypes
import concourse.bass as bass
import concourse.mybir as mybir
from concourse.bass_utils import run_bass_kernel_spmd

F32 = mybir.dt.float32
BF16 = mybir.dt.bfloat16
AF = mybir.ActivationFunctionType
ALU = mybir.AluOpType

N_DMA_SEMS = 6
D = 2048
KC = 16
DFF = 5504
NJ = 43
T = 2304
TT = 384
NTT = 6
CTX = 256
SEQ = 2048
EPS = 1e-6
CH = 32
NCH = T // CH
INW = 7168
NCORE = 8
SZ_ADA = 256 * 9 * D
SZ_FIN = 256 * 2 * DFF
SZ_FOUT = DFF * 256
SZ_WIN = 256 * INW
SZ_WOUT = 256 * D
OFF_FIN = 0
OFF_WIN = OFF_FIN + 4 * SZ_FIN
NSH0 = OFF_WIN + 2 * SZ_WIN
OFF_ADA = 0
OFF_FOUT = OFF_ADA + 2 * SZ_ADA
OFF_WOUT = OFF_FOUT + 4 * SZ_FOUT
NSH1 = OFF_WOUT + 2 * SZ_WOUT
NSHG = (NSH0, NSH1)
NSH = NSH0 + NSH1
assert NSH % 2048 == 0


class Prog:
    def __init__(self, nc, arena_bytes):
        self.nc = nc
        self.ops = []
        self.lastw = {}
        self.readers = {}
        self.arena = nc.alloc_sbuf_tensor("arena", [128, arena_bytes // 4], F32)
        self.arena_bytes = arena_bytes
        self.off = 0
        self.hw = 0
        self.last_on = {}
        self.dma_hist = {}

    def alloc(self, shape, dtype=F32):
        esz = 4 if dtype == F32 else 2
        n = 1
        for s in shape[1:]:
            n *= s
        nbytes = (n * esz + 31) // 32 * 32
        assert self.off + nbytes <= self.arena_bytes, ("SBUF arena overflow", self.off, nbytes)
        a = self.arena[:, self.off // 4:(self.off + nbytes) // 4]
        if dtype != F32:
            a = a.bitcast(dtype)
        a = a[0:shape[0], 0:n]
        if len(shape) == 3:
            a = a.rearrange("p (a b) -> p a b", a=shape[1])
        elif len(shape) == 4:
            a = a.rearrange("p (a b c) -> p a b c", a=shape[1], b=shape[2])
        self.off += nbytes
        self.hw = max(self.hw, self.off)
        return a

    def mark(self):
        return self.off

    def release(self, m):
        self.barrier()
        self.off = m

    def _expand(self, keys):
        g = getattr(self, "groups", None)
        if not g:
            return keys
        out = []
        for k in keys:
            out.extend(g.get(k, (k,)))
        return out

    def op(self, eng, fn, reads=(), writes=(), dma=False, force=False):
        reads = self._expand(reads)
        writes = self._expand(writes)
        i = len(self.ops)
        deps = set()
        for r in reads:
            if r in self.lastw:
                deps.add(self.lastw[r])
        for w in writes:
            if w in self.lastw:
                deps.add(self.lastw[w])
            for rd in self.readers.get(w, ()):
                deps.add(rd)
        for r in reads:
            self.readers.setdefault(r, []).append(i)
        for w in writes:
            self.lastw[w] = i
            self.readers[w] = []
        self.ops.append(dict(eng=eng, fn=fn, deps=deps, dma=dma, force=(force or getattr(self, "force_all", False))))
        if dma:
            self.dma_hist.setdefault(eng, []).append(i)
        else:
            self.last_on[eng] = i
        return i

    def barrier(self):
        last = [v for v in self.last_on.values()]
        dm = []
        for q, h in self.dma_hist.items():
            dm += h[-N_DMA_SEMS:]
        engs = set(self.last_on.keys()) | set(self.dma_hist.keys()) | {"tensor", "vector", "scalar", "gpsimd", "sync"}
        for e in sorted(engs):
            i = len(self.ops)
            self.ops.append(dict(eng=e, fn=lambda en: en.nop(), deps=set(last) | set(dm), dma=False))
            self.last_on[e] = i
        self.lastw = {}
        self.readers = {}

    def mm(self, out, lhsT, rhs, start=True, stop=True, reads=(), writes=()):
        return self.op("tensor", lambda e: e.matmul(out, lhsT, rhs, start=start, stop=stop), reads, writes)

    def dma(self, q, out, in_, reads=(), writes=(), **kw):
        return self.op(q, lambda e: e.dma_start(out=out, in_=in_, **kw), reads, writes, dma=True)

    def V(self, fn, reads=(), writes=(), force=False):
        return self.op("vector", fn, reads, writes, force=force)

    def S(self, fn, reads=(), writes=()):
        return self.op("scalar", fn, reads, writes)

    def G(self, fn, reads=(), writes=()):
        return self.op("gpsimd", fn, reads, writes)

    def emit(self):
        nc = self.nc
        ops = self.ops
        need_sig = [False] * len(ops)
        for i, o in enumerate(ops):
            for d in o["deps"]:
                pd = ops[d]
                if pd["dma"]:
                    continue
                if pd["eng"] != o["eng"] or o["dma"] or o.get("force"):
                    need_sig[d] = True
        used = sorted({o["eng"] for o in ops})
        esem = {e: nc.alloc_semaphore("es_" + e) for e in used}
        dq = [e for e in used if any(o["dma"] and o["eng"] == e for o in ops)]
        dsems = {e: [nc.alloc_semaphore("ds_%s%d" % (e, k)) for k in range(N_DMA_SEMS)] for e in dq}
        cnt = {e: 0 for e in used}
        dcnt = {e: 0 for e in used}
        for i, o in enumerate(ops):
            if o["dma"]:
                k = dcnt[o["eng"]]
                dcnt[o["eng"]] += 1
                o["dsem"] = dsems[o["eng"]][k % N_DMA_SEMS]
                o["dval"] = 16 * (k // N_DMA_SEMS + 1)
            elif need_sig[i]:
                cnt[o["eng"]] += 1
                o["sig"] = cnt[o["eng"]]
        streams = {e: [] for e in used}
        for i, o in enumerate(ops):
            streams[o["eng"]].append(i)
        self.n_waits = 0

        def run_engine(ename, eng):
            waited = {}

            def wait(sem, val):
                key = id(sem)
                if waited.get(key, 0) >= val:
                    return
                waited[key] = val
                eng.wait_ge(sem, val)
                self.n_waits += 1

            for i in streams.get(ename, ()):
                o = ops[i]
                for d in sorted(o["deps"]):
                    pd = ops[d]
                    if pd["dma"]:
                        wait(pd["dsem"], pd["dval"])
                    elif "sig" in pd:
                        if pd["eng"] == ename and not o["dma"] and not o.get("force"):
                            continue
                        wait(esem[pd["eng"]], pd["sig"])
                if o["dma"]:
                    if o["dval"] > 16:
                        wait(o["dsem"], o["dval"] - 16)
                    o["fn"](eng).then_inc(o["dsem"], 16)
                else:
                    ins = o["fn"](eng)
                    if "sig" in o:
                        ins.then_inc(esem[ename], 1)
            if ename in dsems:
                done = {}
                for i in streams.get(ename, ()):
                    o = ops[i]
                    if o["dma"]:
                        done[id(o["dsem"])] = (o["dsem"], o["dval"])
                for sem, val in done.values():
                    wait(sem, val)

        with nc.Block() as block:
            if "sync" in streams:
                @block.sync
                def _(e):
                    run_engine("sync", e)
            if "tensor" in streams:
                @block.tensor
                def _(e):
                    run_engine("tensor", e)
            if "vector" in streams:
                @block.vector
                def _(e):
                    run_engine("vector", e)
            if "scalar" in streams:
                @block.scalar
                def _(e):
                    run_engine("scalar", e)
            if "gpsimd" in streams:
                @block.gpsimd
                def _(e):
                    run_engine("gpsimd", e)


def tile_segs(tt):
    if tt == 0:
        return [(0, 256, 1), (256, 384, 0)]
    return [(0, 384, 0)]


class Core:
    def __init__(self, nc, gathered):
        self.nc = nc
        self.gathered = gathered
        self.P = Prog(nc, 204 * 1024)
        P = self.P
        self.ps = [nc.alloc_psum_tensor("ps%d" % i, [128, 512], F32) for i in range(7)]
        self.psT = nc.alloc_psum_tensor("psT", [128, 1024], BF16)
        self.ps_rr = 0
        self.ext = {}
        P.force_all = True
        self.ones_bf = P.alloc([128, 128], BF16)
        self.ones1 = P.alloc([128, 128], BF16)
        self.ones64 = P.alloc([128, 128], BF16)
        self.ones128 = P.alloc([128, 128], BF16)
        P.G(lambda e: e.memset(self.ones_bf, 1.0 / D), [], ["c_ones"])
        P.G(lambda e: e.memset(self.ones1, 1.0), [], ["c_ones"])
        P.G(lambda e: e.memset(self.ones64, 1.0 / 64), [], ["c_ones"])
        P.G(lambda e: e.memset(self.ones128, 1.0 / 128), [], ["c_ones"])
        self.eps_t = P.alloc([128, 1])
        P.G(lambda e: e.memset(self.eps_t, EPS), [], ["eps_t"])
        self.one_t = P.alloc([128, 1])
        P.G(lambda e: e.memset(self.one_t, 1.0), [], ["eps_t"])
        self.ident = P.alloc([128, 128])
        self.ident_bf = P.alloc([128, 128], BF16)
        P.G(lambda e: e.memset(self.ident, 0.0), [], ["ident"])
        P.G(lambda e: e.affine_select(self.ident, self.ident, pattern=[[-1, 128]], compare_op=ALU.not_equal,
                                      fill=1.0, base=0, channel_multiplier=1), ["ident"], ["ident"])
        P.V(lambda e: e.tensor_copy(self.ident_bf, self.ident), ["ident"], ["ident_bf"])
        self.mask = [P.alloc([CH, CH]), P.alloc([CH, CH])]
        for d in range(2):
            m = self.mask[d]
            P.G(lambda e, m=m: e.memset(m, 1.0), [], [("mask", d)])
            pat, cm = ([[1, CH]], -1) if d == 0 else ([[-1, CH]], 1)
            P.G(lambda e, m=m, pat=pat, cm=cm: e.affine_select(m, m, pattern=pat, compare_op=ALU.is_ge, fill=0.0,
                                                              base=0, channel_multiplier=cm),
                [("mask", d)], [("mask", d)])
        P.force_all = False

    def din(self, name, shape, dtype=F32):
        if name not in self.ext:
            self.ext[name] = self.nc.dram_tensor(name, list(shape), dtype, kind="ExternalInput").ap()
        return self.ext[name]

    def dout(self, name, shape, dtype=F32):
        return self.nc.dram_tensor(name, list(shape), dtype, kind="ExternalOutput").ap()

    def dint(self, name, shape, dtype=F32, **kw):
        return self.nc.dram_tensor(name, list(shape), dtype, **kw).ap()

    def psum(self, lo=0, hi=6):
        i = lo + self.ps_rr % (hi - lo)
        self.ps_rr += 1
        return self.ps[i], ("ps", i)

    def gather_weights(self):
        P = self.P
        nc = self.nc
        self.G = []
        ccsem = nc.alloc_semaphore("ccsem")
        rg = [list(range(NCORE))]
        mk = P.mark()
        for gi, NS in enumerate(NSHG):
            R = NS // 2048
            wsh = self.din("wshard%d" % gi, [R, 2048])
            wbf = self.dint("wbf%d" % gi, [R, 2048], BF16)
            G = self.dint("wgath%d" % gi, [NCORE * R, 2048], BF16, addr_space="Shared")
            n = NS // 128
            cs = n // 32
            src = wsh.rearrange("a c -> (a c)").rearrange("(p n) -> p n", p=128)
            dst = wbf.rearrange("a c -> (a c)").rearrange("(p n) -> p n", p=128)
            bb = [P.alloc([128, cs], BF16) for _ in range(3)]
            for i in range(32):
                b, bk = bb[i % 3], ("castb", gi, i % 3)
                P.dma("gpsimd", b, src[:, i * cs:(i + 1) * cs], writes=[bk])
                P.dma("sync", dst[:, i * cs:(i + 1) * cs], b, reads=[bk], writes=[("wbf", gi)])

            def ccfn(e, wbf=wbf, G=G, gi=gi):
                e.collective_compute("AllGather", ALU.bypass, replica_groups=rg, ins=[wbf.opt()],
                                     outs=[G.opt()]).then_inc(ccsem)
                e.wait_ge(ccsem, gi + 1)
                return e.nop()
            P.op("gpsimd", ccfn, reads=[("wbf", gi)], writes=["wgath"])
            self.G.append(G.rearrange("(r a) c -> r (a c)", r=NCORE))
        P.release(mk)

    def _wmeta(self, name, idx):
        one = getattr(self, "test_one", False)
        if name == "w_ada":
            return 1, OFF_ADA + idx * SZ_ADA, 9 * D, [1 if one else 2, D, 9 * D]
        if name == "ffn_w_in":
            return 0, OFF_FIN + idx * SZ_FIN, 2 * DFF, [1 if one else 4, D, 2 * DFF]
        if name == "w_in":
            return 0, OFF_WIN + idx * SZ_WIN, INW, [2, D, INW]
        if name == "w_out":
            return 1, OFF_WOUT + idx * SZ_WOUT, D, [2, D, D]
        raise KeyError(name)

    def load_w(self, dst, name, idx, c0, ncols, key):
        P = self.P
        grp, off, cols, shp = self._wmeta(name, idx)
        if self.gathered:
            if not hasattr(P, "groups"):
                P.groups = {}
            P.groups[key] = [(key, "r", r) for r in range(NCORE)]
            for r in range(NCORE):
                src = self.G[grp][r:r + 1, off:off + 256 * cols].rearrange("o (kk p c) -> p (o kk) c", p=128, c=cols)[
                    :, :, c0:c0 + ncols]
                P.dma("gpsimd", dst[:, 2 * r:2 * r + 2, :], src, reads=["wgath"], writes=[(key, "r", r)])
        else:
            w = self.din(name, shp)
            src = w[idx].rearrange("(k p) c -> p k c", p=128)[:, :, c0:c0 + ncols]
            P.dma("gpsimd", dst, src, writes=[key])

    def load_w_fout(self, dst, idx, j0, j1, m, key):
        P = self.P
        if self.gathered:
            off = OFF_FOUT + idx * SZ_FOUT
            r = m // 2
            src = self.G[1][r:r + 1, off:off + SZ_FOUT].rearrange("o (j p c) -> p (o j) c", p=128, c=256)[
                :, j0:j1, (m % 2) * 128:(m % 2) * 128 + 128]
            P.dma("gpsimd", dst, src, reads=["wgath"], writes=[key])
        else:
            w = self.din("ffn_w_out", [1 if getattr(self, "test_one", False) else 4, DFF, D])
            src = w[idx].rearrange("(j p) c -> p j c", p=128)[:, j0:j1, m * 128:(m + 1) * 128]
            P.dma("gpsimd", dst, src, writes=[key])

    def load_small(self):
        P = self.P
        self.gpre = P.alloc([128, 2 * 3 * KC])
        self.gpost = P.alloc([128, 2 * 3 * KC])
        self.bada = P.alloc([128, 2 * 144])
        self.cs_f = P.alloc([128, 2 * KC])
        P.dma("sync", self.gpre, self.din("gpre", [128, 2 * 3 * KC]), writes=["gpre"])
        P.dma("sync", self.gpost, self.din("gpost", [128, 2 * 3 * KC]), writes=["gpost"])
        P.dma("sync", self.bada, self.din("bada", [128, 2 * 144]), writes=["bada"])
        P.dma("sync", self.cs_f, self.din("cvec", [128, 2 * KC]), writes=["cs_f"])
        self.csT = P.alloc([128, KC, 2], BF16)
        P.S(lambda e: e.activation(self.csT.rearrange("p k v -> p v k"),
                                   self.cs_f.rearrange("p (v k) -> p v k", v=2), AF.Silu),
            ["cs_f"], ["csT"])
        self.mod = [None, None]
        self.Apre = {}
        self.Cg = {}

    def emit_mods(self, l):
        P = self.P
        mod = P.alloc([128, 144, 2])
        self.mod[l] = mod
        for s in range(3):
            self.Apre[(l, s)] = P.alloc([128, KC, 2])
            self.Cg[(l, s)] = P.alloc([128, KC, 2])
        mk = P.mark()
        wb = [P.alloc([128, KC, 256], BF16) for _ in range(3)]
        pst, psk = self.ps[6], ("ps", 6)
        for g in range(72):
            b, bk = wb[g % 3], ("wada", g % 3)
            self.load_w(b, "w_ada", l, g * 256, 256, bk)
            for jj in range(2):
                cc = g * 2 + jj
                for k in range(KC):
                    P.mm(pst[:, cc * 2:cc * 2 + 2], b[:, k, jj * 128:(jj + 1) * 128], self.csT[:, k, :],
                         start=(k == 0), stop=(k == KC - 1), reads=[bk, "csT"], writes=[psk])
        P.force_all = True
        P.V(lambda e: e.tensor_tensor(mod, pst[:, 0:288].rearrange("p (c v) -> p c v", v=2),
                                      self.bada[:, l * 144:(l + 1) * 144].unsqueeze(2).to_broadcast([128, 144, 2]),
                                      ALU.add),
            [psk, "bada"], [("mod", l)])
        for s in range(3):
            A = self.Apre[(l, s)]
            C = self.Cg[(l, s)]
            sc = mod[:, (3 * s + 1) * KC:(3 * s + 2) * KC, :]
            gt = mod[:, (3 * s + 2) * KC:(3 * s + 3) * KC, :]
            o0 = (l * 3 + s) * KC
            gp = self.gpre[:, o0:o0 + KC].unsqueeze(2).to_broadcast([128, KC, 2])
            gq = self.gpost[:, o0:o0 + KC].unsqueeze(2).to_broadcast([128, KC, 2])
            P.V(lambda e, A=A, sc=sc, gp=gp: e.scalar_tensor_tensor(A, sc, 1.0, gp, ALU.add, ALU.mult),
                [("mod", l), "gpre"], [("A", l, s)])
            rs = 1.0 if s == 1 else 0.5
            P.V(lambda e, C=C, gt=gt, gq=gq, rs=rs: e.scalar_tensor_tensor(C, gt, rs, gq, ALU.mult, ALU.mult),
                [("mod", l), "gpost"], [("C", l, s)])
        P.force_all = False
        P.release(mk)

    def shift(self, l, s, k, v):
        return self.mod[l][:, 3 * s * KC + k, v:v + 1]

    def alloc_tile_bufs(self):
        P = self.P
        self.h = P.alloc([128, KC, TT])
        self.u = P.alloc([128, KC, TT], BF16)
        self.hT = P.alloc([128, NJ, TT], BF16)
        self.y = P.alloc([128, KC, TT])
        self.wb = [P.alloc([128, 4096], BF16) for _ in range(4)]
        self.wbi = 0
        self.sqb = [P.alloc([128, TT], BF16) for _ in range(2)]
        self.tmpf = [P.alloc([128, TT]) for _ in range(2)]
        self.rstd = P.alloc([128, TT])
        self.sg = [P.alloc([128, TT]) for _ in range(2)]

    def emit_norm_mod(self, l, s, src, src_key, dst, dst_key, segs):
        P = self.P
        pss, pssk = self.ps[6], ("ps", 6)
        rstd = self.rstd
        for k in range(KC):
            b, bk = self.sqb[k % 2], ("sqb", k % 2)
            P.S(lambda e, b=b, k=k: e.activation(b, src[:, k, :], AF.Square), [src_key], [bk])
            P.mm(pss[:, 0:TT], self.ones_bf, b, start=(k == 0), stop=(k == KC - 1),
                 reads=[bk, "c_ones"], writes=[pssk])
        P.S(lambda e: e.activation(rstd, pss[:, 0:TT], AF.Sqrt, bias=self.eps_t), [pssk, "eps_t"], ["rstd"])
        P.V(lambda e: e.reciprocal(rstd, rstd), ["rstd"], ["rstd"])
        A = self.Apre[(l, s)]
        for k in range(KC):
            t, tk = self.tmpf[k % 2], ("tmpf", k % 2)
            for (a, bnd, v) in segs:
                P.V(lambda e, t=t, k=k, a=a, bnd=bnd, v=v: e.scalar_tensor_tensor(
                    t[:, a:bnd], src[:, k, a:bnd], A[:, k, v:v + 1], rstd[:, a:bnd], ALU.mult, ALU.mult),
                    [src_key, "rstd", ("A", l, s)], [tk])
            for (a, bnd, v) in segs:
                P.S(lambda e, t=t, k=k, a=a, bnd=bnd, v=v: e.activation(
                    dst[:, k, a:bnd], t[:, a:bnd], AF.Identity, bias=self.shift(l, s, k, v)),
                    [tk, ("mod", l)], [(dst_key, k)])

    def emit_post_resid(self, l, s, segs):
        P = self.P
        pss, pssk = self.ps[6], ("ps", 6)
        rstd, h, y = self.rstd, self.h, self.y
        P.S(lambda e: e.activation(rstd, pss[:, 0:TT], AF.Sqrt, bias=self.eps_t), [pssk, "eps_t"], ["rstd"])
        P.V(lambda e: e.reciprocal(rstd, rstd), ["rstd"], ["rstd"])
        C = self.Cg[(l, s)]
        for k in range(KC):
            t, tk = self.tmpf[k % 2], ("tmpf", k % 2)
            for (a, bnd, v) in segs:
                P.V(lambda e, t=t, k=k, a=a, bnd=bnd, v=v: e.scalar_tensor_tensor(
                    t[:, a:bnd], y[:, k, a:bnd], C[:, k, v:v + 1], rstd[:, a:bnd], ALU.mult, ALU.mult),
                    [("y", k), "rstd", ("C", l, s)], [tk])
            P.V(lambda e, t=t, k=k: e.tensor_tensor(h[:, k, :], h[:, k, :], t, ALU.add), [tk, "h"], ["h"])

    def evac_y(self, m, py, pyk):
        P = self.P
        pss, pssk = self.ps[6], ("ps", 6)
        P.V(lambda e: e.tensor_copy(self.y[:, m, :], py[:, 0:TT]), [pyk], [("y", m)])
        import os
        if os.environ.get("KDBG2", "") == "ev1":
            return
        b2, b2k = self.sqb[m % 2], ("sqb", m % 2)
        P.S(lambda e: e.activation(b2, self.y[:, m, :], AF.Square), [("y", m)], [b2k])
        if os.environ.get("KDBG2", "") == "ev2":
            return
        P.mm(pss[:, 0:TT], self.ones_bf, b2, start=(m == 0), stop=(m == KC - 1),
             reads=[b2k, "c_ones"], writes=[pssk])

    def emit_ffn_tile(self, l, f, s, segs):
        P = self.P
        u, hT = self.u, self.hT
        wb = self.wb
        self.emit_norm_mod(l, s, self.h, "h", u, "u", segs)
        idx = l * 2 + f
        if not hasattr(self, "wc"):
            self.wc, self.wc_done = {}, set()
        if idx not in self.wc:
            self.wc[idx] = (self.dint("wci%d" % idx, [22, 2, 128, 4096], BF16),
                            self.dint("wco%d" % idx, [KC, 2, 128, 2816], BF16))
        wci, wco = self.wc[idx]
        cached = idx in self.wc_done
        import os
        dbg = int(os.environ.get("KDBG", "9"))
        if dbg < 2:
            return
        for g in range((NJ + 1) // 2):
            nj = 2 if 2 * g + 1 < NJ else 1
            bg, bu = wb[self.wbi % 4], wb[(self.wbi + 1) % 4]
            kg, ku = ("wb", self.wbi % 4), ("wb", (self.wbi + 1) % 4)
            self.wbi += 2
            bgv = bg[:, 0:KC * nj * 128].rearrange("p (k c) -> p k c", k=KC)
            buv = bu[:, 0:KC * nj * 128].rearrange("p (k c) -> p k c", k=KC)
            nel = KC * nj * 128
            if not cached:
                self.load_w(bgv, "ffn_w_in", idx, g * 256, nj * 128, kg)
                self.load_w(buv, "ffn_w_in", idx, DFF + g * 256, nj * 128, ku)
                P.dma("sync", wci[g, 0, :, 0:nel], bg[:, 0:nel], reads=[kg], writes=[("wc", idx, "i", g, 0)])
                P.dma("sync", wci[g, 1, :, 0:nel], bu[:, 0:nel], reads=[ku], writes=[("wc", idx, "i", g, 1)])
            else:
                P.dma("gpsimd", bg[:, 0:nel], wci[g, 0, :, 0:nel], reads=[("wc", idx, "i", g, 0)], writes=[kg])
                P.dma("gpsimd", bu[:, 0:nel], wci[g, 1, :, 0:nel], reads=[("wc", idx, "i", g, 1)], writes=[ku])
            for jj in range(nj):
                j = 2 * g + jj
                pg, pgk = self.psum()
                pu, puk = self.psum()
                for k in range(KC):
                    P.mm(pg[:, 0:TT], bgv[:, k, jj * 128:(jj + 1) * 128], u[:, k, :],
                         start=(k == 0), stop=(k == KC - 1), reads=[kg, ("u", k)], writes=[pgk])
                for k in range(KC):
                    P.mm(pu[:, 0:TT], buv[:, k, jj * 128:(jj + 1) * 128], u[:, k, :],
                         start=(k == 0), stop=(k == KC - 1), reads=[ku, ("u", k)], writes=[puk])
                sgb, sk = self.sg[j % 2], ("sg", j % 2)
                P.S(lambda e, sgb=sgb, pg=pg: e.activation(sgb, pg[:, 0:TT], AF.Silu), [pgk], [sk])
                P.V(lambda e, sgb=sgb, pu=pu, j=j: e.tensor_tensor(hT[:, j, :], sgb, pu[:, 0:TT], ALU.mult),
                    [sk, puk], [("hT", j)])
        if dbg < 3:
            return
        for m in range(KC):
            py, pyk = self.psum()
            for half in range(2):
                j0, j1 = (0, 22) if half == 0 else (22, NJ)
                b, bk = wb[self.wbi % 4], ("wb", self.wbi % 4)
                self.wbi += 1
                bv = b[:, 0:(j1 - j0) * 128].rearrange("p (j c) -> p j c", c=128)
                nel = (j1 - j0) * 128
                if not cached:
                    self.load_w_fout(bv, idx, j0, j1, m, bk)
                    P.dma("sync", wco[m, half, :, 0:nel], b[:, 0:nel], reads=[bk], writes=[("wc", idx, "o", m, half)])
                else:
                    P.dma("gpsimd", b[:, 0:nel], wco[m, half, :, 0:nel], reads=[("wc", idx, "o", m, half)], writes=[bk])
                if os.environ.get("KDBG2", "") == "dma":
                    continue
                for j in range(j0, j1):
                    P.mm(py[:, 0:TT], bv[:, j - j0, :], hT[:, j, :], start=(j == 0), stop=(j == NJ - 1),
                         reads=[bk, ("hT", j)], writes=[pyk])
            if os.environ.get("KDBG2", "") in ("dma", "mm"):
                continue
            self.evac_y(m, py, pyk)
        self.wc_done.add(idx)
        self.emit_post_resid(l, s, segs)

    def emit_wout_tile(self, l, tt, segs):
        P = self.P
        mixt = self.hT[:, 0:KC, :]
        t0 = tt * TT
        for k in range(KC):
            P.dma("sync", mixt[:, k, :], self.Mx[k * 128:(k + 1) * 128, t0:t0 + TT], reads=["Mx"], writes=[("hT", k)])
        for mp in range(KC // 2):
            b, bk = self.wb[self.wbi % 4], ("wb", self.wbi % 4)
            self.wbi += 1
            bv = b[:, 0:KC * 256].rearrange("p (k c) -> p k c", k=KC)
            self.load_w(bv, "w_out", l, mp * 256, 256, bk)
            for jj in range(2):
                m = 2 * mp + jj
                py, pyk = self.psum()
                for k in range(KC):
                    P.mm(py[:, 0:TT], bv[:, k, jj * 128:(jj + 1) * 128], mixt[:, k, :],
                         start=(k == 0), stop=(k == KC - 1), reads=[bk, ("hT", k)], writes=[pyk])
                self.evac_y(m, py, pyk)
        self.emit_post_resid(l, 1, segs)

    def load_mixer_params(self):
        P = self.P
        P.force_all = True
        ld = lambda name, shape: (P.alloc(shape), self.din(name, shape))
        def L(name, shape):
            t, src = ld(name, shape)
            P.dma("sync", t, src, writes=[name])
            return t
        self.lbl = L("lbl", [64, 2 * 2 * 8])
        self.anorm = L("anorm", [64, 2])
        self.bnorm = L("bnorm", [64, 2])
        self.cnorm = L("cnorm", [128, 2])
        self.clam = L("clam", [64, 2 * 4])
        self.convw = L("convw", [128, 2 * 4 * 4])
        self.convb = L("convb", [128, 2 * 4])
        self.dbr = L("dbr", [128, 2 * 2 * 4])
        self.dbi = L("dbi", [128, 2 * 2 * 4])
        self.dlam = L("dlam", [128, 2 * 2 * 4])
        self.cos_d = self.din("rope_cos", [64, T])
        self.sin_d = self.din("rope_sin", [64, T])
        self.lb1 = P.alloc([64, 16])
        self.omlb1 = P.alloc([64, 16])
        P.V(lambda e: e.tensor_tensor(self.lb1, self.lbl[:, 16:32], self.lbl[:, 0:16], ALU.subtract), ["lbl"], ["lb1"])
        P.S(lambda e: e.activation(self.lb1, self.lb1, AF.Sigmoid), ["lb1"], ["lb1"])
        P.V(lambda e: e.tensor_scalar(self.omlb1, self.lb1, -1.0, 1.0, ALU.mult, ALU.add), ["lb1"], ["omlb1"])
        self.zero_t = P.alloc([128, 1])
        P.G(lambda e: e.memset(self.zero_t, 0.0), [], ["eps_t"])
        self.cm = P.alloc([64, T])
        P.G(lambda e: e.memset(self.cm, 1.0), [], ["cm"])
        P.G(lambda e: e.memset(self.cm.rearrange("p (c i) -> p c i", i=CH)[:, :, 0:1], 0.0), ["cm"], ["cm"])
        self.neglam = P.alloc([128, 2])
        self.cgain = P.alloc([128, 2])
        prod = P.alloc([64, 2, 2])
        cl = self.clam.rearrange("p (l f) -> p l f", l=2)
        P.V(lambda e: e.tensor_tensor(prod[:, :, 0], cl[:, :, 0], cl[:, :, 1], ALU.mult), ["clam"], ["prod"])
        P.V(lambda e: e.tensor_tensor(prod[:, :, 1], cl[:, :, 2], cl[:, :, 3], ALU.mult), ["clam", "prod"], ["prod"])
        prodb = P.alloc([64, 4], BF16)
        P.V(lambda e: e.tensor_copy(prodb, prod.rearrange("p l f -> p (l f)")), ["prod"], ["prodb"])
        pl, plk = self.ps[6], ("ps", 6)
        P.mm(pl[:, 0:4], self.ones1[0:64, :], prodb, reads=["c_ones", "prodb"], writes=[plk])
        ex = P.alloc([128, 4])
        P.S(lambda e: e.activation(ex, pl[:, 0:4], AF.Exp), [plk], ["ex"])
        for l in range(2):
            li = 0.8 - 0.6 * math.exp(-0.3 * l)
            P.V(lambda e, l=l: e.tensor_tensor(self.neglam[:, l:l + 1], ex[:, 2 * l + 1:2 * l + 2], ex[:, 2 * l:2 * l + 1],
                                               ALU.subtract), ["ex"], ["neglam"])
            P.V(lambda e, l=l, li=li: e.tensor_scalar(self.neglam[:, l:l + 1], self.neglam[:, l:l + 1], -li, None, ALU.add),
                ["neglam"], ["neglam"])
            P.V(lambda e, l=l, li=li: e.tensor_scalar(self.cgain[:, l:l + 1], self.cnorm[:, l:l + 1], 1.0 - li, None, ALU.mult),
                ["cnorm"], ["cgain"])
        self.c1 = P.alloc([128, 16])
        P.S(lambda e: e.activation(self.c1, self.dlam, AF.Exp, scale=-1.0), ["dlam"], ["c1"])
        P.S(lambda e: e.activation(self.c1, self.c1, AF.Ln, bias=self.one_t), ["c1", "eps_t"], ["c1"])
        P.V(lambda e: e.tensor_scalar(self.c1, self.c1, -8.0, None, ALU.mult), ["c1"], ["c1"])
        P.force_all = False

    def alloc_scratch(self):
        self.H = self.dint("Hres", [128, KC, T])
        self.U = self.dint("Umix", [128, KC, T], BF16)
        self.Z = self.dint("Zfm", [INW, T])
        self.Zt = self.dint("Ztm", [T, 1536])
        self.Mx = self.dint("Mx", [2048, T], BF16)

    def emit_proj(self, l):
        P = self.P
        mk = P.mark()
        uf = P.alloc([128, KC, T], BF16)
        P.dma("sync", uf, self.U, reads=["U"], writes=["uf"])
        wb = [P.alloc([128, KC, 256], BF16) for _ in range(3)]
        ev = [P.alloc([128, 512]) for _ in range(4)]
        tbl = [(0, 512), (512, 1024), (1024, 1536), (1536, 2048), (2048, 2304)]
        n = 0
        for g in range(28):
            b, bk = wb[g % 3], ("pw", g % 3)
            self.load_w(b, "w_in", l, g * 256, 256, bk)
            for jj in range(2):
                r0 = g * 256 + jj * 128
                for (a, e_) in tbl:
                    nn = e_ - a
                    ps, psk = self.psum()
                    for k in range(KC):
                        P.mm(ps[:, 0:nn], b[:, k, jj * 128:(jj + 1) * 128], uf[:, k, a:e_],
                             start=(k == 0), stop=(k == KC - 1), reads=[bk, "uf"], writes=[psk])
                    evb, evk = ev[n % 4], ("ev", n % 4)
                    if n % 2 == 0:
                        P.V(lambda e, evb=evb, ps=ps, nn=nn: e.tensor_copy(evb[:, 0:nn], ps[:, 0:nn]), [psk], [evk])
                    else:
                        P.S(lambda e, evb=evb, ps=ps, nn=nn: e.copy(evb[:, 0:nn], ps[:, 0:nn]), [psk], [evk])
                    P.dma("sync", self.Z[r0:r0 + 128, a:e_], evb[:, 0:nn], reads=[evk], writes=[("Z", r0 // 128)])
                    n += 1
        for bi, blk in enumerate([3, 7, 11]):
            for half in range(2):
                b, bk = wb[n % 3], ("pw", n % 3)
                self.load_w(b, "w_in", l, blk * 512 + half * 256, 256, bk)
                for tb in range(T // 128):
                    ps, psk = self.psum()
                    for k in range(KC):
                        P.mm(ps[:, 0:256], uf[:, k, tb * 128:(tb + 1) * 128], b[:, k, :],
                             start=(k == 0), stop=(k == KC - 1), reads=[bk, "uf"], writes=[psk])
                    evb, evk = ev[n % 4], ("ev", n % 4)
                    if n % 2 == 0:
                        P.V(lambda e, evb=evb, ps=ps: e.tensor_copy(evb[:, 0:256], ps[:, 0:256]), [psk], [evk])
                    else:
                        P.S(lambda e, evb=evb, ps=ps: e.copy(evb[:, 0:256], ps[:, 0:256]), [psk], [evk])
                    c0 = bi * 512 + half * 256
                    P.dma("sync", self.Zt[tb * 128:(tb + 1) * 128, c0:c0 + 256], evb[:, 0:256], reads=[evk],
                          writes=[("Zt", bi, half, tb)])
                    n += 1
        P.release(mk)

    def emit_gla(self, l, kind, heads):
        P = self.P
        mk = P.mark()
        CH = 128 if kind == "B" else 32
        NCH = T // CH
        NCTX = CTX // CH
        Fa = lambda: P.alloc([64, T])
        Ba = lambda: P.alloc([64, T], BF16)
        q, kk, lf, bf, b, tmp = [Fa() for _ in range(6)]
        E = tmp
        kh = [Ba(), Ba()]
        Z, Zt = self.Z, self.Zt
        three = lambda a: a.rearrange("p (c i) -> p c i", i=CH)
        if CH == 32:
            cm, masks = self.cm, self.mask
        else:
            cm = Fa()
            P.op("gpsimd", lambda e: e.memset(cm, 1.0), [], ["cmL"], force=True)
            P.op("gpsimd", lambda e: e.memset(cm.rearrange("p (c i) -> p c i", i=CH)[:, :, 0:1], 0.0), ["cmL"], ["cmL"],
                 force=True)
            masks = [P.alloc([CH, CH]), P.alloc([CH, CH])]
            for d in range(2):
                m = masks[d]
                pat, cmul = ([[1, CH]], -1) if d == 0 else ([[-1, CH]], 1)
                P.op("gpsimd", lambda e, m=m: e.memset(m, 1.0), [], [("maskL", d)], force=True)
                P.op("gpsimd", lambda e, m=m, pat=pat, cmul=cmul: e.affine_select(
                    m, m, pattern=pat, compare_op=ALU.is_ge, fill=0.0, base=0, channel_multiplier=cmul),
                    [("maskL", d)], [("maskL", d)], force=True)
        LS = []
        for hi, h in enumerate(heads):
            ls = dict(qt=[Ba(), Ba()], kt=[Ba(), Ba()],
                      khT=[P.alloc([CH, NCH, 64], BF16) for _ in range(2)],
                      vtm=P.alloc([CH, NCH, 64], BF16), oacc=Fa(),
                      gdec=[P.alloc([64, NCH]) for _ in range(2)],
                      S32=[P.alloc([64, 64]) for _ in range(2)],
                      Sbf=[P.alloc([64, 64], BF16) for _ in range(2)],
                      sTb=[[P.alloc([CH, CH], BF16) for _ in range(2)] for _ in range(2)])
            LS.append(ls)
        for hi, h in enumerate(heads):
            ls = LS[hi]
            qt, kt, khT, vtm, oacc, gdec = ls["qt"], ls["kt"], ls["khT"], ls["vtm"], ls["oacc"], ls["gdec"]
            rows = lambda blk, h=h: Z[blk * 512 + 64 * h:blk * 512 + 64 * h + 64, :]

            def rows_sw(dst, blk, key, h=h):
                r0 = blk * 512 + 64 * h
                P.dma("sync", dst[0:32, :], Z[r0 + 32:r0 + 64, :], writes=[key])
                P.dma("sync", dst[32:64, :], Z[r0:r0 + 32, :], writes=[key])

            if kind == "A":
                P.dma("sync", q, rows(0), writes=["q"])
                P.dma("gpsimd", vtm, Zt[:, 64 * h:64 * h + 64].rearrange("(c p) d -> p c d", p=CH),
                      writes=[("vtm", hi)])
                P.S(lambda e: e.activation(q, q, AF.Silu), ["q"], ["q"])
            else:
                cos, sin = bf, b
                P.dma("sync", cos, self.cos_d, writes=["bf"])
                P.dma("sync", sin, self.sin_d, writes=["b"])
                P.dma("gpsimd", vtm, Zt[:, 512 + 64 * h:512 + 64 * h + 64].rearrange("(c p) d -> p c d", p=CH),
                      writes=[("vtm", hi)])
                for (dst, blk, sc) in ((q, 5, 1.0), (kk, 6, 0.125)):
                    dk = "q" if dst is q else "kk"
                    P.dma("sync", dst, rows(blk), writes=[dk])
                    rows_sw(tmp, blk, "tmp")
                    P.V(lambda e, dst=dst: e.tensor_tensor(dst, dst, cos, ALU.mult), [dk, "bf"], [dk])
                    P.V(lambda e: e.tensor_tensor(tmp, tmp, sin, ALU.mult), ["tmp", "b"], ["tmp"])
                    P.V(lambda e, dst=dst: e.tensor_tensor(dst, dst, tmp, ALU.add), [dk, "tmp"], [dk])
                    if sc != 1.0:
                        P.V(lambda e, dst=dst, sc=sc: e.tensor_scalar(dst, dst, sc, None, ALU.mult), [dk], [dk])
                lg = math.log1p(-2.0 ** (-5.0 - h))
            P.G(lambda e, oacc=oacc: e.memset(oacc, 0.0), [], [("oacc", hi)])
            for d in range(2):
                if kind == "A":
                    P.dma("sync", tmp, rows(1 + d), writes=["tmp"])
                    P.S(lambda e: e.activation(tmp, tmp, AF.Sigmoid), ["tmp"], ["tmp"])
                    if l == 0:
                        lbs, oms = self.zero_t[0:64, :], self.one_t[0:64, :]
                    else:
                        lbs, oms = self.lb1[:, d * 8 + h:d * 8 + h + 1], self.omlb1[:, d * 8 + h:d * 8 + h + 1]
                    P.V(lambda e, lbs=lbs, oms=oms: e.tensor_scalar(tmp, tmp, oms, lbs, ALU.mult, ALU.add),
                        ["tmp", "lb1", "omlb1", "eps_t"], ["tmp"])
                    P.V(lambda e: e.tensor_scalar(kk, tmp, -1.0, 1.0, ALU.mult, ALU.add), ["tmp"], ["kk"])
                    P.S(lambda e: e.activation(lf, tmp, AF.Ln), ["tmp"], ["lf"])
                elif d == 0:
                    P.G(lambda e, lg=lg: e.memset(lf, lg), [], ["lf"])
                P.V(lambda e: e.tensor_tensor_scan(bf, cm, lf, 0.0, ALU.mult, ALU.add), ["cm", "cmL", "lf"], ["bf"])
                btot = three(bf)[:, :, CH - 1:CH]
                btb = btot.to_broadcast([64, NCH, CH])
                if d == 0:
                    P.V(lambda e: e.tensor_copy(b, bf), ["bf"], ["b"])
                else:
                    P.V(lambda e, btb=btb: e.tensor_tensor(three(b), btb, three(bf), ALU.subtract), ["bf"], ["b"])
                    P.V(lambda e: e.tensor_tensor(b, b, lf, ALU.add), ["b", "lf"], ["b"])
                P.S(lambda e: e.activation(E, b, AF.Exp), ["b"], ["tmp"])
                P.V(lambda e, d=d, qt=qt: e.tensor_tensor(qt[d], q, E, ALU.mult), ["q", "tmp"], [("qt", hi, d)])
                P.S(lambda e: e.activation(E, b, AF.Exp, scale=-1.0), ["b"], ["tmp"])
                P.V(lambda e, d=d, kt=kt: e.tensor_tensor(kt[d], kk, E, ALU.mult), ["kk", "tmp"], [("kt", hi, d)])
                P.V(lambda e, btb=btb: e.tensor_tensor(three(E), btb, three(b), ALU.subtract), ["bf", "b"], ["tmp"])
                P.S(lambda e: e.activation(E, E, AF.Exp), ["tmp"], ["tmp"])
                P.V(lambda e, d=d: e.tensor_tensor(kh[d], kk, E, ALU.mult), ["kk", "tmp"], [("kh", d)])
                P.S(lambda e, d=d, gdec=gdec, btot=btot: e.activation(gdec[d], btot.rearrange("p c o -> p (c o)"), AF.Exp),
                    ["bf"], [("gdec", hi, d)])
                for c0 in range(0, NCH, 8):
                    ptb, ptk = self.psT, "psT"
                    nb = min(8, NCH - c0)
                    for c in range(c0, c0 + nb):
                        P.op("tensor", lambda e, d=d, c=c, c0=c0: e.transpose(
                            ptb[0:CH, (c - c0) * 64:(c - c0) * 64 + 64], kh[d][:, c * CH:(c + 1) * CH],
                            self.ident_bf[0:64, 0:64]), [("kh", d), "ident_bf"], [ptk])
                    P.V(lambda e, d=d, c0=c0, khT=khT, nb=nb: e.tensor_copy(
                        khT[d][:, c0:c0 + nb, :], ptb[0:CH, 0:nb * 64].rearrange("p (c k) -> p c k", k=64)),
                        [ptk], [("khT", hi, d)])
        order = [list(range(NCH)), list(range(NCTX - 1, -1, -1)) + list(range(NCH - 1, NCTX - 1, -1))]
        slot_w = 4 * CH
        nslots = 512 // slot_w
        nstep = 0
        for i in range(NCH):
            for hi in range(len(heads)):
                ls = LS[hi]
                qt, kt, khT, vtm, oacc, gdec = ls["qt"], ls["kt"], ls["khT"], ls["vtm"], ls["oacc"], ls["gdec"]
                S32, Sbf, sTb = ls["S32"], ls["Sbf"], ls["sTb"]
                for d in range(2):
                    c = order[d][i]
                    ts = slice(c * CH, (c + 1) * CH)
                    first = (i == 0)
                    bank, slot = nstep % 6, (nstep // 6) % nslots
                    nstep += 1
                    base = slot * slot_w
                    pbank = self.ps[bank]
                    pss_, psk = pbank[:, base:base + CH], ("psr", bank, slot, 0)
                    po, pok = pbank[:, base + CH:base + 2 * CH], ("psr", bank, slot, 1)
                    pu, puk = pbank[:, base + 2 * CH:base + 2 * CH + 64], ("psr", bank, slot, 2)
                    P.mm(pss_[0:CH, 0:CH], kt[d][:, ts], qt[d][:, ts], reads=[("kt", hi, d), ("qt", hi, d)], writes=[psk])
                    sT, sTk = sTb[d][i % 2], ("sT", hi, d, i % 2)
                    P.V(lambda e, sT=sT, pss_=pss_, d=d: e.tensor_tensor(sT, pss_[0:CH, 0:CH], masks[d], ALU.mult),
                        [psk, ("mask", d), ("maskL", d)], [sTk])
                    if not first:
                        P.mm(po[0:64, 0:CH], Sbf[d], qt[d][:, ts], start=True, stop=False,
                             reads=[("Sbf", hi, d), ("qt", hi, d)], writes=[pok])
                    P.mm(po[0:64, 0:CH], vtm[:, c, :], sT, start=first, stop=True, reads=[("vtm", hi), sTk], writes=[pok])
                    P.V(lambda e, po=po, ts=ts, oacc=oacc: e.tensor_tensor(oacc[:, ts], oacc[:, ts], po[0:64, 0:CH], ALU.add),
                        [pok, ("oacc", hi)], [("oacc", hi)])
                    P.mm(pu[0:64, 0:64], khT[d][:, c, :], vtm[:, c, :], reads=[("khT", hi, d), ("vtm", hi)], writes=[puk])
                    if first:
                        P.V(lambda e, d=d, pu=pu, S32=S32: e.tensor_copy(S32[d], pu[0:64, 0:64]), [puk], [("S32", hi, d)])
                    else:
                        P.V(lambda e, d=d, pu=pu, c=c, S32=S32, gdec=gdec: e.scalar_tensor_tensor(
                            S32[d], S32[d], gdec[d][:, c:c + 1], pu[0:64, 0:64], ALU.mult, ALU.add),
                            [puk, ("S32", hi, d), ("gdec", hi, d)], [("S32", hi, d)])
                    P.S(lambda e, d=d, S32=S32, Sbf=Sbf: e.copy(Sbf[d], S32[d]), [("S32", hi, d)], [("Sbf", hi, d)])
        sqb = [P.alloc([64, 512], BF16) for _ in range(2)]
        ob = [P.alloc([64, 512], BF16) for _ in range(2)]
        rs = [P.alloc([64, 512]) for _ in range(2)]
        zg = q
        for hi, h in enumerate(heads):
            oacc = LS[hi]["oacc"]
            if kind == "A":
                mrow, gain, gblk = 64 * h, self.anorm[:, l:l + 1], 4
            else:
                mrow, gain, gblk = 512 + 64 * h, self.bnorm[:, l:l + 1], 8
            P.dma("sync", zg, Z[gblk * 512 + 64 * h:gblk * 512 + 64 * h + 64, :], writes=["q"])
            P.S(lambda e: e.activation(zg, zg, AF.Silu), ["q"], ["q"])
            for bi_, (a, e_) in enumerate([(0, 512), (512, 1024), (1024, 1536), (1536, 2048), (2048, 2304)]):
                nn = e_ - a
                sq, sqk = sqb[bi_ % 2], ("gsq", bi_ % 2)
                P.S(lambda e, sq=sq, a=a, e_=e_, nn=nn, oacc=oacc: e.activation(sq[:, 0:nn], oacc[:, a:e_], AF.Square),
                    [("oacc", hi)], [sqk])
                pn, pnk = self.ps[6], ("ps", 6)
                P.mm(pn[0:64, 0:nn], self.ones64[0:64, 0:64], sq[:, 0:nn], reads=[sqk, "c_ones"], writes=[pnk])
                r, rk = rs[bi_ % 2], ("grs", bi_ % 2)
                P.S(lambda e, r=r, pn=pn, nn=nn: e.activation(r[:, 0:nn], pn[0:64, 0:nn], AF.Sqrt, bias=self.eps_t[0:64, :]),
                    [pnk, "eps_t"], [rk])
                P.V(lambda e, r=r, nn=nn: e.reciprocal(r[:, 0:nn], r[:, 0:nn]), [rk], [rk])
                P.V(lambda e, r=r, a=a, e_=e_, nn=nn, oacc=oacc, gain=gain: e.scalar_tensor_tensor(
                    r[:, 0:nn], oacc[:, a:e_], gain, r[:, 0:nn], ALU.mult, ALU.mult),
                    [rk, ("oacc", hi), "anorm", "bnorm"], [rk])
                o_, ok_ = ob[bi_ % 2], ("gob", bi_ % 2)
                P.V(lambda e, o_=o_, r=r, a=a, e_=e_, nn=nn: e.tensor_tensor(o_[:, 0:nn], r[:, 0:nn], zg[:, a:e_], ALU.mult),
                    [rk, "q"], [ok_])
                P.dma("sync", self.Mx[mrow:mrow + 64, a:e_], o_[:, 0:nn], reads=[ok_], writes=[("Mx", mrow)])
        P.release(mk)

    def emit_diffattn(self, l, h, ctx_out):
        P = self.P
        mk = P.mark()
        Z, Zt = self.Z, self.Zt
        t1 = P.alloc([128, T])
        t2 = P.alloc([128, T])
        cos = P.alloc([128, T])
        sin = P.alloc([128, T])
        qb = P.alloc([128, T], BF16)
        kb_ = P.alloc([128, T], BF16)
        vtm = P.alloc([128, T // 128, 128], BF16)
        for hf in range(2):
            P.dma("sync", cos[64 * hf:64 * hf + 64, :], self.cos_d, writes=["cos"])
            P.dma("sync", sin[64 * hf:64 * hf + 64, :], self.sin_d, writes=["sin"])
        P.dma("gpsimd", vtm, Zt[:, 1024 + 128 * h:1024 + 128 * h + 128].rearrange("(kb p) d -> p kb d", p=128),
              writes=["vtm"])
        for (dst, blk) in ((qb, 9), (kb_, 10)):
            r0 = blk * 512 + 128 * h
            P.dma("sync", t1, Z[r0:r0 + 128, :], writes=["t1"])
            for hf in range(2):
                P.dma("sync", t2[64 * hf:64 * hf + 32, :], Z[r0 + 64 * hf + 32:r0 + 64 * hf + 64, :], writes=["t2"])
                P.dma("sync", t2[64 * hf + 32:64 * hf + 64, :], Z[r0 + 64 * hf:r0 + 64 * hf + 32, :], writes=["t2"])
            P.V(lambda e: e.tensor_tensor(t1, t1, cos, ALU.mult), ["t1", "cos"], ["t1"])
            P.V(lambda e: e.tensor_tensor(t2, t2, sin, ALU.mult), ["t2", "sin"], ["t2"])
            dk = "qb" if dst is qb else "kb"
            P.V(lambda e, dst=dst: e.tensor_tensor(dst, t1, t2, ALU.add), ["t1", "t2"], [dk])
        pex = [P.alloc([128, 512], BF16) for _ in range(3)]
        rr = [P.alloc([128, 512]) for _ in range(2)]
        oo = [P.alloc([128, 512]) for _ in range(2)]
        sqo = P.alloc([128, 512], BF16)
        outb = [P.alloc([128, 512], BF16) for _ in range(2)]
        neglam = self.neglam[:, l:l + 1]
        cgain = self.cgain[:, l:l + 1]
        qblocks = [(CTX + 512 * i, 512, T // 128) for i in range(4)]
        if ctx_out:
            qblocks.append((0, CTX, CTX // 128))
        npx = 0
        for qi, (q0, nq, nkb) in enumerate(qblocks):
            accO = [(self.ps[2], ("ps", 2)), (self.ps[3], ("ps", 3))]
            accS = [(self.ps[4], ("ps", 4)), (self.ps[5], ("ps", 5))]
            for kbi in range(nkb):
                for hf in range(2):
                    pS, pSk = self.psum(0, 2)
                    P.mm(pS[:, 0:nq], kb_[64 * hf:64 * hf + 64, kbi * 128:(kbi + 1) * 128],
                         qb[64 * hf:64 * hf + 64, q0:q0 + nq], reads=["kb", "qb"], writes=[pSk])
                    px, pxk = pex[npx % 3], ("pex", npx % 3)
                    npx += 1
                    P.S(lambda e, px=px, pS=pS, nq=nq: e.activation(px[:, 0:nq], pS[:, 0:nq], AF.Exp, scale=0.125),
                        [pSk], [pxk])
                    P.mm(accO[hf][0][:, 0:nq], vtm[:, kbi, :], px[:, 0:nq], start=(kbi == 0), stop=(kbi == nkb - 1),
                         reads=["vtm", pxk], writes=[accO[hf][1]])
                    P.mm(accS[hf][0][:, 0:nq], self.ones1, px[:, 0:nq], start=(kbi == 0), stop=(kbi == nkb - 1),
                         reads=["c_ones", pxk], writes=[accS[hf][1]])
            for hf in range(2):
                r, rk = rr[hf], ("rr", hf)
                o, ok_ = oo[hf], ("oo", hf)
                P.V(lambda e, r=r, hf=hf, nq=nq, accS=accS: e.reciprocal(r[:, 0:nq], accS[hf][0][:, 0:nq]),
                    [accS[hf][1]], [rk])
                P.V(lambda e, r=r, o=o, hf=hf, nq=nq, accO=accO: e.tensor_tensor(o[:, 0:nq], accO[hf][0][:, 0:nq],
                                                                                r[:, 0:nq], ALU.mult),
                    [accO[hf][1], rk], [ok_])
            o = oo[0]
            P.V(lambda e, nq=nq: e.scalar_tensor_tensor(oo[0][:, 0:nq], oo[1][:, 0:nq], neglam, oo[0][:, 0:nq],
                                                        ALU.mult, ALU.add),
                [("oo", 0), ("oo", 1), "neglam"], [("oo", 0)])
            P.S(lambda e, nq=nq: e.activation(sqo[:, 0:nq], oo[0][:, 0:nq], AF.Square), [("oo", 0)], ["sqo"])
            pn, pnk = self.ps[6], ("ps", 6)
            P.mm(pn[:, 0:nq], self.ones128, sqo[:, 0:nq], reads=["sqo", "c_ones"], writes=[pnk])
            r = rr[0]
            P.S(lambda e, nq=nq: e.activation(rr[0][:, 0:nq], pn[:, 0:nq], AF.Sqrt, bias=self.eps_t), [pnk, "eps_t"],
                [("rr", 0)])
            P.V(lambda e, nq=nq: e.reciprocal(rr[0][:, 0:nq], rr[0][:, 0:nq]), [("rr", 0)], [("rr", 0)])
            ob, obk = outb[qi % 2], ("outb", qi % 2)
            P.V(lambda e, nq=nq, ob=ob: e.scalar_tensor_tensor(ob[:, 0:nq], oo[0][:, 0:nq], cgain, rr[0][:, 0:nq],
                                                               ALU.mult, ALU.mult),
                [("oo", 0), ("rr", 0), "cgain"], [obk])
            P.dma("sync", self.Mx[1024 + 128 * h:1024 + 128 * h + 128, q0:q0 + nq], ob[:, 0:nq], reads=[obk],
                  writes=[("Mx", 1024 + 128 * h)])
        P.release(mk)

    def emit_rglru(self, l, cc):
        P = self.P
        mk = P.mark()
        Z = self.Z
        Fa = lambda: P.alloc([128, T])
        x, gate, xc, ra, ia, aa, uu, hs, hsum = [Fa() for _ in range(9)]
        xcb = P.alloc([128, T], BF16)
        outb = P.alloc([128, T], BF16)
        P.dma("sync", x, Z[12 * 512 + 128 * cc:12 * 512 + 128 * cc + 128, :], writes=["x"])
        P.dma("sync", gate, Z[13 * 512 + 128 * cc:13 * 512 + 128 * cc + 128, :], writes=["gate"])
        cw = lambda j: self.convw[:, (l * 4 + cc) * 4 + j:(l * 4 + cc) * 4 + j + 1]
        cb = self.convb[:, l * 4 + cc:l * 4 + cc + 1]
        P.S(lambda e: e.activation(xc, x, AF.Identity, bias=cb, scale=cw(1)), ["x", "convw", "convb"], ["xc"])
        for (a, e_) in ((0, CTX), (CTX, T)):
            P.V(lambda e, a=a, e_=e_: e.scalar_tensor_tensor(xc[:, a + 1:e_], x[:, a:e_ - 1], cw(0), xc[:, a + 1:e_],
                                                            ALU.mult, ALU.add), ["x", "xc", "convw"], ["xc"])
            P.V(lambda e, a=a, e_=e_: e.scalar_tensor_tensor(xc[:, a:e_ - 1], x[:, a + 1:e_], cw(2), xc[:, a:e_ - 1],
                                                            ALU.mult, ALU.add), ["x", "xc", "convw"], ["xc"])
            P.V(lambda e, a=a, e_=e_: e.scalar_tensor_tensor(xc[:, a:e_ - 2], x[:, a + 2:e_], cw(3), xc[:, a:e_ - 2],
                                                            ALU.mult, ALU.add), ["x", "xc", "convw"], ["xc"])
        P.V(lambda e: e.tensor_copy(xcb, xc), ["xc"], ["xcb"])
        wbd = [[P.alloc([128, 128], BF16) for _ in range(2)] for _ in range(2)]
        dwr = self.din("d_w_r", [2, 2, 8, 64, 64])
        dwi = self.din("d_w_i", [2, 2, 8, 64, 64])
        for d in range(2):
            for ri, src in enumerate((dwr, dwi)):
                w = wbd[d][ri]
                wk = ("wbd", d, ri)
                P.G(lambda e, w=w: e.memset(w, 0.0), [], [wk])
                for g2 in range(2):
                    P.dma("gpsimd", w[64 * g2:64 * g2 + 64, 64 * g2:64 * g2 + 64], src[l, d, 2 * cc + g2], writes=[wk])
        tbl = [(0, 512), (512, 1024), (1024, 1536), (1536, 2048), (2048, 2304)]
        for d in range(2):
            pidx = (l * 2 + d) * 4 + cc
            br = self.dbr[:, pidx:pidx + 1]
            bi = self.dbi[:, pidx:pidx + 1]
            c1 = self.c1[:, pidx:pidx + 1]
            for (a, e_) in tbl:
                nn = e_ - a
                pr, prk = self.psum()
                P.mm(pr[:, 0:nn], wbd[d][0], xcb[:, a:e_], reads=[("wbd", d, 0), "xcb"], writes=[prk])
                P.S(lambda e, pr=pr, a=a, e_=e_, nn=nn, br=br: e.activation(ra[:, a:e_], pr[:, 0:nn], AF.Sigmoid, bias=br),
                    [prk, "dbr"], ["ra"])
                pi_, pik = self.psum()
                P.mm(pi_[:, 0:nn], wbd[d][1], xcb[:, a:e_], reads=[("wbd", d, 1), "xcb"], writes=[pik])
                P.S(lambda e, pi_=pi_, a=a, e_=e_, nn=nn, bi=bi: e.activation(ia[:, a:e_], pi_[:, 0:nn], AF.Sigmoid, bias=bi),
                    [pik, "dbi"], ["ia"])
            P.S(lambda e, c1=c1: e.activation(aa, ra, AF.Exp, scale=c1), ["ra", "c1"], ["aa"])
            P.V(lambda e: e.tensor_tensor(ra, aa, aa, ALU.mult), ["aa", "ra"], ["ra"])
            P.S(lambda e: e.activation(ra, ra, AF.Sqrt, bias=self.one_t, scale=-1.0), ["ra", "eps_t"], ["ra"])
            P.V(lambda e: e.tensor_tensor(uu, ia, xc, ALU.mult), ["ia", "xc"], ["uu"])
            P.V(lambda e: e.tensor_tensor(uu, uu, ra, ALU.mult), ["uu", "ra"], ["uu"])
            if d == 0:
                P.V(lambda e: e.tensor_tensor_scan(hsum, aa, uu, 0.0, ALU.mult, ALU.add), ["aa", "uu"], ["hsum"])
            else:
                rv = lambda t_, a, e_: t_[:, a:e_][:, ::-1]
                P.V(lambda e: e.tensor_tensor_scan(rv(hs, 0, CTX), rv(aa, 0, CTX), rv(uu, 0, CTX), 0.0, ALU.mult, ALU.add),
                    ["aa", "uu"], ["hs"])
                P.V(lambda e: e.tensor_tensor_scan(rv(hs, CTX, T), rv(aa, CTX, T), rv(uu, CTX, T), hs[:, 0:1],
                                                   ALU.mult, ALU.add), ["aa", "uu", "hs"], ["hs"], force=True)
                P.V(lambda e: e.tensor_tensor(hsum, hsum, hs, ALU.add), ["hsum", "hs"], ["hsum"])
        P.S(lambda e: e.activation(x, gate, AF.Square), ["gate", "x"], ["x"])
        P.V(lambda e: e.tensor_scalar(x, x, 0.044715, 1.0, ALU.mult, ALU.add), ["x"], ["x"])
        P.V(lambda e: e.tensor_tensor(x, x, gate, ALU.mult), ["x", "gate"], ["x"])
        P.S(lambda e: e.activation(x, x, AF.Sigmoid, scale=1.5957691216057308), ["x"], ["x"])
        P.V(lambda e: e.tensor_tensor(x, x, gate, ALU.mult), ["x", "gate"], ["x"])
        P.V(lambda e: e.tensor_tensor(outb, x, hsum, ALU.mult), ["x", "hsum"], ["outb"])
        P.dma("sync", self.Mx[1536 + 128 * cc:1536 + 128 * cc + 128, :], outb, reads=["outb"], writes=[("Mx", 1536 + 128 * cc)])
        P.release(mk)

    def emit_pass(self, which, xT=None, out=None):
        P = self.P
        mk = P.mark()
        self.alloc_tile_bufs()
        h = self.h
        for tt in range(NTT):
            segs = tile_segs(tt)
            t0 = tt * TT
            if which == 0:
                P.dma("sync", h, xT[:, :, t0:t0 + TT], writes=["h"])
                self.emit_ffn_tile(0, 0, 0, segs)
                nl = 0
            elif which == 1:
                P.dma("sync", h, self.H[:, :, t0:t0 + TT], reads=[("H", tt)], writes=["h"])
                self.emit_wout_tile(0, tt, segs)
                self.emit_ffn_tile(0, 1, 2, segs)
                self.emit_ffn_tile(1, 0, 0, segs)
                nl = 1
            else:
                P.dma("sync", h, self.H[:, :, t0:t0 + TT], reads=[("H", tt)], writes=["h"])
                self.emit_wout_tile(1, tt, segs)
                self.emit_ffn_tile(1, 1, 2, segs)
                if tt == 0:
                    P.dma("sync", out[:, :, 0:128], h[:, :, 256:384], reads=["h"])
                else:
                    o0 = 128 + (tt - 1) * TT
                    P.dma("sync", out[:, :, o0:o0 + TT], h, reads=["h"])
                continue
            self.emit_norm_mod(nl, 1, h, "h", self.u, "u", segs)
            P.dma("sync", self.U[:, :, t0:t0 + TT], self.u, reads=[("u", k) for k in range(KC)], writes=[("U", tt)])
            P.dma("sync", self.H[:, :, t0:t0 + TT], h, reads=["h"], writes=[("H", tt)])
        P.release(mk)

    def emit_mixer(self, l, parts="ABCD"):
        self.emit_proj(l)
        if "A" in parts:
            for h in range(0, 8, 2):
                self.emit_gla(l, "A", [h, h + 1])
        if "B" in parts:
            for h in range(0, 8, 2):
                self.emit_gla(l, "B", [h, h + 1])
        if "C" in parts:
            for h in range(4):
                self.emit_diffattn(l, h, l == 0)
        if "D" in parts:
            for cc in range(4):
                self.emit_rglru(l, cc)


def build_full(gathered=True):
    nc = bass.Bass("TRN2", target_bir_lowering=False)
    C = Core(nc, gathered)
    if gathered:
        C.gather_weights()
    C.load_small()
    C.load_mixer_params()
    C.alloc_scratch()
    xT = C.din("xT", [128, KC, T])
    out = C.dout("out", [128, KC, SEQ])
    C.emit_mods(0)
    C.emit_mods(1)
    C.emit_pass(0, xT=xT)
    C.emit_mixer(0)
    C.emit_pass(1)
    C.emit_mixer(1)
    C.emit_pass(2, out=out)
    C.P.emit()
    return nc, C


def build_pass0_test():
    nc = bass.Bass("TRN2", target_bir_lowering=False)
    C = Core(nc, False)
    C.test_one = True
    C.load_small()
    C.H = C.dout("H_out", [128, KC, T])
    C.U = C.dout("U_out", [128, KC, T], BF16)
    xT = C.din("xT", [128, KC, T])
    C.emit_mods(0)
    modo = C.dout("mod_out", [128, 144, 2])
    C.P.dma("sync", modo, C.mod[0], reads=[("mod", 0)])
    C.emit_pass(0, xT=xT)
    C.P.emit()
    return nc, C


def build_mixer_test(l, parts):
    nc = bass.Bass("TRN2", target_bir_lowering=False)
    C = Core(nc, False)
    C.load_mixer_params()
    C.alloc_scratch()
    uin = C.din("U_in", [128, KC, T], BF16)
    mout = C.dout("Mx_out", [2048, T], BF16)
    C.P.dma("sync", C.U, uin, writes=["U"])
    C.P.barrier()
    C.emit_mixer(l, parts)
    C.P.dma("sync", mout, C.Mx, reads=[("Mx", r) for r in range(0, 2048, 64)])
    C.P.emit()
    return nc, C


def fm(a):
    a = np.asarray(a, np.float32)
    lead = a.shape[:-1]
    r = a.reshape(lead + (a.shape[-1] // 128, 128))
    return np.ascontiguousarray(np.moveaxis(r, -1, 0))


def tokens_T(x, ctx, b):
    t = np.concatenate([ctx[b], x[b]], 0)
    return np.ascontiguousarray(t.T.reshape(KC, 128, T).transpose(1, 0, 2))


def rope_tables():
    quarter = 16
    inv = 10000.0 ** (-np.arange(quarter, dtype=np.float32) / quarter)
    rows = SEQ // 64
    row = np.repeat(np.arange(rows, dtype=np.float32), 64)
    col = np.tile(np.arange(64, dtype=np.float32), rows)
    ang = np.concatenate([row[:, None] * inv, col[:, None] * inv], -1).astype(np.float32)
    cos = np.cos(ang).T
    sin = np.sin(ang).T
    c2 = np.ones((64, T), np.float32)
    s2 = np.zeros((64, T), np.float32)
    c2[0:32, CTX:] = cos
    c2[32:64, CTX:] = cos
    s2[0:32, CTX:] = -sin
    s2[32:64, CTX:] = sin
    return c2, s2


def small_inputs(inp, b):
    f32 = lambda a: np.asarray(a, np.float32)
    d = {}
    d["gpre"] = fm(inp["norm_pre"]).reshape(128, -1)
    d["gpost"] = fm(inp["norm_post"]).reshape(128, -1)
    d["bada"] = np.ascontiguousarray(f32(inp["b_ada"]).reshape(2, 144, 128).transpose(2, 0, 1)).reshape(128, -1)
    d["cvec"] = fm(np.stack([f32(inp["c"])[b], f32(inp["c_ctx"])], 0)).reshape(128, -1)
    return d


def mixer_inputs(inp):
    f32 = lambda a: np.asarray(a, np.float32)
    d = {}
    d["lbl"] = np.ascontiguousarray(f32(inp["lb_logits"]).reshape(2, 2, 8, 64).transpose(3, 0, 1, 2)).reshape(64, -1)
    d["anorm"] = np.ascontiguousarray(f32(inp["a_norm"]).T)
    d["bnorm"] = np.ascontiguousarray(f32(inp["b_norm"]).T)
    d["cnorm"] = np.ascontiguousarray(f32(inp["c_norm"]).T)
    d["clam"] = np.ascontiguousarray(f32(inp["c_lambda"]).transpose(2, 0, 1)).reshape(64, -1)
    d["convw"] = np.ascontiguousarray(f32(inp["d_conv_w"]).reshape(2, 4, 4, 128).transpose(3, 0, 2, 1)).reshape(128, -1)
    d["convb"] = np.ascontiguousarray(f32(inp["d_conv_b"]).reshape(2, 4, 128).transpose(2, 0, 1)).reshape(128, -1)
    for nm, src in (("dbr", "d_b_r"), ("dbi", "d_b_i"), ("dlam", "d_lambda")):
        d[nm] = np.ascontiguousarray(f32(inp[src]).reshape(2, 2, 4, 128).transpose(3, 0, 1, 2)).reshape(128, -1)
    c2, s2 = rope_tables()
    d["rope_cos"] = c2
    d["rope_sin"] = s2
    d["d_w_r"] = f32(inp["d_w_r"])
    d["d_w_i"] = f32(inp["d_w_i"])
    return d


def weight_shards(inp, r):
    f32 = lambda a: np.asarray(a, np.float32)
    rs = slice(256 * r, 256 * r + 256)
    p0, p1 = [], []
    for l in range(2):
        for f in range(2):
            p0.append(f32(inp["ffn_w_in"])[l, f, rs, :].reshape(-1))
    for l in range(2):
        p0.append(f32(inp["w_in"])[l, rs, :].reshape(-1))
    for l in range(2):
        p1.append(f32(inp["w_ada"])[l, rs, :].reshape(-1))
    for l in range(2):
        for f in range(2):
            p1.append(np.ascontiguousarray(f32(inp["ffn_w_out"])[l, f, :, rs]).reshape(-1))
    for l in range(2):
        p1.append(f32(inp["w_out"])[l, rs, :].reshape(-1))
    w0 = np.concatenate(p0)
    w1 = np.concatenate(p1)
    assert w0.size == NSH0 and w1.size == NSH1
    return w0.reshape(NSH0 // 2048, 2048), w1.reshape(NSH1 // 2048, 2048)


_CACHE = {}


USE_GATHER = False
N_USED = 4


def kernel(**inp):
    if "nc" not in _CACHE:
        _CACHE["nc"] = build_full(USE_GATHER)[0]
    nc = _CACHE["nc"]
    f32 = lambda a: np.ascontiguousarray(np.asarray(a, np.float32))
    mi = mixer_inputs(inp)
    shared = {}
    if not USE_GATHER:
        shared = dict(w_ada=f32(inp["w_ada"]), ffn_w_in=f32(inp["ffn_w_in"]).reshape(4, D, 2 * DFF),
                      ffn_w_out=f32(inp["ffn_w_out"]).reshape(4, DFF, D), w_in=f32(inp["w_in"]), w_out=f32(inp["w_out"]))
    in_maps = []
    for c in range(N_USED):
        b = c % 4
        m = dict(mi)
        m.update(shared)
        m.update(small_inputs(inp, b))
        m["xT"] = tokens_T(np.asarray(inp["x"], np.float32), np.asarray(inp["ctx"], np.float32), b)
        if USE_GATHER:
            m["wshard0"], m["wshard1"] = weight_shards(inp, c)
        in_maps.append(m)
    res = run_bass_kernel_spmd(nc, in_maps, core_ids=list(range(N_USED)))
    outs = []
    for b in range(4):
        o = np.asarray(res.results[b]["out"])
        outs.append(o.transpose(2, 1, 0).reshape(SEQ, D))
    return np.stack(outs, 0).astype(np.float32)
```

```python
import math
import numpy as np
import ml_dtypes
import concourse.bass as bass
import concourse.mybir as mybir
from concourse.bass_utils import run_bass_kernel_spmd

F32 = mybir.dt.float32
BF16 = mybir.dt.bfloat16
AF = mybir.ActivationFunctionType
ALU = mybir.AluOpType

N_DMA_SEMS = 8
D = 2048
KC = 16
DFF = 5504
NJ = 43
T = 2304
TT = 384
NTT = 6
CTX = 256
SEQ = 2048
EPS = 1e-6
CH = 32
NCH = T // CH
INW = 7168
NCORE = 8
SZ_ADA = 256 * 9 * D
SZ_FIN = 256 * 2 * DFF
SZ_FOUT = DFF * 256
SZ_WIN = 256 * INW
SZ_WOUT = 256 * D
OFF_FIN = 0
OFF_WIN = OFF_FIN + 4 * SZ_FIN
NSH0 = OFF_WIN + 2 * SZ_WIN
OFF_ADA = 0
OFF_FOUT = OFF_ADA + 2 * SZ_ADA
OFF_WOUT = OFF_FOUT + 4 * SZ_FOUT
NSH1 = OFF_WOUT + 2 * SZ_WOUT
NSHG = (NSH0, NSH1)
NSH = NSH0 + NSH1
assert NSH % 2048 == 0


class Prog:
    def __init__(self, nc, arena_bytes):
        self.nc = nc
        self.ops = []
        self.lastw = {}
        self.readers = {}
        self.arena = nc.alloc_sbuf_tensor("arena", [128, arena_bytes // 4], F32)
        self.arena_bytes = arena_bytes
        self.off = 0
        self.hw = 0
        self.last_on = {}
        self.dma_hist = {}

    def alloc(self, shape, dtype=F32):
        esz = 4 if dtype == F32 else 2
        n = 1
        for s in shape[1:]:
            n *= s
        nbytes = (n * esz + 31) // 32 * 32
        assert self.off + nbytes <= self.arena_bytes, ("SBUF arena overflow", self.off, nbytes)
        a = self.arena[:, self.off // 4:(self.off + nbytes) // 4]
        if dtype != F32:
            a = a.bitcast(dtype)
        a = a[0:shape[0], 0:n]
        if len(shape) == 3:
            a = a.rearrange("p (a b) -> p a b", a=shape[1])
        elif len(shape) == 4:
            a = a.rearrange("p (a b c) -> p a b c", a=shape[1], b=shape[2])
        self.off += nbytes
        self.hw = max(self.hw, self.off)
        return a

    def mark(self):
        return self.off

    def release(self, m):
        self.barrier()
        self.off = m

    def _expand(self, keys):
        g = getattr(self, "groups", None)
        if not g:
            return keys
        out = []
        for k in keys:
            out.extend(g.get(k, (k,)))
        return out

    def op(self, eng, fn, reads=(), writes=(), dma=False, force=False):
        reads = self._expand(reads)
        writes = self._expand(writes)
        i = len(self.ops)
        deps = set()
        for r in reads:
            if r in self.lastw:
                deps.add(self.lastw[r])
        for w in writes:
            if w in self.lastw:
                deps.add(self.lastw[w])
            for rd in self.readers.get(w, ()):
                deps.add(rd)
        for r in reads:
            self.readers.setdefault(r, []).append(i)
        for w in writes:
            self.lastw[w] = i
            self.readers[w] = []
        self.ops.append(dict(eng=eng, fn=fn, deps=deps, dma=dma, force=(force or getattr(self, "force_all", False))))
        if dma:
            self.dma_hist.setdefault(eng, []).append(i)
        else:
            self.last_on[eng] = i
        return i

    def barrier(self):
        last = [v for v in self.last_on.values()]
        dm = []
        for q, h in self.dma_hist.items():
            dm += h[-N_DMA_SEMS:]
        engs = set(self.last_on.keys()) | set(self.dma_hist.keys()) | {"tensor", "vector", "scalar", "gpsimd", "sync"}
        for e in sorted(engs):
            i = len(self.ops)
            self.ops.append(dict(eng=e, fn=lambda en: en.nop(), deps=set(last) | set(dm), dma=False))
            self.last_on[e] = i
        self.lastw = {}
        self.readers = {}

    def mm(self, out, lhsT, rhs, start=True, stop=True, reads=(), writes=()):
        return self.op("tensor", lambda e: e.matmul(out, lhsT, rhs, start=start, stop=stop), reads, writes)

    def dma(self, q, out, in_, reads=(), writes=(), **kw):
        return self.op(q, lambda e: e.dma_start(out=out, in_=in_, **kw), reads, writes, dma=True)

    def V(self, fn, reads=(), writes=(), force=False):
        return self.op("vector", fn, reads, writes, force=force)

    def S(self, fn, reads=(), writes=()):
        return self.op("scalar", fn, reads, writes)

    def G(self, fn, reads=(), writes=()):
        return self.op("gpsimd", fn, reads, writes)

    def emit(self):
        nc = self.nc
        ops = self.ops
        need_sig = [False] * len(ops)
        for i, o in enumerate(ops):
            for d in o["deps"]:
                pd = ops[d]
                if pd["dma"]:
                    continue
                if pd["eng"] != o["eng"] or o["dma"] or o.get("force"):
                    need_sig[d] = True
        used = sorted({o["eng"] for o in ops})
        esem = {e: nc.alloc_semaphore("es_" + e) for e in used}
        dq = [e for e in used if any(o["dma"] and o["eng"] == e for o in ops)]
        dsems = {e: [nc.alloc_semaphore("ds_%s%d" % (e, k)) for k in range(N_DMA_SEMS)] for e in dq}
        cnt = {e: 0 for e in used}
        dcnt = {e: 0 for e in used}
        for i, o in enumerate(ops):
            if o["dma"]:
                k = dcnt[o["eng"]]
                dcnt[o["eng"]] += 1
                o["dsem"] = dsems[o["eng"]][k % N_DMA_SEMS]
                o["dval"] = 16 * (k // N_DMA_SEMS + 1)
            elif need_sig[i]:
                cnt[o["eng"]] += 1
                o["sig"] = cnt[o["eng"]]
        streams = {e: [] for e in used}
        for i, o in enumerate(ops):
            streams[o["eng"]].append(i)
        self.n_waits = 0

        def run_engine(ename, eng):
            waited = {}

            def wait(sem, val):
                key = id(sem)
                if waited.get(key, 0) >= val:
                    return
                waited[key] = val
                eng.wait_ge(sem, val)
                self.n_waits += 1

            for i in streams.get(ename, ()):
                o = ops[i]
                for d in sorted(o["deps"]):
                    pd = ops[d]
                    if pd["dma"]:
                        wait(pd["dsem"], pd["dval"])
                    elif "sig" in pd:
                        if pd["eng"] == ename and not o["dma"] and not o.get("force"):
                            continue
                        wait(esem[pd["eng"]], pd["sig"])
                if o["dma"]:
                    if o["dval"] > 16:
                        wait(o["dsem"], o["dval"] - 16)
                    o["fn"](eng).then_inc(o["dsem"], 16)
                else:
                    ins = o["fn"](eng)
                    if "sig" in o:
                        ins.then_inc(esem[ename], 1)
            if ename in dsems:
                done = {}
                for i in streams.get(ename, ()):
                    o = ops[i]
                    if o["dma"]:
                        done[id(o["dsem"])] = (o["dsem"], o["dval"])
                for sem, val in done.values():
                    wait(sem, val)

        with nc.Block() as block:
            if "sync" in streams:
                @block.sync
                def _(e):
                    run_engine("sync", e)
            if "tensor" in streams:
                @block.tensor
                def _(e):
                    run_engine("tensor", e)
            if "vector" in streams:
                @block.vector
                def _(e):
                    run_engine("vector", e)
            if "scalar" in streams:
                @block.scalar
                def _(e):
                    run_engine("scalar", e)
            if "gpsimd" in streams:
                @block.gpsimd
                def _(e):
                    run_engine("gpsimd", e)


def tile_segs(tt):
    if tt == 0:
        return [(0, 256, 1), (256, 384, 0)]
    return [(0, 384, 0)]


class Core:
    def __init__(self, nc, gathered):
        self.nc = nc
        self.gathered = gathered
        self.P = Prog(nc, 204 * 1024)
        P = self.P
        self.ps = [nc.alloc_psum_tensor("ps%d" % i, [128, 512], F32) for i in range(7)]
        self.psT = nc.alloc_psum_tensor("psT", [128, 1024], BF16)
        self.ps_rr = 0
        self.ext = {}
        P.force_all = True
        self.ones_bf = P.alloc([128, 128], BF16)
        self.ones1 = P.alloc([128, 128], BF16)
        self.ones64 = P.alloc([128, 128], BF16)
        self.ones128 = P.alloc([128, 128], BF16)
        P.G(lambda e: e.memset(self.ones_bf, 1.0 / D), [], ["c_ones"])
        P.G(lambda e: e.memset(self.ones1, 1.0), [], ["c_ones"])
        P.G(lambda e: e.memset(self.ones64, 1.0 / 64), [], ["c_ones"])
        P.G(lambda e: e.memset(self.ones128, 1.0 / 128), [], ["c_ones"])
        self.eps_t = P.alloc([128, 1])
        P.G(lambda e: e.memset(self.eps_t, EPS), [], ["eps_t"])
        self.one_t = P.alloc([128, 1])
        P.G(lambda e: e.memset(self.one_t, 1.0), [], ["eps_t"])
        self.ident = P.alloc([128, 128])
        self.ident_bf = P.alloc([128, 128], BF16)
        P.G(lambda e: e.memset(self.ident, 0.0), [], ["ident"])
        P.G(lambda e: e.affine_select(self.ident, self.ident, pattern=[[-1, 128]], compare_op=ALU.not_equal,
                                      fill=1.0, base=0, channel_multiplier=1), ["ident"], ["ident"])
        P.V(lambda e: e.tensor_copy(self.ident_bf, self.ident), ["ident"], ["ident_bf"])
        self.mask = [P.alloc([CH, CH]), P.alloc([CH, CH])]
        for d in range(2):
            m = self.mask[d]
            P.G(lambda e, m=m: e.memset(m, 1.0), [], [("mask", d)])
            pat, cm = ([[1, CH]], -1) if d == 0 else ([[-1, CH]], 1)
            P.G(lambda e, m=m, pat=pat, cm=cm: e.affine_select(m, m, pattern=pat, compare_op=ALU.is_ge, fill=0.0,
                                                              base=0, channel_multiplier=cm),
                [("mask", d)], [("mask", d)])
        P.force_all = False

    def din(self, name, shape, dtype=F32):
        if name not in self.ext:
            self.ext[name] = self.nc.dram_tensor(name, list(shape), dtype, kind="ExternalInput").ap()
        return self.ext[name]

    def dout(self, name, shape, dtype=F32):
        return self.nc.dram_tensor(name, list(shape), dtype, kind="ExternalOutput").ap()

    def dint(self, name, shape, dtype=F32, **kw):
        return self.nc.dram_tensor(name, list(shape), dtype, **kw).ap()

    def psum(self, lo=0, hi=6):
        i = lo + self.ps_rr % (hi - lo)
        self.ps_rr += 1
        return self.ps[i], ("ps", i)

    def gather_weights(self):
        P = self.P
        nc = self.nc
        self.G = []
        ccsem = nc.alloc_semaphore("ccsem")
        rg = [list(range(NCORE))]
        mk = P.mark()
        for gi, NS in enumerate(NSHG):
            R = NS // 2048
            wsh = self.din("wshard%d" % gi, [R, 2048])
            wbf = self.dint("wbf%d" % gi, [R, 2048], BF16)
            G = self.dint("wgath%d" % gi, [NCORE * R, 2048], BF16, addr_space="Shared")
            n = NS // 128
            cs = n // 32
            src = wsh.rearrange("a c -> (a c)").rearrange("(p n) -> p n", p=128)
            dst = wbf.rearrange("a c -> (a c)").rearrange("(p n) -> p n", p=128)
            bb = [P.alloc([128, cs], BF16) for _ in range(3)]
            for i in range(32):
                b, bk = bb[i % 3], ("castb", gi, i % 3)
                P.dma("gpsimd", b, src[:, i * cs:(i + 1) * cs], writes=[bk])
                P.dma("sync", dst[:, i * cs:(i + 1) * cs], b, reads=[bk], writes=[("wbf", gi)])

            def ccfn(e, wbf=wbf, G=G, gi=gi):
                e.collective_compute("AllGather", ALU.bypass, replica_groups=rg, ins=[wbf.opt()],
                                     outs=[G.opt()]).then_inc(ccsem)
                e.wait_ge(ccsem, gi + 1)
                return e.nop()
            P.op("gpsimd", ccfn, reads=[("wbf", gi)], writes=["wgath"])
            self.G.append(G.rearrange("(r a) c -> r (a c)", r=NCORE))
        P.release(mk)

    def _wmeta(self, name, idx):
        one = getattr(self, "test_one", False)
        if name == "w_ada":
            return 1, OFF_ADA + idx * SZ_ADA, 9 * D, [1 if one else 2, D, 9 * D]
        if name == "ffn_w_in":
            return 0, OFF_FIN + idx * SZ_FIN, 2 * DFF, [1 if one else 4, D, 2 * DFF]
        if name == "w_in":
            return 0, OFF_WIN + idx * SZ_WIN, INW, [2, D, INW]
        if name == "w_out":
            return 1, OFF_WOUT + idx * SZ_WOUT, D, [2, D, D]
        raise KeyError(name)

    def load_w(self, dst, name, idx, c0, ncols, key):
        P = self.P
        grp, off, cols, shp = self._wmeta(name, idx)
        if self.gathered:
            if not hasattr(P, "groups"):
                P.groups = {}
            P.groups[key] = [(key, "r", r) for r in range(NCORE)]
            for r in range(NCORE):
                src = self.G[grp][r:r + 1, off:off + 256 * cols].rearrange("o (kk p c) -> p (o kk) c", p=128, c=cols)[
                    :, :, c0:c0 + ncols]
                P.dma("gpsimd", dst[:, 2 * r:2 * r + 2, :], src, reads=["wgath"], writes=[(key, "r", r)])
        else:
            w = self.din(name, shp)
            src = w[idx].rearrange("(k p) c -> p k c", p=128)[:, :, c0:c0 + ncols]
            P.dma("gpsimd", dst, src, writes=[key])

    def load_w_fout(self, dst, idx, j0, j1, m, key):
        P = self.P
        if self.gathered:
            off = OFF_FOUT + idx * SZ_FOUT
            r = m // 2
            src = self.G[1][r:r + 1, off:off + SZ_FOUT].rearrange("o (j p c) -> p (o j) c", p=128, c=256)[
                :, j0:j1, (m % 2) * 128:(m % 2) * 128 + 128]
            P.dma("gpsimd", dst, src, reads=["wgath"], writes=[key])
        else:
            w = self.din("ffn_w_out", [1 if getattr(self, "test_one", False) else 4, DFF, D])
            src = w[idx].rearrange("(j p) c -> p j c", p=128)[:, j0:j1, m * 128:(m + 1) * 128]
            P.dma("gpsimd", dst, src, writes=[key])

    def load_small(self):
        P = self.P
        self.gpre = P.alloc([128, 2 * 3 * KC])
        self.gpost = P.alloc([128, 2 * 3 * KC])
        self.bada = P.alloc([128, 2 * 144])
        self.cs_f = P.alloc([128, 2 * KC])
        P.dma("sync", self.gpre, self.din("gpre", [128, 2 * 3 * KC]), writes=["gpre"])
        P.dma("sync", self.gpost, self.din("gpost", [128, 2 * 3 * KC]), writes=["gpost"])
        P.dma("sync", self.bada, self.din("bada", [128, 2 * 144]), writes=["bada"])
        P.dma("sync", self.cs_f, self.din("cvec", [128, 2 * KC]), writes=["cs_f"])
        self.csT = P.alloc([128, KC, 2], BF16)
        P.S(lambda e: e.activation(self.csT.rearrange("p k v -> p v k"),
                                   self.cs_f.rearrange("p (v k) -> p v k", v=2), AF.Silu),
            ["cs_f"], ["csT"])
        self.mod = [None, None]
        self.Apre = {}
        self.Cg = {}

    def emit_mods(self, l):
        P = self.P
        mod = P.alloc([128, 144, 2])
        self.mod[l] = mod
        for s in range(3):
            self.Apre[(l, s)] = P.alloc([128, KC, 2])
            self.Cg[(l, s)] = P.alloc([128, KC, 2])
        mk = P.mark()
        wb = [P.alloc([128, KC, 256], BF16) for _ in range(3)]
        pst, psk = self.ps[6], ("ps", 6)
        for g in range(72):
            b, bk = wb[g % 3], ("wada", g % 3)
            self.load_w(b, "w_ada", l, g * 256, 256, bk)
            for jj in range(2):
                cc = g * 2 + jj
                for k in range(KC):
                    P.mm(pst[:, cc * 2:cc * 2 + 2], b[:, k, jj * 128:(jj + 1) * 128], self.csT[:, k, :],
                         start=(k == 0), stop=(k == KC - 1), reads=[bk, "csT"], writes=[psk])
        P.force_all = True
        P.V(lambda e: e.tensor_tensor(mod, pst[:, 0:288].rearrange("p (c v) -> p c v", v=2),
                                      self.bada[:, l * 144:(l + 1) * 144].unsqueeze(2).to_broadcast([128, 144, 2]),
                                      ALU.add),
            [psk, "bada"], [("mod", l)])
        for s in range(3):
            A = self.Apre[(l, s)]
            C = self.Cg[(l, s)]
            sc = mod[:, (3 * s + 1) * KC:(3 * s + 2) * KC, :]
            gt = mod[:, (3 * s + 2) * KC:(3 * s + 3) * KC, :]
            o0 = (l * 3 + s) * KC
            gp = self.gpre[:, o0:o0 + KC].unsqueeze(2).to_broadcast([128, KC, 2])
            gq = self.gpost[:, o0:o0 + KC].unsqueeze(2).to_broadcast([128, KC, 2])
            P.V(lambda e, A=A, sc=sc, gp=gp: e.scalar_tensor_tensor(A, sc, 1.0, gp, ALU.add, ALU.mult),
                [("mod", l), "gpre"], [("A", l, s)])
            rs = 1.0 if s == 1 else 0.5
            P.V(lambda e, C=C, gt=gt, gq=gq, rs=rs: e.scalar_tensor_tensor(C, gt, rs, gq, ALU.mult, ALU.mult),
                [("mod", l), "gpost"], [("C", l, s)])
        P.force_all = False
        P.release(mk)

    def shift(self, l, s, k, v):
        return self.mod[l][:, 3 * s * KC + k, v:v + 1]

    def alloc_tile_bufs(self):
        P = self.P
        self.h = P.alloc([128, KC, TT])
        self.u = P.alloc([128, KC, TT], BF16)
        self.hT = P.alloc([128, NJ, TT], BF16)
        self.y = P.alloc([128, KC, TT])
        self.wb = [P.alloc([128, 4096], BF16) for _ in range(4)]
        self.wbi = 0
        self.sqb = [P.alloc([128, TT], BF16) for _ in range(2)]
        self.tmpf = [P.alloc([128, TT]) for _ in range(2)]
        self.rstd = P.alloc([128, TT])
        self.sg = [P.alloc([128, TT]) for _ in range(2)]

    def emit_norm_mod(self, l, s, src, src_key, dst, dst_key, segs):
        P = self.P
        pss, pssk = self.ps[6], ("ps", 6)
        rstd = self.rstd
        for k in range(KC):
            b, bk = self.sqb[k % 2], ("sqb", k % 2)
            P.S(lambda e, b=b, k=k: e.activation(b, src[:, k, :], AF.Square), [src_key], [bk])
            P.mm(pss[:, 0:TT], self.ones_bf, b, start=(k == 0), stop=(k == KC - 1),
                 reads=[bk, "c_ones"], writes=[pssk])
        P.S(lambda e: e.activation(rstd, pss[:, 0:TT], AF.Sqrt, bias=self.eps_t), [pssk, "eps_t"], ["rstd"])
        P.V(lambda e: e.reciprocal(rstd, rstd), ["rstd"], ["rstd"])
        A = self.Apre[(l, s)]
        for k in range(KC):
            t, tk = self.tmpf[k % 2], ("tmpf", k % 2)
            for (a, bnd, v) in segs:
                P.V(lambda e, t=t, k=k, a=a, bnd=bnd, v=v: e.scalar_tensor_tensor(
                    t[:, a:bnd], src[:, k, a:bnd], A[:, k, v:v + 1], rstd[:, a:bnd], ALU.mult, ALU.mult),
                    [src_key, "rstd", ("A", l, s)], [tk])
            for (a, bnd, v) in segs:
                P.S(lambda e, t=t, k=k, a=a, bnd=bnd, v=v: e.activation(
                    dst[:, k, a:bnd], t[:, a:bnd], AF.Identity, bias=self.shift(l, s, k, v)),
                    [tk, ("mod", l)], [dst_key])

    def emit_post_resid(self, l, s, segs):
        P = self.P
        pss, pssk = self.ps[6], ("ps", 6)
        rstd, h, y = self.rstd, self.h, self.y
        P.S(lambda e: e.activation(rstd, pss[:, 0:TT], AF.Sqrt, bias=self.eps_t), [pssk, "eps_t"], ["rstd"])
        P.V(lambda e: e.reciprocal(rstd, rstd), ["rstd"], ["rstd"])
        C = self.Cg[(l, s)]
        for k in range(KC):
            t, tk = self.tmpf[k % 2], ("tmpf", k % 2)
            for (a, bnd, v) in segs:
                P.V(lambda e, t=t, k=k, a=a, bnd=bnd, v=v: e.scalar_tensor_tensor(
                    t[:, a:bnd], y[:, k, a:bnd], C[:, k, v:v + 1], rstd[:, a:bnd], ALU.mult, ALU.mult),
                    [("y", k), "rstd", ("C", l, s)], [tk])
            P.V(lambda e, t=t, k=k: e.tensor_tensor(h[:, k, :], h[:, k, :], t, ALU.add), [tk, "h"], ["h"])

    def evac_y(self, m, py, pyk):
        P = self.P
        pss, pssk = self.ps[6], ("ps", 6)
        P.V(lambda e: e.tensor_copy(self.y[:, m, :], py[:, 0:TT]), [pyk], [("y", m)])
        import os
        if os.environ.get("KDBG2", "") == "ev1":
            return
        b2, b2k = self.sqb[m % 2], ("sqb", m % 2)
        P.S(lambda e: e.activation(b2, self.y[:, m, :], AF.Square), [("y", m)], [b2k])
        if os.environ.get("KDBG2", "") == "ev2":
            return
        P.mm(pss[:, 0:TT], self.ones_bf, b2, start=(m == 0), stop=(m == KC - 1),
             reads=[b2k, "c_ones"], writes=[pssk])

    def emit_ffn_tile(self, l, f, s, segs):
        P = self.P
        u, hT = self.u, self.hT
        wb = self.wb
        self.emit_norm_mod(l, s, self.h, "h", u, "u", segs)
        idx = l * 2 + f
        if not hasattr(self, "wc"):
            self.wc, self.wc_done = {}, set()
        if idx not in self.wc:
            self.wc[idx] = (self.dint("wci%d" % idx, [22, 2, 128, 4096], BF16),
                            self.dint("wco%d" % idx, [KC, 2, 128, 2816], BF16))
        wci, wco = self.wc[idx]
        cached = idx in self.wc_done
        import os
        dbg = int(os.environ.get("KDBG", "9"))
        if dbg < 2:
            return
        for g in range((NJ + 1) // 2):
            nj = 2 if 2 * g + 1 < NJ else 1
            bg, bu = wb[self.wbi % 4], wb[(self.wbi + 1) % 4]
            kg, ku = ("wb", self.wbi % 4), ("wb", (self.wbi + 1) % 4)
            self.wbi += 2
            bgv = bg[:, 0:KC * nj * 128].rearrange("p (k c) -> p k c", k=KC)
            buv = bu[:, 0:KC * nj * 128].rearrange("p (k c) -> p k c", k=KC)
            nel = KC * nj * 128
            if not cached:
                self.load_w(bgv, "ffn_w_in", idx, g * 256, nj * 128, kg)
                self.load_w(buv, "ffn_w_in", idx, DFF + g * 256, nj * 128, ku)
                P.dma("sync", wci[g, 0, :, 0:nel], bg[:, 0:nel], reads=[kg], writes=[("wc", idx, "i", g, 0)])
                P.dma("sync", wci[g, 1, :, 0:nel], bu[:, 0:nel], reads=[ku], writes=[("wc", idx, "i", g, 1)])
            else:
                P.dma("gpsimd", bg[:, 0:nel], wci[g, 0, :, 0:nel], reads=[("wc", idx, "i", g, 0)], writes=[kg])
                P.dma("gpsimd", bu[:, 0:nel], wci[g, 1, :, 0:nel], reads=[("wc", idx, "i", g, 1)], writes=[ku])
            for jj in range(nj):
                j = 2 * g + jj
                pg, pgk = self.psum()
                pu, puk = self.psum()
                for k in range(KC):
                    P.mm(pg[:, 0:TT], bgv[:, k, jj * 128:(jj + 1) * 128], u[:, k, :],
                         start=(k == 0), stop=(k == KC - 1), reads=[kg, "u"], writes=[pgk])
                for k in range(KC):
                    P.mm(pu[:, 0:TT], buv[:, k, jj * 128:(jj + 1) * 128], u[:, k, :],
                         start=(k == 0), stop=(k == KC - 1), reads=[ku, "u"], writes=[puk])
                sgb, sk = self.sg[j % 2], ("sg", j % 2)
                P.S(lambda e, sgb=sgb, pg=pg: e.activation(sgb, pg[:, 0:TT], AF.Silu), [pgk], [sk])
                P.V(lambda e, sgb=sgb, pu=pu, j=j: e.tensor_tensor(hT[:, j, :], sgb, pu[:, 0:TT], ALU.mult),
                    [sk, puk], [("hT", j)])
        if dbg < 3:
            return
        for m in range(KC):
            py, pyk = self.psum()
            for half in range(2):
                j0, j1 = (0, 22) if half == 0 else (22, NJ)
                b, bk = wb[self.wbi % 4], ("wb", self.wbi % 4)
                self.wbi += 1
                bv = b[:, 0:(j1 - j0) * 128].rearrange("p (j c) -> p j c", c=128)
                nel = (j1 - j0) * 128
                if not cached:
                    self.load_w_fout(bv, idx, j0, j1, m, bk)
                    P.dma("sync", wco[m, half, :, 0:nel], b[:, 0:nel], reads=[bk], writes=[("wc", idx, "o", m, half)])
                else:
                    P.dma("gpsimd", b[:, 0:nel], wco[m, half, :, 0:nel], reads=[("wc", idx, "o", m, half)], writes=[bk])
                if os.environ.get("KDBG2", "") == "dma":
                    continue
                for j in range(j0, j1):
                    P.mm(py[:, 0:TT], bv[:, j - j0, :], hT[:, j, :], start=(j == 0), stop=(j == NJ - 1),
                         reads=[bk, ("hT", j)], writes=[pyk])
            if os.environ.get("KDBG2", "") in ("dma", "mm"):
                continue
            self.evac_y(m, py, pyk)
        self.wc_done.add(idx)
        self.emit_post_resid(l, s, segs)

    def emit_wout_tile(self, l, tt, segs):
        P = self.P
        mixt = self.hT[:, 0:KC, :]
        t0 = tt * TT
        for k in range(KC):
            P.dma("sync", mixt[:, k, :], self.Mx[k * 128:(k + 1) * 128, t0:t0 + TT], reads=["Mx"], writes=[("hT", k)])
        for mp in range(KC // 2):
            b, bk = self.wb[self.wbi % 4], ("wb", self.wbi % 4)
            self.wbi += 1
            bv = b[:, 0:KC * 256].rearrange("p (k c) -> p k c", k=KC)
            self.load_w(bv, "w_out", l, mp * 256, 256, bk)
            for jj in range(2):
                m = 2 * mp + jj
                py, pyk = self.psum()
                for k in range(KC):
                    P.mm(py[:, 0:TT], bv[:, k, jj * 128:(jj + 1) * 128], mixt[:, k, :],
                         start=(k == 0), stop=(k == KC - 1), reads=[bk, ("hT", k)], writes=[pyk])
                self.evac_y(m, py, pyk)
        self.emit_post_resid(l, 1, segs)

    def load_mixer_params(self):
        P = self.P
        P.force_all = True
        ld = lambda name, shape: (P.alloc(shape), self.din(name, shape))
        def L(name, shape):
            t, src = ld(name, shape)
            P.dma("sync", t, src, writes=[name])
            return t
        self.lbl = L("lbl", [64, 2 * 2 * 8])
        self.anorm = L("anorm", [64, 2])
        self.bnorm = L("bnorm", [64, 2])
        self.cnorm = L("cnorm", [128, 2])
        self.clam = L("clam", [64, 2 * 4])
        self.convw = L("convw", [128, 2 * 4 * 4])
        self.convb = L("convb", [128, 2 * 4])
        self.dbr = L("dbr", [128, 2 * 2 * 4])
        self.dbi = L("dbi", [128, 2 * 2 * 4])
        self.dlam = L("dlam", [128, 2 * 2 * 4])
        self.cos_d = self.din("rope_cos", [64, T])
        self.sin_d = self.din("rope_sin", [64, T])
        self.lb1 = P.alloc([64, 16])
        self.omlb1 = P.alloc([64, 16])
        P.V(lambda e: e.tensor_tensor(self.lb1, self.lbl[:, 16:32], self.lbl[:, 0:16], ALU.subtract), ["lbl"], ["lb1"])
        P.S(lambda e: e.activation(self.lb1, self.lb1, AF.Sigmoid), ["lb1"], ["lb1"])
        P.V(lambda e: e.tensor_scalar(self.omlb1, self.lb1, -1.0, 1.0, ALU.mult, ALU.add), ["lb1"], ["omlb1"])
        self.zero_t = P.alloc([128, 1])
        P.G(lambda e: e.memset(self.zero_t, 0.0), [], ["eps_t"])
        self.cm = P.alloc([64, T])
        P.G(lambda e: e.memset(self.cm, 1.0), [], ["cm"])
        P.G(lambda e: e.memset(self.cm.rearrange("p (c i) -> p c i", i=CH)[:, :, 0:1], 0.0), ["cm"], ["cm"])
        self.neglam = P.alloc([128, 2])
        self.cgain = P.alloc([128, 2])
        prod = P.alloc([64, 2, 2])
        cl = self.clam.rearrange("p (l f) -> p l f", l=2)
        P.V(lambda e: e.tensor_tensor(prod[:, :, 0], cl[:, :, 0], cl[:, :, 1], ALU.mult), ["clam"], ["prod"])
        P.V(lambda e: e.tensor_tensor(prod[:, :, 1], cl[:, :, 2], cl[:, :, 3], ALU.mult), ["clam", "prod"], ["prod"])
        prodb = P.alloc([64, 4], BF16)
        P.V(lambda e: e.tensor_copy(prodb, prod.rearrange("p l f -> p (l f)")), ["prod"], ["prodb"])
        pl, plk = self.ps[6], ("ps", 6)
        P.mm(pl[:, 0:4], self.ones1[0:64, :], prodb, reads=["c_ones", "prodb"], writes=[plk])
        ex = P.alloc([128, 4])
        P.S(lambda e: e.activation(ex, pl[:, 0:4], AF.Exp), [plk], ["ex"])
        for l in range(2):
            li = 0.8 - 0.6 * math.exp(-0.3 * l)
            P.V(lambda e, l=l: e.tensor_tensor(self.neglam[:, l:l + 1], ex[:, 2 * l + 1:2 * l + 2], ex[:, 2 * l:2 * l + 1],
                                               ALU.subtract), ["ex"], ["neglam"])
            P.V(lambda e, l=l, li=li: e.tensor_scalar(self.neglam[:, l:l + 1], self.neglam[:, l:l + 1], -li, None, ALU.add),
                ["neglam"], ["neglam"])
            P.V(lambda e, l=l, li=li: e.tensor_scalar(self.cgain[:, l:l + 1], self.cnorm[:, l:l + 1], 1.0 - li, None, ALU.mult),
                ["cnorm"], ["cgain"])
        self.c1 = P.alloc([128, 16])
        P.S(lambda e: e.activation(self.c1, self.dlam, AF.Exp, scale=-1.0), ["dlam"], ["c1"])
        P.S(lambda e: e.activation(self.c1, self.c1, AF.Ln, bias=self.one_t), ["c1", "eps_t"], ["c1"])
        P.V(lambda e: e.tensor_scalar(self.c1, self.c1, -8.0, None, ALU.mult), ["c1"], ["c1"])
        P.force_all = False

    def alloc_scratch(self):
        self.H = self.dint("Hres", [128, KC, T])
        self.U = self.dint("Umix", [128, KC, T], BF16)
        self.Z = self.dint("Zfm", [INW, T])
        self.Zt = self.dint("Ztm", [T, 1536])
        self.Mx = self.dint("Mx", [2048, T], BF16)

    def emit_proj(self, l):
        P = self.P
        mk = P.mark()
        uf = P.alloc([128, KC, T], BF16)
        P.dma("sync", uf, self.U, reads=["U"], writes=["uf"])
        wb = [P.alloc([128, KC, 256], BF16) for _ in range(3)]
        ev = [P.alloc([128, 512]) for _ in range(4)]
        tbl = [(0, 512), (512, 1024), (1024, 1536), (1536, 2048), (2048, 2304)]
        n = 0
        for g in range(28):
            b, bk = wb[g % 3], ("pw", g % 3)
            self.load_w(b, "w_in", l, g * 256, 256, bk)
            for jj in range(2):
                r0 = g * 256 + jj * 128
                for (a, e_) in tbl:
                    nn = e_ - a
                    ps, psk = self.psum()
                    for k in range(KC):
                        P.mm(ps[:, 0:nn], b[:, k, jj * 128:(jj + 1) * 128], uf[:, k, a:e_],
                             start=(k == 0), stop=(k == KC - 1), reads=[bk, "uf"], writes=[psk])
                    evb, evk = ev[n % 4], ("ev", n % 4)
                    if n % 2 == 0:
                        P.V(lambda e, evb=evb, ps=ps, nn=nn: e.tensor_copy(evb[:, 0:nn], ps[:, 0:nn]), [psk], [evk])
                    else:
                        P.S(lambda e, evb=evb, ps=ps, nn=nn: e.copy(evb[:, 0:nn], ps[:, 0:nn]), [psk], [evk])
                    P.dma("sync", self.Z[r0:r0 + 128, a:e_], evb[:, 0:nn], reads=[evk], writes=[("Z", r0 // 128)])
                    n += 1
        for bi, blk in enumerate([3, 7, 11]):
            for half in range(2):
                b, bk = wb[n % 3], ("pw", n % 3)
                self.load_w(b, "w_in", l, blk * 512 + half * 256, 256, bk)
                for tb in range(T // 128):
                    ps, psk = self.psum()
                    for k in range(KC):
                        P.mm(ps[:, 0:256], uf[:, k, tb * 128:(tb + 1) * 128], b[:, k, :],
                             start=(k == 0), stop=(k == KC - 1), reads=[bk, "uf"], writes=[psk])
                    evb, evk = ev[n % 4], ("ev", n % 4)
                    if n % 2 == 0:
                        P.V(lambda e, evb=evb, ps=ps: e.tensor_copy(evb[:, 0:256], ps[:, 0:256]), [psk], [evk])
                    else:
                        P.S(lambda e, evb=evb, ps=ps: e.copy(evb[:, 0:256], ps[:, 0:256]), [psk], [evk])
                    c0 = bi * 512 + half * 256
                    P.dma("sync", self.Zt[tb * 128:(tb + 1) * 128, c0:c0 + 256], evb[:, 0:256], reads=[evk],
                          writes=[("Zt", bi, half, tb)])
                    n += 1
        P.release(mk)

    def emit_gla(self, l, kind, heads):
        P = self.P
        mk = P.mark()
        CH = 128 if kind == "B" else 32
        NCH = T // CH
        NCTX = CTX // CH
        Fa = lambda: P.alloc([64, T])
        Ba = lambda: P.alloc([64, T], BF16)
        q, kk, lf, bf, b, tmp = [Fa() for _ in range(6)]
        E = tmp
        kh = [Ba(), Ba()]
        Z, Zt = self.Z, self.Zt
        three = lambda a: a.rearrange("p (c i) -> p c i", i=CH)
        if CH == 32:
            cm, masks = self.cm, self.mask
        else:
            cm = Fa()
            P.op("gpsimd", lambda e: e.memset(cm, 1.0), [], ["cmL"], force=True)
            P.op("gpsimd", lambda e: e.memset(cm.rearrange("p (c i) -> p c i", i=CH)[:, :, 0:1], 0.0), ["cmL"], ["cmL"],
                 force=True)
            masks = [P.alloc([CH, CH]), P.alloc([CH, CH])]
            for d in range(2):
                m = masks[d]
                pat, cmul = ([[1, CH]], -1) if d == 0 else ([[-1, CH]], 1)
                P.op("gpsimd", lambda e, m=m: e.memset(m, 1.0), [], [("maskL", d)], force=True)
                P.op("gpsimd", lambda e, m=m, pat=pat, cmul=cmul: e.affine_select(
                    m, m, pattern=pat, compare_op=ALU.is_ge, fill=0.0, base=0, channel_multiplier=cmul),
                    [("maskL", d)], [("maskL", d)], force=True)
        LS = []
        for hi, h in enumerate(heads):
            ls = dict(qt=[Ba(), Ba()], kt=[Ba(), Ba()],
                      khT=[P.alloc([CH, NCH, 64], BF16) for _ in range(2)],
                      vtm=P.alloc([CH, NCH, 64], BF16), oacc=Fa(),
                      gdec=[P.alloc([64, NCH]) for _ in range(2)],
                      S32=[P.alloc([64, 64]) for _ in range(2)],
                      Sbf=[P.alloc([64, 64], BF16) for _ in range(2)],
                      sTb=[[P.alloc([CH, CH], BF16) for _ in range(2)] for _ in range(2)])
            LS.append(ls)
        for hi, h in enumerate(heads):
            ls = LS[hi]
            qt, kt, khT, vtm, oacc, gdec = ls["qt"], ls["kt"], ls["khT"], ls["vtm"], ls["oacc"], ls["gdec"]
            rows = lambda blk, h=h: Z[blk * 512 + 64 * h:blk * 512 + 64 * h + 64, :]

            def rows_sw(dst, blk, key, h=h):
                r0 = blk * 512 + 64 * h
                P.dma("sync", dst[0:32, :], Z[r0 + 32:r0 + 64, :], writes=[key])
                P.dma("sync", dst[32:64, :], Z[r0:r0 + 32, :], writes=[key])

            if kind == "A":
                P.dma("sync", q, rows(0), writes=["q"])
                P.dma("gpsimd", vtm, Zt[:, 64 * h:64 * h + 64].rearrange("(c p) d -> p c d", p=CH),
                      writes=[("vtm", hi)])
                P.S(lambda e: e.activation(q, q, AF.Silu), ["q"], ["q"])
            else:
                cos, sin = bf, b
                P.dma("sync", cos, self.cos_d, writes=["bf"])
                P.dma("sync", sin, self.sin_d, writes=["b"])
                P.dma("gpsimd", vtm, Zt[:, 512 + 64 * h:512 + 64 * h + 64].rearrange("(c p) d -> p c d", p=CH),
                      writes=[("vtm", hi)])
                for (dst, blk, sc) in ((q, 5, 1.0), (kk, 6, 0.125)):
                    dk = "q" if dst is q else "kk"
                    P.dma("sync", dst, rows(blk), writes=[dk])
                    rows_sw(tmp, blk, "tmp")
                    P.V(lambda e, dst=dst: e.tensor_tensor(dst, dst, cos, ALU.mult), [dk, "bf"], [dk])
                    P.V(lambda e: e.tensor_tensor(tmp, tmp, sin, ALU.mult), ["tmp", "b"], ["tmp"])
                    P.V(lambda e, dst=dst: e.tensor_tensor(dst, dst, tmp, ALU.add), [dk, "tmp"], [dk])
                    if sc != 1.0:
                        P.V(lambda e, dst=dst, sc=sc: e.tensor_scalar(dst, dst, sc, None, ALU.mult), [dk], [dk])
                lg = math.log1p(-2.0 ** (-5.0 - h))
            P.G(lambda e, oacc=oacc: e.memset(oacc, 0.0), [], [("oacc", hi)])
            for d in range(2):
                if kind == "A":
                    P.dma("sync", tmp, rows(1 + d), writes=["tmp"])
                    P.S(lambda e: e.activation(tmp, tmp, AF.Sigmoid), ["tmp"], ["tmp"])
                    if l == 0:
                        lbs, oms = self.zero_t[0:64, :], self.one_t[0:64, :]
                    else:
                        lbs, oms = self.lb1[:, d * 8 + h:d * 8 + h + 1], self.omlb1[:, d * 8 + h:d * 8 + h + 1]
                    P.V(lambda e, lbs=lbs, oms=oms: e.tensor_scalar(tmp, tmp, oms, lbs, ALU.mult, ALU.add),
                        ["tmp", "lb1", "omlb1", "eps_t"], ["tmp"])
                    P.V(lambda e: e.tensor_scalar(kk, tmp, -1.0, 1.0, ALU.mult, ALU.add), ["tmp"], ["kk"])
                    P.S(lambda e: e.activation(lf, tmp, AF.Ln), ["tmp"], ["lf"])
                elif d == 0:
                    P.G(lambda e, lg=lg: e.memset(lf, lg), [], ["lf"])
                P.V(lambda e: e.tensor_tensor_scan(bf, cm, lf, 0.0, ALU.mult, ALU.add), ["cm", "cmL", "lf"], ["bf"])
                btot = three(bf)[:, :, CH - 1:CH]
                btb = btot.to_broadcast([64, NCH, CH])
                if d == 0:
                    P.V(lambda e: e.tensor_copy(b, bf), ["bf"], ["b"])
                else:
                    P.V(lambda e, btb=btb: e.tensor_tensor(three(b), btb, three(bf), ALU.subtract), ["bf"], ["b"])
                    P.V(lambda e: e.tensor_tensor(b, b, lf, ALU.add), ["b", "lf"], ["b"])
                P.S(lambda e: e.activation(E, b, AF.Exp), ["b"], ["tmp"])
                P.V(lambda e, d=d, qt=qt: e.tensor_tensor(qt[d], q, E, ALU.mult), ["q", "tmp"], [("qt", hi, d)])
                P.S(lambda e: e.activation(E, b, AF.Exp, scale=-1.0), ["b"], ["tmp"])
                P.V(lambda e, d=d, kt=kt: e.tensor_tensor(kt[d], kk, E, ALU.mult), ["kk", "tmp"], [("kt", hi, d)])
                P.V(lambda e, btb=btb: e.tensor_tensor(three(E), btb, three(b), ALU.subtract), ["bf", "b"], ["tmp"])
                P.S(lambda e: e.activation(E, E, AF.Exp), ["tmp"], ["tmp"])
                P.V(lambda e, d=d: e.tensor_tensor(kh[d], kk, E, ALU.mult), ["kk", "tmp"], [("kh", d)])
                P.S(lambda e, d=d, gdec=gdec, btot=btot: e.activation(gdec[d], btot.rearrange("p c o -> p (c o)"), AF.Exp),
                    ["bf"], [("gdec", hi, d)])
                for c0 in range(0, NCH, 8):
                    ptb, ptk = self.psT, "psT"
                    nb = min(8, NCH - c0)
                    for c in range(c0, c0 + nb):
                        P.op("tensor", lambda e, d=d, c=c, c0=c0: e.transpose(
                            ptb[0:CH, (c - c0) * 64:(c - c0) * 64 + 64], kh[d][:, c * CH:(c + 1) * CH],
                            self.ident_bf[0:64, 0:64]), [("kh", d), "ident_bf"], [ptk])
                    P.V(lambda e, d=d, c0=c0, khT=khT, nb=nb: e.tensor_copy(
                        khT[d][:, c0:c0 + nb, :], ptb[0:CH, 0:nb * 64].rearrange("p (c k) -> p c k", k=64)),
                        [ptk], [("khT", hi, d)])
        order = [list(range(NCH)), list(range(NCTX - 1, -1, -1)) + list(range(NCH - 1, NCTX - 1, -1))]
        slot_w = 4 * CH
        nslots = 512 // slot_w
        nstep = 0
        for i in range(NCH):
            for hi in range(len(heads)):
                ls = LS[hi]
                qt, kt, khT, vtm, oacc, gdec = ls["qt"], ls["kt"], ls["khT"], ls["vtm"], ls["oacc"], ls["gdec"]
                S32, Sbf, sTb = ls["S32"], ls["Sbf"], ls["sTb"]
                for d in range(2):
                    c = order[d][i]
                    ts = slice(c * CH, (c + 1) * CH)
                    first = (i == 0)
                    bank, slot = nstep % 6, (nstep // 6) % nslots
                    nstep += 1
                    base = slot * slot_w
                    pbank = self.ps[bank]
                    pss_, psk = pbank[:, base:base + CH], ("psr", bank, slot, 0)
                    po, pok = pbank[:, base + CH:base + 2 * CH], ("psr", bank, slot, 1)
                    pu, puk = pbank[:, base + 2 * CH:base + 2 * CH + 64], ("psr", bank, slot, 2)
                    P.mm(pss_[0:CH, 0:CH], kt[d][:, ts], qt[d][:, ts], reads=[("kt", hi, d), ("qt", hi, d)], writes=[psk])
                    sT, sTk = sTb[d][i % 2], ("sT", hi, d, i % 2)
                    P.V(lambda e, sT=sT, pss_=pss_, d=d: e.tensor_tensor(sT, pss_[0:CH, 0:CH], masks[d], ALU.mult),
                        [psk, ("mask", d), ("maskL", d)], [sTk])
                    if not first:
                        P.mm(po[0:64, 0:CH], Sbf[d], qt[d][:, ts], start=True, stop=False,
                             reads=[("Sbf", hi, d), ("qt", hi, d)], writes=[pok])
                    P.mm(po[0:64, 0:CH], vtm[:, c, :], sT, start=first, stop=True, reads=[("vtm", hi), sTk], writes=[pok])
                    P.V(lambda e, po=po, ts=ts, oacc=oacc: e.tensor_tensor(oacc[:, ts], oacc[:, ts], po[0:64, 0:CH], ALU.add),
                        [pok, ("oacc", hi)], [("oacc", hi)])
                    P.mm(pu[0:64, 0:64], khT[d][:, c, :], vtm[:, c, :], reads=[("khT", hi, d), ("vtm", hi)], writes=[puk])
                    if first:
                        P.V(lambda e, d=d, pu=pu, S32=S32: e.tensor_copy(S32[d], pu[0:64, 0:64]), [puk], [("S32", hi, d)])
                    else:
                        P.V(lambda e, d=d, pu=pu, c=c, S32=S32, gdec=gdec: e.scalar_tensor_tensor(
                            S32[d], S32[d], gdec[d][:, c:c + 1], pu[0:64, 0:64], ALU.mult, ALU.add),
                            [puk, ("S32", hi, d), ("gdec", hi, d)], [("S32", hi, d)])
                    P.S(lambda e, d=d, S32=S32, Sbf=Sbf: e.copy(Sbf[d], S32[d]), [("S32", hi, d)], [("Sbf", hi, d)])
        sqb = [P.alloc([64, 512], BF16) for _ in range(2)]
        ob = [P.alloc([64, 512], BF16) for _ in range(2)]
        rs = [P.alloc([64, 512]) for _ in range(2)]
        zg = q
        for hi, h in enumerate(heads):
            oacc = LS[hi]["oacc"]
            if kind == "A":
                mrow, gain, gblk = 64 * h, self.anorm[:, l:l + 1], 4
            else:
                mrow, gain, gblk = 512 + 64 * h, self.bnorm[:, l:l + 1], 8
            P.dma("sync", zg, Z[gblk * 512 + 64 * h:gblk * 512 + 64 * h + 64, :], writes=["q"])
            P.S(lambda e: e.activation(zg, zg, AF.Silu), ["q"], ["q"])
            for bi_, (a, e_) in enumerate([(0, 512), (512, 1024), (1024, 1536), (1536, 2048), (2048, 2304)]):
                nn = e_ - a
                sq, sqk = sqb[bi_ % 2], ("gsq", bi_ % 2)
                P.S(lambda e, sq=sq, a=a, e_=e_, nn=nn, oacc=oacc: e.activation(sq[:, 0:nn], oacc[:, a:e_], AF.Square),
                    [("oacc", hi)], [sqk])
                pn, pnk = self.ps[6], ("ps", 6)
                P.mm(pn[0:64, 0:nn], self.ones64[0:64, 0:64], sq[:, 0:nn], reads=[sqk, "c_ones"], writes=[pnk])
                r, rk = rs[bi_ % 2], ("grs", bi_ % 2)
                P.S(lambda e, r=r, pn=pn, nn=nn: e.activation(r[:, 0:nn], pn[0:64, 0:nn], AF.Sqrt, bias=self.eps_t[0:64, :]),
                    [pnk, "eps_t"], [rk])
                P.V(lambda e, r=r, nn=nn: e.reciprocal(r[:, 0:nn], r[:, 0:nn]), [rk], [rk])
                P.V(lambda e, r=r, a=a, e_=e_, nn=nn, oacc=oacc, gain=gain: e.scalar_tensor_tensor(
                    r[:, 0:nn], oacc[:, a:e_], gain, r[:, 0:nn], ALU.mult, ALU.mult),
                    [rk, ("oacc", hi), "anorm", "bnorm"], [rk])
                o_, ok_ = ob[bi_ % 2], ("gob", bi_ % 2)
                P.V(lambda e, o_=o_, r=r, a=a, e_=e_, nn=nn: e.tensor_tensor(o_[:, 0:nn], r[:, 0:nn], zg[:, a:e_], ALU.mult),
                    [rk, "q"], [ok_])
                P.dma("sync", self.Mx[mrow:mrow + 64, a:e_], o_[:, 0:nn], reads=[ok_], writes=[("Mx", mrow)])
        P.release(mk)

    def emit_diffattn(self, l, h, ctx_out):
        P = self.P
        mk = P.mark()
        Z, Zt = self.Z, self.Zt
        t1 = P.alloc([128, T])
        t2 = P.alloc([128, T])
        cos = P.alloc([128, T])
        sin = P.alloc([128, T])
        qb = P.alloc([128, T], BF16)
        kb_ = P.alloc([128, T], BF16)
        vtm = P.alloc([128, T // 128, 128], BF16)
        for hf in range(2):
            P.dma("sync", cos[64 * hf:64 * hf + 64, :], self.cos_d, writes=["cos"])
            P.dma("sync", sin[64 * hf:64 * hf + 64, :], self.sin_d, writes=["sin"])
        P.dma("gpsimd", vtm, Zt[:, 1024 + 128 * h:1024 + 128 * h + 128].rearrange("(kb p) d -> p kb d", p=128),
              writes=["vtm"])
        for (dst, blk) in ((qb, 9), (kb_, 10)):
            r0 = blk * 512 + 128 * h
            P.dma("sync", t1, Z[r0:r0 + 128, :], writes=["t1"])
            for hf in range(2):
                P.dma("sync", t2[64 * hf:64 * hf + 32, :], Z[r0 + 64 * hf + 32:r0 + 64 * hf + 64, :], writes=["t2"])
                P.dma("sync", t2[64 * hf + 32:64 * hf + 64, :], Z[r0 + 64 * hf:r0 + 64 * hf + 32, :], writes=["t2"])
            P.V(lambda e: e.tensor_tensor(t1, t1, cos, ALU.mult), ["t1", "cos"], ["t1"])
            P.V(lambda e: e.tensor_tensor(t2, t2, sin, ALU.mult), ["t2", "sin"], ["t2"])
            dk = "qb" if dst is qb else "kb"
            P.V(lambda e, dst=dst: e.tensor_tensor(dst, t1, t2, ALU.add), ["t1", "t2"], [dk])
        pex = [P.alloc([128, 512], BF16) for _ in range(3)]
        rr = [P.alloc([128, 512]) for _ in range(2)]
        oo = [P.alloc([128, 512]) for _ in range(2)]
        sqo = P.alloc([128, 512], BF16)
        outb = [P.alloc([128, 512], BF16) for _ in range(2)]
        neglam = self.neglam[:, l:l + 1]
        cgain = self.cgain[:, l:l + 1]
        qblocks = [(CTX + 512 * i, 512, T // 128) for i in range(4)]
        if ctx_out:
            qblocks.append((0, CTX, CTX // 128))
        npx = 0
        for qi, (q0, nq, nkb) in enumerate(qblocks):
            accO = [(self.ps[2], ("ps", 2)), (self.ps[3], ("ps", 3))]
            accS = [(self.ps[4], ("ps", 4)), (self.ps[5], ("ps", 5))]
            for kbi in range(nkb):
                for hf in range(2):
                    pS, pSk = self.psum(0, 2)
                    P.mm(pS[:, 0:nq], kb_[64 * hf:64 * hf + 64, kbi * 128:(kbi + 1) * 128],
                         qb[64 * hf:64 * hf + 64, q0:q0 + nq], reads=["kb", "qb"], writes=[pSk])
                    px, pxk = pex[npx % 3], ("pex", npx % 3)
                    npx += 1
                    P.S(lambda e, px=px, pS=pS, nq=nq: e.activation(px[:, 0:nq], pS[:, 0:nq], AF.Exp, scale=0.125),
                        [pSk], [pxk])
                    P.mm(accO[hf][0][:, 0:nq], vtm[:, kbi, :], px[:, 0:nq], start=(kbi == 0), stop=(kbi == nkb - 1),
                         reads=["vtm", pxk], writes=[accO[hf][1]])
                    P.mm(accS[hf][0][:, 0:nq], self.ones1, px[:, 0:nq], start=(kbi == 0), stop=(kbi == nkb - 1),
                         reads=["c_ones", pxk], writes=[accS[hf][1]])
            for hf in range(2):
                r, rk = rr[hf], ("rr", hf)
                o, ok_ = oo[hf], ("oo", hf)
                P.V(lambda e, r=r, hf=hf, nq=nq, accS=accS: e.reciprocal(r[:, 0:nq], accS[hf][0][:, 0:nq]),
                    [accS[hf][1]], [rk])
                P.V(lambda e, r=r, o=o, hf=hf, nq=nq, accO=accO: e.tensor_tensor(o[:, 0:nq], accO[hf][0][:, 0:nq],
                                                                                r[:, 0:nq], ALU.mult),
                    [accO[hf][1], rk], [ok_])
            o = oo[0]
            P.V(lambda e, nq=nq: e.scalar_tensor_tensor(oo[0][:, 0:nq], oo[1][:, 0:nq], neglam, oo[0][:, 0:nq],
                                                        ALU.mult, ALU.add),
                [("oo", 0), ("oo", 1), "neglam"], [("oo", 0)])
            P.S(lambda e, nq=nq: e.activation(sqo[:, 0:nq], oo[0][:, 0:nq], AF.Square), [("oo", 0)], ["sqo"])
            pn, pnk = self.ps[6], ("ps", 6)
            P.mm(pn[:, 0:nq], self.ones128, sqo[:, 0:nq], reads=["sqo", "c_ones"], writes=[pnk])
            r = rr[0]
            P.S(lambda e, nq=nq: e.activation(rr[0][:, 0:nq], pn[:, 0:nq], AF.Sqrt, bias=self.eps_t), [pnk, "eps_t"],
                [("rr", 0)])
            P.V(lambda e, nq=nq: e.reciprocal(rr[0][:, 0:nq], rr[0][:, 0:nq]), [("rr", 0)], [("rr", 0)])
            ob, obk = outb[qi % 2], ("outb", qi % 2)
            P.V(lambda e, nq=nq, ob=ob: e.scalar_tensor_tensor(ob[:, 0:nq], oo[0][:, 0:nq], cgain, rr[0][:, 0:nq],
                                                               ALU.mult, ALU.mult),
                [("oo", 0), ("rr", 0), "cgain"], [obk])
            P.dma("sync", self.Mx[1024 + 128 * h:1024 + 128 * h + 128, q0:q0 + nq], ob[:, 0:nq], reads=[obk],
                  writes=[("Mx", 1024 + 128 * h)])
        P.release(mk)

    def emit_rglru(self, l, cc):
        P = self.P
        mk = P.mark()
        Z = self.Z
        Fa = lambda: P.alloc([128, T])
        x, gate, xc, ra, ia, aa, uu, hs, hsum = [Fa() for _ in range(9)]
        xcb = P.alloc([128, T], BF16)
        outb = P.alloc([128, T], BF16)
        P.dma("sync", x, Z[12 * 512 + 128 * cc:12 * 512 + 128 * cc + 128, :], writes=["x"])
        P.dma("sync", gate, Z[13 * 512 + 128 * cc:13 * 512 + 128 * cc + 128, :], writes=["gate"])
        cw = lambda j: self.convw[:, (l * 4 + cc) * 4 + j:(l * 4 + cc) * 4 + j + 1]
        cb = self.convb[:, l * 4 + cc:l * 4 + cc + 1]
        P.S(lambda e: e.activation(xc, x, AF.Identity, bias=cb, scale=cw(1)), ["x", "convw", "convb"], ["xc"])
        for (a, e_) in ((0, CTX), (CTX, T)):
            P.V(lambda e, a=a, e_=e_: e.scalar_tensor_tensor(xc[:, a + 1:e_], x[:, a:e_ - 1], cw(0), xc[:, a + 1:e_],
                                                            ALU.mult, ALU.add), ["x", "xc", "convw"], ["xc"])
            P.V(lambda e, a=a, e_=e_: e.scalar_tensor_tensor(xc[:, a:e_ - 1], x[:, a + 1:e_], cw(2), xc[:, a:e_ - 1],
                                                            ALU.mult, ALU.add), ["x", "xc", "convw"], ["xc"])
            P.V(lambda e, a=a, e_=e_: e.scalar_tensor_tensor(xc[:, a:e_ - 2], x[:, a + 2:e_], cw(3), xc[:, a:e_ - 2],
                                                            ALU.mult, ALU.add), ["x", "xc", "convw"], ["xc"])
        P.V(lambda e: e.tensor_copy(xcb, xc), ["xc"], ["xcb"])
        wbd = [[P.alloc([128, 128], BF16) for _ in range(2)] for _ in range(2)]
        dwr = self.din("d_w_r", [2, 2, 8, 64, 64])
        dwi = self.din("d_w_i", [2, 2, 8, 64, 64])
        for d in range(2):
            for ri, src in enumerate((dwr, dwi)):
                w = wbd[d][ri]
                wk = ("wbd", d, ri)
                P.G(lambda e, w=w: e.memset(w, 0.0), [], [wk])
                for g2 in range(2):
                    P.dma("gpsimd", w[64 * g2:64 * g2 + 64, 64 * g2:64 * g2 + 64], src[l, d, 2 * cc + g2], writes=[wk])
        tbl = [(0, 512), (512, 1024), (1024, 1536), (1536, 2048), (2048, 2304)]
        for d in range(2):
            pidx = (l * 2 + d) * 4 + cc
            br = self.dbr[:, pidx:pidx + 1]
            bi = self.dbi[:, pidx:pidx + 1]
            c1 = self.c1[:, pidx:pidx + 1]
            for (a, e_) in tbl:
                nn = e_ - a
                pr, prk = self.psum()
                P.mm(pr[:, 0:nn], wbd[d][0], xcb[:, a:e_], reads=[("wbd", d, 0), "xcb"], writes=[prk])
                P.S(lambda e, pr=pr, a=a, e_=e_, nn=nn, br=br: e.activation(ra[:, a:e_], pr[:, 0:nn], AF.Sigmoid, bias=br),
                    [prk, "dbr"], ["ra"])
                pi_, pik = self.psum()
                P.mm(pi_[:, 0:nn], wbd[d][1], xcb[:, a:e_], reads=[("wbd", d, 1), "xcb"], writes=[pik])
                P.S(lambda e, pi_=pi_, a=a, e_=e_, nn=nn, bi=bi: e.activation(ia[:, a:e_], pi_[:, 0:nn], AF.Sigmoid, bias=bi),
                    [pik, "dbi"], ["ia"])
            P.S(lambda e, c1=c1: e.activation(aa, ra, AF.Exp, scale=c1), ["ra", "c1"], ["aa"])
            P.V(lambda e: e.tensor_tensor(ra, aa, aa, ALU.mult), ["aa", "ra"], ["ra"])
            P.S(lambda e: e.activation(ra, ra, AF.Sqrt, bias=self.one_t, scale=-1.0), ["ra", "eps_t"], ["ra"])
            P.V(lambda e: e.tensor_tensor(uu, ia, xc, ALU.mult), ["ia", "xc"], ["uu"])
            P.V(lambda e: e.tensor_tensor(uu, uu, ra, ALU.mult), ["uu", "ra"], ["uu"])
            if d == 0:
                P.V(lambda e: e.tensor_tensor_scan(hsum, aa, uu, 0.0, ALU.mult, ALU.add), ["aa", "uu"], ["hsum"])
            else:
                rv = lambda t_, a, e_: t_[:, a:e_][:, ::-1]
                P.V(lambda e: e.tensor_tensor_scan(rv(hs, 0, CTX), rv(aa, 0, CTX), rv(uu, 0, CTX), 0.0, ALU.mult, ALU.add),
                    ["aa", "uu"], ["hs"])
                P.V(lambda e: e.tensor_tensor_scan(rv(hs, CTX, T), rv(aa, CTX, T), rv(uu, CTX, T), hs[:, 0:1],
                                                   ALU.mult, ALU.add), ["aa", "uu", "hs"], ["hs"], force=True)
                P.V(lambda e: e.tensor_tensor(hsum, hsum, hs, ALU.add), ["hsum", "hs"], ["hsum"])
        P.S(lambda e: e.activation(x, gate, AF.Square), ["gate", "x"], ["x"])
        P.V(lambda e: e.tensor_scalar(x, x, 0.044715, 1.0, ALU.mult, ALU.add), ["x"], ["x"])
        P.V(lambda e: e.tensor_tensor(x, x, gate, ALU.mult), ["x", "gate"], ["x"])
        P.S(lambda e: e.activation(x, x, AF.Sigmoid, scale=1.5957691216057308), ["x"], ["x"])
        P.V(lambda e: e.tensor_tensor(x, x, gate, ALU.mult), ["x", "gate"], ["x"])
        P.V(lambda e: e.tensor_tensor(outb, x, hsum, ALU.mult), ["x", "hsum"], ["outb"])
        P.dma("sync", self.Mx[1536 + 128 * cc:1536 + 128 * cc + 128, :], outb, reads=["outb"], writes=[("Mx", 1536 + 128 * cc)])
        P.release(mk)

    def emit_pass(self, which, xT=None, out=None):
        P = self.P
        mk = P.mark()
        self.alloc_tile_bufs()
        h = self.h
        for tt in range(NTT):
            segs = tile_segs(tt)
            t0 = tt * TT
            if which == 0:
                P.dma("sync", h, xT[:, :, t0:t0 + TT], writes=["h"])
                self.emit_ffn_tile(0, 0, 0, segs)
                nl = 0
            elif which == 1:
                P.dma("sync", h, self.H[:, :, t0:t0 + TT], reads=[("H", tt)], writes=["h"])
                self.emit_wout_tile(0, tt, segs)
                self.emit_ffn_tile(0, 1, 2, segs)
                self.emit_ffn_tile(1, 0, 0, segs)
                nl = 1
            else:
                P.dma("sync", h, self.H[:, :, t0:t0 + TT], reads=[("H", tt)], writes=["h"])
                self.emit_wout_tile(1, tt, segs)
                self.emit_ffn_tile(1, 1, 2, segs)
                if tt == 0:
                    P.dma("sync", out[:, :, 0:128], h[:, :, 256:384], reads=["h"])
                else:
                    o0 = 128 + (tt - 1) * TT
                    P.dma("sync", out[:, :, o0:o0 + TT], h, reads=["h"])
                continue
            self.emit_norm_mod(nl, 1, h, "h", self.u, "u", segs)
            P.dma("sync", self.U[:, :, t0:t0 + TT], self.u, reads=["u"], writes=[("U", tt)])
            P.dma("sync", self.H[:, :, t0:t0 + TT], h, reads=["h"], writes=[("H", tt)])
        P.release(mk)

    def emit_mixer(self, l, parts="ABCD"):
        self.emit_proj(l)
        if "A" in parts:
            for h in range(0, 8, 2):
                self.emit_gla(l, "A", [h, h + 1])
        if "B" in parts:
            for h in range(0, 8, 2):
                self.emit_gla(l, "B", [h, h + 1])
        if "C" in parts:
            for h in range(4):
                self.emit_diffattn(l, h, l == 0)
        if "D" in parts:
            for cc in range(4):
                self.emit_rglru(l, cc)


def build_full(gathered=True):
    nc = bass.Bass("TRN2", target_bir_lowering=False)
    C = Core(nc, gathered)
    if gathered:
        C.gather_weights()
    C.load_small()
    C.load_mixer_params()
    C.alloc_scratch()
    xT = C.din("xT", [128, KC, T])
    out = C.dout("out", [128, KC, SEQ])
    C.emit_mods(0)
    C.emit_mods(1)
    C.emit_pass(0, xT=xT)
    C.emit_mixer(0)
    C.emit_pass(1)
    C.emit_mixer(1)
    C.emit_pass(2, out=out)
    C.P.emit()
    return nc, C


def build_pass0_test():
    nc = bass.Bass("TRN2", target_bir_lowering=False)
    C = Core(nc, False)
    C.test_one = True
    C.load_small()
    C.H = C.dout("H_out", [128, KC, T])
    C.U = C.dout("U_out", [128, KC, T], BF16)
    xT = C.din("xT", [128, KC, T])
    C.emit_mods(0)
    modo = C.dout("mod_out", [128, 144, 2])
    C.P.dma("sync", modo, C.mod[0], reads=[("mod", 0)])
    C.emit_pass(0, xT=xT)
    C.P.emit()
    return nc, C


def build_mixer_test(l, parts):
    nc = bass.Bass("TRN2", target_bir_lowering=False)
    C = Core(nc, False)
    C.load_mixer_params()
    C.alloc_scratch()
    uin = C.din("U_in", [128, KC, T], BF16)
    mout = C.dout("Mx_out", [2048, T], BF16)
    C.P.dma("sync", C.U, uin, writes=["U"])
    C.P.barrier()
    C.emit_mixer(l, parts)
    C.P.dma("sync", mout, C.Mx, reads=[("Mx", r) for r in range(0, 2048, 64)])
    C.P.emit()
    return nc, C


def fm(a):
    a = np.asarray(a, np.float32)
    lead = a.shape[:-1]
    r = a.reshape(lead + (a.shape[-1] // 128, 128))
    return np.ascontiguousarray(np.moveaxis(r, -1, 0))


def tokens_T(x, ctx, b):
    t = np.concatenate([ctx[b], x[b]], 0)
    return np.ascontiguousarray(t.T.reshape(KC, 128, T).transpose(1, 0, 2))


def rope_tables():
    quarter = 16
    inv = 10000.0 ** (-np.arange(quarter, dtype=np.float32) / quarter)
    rows = SEQ // 64
    row = np.repeat(np.arange(rows, dtype=np.float32), 64)
    col = np.tile(np.arange(64, dtype=np.float32), rows)
    ang = np.concatenate([row[:, None] * inv, col[:, None] * inv], -1).astype(np.float32)
    cos = np.cos(ang).T
    sin = np.sin(ang).T
    c2 = np.ones((64, T), np.float32)
    s2 = np.zeros((64, T), np.float32)
    c2[0:32, CTX:] = cos
    c2[32:64, CTX:] = cos
    s2[0:32, CTX:] = -sin
    s2[32:64, CTX:] = sin
    return c2, s2


def small_inputs(inp, b):
    f32 = lambda a: np.asarray(a, np.float32)
    d = {}
    d["gpre"] = fm(inp["norm_pre"]).reshape(128, -1)
    d["gpost"] = fm(inp["norm_post"]).reshape(128, -1)
    d["bada"] = np.ascontiguousarray(f32(inp["b_ada"]).reshape(2, 144, 128).transpose(2, 0, 1)).reshape(128, -1)
    d["cvec"] = fm(np.stack([f32(inp["c"])[b], f32(inp["c_ctx"])], 0)).reshape(128, -1)
    return d


def mixer_inputs(inp):
    f32 = lambda a: np.asarray(a, np.float32)
    d = {}
    d["lbl"] = np.ascontiguousarray(f32(inp["lb_logits"]).reshape(2, 2, 8, 64).transpose(3, 0, 1, 2)).reshape(64, -1)
    d["anorm"] = np.ascontiguousarray(f32(inp["a_norm"]).T)
    d["bnorm"] = np.ascontiguousarray(f32(inp["b_norm"]).T)
    d["cnorm"] = np.ascontiguousarray(f32(inp["c_norm"]).T)
    d["clam"] = np.ascontiguousarray(f32(inp["c_lambda"]).transpose(2, 0, 1)).reshape(64, -1)
    d["convw"] = np.ascontiguousarray(f32(inp["d_conv_w"]).reshape(2, 4, 4, 128).transpose(3, 0, 2, 1)).reshape(128, -1)
    d["convb"] = np.ascontiguousarray(f32(inp["d_conv_b"]).reshape(2, 4, 128).transpose(2, 0, 1)).reshape(128, -1)
    for nm, src in (("dbr", "d_b_r"), ("dbi", "d_b_i"), ("dlam", "d_lambda")):
        d[nm] = np.ascontiguousarray(f32(inp[src]).reshape(2, 2, 4, 128).transpose(3, 0, 1, 2)).reshape(128, -1)
    c2, s2 = rope_tables()
    d["rope_cos"] = c2
    d["rope_sin"] = s2
    d["d_w_r"] = f32(inp["d_w_r"])
    d["d_w_i"] = f32(inp["d_w_i"])
    return d


def weight_shards(inp, r):
    f32 = lambda a: np.asarray(a, np.float32)
    rs = slice(256 * r, 256 * r + 256)
    p0, p1 = [], []
    for l in range(2):
        for f in range(2):
            p0.append(f32(inp["ffn_w_in"])[l, f, rs, :].reshape(-1))
    for l in range(2):
        p0.append(f32(inp["w_in"])[l, rs, :].reshape(-1))
    for l in range(2):
        p1.append(f32(inp["w_ada"])[l, rs, :].reshape(-1))
    for l in range(2):
        for f in range(2):
            p1.append(np.ascontiguousarray(f32(inp["ffn_w_out"])[l, f, :, rs]).reshape(-1))
    for l in range(2):
        p1.append(f32(inp["w_out"])[l, rs, :].reshape(-1))
    w0 = np.concatenate(p0)
    w1 = np.concatenate(p1)
    assert w0.size == NSH0 and w1.size == NSH1
    return w0.reshape(NSH0 // 2048, 2048), w1.reshape(NSH1 // 2048, 2048)


_CACHE = {}


USE_GATHER = False
N_USED = 4


def kernel(**inp):
    if "nc" not in _CACHE:
        _CACHE["nc"] = build_full(USE_GATHER)[0]
    nc = _CACHE["nc"]
    f32 = lambda a: np.ascontiguousarray(np.asarray(a, np.float32))
    mi = mixer_inputs(inp)
    shared = {}
    if not USE_GATHER:
        shared = dict(w_ada=f32(inp["w_ada"]), ffn_w_in=f32(inp["ffn_w_in"]).reshape(4, D, 2 * DFF),
                      ffn_w_out=f32(inp["ffn_w_out"]).reshape(4, DFF, D), w_in=f32(inp["w_in"]), w_out=f32(inp["w_out"]))
    in_maps = []
    for c in range(N_USED):
        b = c % 4
        m = dict(mi)
        m.update(shared)
        m.update(small_inputs(inp, b))
        m["xT"] = tokens_T(np.asarray(inp["x"], np.float32), np.asarray(inp["ctx"], np.float32), b)
        if USE_GATHER:
            m["wshard0"], m["wshard1"] = weight_shards(inp, c)
        in_maps.append(m)
    res = run_bass_kernel_spmd(nc, in_maps, core_ids=list(range(N_USED)))
    outs = []
    for b in range(4):
        o = np.asarray(res.results[b]["out"])
        outs.append(o.transpose(2, 1, 0).reshape(SEQ, D))
    return np.stack(outs, 0).astype(np.float32)
```
